# Optimizing a Trainium2 kernel written in Bass

```python
import math
import jax
import jax.numpy as jnp
from jax import lax
import numpy as np

D_MODEL = 1024
BATCH = 8
SEQ = 4096
DEPTH = 2

HEAD_DIM = 64
M_HEADS = 4
M_WIDTH = M_HEADS * HEAD_DIM
M_CONV = 4
M_CHUNK = 64
M_NORM_EPS = 1e-6
R_HEADS = 4
R_WIDTH = R_HEADS * HEAD_DIM
R_DECAY_LORA = 32
R_AAA_LORA = 32
R_GATE_LORA = 64
R_LN_EPS = 64e-5
A_HEADS = 4
A_QK_DIM = HEAD_DIM
A_V_DIM = 2 * HEAD_DIM
A_WIDTH = A_HEADS * A_V_DIM
A_BLOCK = 128
A_NORM_EPS = 1e-5

MIX_WIDTH = M_WIDTH + R_WIDTH + A_WIDTH
D_FF = 4 * D_MODEL
NORM_EPS = 1e-6

M_SEGS = (M_WIDTH, M_WIDTH, M_WIDTH, M_WIDTH, M_HEADS, M_HEADS)
R_SEGS = (R_WIDTH, R_WIDTH, R_WIDTH, R_DECAY_LORA, R_AAA_LORA, R_GATE_LORA)
A_SEGS = (A_HEADS * 2 * A_QK_DIM, A_HEADS * 2 * A_QK_DIM, A_WIDTH)
M_PROJ = sum(M_SEGS)
R_PROJ = sum(R_SEGS)
A_PROJ = sum(A_SEGS)
IN_PROJ = M_PROJ + R_PROJ + A_PROJ

kernel_name = 'hybrid_mlstm_rwkv7_diffattn'


def _split(t, sizes):
    idx = [int(i) for i in np.cumsum(sizes)[:-1]]
    return jnp.split(t, idx, axis=-1)


def _rms_norm(x, g, eps=NORM_EPS):
    xf = x.astype(jnp.float32)
    y = xf * lax.rsqrt(jnp.mean(jnp.square(xf), axis=-1, keepdims=True) + eps)
    return (y * g.astype(jnp.float32)).astype(x.dtype)


def _head_rms(h, g, eps):
    return h * lax.rsqrt(jnp.mean(jnp.square(h), axis=-1, keepdims=True) + eps) * g.astype(jnp.float32)


def _causal_dwconv(x, w, b):
    k = w.shape[0]
    y = lax.conv_general_dilated(x, w[:, None, :], window_strides=(1,), padding=[(k - 1, 0)],
                                 dimension_numbers=('NWC', 'WIO', 'NWC'),
                                 feature_group_count=x.shape[-1])
    return y + b


def _mlstm(q, k, v, o_pre, i_pre, f_pre, conv_w, conv_b, b_i, b_f, norm_g):
    B, S = q.shape[:2]
    qk = jax.nn.silu(_causal_dwconv(jnp.concatenate([q, k], axis=-1), conv_w, conv_b))
    q, k = jnp.split(qk.astype(jnp.float32), 2, axis=-1)

    def heads(t):
        return t.reshape(B, S, M_HEADS, HEAD_DIM).transpose(0, 2, 1, 3)

    q = heads(q)
    k = heads(k) * (HEAD_DIM ** -0.5)
    v = heads(v.astype(jnp.float32))
    ig = (i_pre.astype(jnp.float32) + b_i.astype(jnp.float32)).transpose(0, 2, 1)
    lf = jax.nn.log_sigmoid(f_pre.astype(jnp.float32) + b_f.astype(jnp.float32)).transpose(0, 2, 1)
    nc = S // M_CHUNK

    def chunks(t):
        t = t.reshape(B, M_HEADS, nc, M_CHUNK, *t.shape[3:])
        return jnp.moveaxis(t, 2, 0)

    tril = jnp.tril(jnp.ones((M_CHUNK, M_CHUNK), dtype=bool))

    def step(carry, xs):
        C, n, m = carry
        qc, kc, vc, igc, lfc = xs
        b = jnp.cumsum(lfc, axis=-1)
        D = jnp.where(tril, b[..., :, None] - b[..., None, :] + igc[..., None, :], -jnp.inf)
        inter = b + m[..., None]
        mt = jnp.maximum(inter, jnp.max(D, axis=-1))
        Dw = jnp.exp(D - mt[..., None])
        iw = jnp.exp(inter - mt)
        sqk = jnp.einsum('bhtd,bhsd->bhts', qc, kc) * Dw
        num = iw[..., None] * jnp.einsum('bhtd,bhde->bhte', qc, C) + jnp.einsum('bhts,bhse->bhte', sqk, vc)
        den = iw * jnp.einsum('bhtd,bhd->bht', qc, n) + jnp.sum(sqk, axis=-1)
        h = num / jnp.maximum(jnp.abs(den), jnp.exp(-mt))[..., None]
        bl = b[..., -1]
        g_end = bl[..., None] - b + igc
        m_new = jnp.maximum(bl + m, jnp.max(g_end, axis=-1))
        wk = jnp.exp(g_end - m_new[..., None])
        cs = jnp.exp(bl + m - m_new)
        C = cs[..., None, None] * C + jnp.einsum('bhs,bhsd,bhse->bhde', wk, kc, vc)
        n = cs[..., None] * n + jnp.einsum('bhs,bhsd->bhd', wk, kc)
        return (C, n, m_new), h

    init = (jnp.zeros((B, M_HEADS, HEAD_DIM, HEAD_DIM), jnp.float32),
            jnp.zeros((B, M_HEADS, HEAD_DIM), jnp.float32),
            jnp.zeros((B, M_HEADS), jnp.float32))
    _, h = lax.scan(step, init, (chunks(q), chunks(k), chunks(v), chunks(ig), chunks(lf)))
    h = jnp.moveaxis(h, 0, 2).reshape(B, M_HEADS, S, HEAD_DIM).transpose(0, 2, 1, 3)
    h = _head_rms(h, norm_g.reshape(M_HEADS, HEAD_DIM), M_NORM_EPS)
    return h.reshape(B, S, M_WIDTH) * jax.nn.sigmoid(o_pre.astype(jnp.float32))


def _rwkv7(p, mu, w0, w_up, a0, a_up, g_up, k_k, k_a, r_k, ln_g, ln_b):
    B, S = p.shape[:2]
    p = p.astype(jnp.float32)
    p_prev = jnp.pad(p, ((0, 0), (1, 0), (0, 0)))[:, :-1]
    p = p + (p_prev - p) * mu.astype(jnp.float32)
    r, k, v, wd, ad, gd = _split(p, R_SEGS)
    w_log = -jax.nn.softplus(-(w0 + jnp.tanh(wd) @ w_up.astype(jnp.float32))) - 0.5
    decay = jnp.exp(-jnp.exp(w_log))
    a = jax.nn.sigmoid(a0 + ad @ a_up.astype(jnp.float32))
    g = jax.nn.sigmoid(gd) @ g_up.astype(jnp.float32)

    def heads(t):
        return t.reshape(B, S, R_HEADS, HEAD_DIM)

    kk = heads(k * k_k)
    kk = kk / jnp.maximum(jnp.sqrt(jnp.sum(jnp.square(kk), axis=-1, keepdims=True)), 1e-12)
    k = k * (1.0 + (a - 1.0) * k_a)
    r, k, v, decay, a = heads(r), heads(k), heads(v), heads(decay), heads(a)

    def step(st, xs):
        rt, wt, kt, vt, kkt, at = xs
        sa = jnp.einsum('bhij,bhj->bhi', st, -kkt)
        st = (st * wt[:, :, None, :] + sa[..., :, None] * (kkt * at)[:, :, None, :]
              + vt[..., :, None] * kt[:, :, None, :])
        return st, jnp.einsum('bhij,bhj->bhi', st, rt)

    xs = tuple(jnp.moveaxis(t, 1, 0) for t in (r, decay, k, v, kk, a))
    _, y = lax.scan(step, jnp.zeros((B, R_HEADS, HEAD_DIM, HEAD_DIM), jnp.float32), xs)
    y = jnp.moveaxis(y, 0, 1)
    mean = jnp.mean(y, axis=-1, keepdims=True)
    var = jnp.mean(jnp.square(y - mean), axis=-1, keepdims=True)
    y = ((y - mean) * lax.rsqrt(var + R_LN_EPS) * ln_g.reshape(R_HEADS, HEAD_DIM)
         + ln_b.reshape(R_HEADS, HEAD_DIM))
    y = y + jnp.sum(r * k * r_k, axis=-1, keepdims=True) * v
    return y.reshape(B, S, R_WIDTH) * g


def _diff_attn(q, k, v, lq1, lk1, lq2, lk2, norm_g, lam_init):
    B, S = q.shape[:2]
    nb = S // A_BLOCK
    q = q.astype(jnp.float32).reshape(B, S, A_HEADS, 2, A_QK_DIM).transpose(0, 2, 3, 1, 4)
    k = k.astype(jnp.float32).reshape(B, S, A_HEADS, 2, A_QK_DIM).transpose(0, 2, 3, 1, 4)
    v = v.astype(jnp.float32).reshape(B, S, A_HEADS, A_V_DIM).transpose(0, 2, 1, 3)
    f32 = jnp.float32
    lam = (jnp.exp(jnp.sum(lq1.astype(f32) * lk1.astype(f32)))
           - jnp.exp(jnp.sum(lq2.astype(f32) * lk2.astype(f32))) + lam_init)
    q_blocks = jnp.moveaxis(q.reshape(B, A_HEADS, 2, nb, A_BLOCK, A_QK_DIM), 3, 0)
    key_pos = jnp.arange(S)
    scale = A_QK_DIM ** -0.5

    def block(args):
        qb, bi = args
        s = jnp.einsum('bhcqd,bhckd->bhcqk', qb, k) * scale
        q_pos = bi * A_BLOCK + jnp.arange(A_BLOCK)
        mask = key_pos[None, :] <= q_pos[:, None]
        pr = jax.nn.softmax(jnp.where(mask, s, -jnp.inf), axis=-1)
        attn = pr[:, :, 0] - lam * pr[:, :, 1]
        return jnp.einsum('bhqk,bhke->bhqe', attn, v)

    o = lax.map(block, (q_blocks, jnp.arange(nb)))
    o = jnp.moveaxis(o, 0, 2).reshape(B, A_HEADS, S, A_V_DIM).transpose(0, 2, 1, 3)
    o = _head_rms(o, norm_g.reshape(A_HEADS, A_V_DIM), A_NORM_EPS) * (1.0 - lam_init)
    return o.reshape(B, S, A_WIDTH)


def setup_inputs(seed: int = 0) -> dict:
    key = jax.random.key(seed)
    ks = list(jax.random.split(key, 32))
    L = DEPTH

    def nrm(i, shape, s):
        return jax.random.normal(ks[i], shape, jnp.float32) * s

    def uni(i, shape, lo, hi):
        return jax.random.uniform(ks[i], shape, jnp.float32, lo, hi)

    return {
        'x': nrm(0, (BATCH, SEQ, D_MODEL), 1.0),
        'norm1_g': 1.0 + nrm(1, (L, D_MODEL), 0.02),
        'w_in': nrm(2, (L, D_MODEL, IN_PROJ), D_MODEL ** -0.5),
        'm_conv_w': nrm(3, (L, M_CONV, 2 * M_WIDTH), M_CONV ** -0.5),
        'm_conv_b': nrm(4, (L, 2 * M_WIDTH), 0.02),
        'm_b_i': nrm(5, (L, M_HEADS), 0.1),
        'm_b_f': uni(6, (L, M_HEADS), 3.0, 6.0),
        'm_norm_g': 1.0 + nrm(7, (L, M_WIDTH), 0.02),
        'r_mu': uni(8, (L, R_PROJ), 0.0, 1.0),
        'r_w0': uni(9, (L, R_WIDTH), -6.0, -1.0),
        'r_w_up': nrm(10, (L, R_DECAY_LORA, R_WIDTH), R_DECAY_LORA ** -0.5),
        'r_a0': nrm(11, (L, R_WIDTH), 0.1),
        'r_a_up': nrm(12, (L, R_AAA_LORA, R_WIDTH), R_AAA_LORA ** -0.5),
        'r_g_up': nrm(13, (L, R_GATE_LORA, R_WIDTH), R_GATE_LORA ** -0.5),
        'r_k_k': 0.85 + nrm(14, (L, R_WIDTH), 0.02),
        'r_k_a': 1.0 + nrm(15, (L, R_WIDTH), 0.02),
        'r_r_k': nrm(16, (L, R_HEADS, HEAD_DIM), 0.1),
        'r_ln_g': 1.0 + nrm(17, (L, R_WIDTH), 0.02),
        'r_ln_b': nrm(18, (L, R_WIDTH), 0.02),
        'a_lq1': nrm(19, (L, A_QK_DIM), 0.1),
        'a_lk1': nrm(20, (L, A_QK_DIM), 0.1),
        'a_lq2': nrm(21, (L, A_QK_DIM), 0.1),
        'a_lk2': nrm(22, (L, A_QK_DIM), 0.1),
        'a_norm_g': 1.0 + nrm(23, (L, A_WIDTH), 0.02),
        'w_out': nrm(24, (L, MIX_WIDTH, D_MODEL), MIX_WIDTH ** -0.5),
        'norm2_g': 1.0 + nrm(25, (L, D_MODEL), 0.02),
        'w_ff_up': nrm(26, (L, D_MODEL, D_FF), D_MODEL ** -0.5),
        'w_ff_down': nrm(27, (L, D_FF, D_MODEL), D_FF ** -0.5),
        'final_g': 1.0 + nrm(28, (D_MODEL,), 0.02),
    }


def reference(x, norm1_g, w_in, m_conv_w, m_conv_b, m_b_i, m_b_f, m_norm_g,
              r_mu, r_w0, r_w_up, r_a0, r_a_up, r_g_up, r_k_k, r_k_a, r_r_k, r_ln_g, r_ln_b,
              a_lq1, a_lk1, a_lq2, a_lk2, a_norm_g, w_out, norm2_g, w_ff_up, w_ff_down, final_g):
    for l in range(DEPTH):
        h = _rms_norm(x, norm1_g[l])
        proj = h @ w_in[l]
        pm, pr, pa = _split(proj, (M_PROJ, R_PROJ, A_PROJ))
        mq, mk, mv, mo, mi, mf = _split(pm, M_SEGS)
        y_m = _mlstm(mq, mk, mv, mo, mi, mf, m_conv_w[l], m_conv_b[l], m_b_i[l], m_b_f[l], m_norm_g[l])
        y_r = _rwkv7(pr, r_mu[l], r_w0[l], r_w_up[l], r_a0[l], r_a_up[l], r_g_up[l],
                     r_k_k[l], r_k_a[l], r_r_k[l], r_ln_g[l], r_ln_b[l])
        aq, ak, av = _split(pa, A_SEGS)
        lam_init = 0.8 - 0.6 * math.exp(-0.3 * l)
        y_a = _diff_attn(aq, ak, av, a_lq1[l], a_lk1[l], a_lq2[l], a_lk2[l], a_norm_g[l], lam_init)
        mix = jnp.concatenate([y_m, y_r, y_a], axis=-1).astype(x.dtype)
        x = x + mix @ w_out[l]
        h = _rms_norm(x, norm2_g[l])
        x = x + jnp.square(jax.nn.relu(h @ w_ff_up[l])) @ w_ff_down[l]
    return _rms_norm(x, final_g)
```

```python
import math
from contextlib import ExitStack

import numpy as np
import concourse.bass as bass
import concourse.mybir as mybir
from concourse.bass_utils import run_bass_kernel_spmd

F32 = mybir.dt.float32
BF16 = mybir.dt.bfloat16
ALU = mybir.AluOpType
AF = mybir.ActivationFunctionType
AX = mybir.AxisListType

D = 1024
S = 4096
DEPTH = 2
NT = S // 128
IN_PROJ = 3464
DFF = 4096
NDS = 8
EPS = 1e-6


class Res:
    __slots__ = ("w", "r", "f", "name", "excl")

    def __init__(self, name="", excl=False):
        self.f = {}
        self.w = {}
        self.r = {}
        self.name = name
        self.excl = excl


def PRes():
    return Res("psum", True)


class Sched:
    def __init__(self, nc, es):
        self.nc = nc
        self.eng = {"pe": nc.tensor, "dve": nc.vector, "act": nc.scalar,
                    "pool": nc.gpsimd, "sp": nc.sync}
        self.semobj = {}
        self.cnt = {}
        for k in self.eng:
            self.semobj[k] = es.enter_context(nc.semaphore("s_" + k))
            self.cnt[k] = 0
        self.seen = {k: {} for k in self.eng}
        self.dcnt = {}
        self.dnext = {}
        for k in ("sp", "act", "pool"):
            self.dnext[k] = 0
            for i in range(NDS):
                key = "d_%s%d" % (k, i)
                self.semobj[key] = es.enter_context(nc.semaphore(key))
                self.dcnt[key] = 0

    def _wait(self, eng, key, val):
        if eng == "pe" and key == "pe":
            return
        if self.seen[eng].get(key, 0) >= val:
            return
        self.eng[eng].wait_ge(self.semobj[key], val)
        self.seen[eng][key] = val

    def _deps(self, eng, reads, writes, wacc=()):
        deps = {}
        for w in wacc:
            for k, v in w.r.items():
                if deps.get(k, 0) < v:
                    deps[k] = v
            for k, v in w.f.items():
                if deps.get(k, 0) < v:
                    deps[k] = v
        for r in reads:
            for k, v in r.w.items():
                if deps.get(k, 0) < v:
                    deps[k] = v
        for w in writes:
            for k, v in w.w.items():
                if deps.get(k, 0) < v:
                    deps[k] = v
            for k, v in w.r.items():
                if deps.get(k, 0) < v:
                    deps[k] = v
        for k, v in deps.items():
            self._wait(eng, k, v)

    def _mark(self, key, val, reads, writes, wacc=()):
        for r in reads:
            if r.r.get(key, 0) < val:
                r.r[key] = val
        for w in wacc:
            if w.w.get(key, 0) < val:
                w.w[key] = val
        for w in writes:
            w.w = {key: val}
            w.f = {key: val}
            w.r = {}

    limit = None
    nops = 0

    def op(self, eng, fn, reads=(), writes=(), wacc=()):
        self.nops += 1
        if self.limit is not None and self.nops > self.limit:
            return
        if any(r.excl for r in reads):
            writes = list(writes) + [r for r in reads if r.excl]
            reads = [r for r in reads if not r.excl]
        self._deps(eng, reads, writes, wacc)
        inst = fn(self.eng[eng])
        self.cnt[eng] += 1
        v = self.cnt[eng]
        inst.then_inc(self.semobj[eng], 1)
        self._mark(eng, v, reads, writes, wacc)

    def dma(self, eng, out, in_, reads=(), writes=(), wacc=(), **kw):
        self.nops += 1
        if self.limit is not None and self.nops > self.limit:
            return
        i = self.dnext[eng]
        self.dnext[eng] = (i + 1) % NDS
        key = "d_%s%d" % (eng, i)
        prev = self.dcnt[key]
        if prev > 0:
            self._wait(eng, key, prev)
        self._deps(eng, reads, writes, wacc)
        inst = self.eng[eng].dma_start(out=out, in_=in_, **kw)
        val = prev + 16
        self.dcnt[key] = val
        inst.then_inc(self.semobj[key], 16)
        self._mark(key, val, reads, writes, wacc)

    def barrier(self):
        for e in self.eng:
            for k, v in self.dcnt.items():
                if v > 0:
                    self._wait(e, k, v)
            for k in self.eng:
                if self.cnt[k] > 0:
                    self._wait(e, k, self.cnt[k])

    def finish(self):
        for k, v in self.dcnt.items():
            if v > 0:
                self._wait("sp", k, v)
        for k in self.eng:
            if k != "sp" and self.cnt[k] > 0:
                self._wait("sp", k, self.cnt[k])


class Ctx:
    pass


_UID = [0]


def _sb(es, nc, name, shape, dt):
    _UID[0] += 1
    return es.enter_context(nc.sbuf_tensor("%s_%d" % (name, _UID[0]), shape, dt))


def _ps(es, nc, name, shape, dt):
    _UID[0] += 1
    return es.enter_context(nc.psum_tensor("%s_%d" % (name, _UID[0]), shape, dt))


def load_w_bf16(C, dst, dst_res, src2d, nk, ncols, stage, stage_res, cnt0=0):
    sch = C.sch
    step = stage.shape[2]
    i = cnt0
    for kc in range(nk):
        for c0 in range(0, ncols, step):
            n = min(step, ncols - c0)
            j = i % stage.shape[1]
            sch.dma("sp", stage[:, j, 0:n], src2d[kc * 128:(kc + 1) * 128, c0:c0 + n],
                    writes=[stage_res[j]])
            eng = ("dve", "pool", "act")[i % 3]
            if eng == "act":
                sch.op("act", lambda e, j=j, n=n, kc=kc, c0=c0: e.copy(
                    out=dst[:, kc, c0:c0 + n], in_=stage[:, j, 0:n]),
                    reads=[stage_res[j]], wacc=[dst_res[kc]])
            else:
                sch.op(eng, lambda e, j=j, n=n, kc=kc, c0=c0: e.tensor_copy(
                    out=dst[:, kc, c0:c0 + n], in_=stage[:, j, 0:n]),
                    reads=[stage_res[j]], wacc=[dst_res[kc]])
            i += 1
    return i


def load_w_gen(C, dst, dst_res, src2d, nk, ncols, stage, stage_res, cnt):
    sch = C.sch
    step = stage.shape[2]
    nslot = stage.shape[1]
    for kc in range(nk):
        for c0 in range(0, ncols, step):
            n = min(step, ncols - c0)
            i = cnt[0]
            cnt[0] += 1
            j = i % nslot
            sch.dma("sp", stage[:, j, 0:n], src2d[kc * 128:(kc + 1) * 128, c0:c0 + n], writes=[stage_res[j]])
            eng = ("dve", "pool")[i % 2]
            sch.op(eng, lambda e, j=j, n=n, kc=kc, c0=c0: e.tensor_copy(
                out=dst[:, kc, c0:c0 + n], in_=stage[:, j, 0:n]), reads=[stage_res[j]], wacc=[dst_res[kc]])
            yield


class OWeights:
    def __init__(self, C, es, l):
        nc = C.nc
        self.wout = _sb(es, nc, "O_wout", [128, 8, D], BF16)
        self.wup = _sb(es, nc, "O_wup", [128, 8, DFF], BF16)
        self.wdn = _sb(es, nc, "O_wdn", [128, 32, D], BF16)
        self.wout_r = [Res() for _ in range(8)]
        self.wup_r = [Res() for _ in range(8)]
        self.wdn_r = [Res() for _ in range(32)]
        self.stage = _sb(es, nc, "O_stage", [128, 2, 512], F32)
        self.stage_r = [Res() for _ in range(2)]
        self.l = l
        self.C = C

    def gen(self):
        C, d, l = self.C, self.C.d, self.l
        cnt = [0]
        yield from load_w_gen(C, self.wout, self.wout_r, d["w_out"][l], 8, D, self.stage, self.stage_r, cnt)
        yield from load_w_gen(C, self.wup, self.wup_r, d["w_ff_up"][l], 8, DFF, self.stage, self.stage_r, cnt)
        yield from load_w_gen(C, self.wdn, self.wdn_r, d["w_ff_down"][l], 32, D, self.stage, self.stage_r, cnt)


def rms_rstd(C, ss, ms, sd, rstd, r_ss, r_tmp, r_rstd, n, eps):
    sch = C.sch
    sch.op("dve", lambda e: e.tensor_scalar(out=ms, in0=ss, scalar1=1.0 / n, scalar2=eps,
                                            op0=ALU.mult, op1=ALU.add),
           reads=[r_ss], writes=[r_tmp])
    sch.op("act", lambda e: e.activation(out=sd, in_=ms, func=AF.Ln),
           reads=[r_tmp], writes=[r_tmp])
    sch.op("act", lambda e: e.activation(out=rstd, in_=sd, func=AF.Exp, scale=-0.5), reads=[r_tmp], writes=[r_rstd])


def norm_transpose(C, xt, nsub, g, hn, hT, tps, R):
    sch = C.sch
    for s in range(nsub):
        sch.op("act", lambda e, s=s: e.activation(out=R["junk_t"][:, :], in_=xt[:, s, :], func=AF.Square,
                                                  accum_out=R["ss_t"][:, s:s + 1]),
               reads=[R["xt"]], writes=[R["junk"], R["ss"]])
    rms_rstd(C, R["ss_t"][:, 0:nsub], R["ms_t"][:, 0:nsub], R["sd_t"][:, 0:nsub], R["rstd_t"][:, 0:nsub],
             R["ss"], R["tmp"], R["rstd"], D, EPS)
    for s in range(nsub):
        eng = "dve" if s % 2 == 0 else "pool"
        sch.op(eng, lambda e, s=s: e.tensor_scalar(out=hn[:, s, :], in0=xt[:, s, :],
                                                   scalar1=R["rstd_t"][:, s:s + 1], scalar2=0.0, op0=ALU.mult, op1=ALU.add),
               reads=[R["xt"], R["rstd"]], writes=[R["hn"]])
    for kc in range(8):
        tp, rtp = tps[kc % 2]
        for s in range(nsub):
            sch.op("pe", lambda e, s=s, kc=kc, tp=tp: e.transpose(
                out=tp[:, s * 128:(s + 1) * 128], in_=hn[:, s, kc * 128:(kc + 1) * 128], identity=C.identb[:, :]),
                reads=[R["hn"]], writes=[rtp])
        if kc % 2 == 0:
            sch.op("dve", lambda e, kc=kc, tp=tp: e.tensor_scalar(
                out=hT[:, kc, :], in0=tp[:, 0:nsub * 128], scalar1=g[:, kc:kc + 1], scalar2=None, op0=ALU.mult),
                reads=[rtp, R["g"]], writes=[R["hT"][kc]])
        else:
            sch.op("act", lambda e, kc=kc, tp=tp: e.activation(
                out=hT[:, kc, :], in_=tp[:, 0:nsub * 128], func=AF.Copy, scale=g[:, kc:kc + 1]),
                reads=[rtp, R["g"]], writes=[R["hT"][kc]])


def evac(C, i, out, in_, reads, writes=(), wacc=()):
    if i % 2 == 0:
        C.sch.op("act", lambda e: e.copy(out=out, in_=in_), reads=reads, writes=writes, wacc=wacc)
    else:
        C.sch.op("dve", lambda e: e.tensor_copy(out=out, in_=in_), reads=reads, writes=writes, wacc=wacc)


def phase_A(C, l, X, rX):
    nc, sch, d = C.nc, C.sch, C.d
    TB = 512
    with ExitStack() as es:
        win = _sb(es, nc, "A_win", [128, 8, IN_PROJ], BF16)
        win_r = [Res() for _ in range(8)]
        stage = _sb(es, nc, "A_stage", [128, 4, 1732], F32)
        stage_r = [Res() for _ in range(4)]
        g1 = _sb(es, nc, "A_g1", [128, 8], F32)
        xts = [_sb(es, nc, "A_xt%d" % i, [128, 4, D], F32) for i in range(2)]
        xr = [Res(), Res()]
        hn = _sb(es, nc, "A_hn", [128, 4, D], BF16)
        hT = _sb(es, nc, "A_hT", [128, 8, TB], BF16)
        R = {"junk": Res(), "ss": Res(), "tmp": Res(), "rstd": Res(), "hn": Res(), "g": Res(),
             "hT": [Res() for _ in range(8)]}
        R["junk_t"] = _sb(es, nc, "A_junk", [128, D], BF16)
        R["ss_t"] = _sb(es, nc, "A_ss", [128, 4], F32)
        R["ms_t"] = _sb(es, nc, "A_ms", [128, 4], F32)
        R["sd_t"] = _sb(es, nc, "A_sd", [128, 4], F32)
        R["rstd_t"] = _sb(es, nc, "A_rstd", [128, 4], F32)
        st_mqk = _sb(es, nc, "A_smqk", [128, 4, TB], F32)
        st_g = _sb(es, nc, "A_sg", [8, TB], F32)
        st_r = _sb(es, nc, "A_sr", [128, 7, TB], F32)
        st_aqk = _sb(es, nc, "A_saqk", [128, 8, TB], BF16)
        st_mvo = _sb(es, nc, "A_smvo", [128, 4, 512], F32)
        st_av = _sb(es, nc, "A_sav", [128, 4, 512], BF16)
        r_mqk, r_g, r_r, r_aqk, r_mvo, r_av = Res(), Res(), Res(), Res(), Res(), Res()
        tps = [(_ps(es, nc, "A_tp%d" % i, [128, 1024], BF16), PRes()) for i in range(2)]
        mms = [(_ps(es, nc, "A_mm%d" % i, [128, 512], F32), PRes()) for i in range(4)]

        def load_x(blk):
            sch.dma("sp", xts[blk % 2][:, :, :],
                    X[blk * TB:(blk + 1) * TB, :].rearrange("(s p) d -> p s d", p=128),
                    reads=[rX], writes=[xr[blk % 2]])

        load_x(0)
        sch.dma("sp", g1[:, :], d["norm1_g"][l], writes=[R["g"]])
        load_w_bf16(C, win, win_r, d["w_in"][l], 8, IN_PROJ, stage, stage_r)

        fm = []
        for i in range(4):
            fm.append((i * 128, 128, st_mqk, i, r_mqk))
        fm.append((1024, 8, st_g, None, r_g))
        for i in range(7):
            fm.append((1032 + i * 128, 128, st_r, i, r_r))
        for i in range(8):
            fm.append((1928 + i * 128, 128, st_aqk, i, r_aqk))
        mmi = 0
        for blk in range(S // TB):
            xt = xts[blk % 2]
            R["xt"] = xr[blk % 2]
            norm_transpose(C, xt, 4, g1, hn, hT, tps, R)
            if blk + 1 < S // TB:
                load_x(blk + 1)
            t0 = blk * TB
            for (c0, n, stt, idx, rs) in fm:
                ps, rps = mms[mmi % 4]
                mmi += 1
                for kc in range(8):
                    sch.op("pe", lambda e, kc=kc, c0=c0, n=n, ps=ps: e.matmul(
                        out=ps[0:n, :], lhsT=win[:, kc, c0:c0 + n], rhs=hT[:, kc, :],
                        start=(kc == 0), stop=(kc == 7)),
                        reads=[win_r[kc], R["hT"][kc]], writes=[rps])
                o = stt[0:n, :] if idx is None else stt[:, idx, :]
                evac(C, mmi, o, ps[0:n, :], [rps], wacc=[rs])
            sch.dma("sp", d["mqkT"][:, t0:t0 + TB].rearrange("(c p) t -> p c t", p=128), st_mqk[:, :, :],
                    reads=[r_mqk], wacc=[C.r_mqkT])
            sch.dma("sp", d["mgT"][:, t0:t0 + TB], st_g[:, :], reads=[r_g], wacc=[C.r_mgT])
            sch.dma("sp", d["rT"][:, t0:t0 + TB].rearrange("(c p) t -> p c t", p=128), st_r[:, :, :],
                    reads=[r_r], wacc=[C.r_rT])
            sch.dma("sp", d["aqkT"][:, t0:t0 + TB].rearrange("(c p) t -> p c t", p=128), st_aqk[:, :, :],
                    reads=[r_aqk], wacc=[C.r_aqkT])
            for (c0, stt, rs) in ((512, st_mvo, r_mvo), (2952, st_av, r_av)):
                for s in range(4):
                    ps, rps = mms[mmi % 4]
                    mmi += 1
                    for kc in range(8):
                        sch.op("pe", lambda e, kc=kc, c0=c0, s=s, ps=ps: e.matmul(
                            out=ps[:, :], lhsT=hT[:, kc, s * 128:(s + 1) * 128], rhs=win[:, kc, c0:c0 + 512],
                            start=(kc == 0), stop=(kc == 7)),
                            reads=[win_r[kc], R["hT"][kc]], writes=[rps])
                    evac(C, mmi, stt[:, s, :], ps[:, :], [rps], wacc=[rs])
            sch.dma("sp", d["mvo"][t0:t0 + TB, :].rearrange("(s p) c -> p s c", p=128), st_mvo[:, :, :],
                    reads=[r_mvo], wacc=[C.r_mvo])
            sch.dma("sp", d["av"][t0:t0 + TB, :].rearrange("(s p) c -> p s c", p=128), st_av[:, :, :],
                    reads=[r_av], wacc=[C.r_av])


def phase_O(C, l, X, rX, XOUT, rXOUT, final, W=None):
    nc, sch, d = C.nc, C.sch, C.d
    TB = 256
    NS = 2
    with ExitStack() as es:
        if W is None:
            wout = _sb(es, nc, "O_wout", [128, 8, D], BF16)
            wup = _sb(es, nc, "O_wup", [128, 8, DFF], BF16)
            wdn = _sb(es, nc, "O_wdn", [128, 32, D], BF16)
            wout_r = [Res() for _ in range(8)]
            wup_r = [Res() for _ in range(8)]
            wdn_r = [Res() for _ in range(32)]
            stage = _sb(es, nc, "O_stage", [128, 2, 512], F32)
            stage_r = [Res(), Res()]
        else:
            wout, wup, wdn = W.wout, W.wup, W.wdn
            wout_r, wup_r, wdn_r = W.wout_r, W.wup_r, W.wdn_r
        g2 = _sb(es, nc, "O_g2", [128, 8], F32)
        gf = _sb(es, nc, "O_gf", [128, 8], F32)
        gfb = _sb(es, nc, "O_gfb", [128, D], F32)
        mixt = _sb(es, nc, "O_mixt", [128, 8, TB], BF16)
        r_mixt = Res()
        xt = _sb(es, nc, "O_xt", [128, NS, D], F32)
        x1 = _sb(es, nc, "O_x1", [128, NS, D], F32)
        r_x1 = Res()
        hn = _sb(es, nc, "O_hn", [128, NS, D], BF16)
        hT = _sb(es, nc, "O_hT", [128, 8, TB], BF16)
        rr = [_sb(es, nc, "O_rr%d" % i, [128, TB], F32) for i in range(2)]
        rr_r = [Res(), Res()]
        u = _sb(es, nc, "O_u", [128, 32, TB], BF16)
        u_r = [Res() for _ in range(32)]
        R = {"junk": Res(), "ss": Res(), "tmp": Res(), "rstd": Res(), "hn": Res(), "g": Res(),
             "hT": [Res() for _ in range(8)], "xt": Res()}
        R["junk_t"] = _sb(es, nc, "O_junk", [128, D], BF16)
        R["ss_t"] = _sb(es, nc, "O_ss", [128, 4], F32)
        R["ms_t"] = _sb(es, nc, "O_ms", [128, 4], F32)
        R["sd_t"] = _sb(es, nc, "O_sd", [128, 4], F32)
        R["rstd_t"] = _sb(es, nc, "O_rstd", [128, 4], F32)
        r_xt = Res()
        r_gf = Res()
        tps = [(_ps(es, nc, "O_tp%d" % i, [128, 1024], BF16), PRes()) for i in range(2)]
        mma = [(_ps(es, nc, "O_mma%d" % i, [128, 512], F32), PRes()) for i in range(3)]
        mmb = [(_ps(es, nc, "O_mmb%d" % i, [128, 512], F32), PRes()) for i in range(3)]

        def load_block(blk):
            t0 = blk * TB
            sch.dma("sp", xt[:, :, :], X[t0:t0 + TB, :].rearrange("(s p) d -> p s d", p=128),
                    reads=[rX], writes=[r_xt])
            sch.dma("sp", mixt[:, :, :], d["mixT"][:, t0:t0 + TB].rearrange("(c p) t -> p c t", p=128),
                    reads=[C.r_mixT], writes=[r_mixt])

        load_block(0)
        sch.dma("sp", g2[:, :], d["norm2_g"][l], writes=[R["g"]])
        if final:
            sch.dma("sp", gfb[:, :], d["final_gb"], writes=[r_gf])
        if W is None:
            i = load_w_bf16(C, wout, wout_r, d["w_out"][l], 8, D, stage, stage_r)
            i = load_w_bf16(C, wup, wup_r, d["w_ff_up"][l], 8, DFF, stage, stage_r, i)
            load_w_bf16(C, wdn, wdn_r, d["w_ff_down"][l], 32, D, stage, stage_r, i)

        ia = 0
        ib = 0
        for blk in range(S // TB):
            t0 = blk * TB
            for s in range(NS):
                for hf in range(2):
                    ps, rps = mma[ia % 3]
                    ia += 1
                    for kc in range(8):
                        sch.op("pe", lambda e, kc=kc, s=s, hf=hf, ps=ps: e.matmul(
                            out=ps[:, :], lhsT=mixt[:, kc, s * 128:(s + 1) * 128],
                            rhs=wout[:, kc, hf * 512:(hf + 1) * 512], start=(kc == 0), stop=(kc == 7)),
                            reads=[r_mixt, wout_r[kc]], writes=[rps])
                    sch.op("dve", lambda e, s=s, hf=hf, ps=ps: e.tensor_tensor(
                        out=x1[:, s, hf * 512:(hf + 1) * 512], in0=ps[:, :], in1=xt[:, s, hf * 512:(hf + 1) * 512],
                        op=ALU.add), reads=[rps, r_xt], wacc=[r_x1])
            if blk + 1 < S // TB:
                load_block(blk + 1)
            R["xt"] = r_x1
            norm_transpose(C, x1, NS, g2, hn, hT, tps, R)
            for j in range(32):
                ps, rps = mmb[ib % 3]
                ib += 1
                for kc in range(8):
                    sch.op("pe", lambda e, kc=kc, j=j, ps=ps: e.matmul(
                        out=ps[:, 0:TB], lhsT=wup[:, kc, j * 128:(j + 1) * 128], rhs=hT[:, kc, :],
                        start=(kc == 0), stop=(kc == 7)),
                        reads=[wup_r[kc], R["hT"][kc]], writes=[rps])
                rt, rtr = rr[j % 2], rr_r[j % 2]
                sch.op("act", lambda e, ps=ps, rt=rt: e.activation(out=rt[:, :], in_=ps[:, 0:TB], func=AF.Relu),
                       reads=[rps], writes=[rtr])
                eng = "pool" if j % 2 == 0 else "dve"
                sch.op(eng, lambda e, j=j, rt=rt: e.tensor_tensor(out=u[:, j, :], in0=rt[:, :], in1=rt[:, :],
                                                                  op=ALU.mult),
                       reads=[rtr], writes=[u_r[j]])
            for s in range(NS):
                for hf in range(2):
                    ps, rps = mma[ia % 3]
                    ia += 1
                    for j in range(32):
                        sch.op("pe", lambda e, j=j, s=s, hf=hf, ps=ps: e.matmul(
                            out=ps[:, :], lhsT=u[:, j, s * 128:(s + 1) * 128],
                            rhs=wdn[:, j, hf * 512:(hf + 1) * 512], start=(j == 0), stop=(j == 31)),
                            reads=[u_r[j], wdn_r[j]], writes=[rps])
                    sch.op("dve", lambda e, s=s, hf=hf, ps=ps: e.tensor_tensor(
                        out=x1[:, s, hf * 512:(hf + 1) * 512], in0=ps[:, :], in1=x1[:, s, hf * 512:(hf + 1) * 512],
                        op=ALU.add), reads=[rps, r_x1], wacc=[r_x1])
            if final:
                for s in range(NS):
                    sch.op("act", lambda e, s=s: e.activation(out=R["junk_t"][:, :], in_=x1[:, s, :], func=AF.Square,
                                                              accum_out=R["ss_t"][:, s:s + 1]),
                           reads=[r_x1], writes=[R["junk"], R["ss"]])
                rms_rstd(C, R["ss_t"][:, 0:NS], R["ms_t"][:, 0:NS], R["sd_t"][:, 0:NS], R["rstd_t"][:, 0:NS],
                         R["ss"], R["tmp"], R["rstd"], D, EPS)
                for s in range(NS):
                    sch.op("dve", lambda e, s=s: e.scalar_tensor_tensor(
                        out=x1[:, s, :], in0=x1[:, s, :], scalar=R["rstd_t"][:, s:s + 1], in1=gfb[:, :],
                        op0=ALU.mult, op1=ALU.mult), reads=[r_x1, R["rstd"], r_gf], wacc=[r_x1])
            sch.dma("sp", XOUT[t0:t0 + TB, :].rearrange("(s p) d -> p s d", p=128), x1[:, :, :],
                    reads=[r_x1], wacc=[rXOUT])


def phase_M(C, l):
    nc, sch, d = C.nc, C.sch, C.d
    L = 128
    NC_ = S // L
    with ExitStack() as es:
        gi = _sb(es, nc, "M_gi", [4, S], F32)
        gf = _sb(es, nc, "M_gf", [4, S], F32)
        Fn = _sb(es, nc, "M_Fn", [4, S], F32)
        r_gi, r_gf, r_Fn = Res(), Res(), Res()
        bi = _sb(es, nc, "M_bi", [4, 1], F32)
        bfn = _sb(es, nc, "M_bfn", [4, 1], F32)
        r_b = Res()
        Mc = _sb(es, nc, "M_Mc", [4, NC_], F32)
        Md = _sb(es, nc, "M_Md", [4, NC_], F32)
        ec = _sb(es, nc, "M_ec", [4, NC_], F32)
        r_Mc, r_ec = Res(), Res()
        gtok = _sb(es, nc, "M_gtok", [128, NC_, 8, 1], F32)
        r_gtok = Res()
        eb = _sb(es, nc, "M_eb", [128, 2, NC_], F32)
        r_eb = Res()
        cw = _sb(es, nc, "M_cw", [128, 4, 4], F32)
        cb = _sb(es, nc, "M_cb", [128, 4], F32)
        mg = _sb(es, nc, "M_mg", [128, 256], F32)
        r_par = Res()
        xins = [_sb(es, nc, "M_xin%d" % i, [128, 3 + S], F32) for i in range(2)]
        accs = [_sb(es, nc, "M_acc%d" % i, [128, S], F32) for i in range(2)]
        r_xins, r_accs = [Res(), Res()], [Res(), Res()]
        qz = [_sb(es, nc, "M_qz%d" % i, [128, S], BF16) for i in range(4)]
        kpair = [_sb(es, nc, "M_kp%d" % i, [128, S], BF16) for i in range(2)]
        r_qz = [Res() for _ in range(4)]
        r_kp = [Res() for _ in range(2)]
        ymT = _sb(es, nc, "M_ymT", [128, 2, S], BF16)
        r_ymT = Res()
        vo = [_sb(es, nc, "M_vo%d" % i, [128, 512], F32) for i in range(2)]
        r_vo = [Res(), Res()]
        va = _sb(es, nc, "M_va", [128, 4, 65], BF16)
        r_va = Res()
        sig = _sb(es, nc, "M_sig", [128, 256], F32)
        t2s = [_sb(es, nc, "M_t2_%d" % i, [128, 256], F32) for i in range(2)]
        r_t2s = [Res(), Res()]
        r_sig = Res()
        ktok = _sb(es, nc, "M_ktok", [128, 256], BF16)
        r_ktok = Res()
        at = _sb(es, nc, "M_at", [128, 4, 128], BF16)
        r_at = Res()
        Cs = _sb(es, nc, "M_Cs", [128, 2, 65], F32)
        Cb = _sb(es, nc, "M_Cb", [128, 2, 65], BF16)
        r_Cs, r_Cb = Res(), Res()
        sm = _sb(es, nc, "M_sm", [128, 8, 4, 1], F32)
        r_sm = Res()
        hh = _sb(es, nc, "M_hh", [128, 4, 64], F32)
        sq = _sb(es, nc, "M_sq", [128, 4, 64], F32)
        y1 = _sb(es, nc, "M_y1", [128, 4, 64], F32)
        yms = [_sb(es, nc, "M_ym%d" % i, [128, 256], BF16) for i in range(2)]
        r_yms = [Res(), Res()]
        r_hh, r_sq, r_y1 = Res(), Res(), Res()
        p_g = _ps(es, nc, "M_pg", [128, 512], F32)
        p_kt = _ps(es, nc, "M_pkt", [128, 1024], BF16)
        p_kv = _ps(es, nc, "M_pkv", [128, 512], F32)
        p_at = _ps(es, nc, "M_pat", [128, 4, 128], F32)
        p_nums = [_ps(es, nc, "M_pnum%d" % i, [128, 512], F32) for i in range(2)]
        p_yt = _ps(es, nc, "M_pyt", [128, 1024], BF16)
        rp_g, rp_kt, rp_kv, rp_at, rp_yt = PRes(), PRes(), PRes(), PRes(), PRes()
        rp_nums = [PRes(), PRes()]

        sch.dma("sp", bi[:, :], d["m_b_i"][l], wacc=[r_b])
        sch.dma("sp", bfn[:, :], d["m_b_f"][l], wacc=[r_b])
        sch.dma("sp", cw[:, :, :], d["m_conv_w"][l], wacc=[r_par])
        sch.dma("sp", cb[:, :], d["m_conv_b"][l], wacc=[r_par])
        sch.dma("sp", mg[:, :], d["m_norm_g"][l], wacc=[r_par])
        sch.dma("sp", gi[:, :], d["mgT"][0:4, :], reads=[C.r_mgT], writes=[r_gi])
        sch.dma("sp", gf[:, :], d["mgT"][4:8, :], reads=[C.r_mgT], writes=[r_gf])
        sch.op("dve", lambda e: e.tensor_scalar(out=bfn[:, :], in0=bfn[:, :], scalar1=-1.0, scalar2=None,
                                                op0=ALU.mult), reads=[r_b], writes=[r_b])
        for i in range(2):
            sch.op("pool", lambda e, i=i: e.memset(xins[i][:, 0:3], 0.0), writes=[r_xins[i]])
        def conv(ch):
            xin, acc, r_xin, r_acc = xins[ch % 2], accs[ch % 2], r_xins[ch % 2], r_accs[ch % 2]
            sch.dma("sp", xin[:, 3:3 + S], d["mqkT"][ch * 128:(ch + 1) * 128, :], reads=[C.r_mqkT], writes=[r_xin])
            sch.op("act", lambda e, ch=ch, xin=xin, acc=acc: e.activation(
                out=acc[:, :], in_=xin[:, 3:3 + S], func=AF.Identity, scale=cw[:, ch, 3:4], bias=cb[:, ch:ch + 1]),
                reads=[r_xin, r_par], writes=[r_acc])
            for j in range(3):
                sch.op("dve", lambda e, ch=ch, j=j: e.scalar_tensor_tensor(
                    out=acc[:, :], in0=xin[:, j:j + S], scalar=cw[:, ch, j:j + 1], in1=acc[:, :],
                    op0=ALU.mult, op1=ALU.add), reads=[r_xin, r_acc, r_par], writes=[r_acc])
            if ch < 2:
                for hf in range(2):
                    hq = 2 * ch + hf
                    ps_ = slice(hf * 64, hf * 64 + 64)
                    zs_ = slice((1 - hf) * 64, (1 - hf) * 64 + 64)
                    sch.op("pool", lambda e, hq=hq, zs_=zs_: e.memset(qz[hq][zs_, :], 0.0), wacc=[r_qz[hq]])
                    sch.op("act", lambda e, hq=hq, ps_=ps_: e.activation(out=qz[hq][ps_, :], in_=acc[ps_, :],
                                                                         func=AF.Silu),
                           reads=[r_acc], wacc=[r_qz[hq]])
            else:
                sch.op("act", lambda e, ch=ch: e.activation(out=kpair[ch - 2][:, :], in_=acc[:, :], func=AF.Silu),
                       reads=[r_acc], writes=[r_kp[ch - 2]])
        sch.op("dve", lambda e: e.tensor_scalar(out=gi[:, :], in0=gi[:, :], scalar1=bi[:, 0:1], scalar2=None,
                                                op0=ALU.add), reads=[r_gi, r_b], writes=[r_gi])
        sch.op("act", lambda e: e.activation(out=gf[:, :], in_=gf[:, :], func=AF.Exp, bias=bfn[:, 0:1], scale=-1.0),
               reads=[r_gf, r_b], writes=[r_gf])
        sch.op("act", lambda e: e.activation(out=gf[:, :], in_=gf[:, :], func=AF.Ln, bias=1.0, scale=1.0),
               reads=[r_gf], writes=[r_gf])
        conv(0)
        sch.op("dve", lambda e: e.tensor_tensor_scan(out=Fn[:, :], data0=gf[:, :], data1=gf[:, :], initial=0.0,
                                                     op0=ALU.add, op1=ALU.max), reads=[r_gf], writes=[r_Fn])
        sch.op("dve", lambda e: e.tensor_tensor(out=gi[:, :], in0=gi[:, :], in1=Fn[:, :], op=ALU.add),
               reads=[r_gi, r_Fn], writes=[r_gi])
        sch.op("dve", lambda e: e.tensor_tensor_scan(out=gf[:, :], data0=gi[:, :], data1=gi[:, :], initial=0.0,
                                                     op0=ALU.max, op1=ALU.max), reads=[r_gi], writes=[r_gf])
        conv(1)
        U3 = gf[:, :].rearrange("p (c t) -> p c t", t=L)
        sch.op("dve", lambda e: e.tensor_copy(out=Mc[:, :].unsqueeze(2), in_=U3[:, :, L - 1:L]),
               reads=[r_gf], writes=[r_Mc])
        Mbc = Mc[:, :].unsqueeze(2).broadcast_to([4, NC_, L])
        u3 = gi[:, :].rearrange("p (c t) -> p c t", t=L)
        sch.op("dve", lambda e: e.tensor_tensor(out=u3, in0=u3, in1=Mbc, op=ALU.subtract),
               reads=[r_gi, r_Mc], writes=[r_gi])
        sch.op("act", lambda e: e.activation(out=gi[:, :], in_=gi[:, :], func=AF.Exp, bias=math.log(0.125), scale=1.0),
               reads=[r_gi], writes=[r_gi])
        conv(2)
        F3 = Fn[:, :].rearrange("p (c t) -> p c t", t=L)
        sch.op("dve", lambda e: e.tensor_tensor(out=F3, in0=F3, in1=Mbc, op=ALU.subtract),
               reads=[r_Fn, r_Mc], writes=[r_Fn])
        sch.op("act", lambda e: e.activation(out=Fn[:, :], in_=Fn[:, :], func=AF.Exp), reads=[r_Fn], writes=[r_Fn])
        conv(3)
        sch.op("dve", lambda e: e.tensor_tensor(out=Md[:, 1:NC_], in0=Mc[:, 0:NC_ - 1], in1=Mc[:, 1:NC_],
                                                op=ALU.subtract), reads=[r_Mc], wacc=[r_ec])
        sch.op("dve", lambda e: e.tensor_scalar(out=Md[:, 0:1], in0=Mc[:, 0:1], scalar1=-1.0, scalar2=None,
                                                op0=ALU.mult), reads=[r_Mc], wacc=[r_ec])
        sch.op("act", lambda e: e.activation(out=ec[:, :], in_=Md[:, :], func=AF.Exp), reads=[r_ec], writes=[r_ec])
        pg3 = p_g[:, 0:NC_ * 8].rearrange("p (c g) -> p c g", g=8)
        for c in range(NC_):
            sch.op("pe", lambda e, c=c: e.transpose(out=pg3[:, c, 0:4], in_=gi[:, c * L:(c + 1) * L],
                                                    identity=C.identf[0:4, 0:4]), reads=[r_gi], writes=[rp_g])
            sch.op("pe", lambda e, c=c: e.transpose(out=pg3[:, c, 4:8], in_=Fn[:, c * L:(c + 1) * L],
                                                    identity=C.identf[0:4, 0:4]), reads=[r_Fn], writes=[rp_g])
        sch.op("dve", lambda e: e.tensor_copy(out=gtok[:, :, :, 0], in_=pg3), reads=[rp_g], writes=[r_gtok])
        for pr in range(2):
            sch.op("pe", lambda e, pr=pr: e.matmul(out=p_g[:, 256 + pr * NC_:256 + (pr + 1) * NC_],
                                                   lhsT=C.sel4[:, pr, :], rhs=ec[:, :], start=True, stop=True),
                   reads=[r_ec, r_gtok], writes=[rp_g])
        sch.op("dve", lambda e: e.tensor_copy(out=eb[:, :, :],
                                              in_=p_g[:, 256:256 + 2 * NC_].rearrange("p (a c) -> p a c", a=2)),
               reads=[rp_g], writes=[r_eb])
        sch.op("pool", lambda e: e.memset(Cs[:, :, :], 0.0), writes=[r_Cs])
        num3s = [pn[:, 0:260].rearrange("p (h e) -> p h e", e=65) for pn in p_nums]
        kv3 = p_kv[:, 0:260].rearrange("p (a e) -> p a e", a=2)
        def epilogue(c):
            ym, r_ym = yms[c % 2], r_yms[c % 2]
            t2, r_t2 = t2s[c % 2], r_t2s[c % 2]
            num3, rp_num = num3s[c % 2], rp_nums[c % 2]
            sch.op("act", lambda e: e.activation(out=sm[:, 0, :, :], in_=num3[:, :, 64:65], func=AF.Abs),
                   reads=[rp_num], writes=[r_sm])
            sch.op("dve", lambda e, c=c: e.tensor_tensor(out=sm[:, 1, :, :], in0=sm[:, 0, :, :],
                                                         in1=gtok[:, c, 4:8, :], op=ALU.max),
                   reads=[r_sm, r_gtok], writes=[r_sm])
            sch.op("dve", lambda e: e.reciprocal(out=sm[:, 2, :, :], in_=sm[:, 1, :, :]), reads=[r_sm], writes=[r_sm])
            sch.op("dve", lambda e: e.tensor_tensor(out=hh[:, :, :], in0=num3[:, :, 0:64],
                                                    in1=sm[:, 2, :, :].broadcast_to([128, 4, 64]), op=ALU.mult),
                   reads=[rp_num, r_sm], writes=[r_hh])
            sch.op("dve", lambda e: e.tensor_tensor(out=sq[:, :, :], in0=hh[:, :, :], in1=hh[:, :, :], op=ALU.mult),
                   reads=[r_hh], writes=[r_sq])
            sch.op("dve", lambda e: e.tensor_reduce(out=sm[:, 3, :, :], in_=sq[:, :, :], axis=AX.X, op=ALU.add),
                   reads=[r_sq], writes=[r_sm])
            rms_rstd(C, sm[:, 3, :, 0], sm[:, 4, :, 0], sm[:, 5, :, 0], sm[:, 6, :, 0], r_sm, r_sm, r_sm, 64, 1e-6)
            sch.op("dve", lambda e: e.tensor_tensor(out=y1[:, :, :], in0=hh[:, :, :],
                                                    in1=sm[:, 6, :, :].broadcast_to([128, 4, 64]), op=ALU.mult),
                   reads=[r_hh, r_sm], writes=[r_y1])
            sch.op("dve", lambda e: e.tensor_tensor(out=ym[:, :], in0=y1[:, :, :].rearrange("p h e -> p (h e)"),
                                                    in1=t2[:, :], op=ALU.mult), reads=[r_y1, r_t2], writes=[r_ym])

        def y_tail(c):
            ym, r_ym = yms[c % 2], r_yms[c % 2]
            cs = slice(c * L, (c + 1) * L)
            for j in range(2):
                sch.op("pe", lambda e, j=j: e.transpose(out=p_yt[:, j * 128:(j + 1) * 128],
                                                        in_=ym[:, j * 128:(j + 1) * 128], identity=C.identb[:, :]),
                       reads=[r_ym], writes=[rp_yt])
            sch.op("act", lambda e: e.copy(out=ymT[:, :, cs], in_=p_yt[:, 0:256].rearrange("p (j t) -> p j t", j=2)),
                   reads=[rp_yt], wacc=[r_ymT])

        for c in range(NC_):
            cs = slice(c * L, (c + 1) * L)
            vt, rv = vo[c % 2], r_vo[c % 2]
            t2, r_t2 = t2s[c % 2], r_t2s[c % 2]
            num3, rp_num = num3s[c % 2], rp_nums[c % 2]
            sch.dma("sp", vt[:, :], d["mvo"][c * L:(c + 1) * L, :], reads=[C.r_mvo], writes=[rv])
            for h in range(4):
                eng = "dve"
                sch.op(eng, lambda e, h=h, vt=vt, c=c: e.tensor_scalar(
                    out=va[:, h, 0:64], in0=vt[:, h * 64:(h + 1) * 64], scalar1=gtok[:, c, h, :], scalar2=0.0,
                    op0=ALU.mult, op1=ALU.add), reads=[rv, r_gtok], wacc=[r_va])
            sch.op("pool", lambda e, c=c: e.tensor_copy(out=va[:, :, 64:65], in_=gtok[:, c, 0:4, :]),
                   reads=[r_gtok], wacc=[r_va])
            sch.op("act", lambda e, vt=vt: e.activation(out=sig[:, :], in_=vt[:, 256:512], func=AF.Exp, scale=-1.0),
                   reads=[rv], writes=[r_sig])
            sch.op("act", lambda e: e.activation(out=sig[:, :], in_=sig[:, :], func=AF.Ln, bias=1.0, scale=1.0),
                   reads=[r_sig], writes=[r_sig])
            sch.op("act", lambda e: e.activation(out=sig[:, :], in_=sig[:, :], func=AF.Exp, scale=-1.0),
                   reads=[r_sig], writes=[r_sig])
            sch.op("pool", lambda e: e.tensor_tensor(out=t2[:, :], in0=sig[:, :], in1=mg[:, :], op=ALU.mult),
                   reads=[r_sig, r_par], writes=[r_t2])
            for pr in range(2):
                sch.op("pe", lambda e, pr=pr, cs=cs: e.transpose(out=p_kt[:, pr * 128:(pr + 1) * 128],
                                                                 in_=kpair[pr][:, cs], identity=C.identb[:, :]),
                       reads=[r_kp[pr]], writes=[rp_kt])
            sch.op("act", lambda e: e.copy(out=ktok[:, :], in_=p_kt[:, 0:256]), reads=[rp_kt], writes=[r_ktok])
            for pr in range(2):
                sch.op("pe", lambda e, pr=pr: e.matmul(out=kv3[:, pr, :], lhsT=ktok[:, pr * 128:(pr + 1) * 128],
                                                       rhs=va[:, 2 * pr:2 * pr + 2, :], start=True, stop=True),
                       reads=[r_ktok, r_va], writes=[rp_kv])
            for h in range(4):
                pr, pb = h // 2, (h % 2) * 64
                sch.op("pe", lambda e, h=h, pr=pr, pb=pb, cs=cs: e.matmul(
                    out=p_at[:, h, :], lhsT=kpair[pr][:, cs], rhs=qz[h][:, cs],
                    start=True, stop=True), reads=[r_kp[pr], r_qz[h]], writes=[rp_at])
            sch.op("dve", lambda e: e.tensor_tensor(out=at[:, :, :], in0=p_at[:, :, :],
                                                    in1=C.trif[:, :].unsqueeze(1).broadcast_to([128, 4, 128]),
                                                    op=ALU.mult), reads=[rp_at], writes=[r_at])
            for pr in range(2):
                sch.op("dve", lambda e, pr=pr, c=c: e.tensor_scalar(
                    out=Cb[:, pr, :], in0=Cs[:, pr, :], scalar1=eb[:, pr, c:c + 1], scalar2=None, op0=ALU.mult),
                    reads=[r_Cs, r_eb], wacc=[r_Cb])
            for h in range(4):
                pr, pb = h // 2, (h % 2) * 64
                sch.op("pe", lambda e, h=h: e.matmul(out=num3[:, h, :], lhsT=at[:, h, :], rhs=va[:, h, :],
                                                     start=True, stop=False), reads=[r_at, r_va], writes=[rp_num])
                sch.op("pe", lambda e, h=h, pr=pr, pb=pb, cs=cs: e.matmul(
                    out=num3[:, h, :], lhsT=qz[h][:, cs], rhs=Cb[:, pr, :],
                    start=False, stop=True), reads=[r_qz[h], r_Cb], writes=[rp_num])
            for pr in range(2):
                for hf in range(2):
                    pb = hf * 64
                    sch.op("dve", lambda e, pr=pr, hf=hf, pb=pb, c=c: e.scalar_tensor_tensor(
                        out=Cs[pb:pb + 64, pr, :], in0=Cs[pb:pb + 64, pr, :], scalar=eb[pb:pb + 64, pr, c:c + 1],
                        in1=kv3[pb:pb + 64, pr, hf * 65:(hf + 1) * 65], op0=ALU.mult, op1=ALU.add),
                        reads=[r_Cs, r_eb, rp_kv, r_Cb], wacc=[r_Cs])
            if c >= 1:
                epilogue(c - 1)
            if c >= 2:
                y_tail(c - 2)
        epilogue(NC_ - 1)
        y_tail(NC_ - 2)
        y_tail(NC_ - 1)
        for j in range(2):
            sch.dma("sp", d["mixT"][j * 128:(j + 1) * 128, :], ymT[:, j, :], reads=[r_ymT], wacc=[C.r_mixT])


def phase_R(C, l):
    nc, sch, d = C.nc, C.sch, C.d
    T = 1024
    L = 64
    NCH = T // L
    NEG = -0.6065306597126334
    with ExitStack() as es:
        def sb(name, shape, dt):
            return _sb(es, nc, "R_" + name, shape, dt)
        mu = sb("mu", [128, 7], F32)
        omu = sb("omu", [128, 7], F32)
        pv = sb("pv", [128, 6, 2], F32)
        lup_f = sb("lupf", [64, 3, 256], F32)
        lup = sb("lup", [64, 3, 256], BF16)
        lin_a = sb("lin_a", [32, T], BF16)
        lin_g = sb("lin_g", [64, T], BF16)
        r_lina, r_ling = Res(), Res()
        lng = sb("lng", [64, 256], F32)
        lnb = sb("lnb", [64, 256], F32)
        r_par = Res()
        praw = sb("praw", [128, 1 + T], F32)
        r_praw = Res()
        pm_r = sb("pm_r", [128, 2, T], F32)
        pm_k = sb("pm_k", [128, 2, T], F32)
        pm_v = sb("pm_v", [128, 2, T], F32)
        pm_l = sb("pm_l", [128, T], F32)
        r_pm = {k: Res() for k in ("r0", "r1", "k0", "k1", "v0", "v1", "l")}
        lin = sb("lin", [128, T], BF16)
        r_lin = Res()
        tmp = sb("tmp", [128, T], F32)
        r_tmp = Res()
        lw = sb("lw", [128, T], F32)
        aa = sb("aa", [128, T], F32)
        kx = sb("kx", [128, T], F32)
        sq = sb("sq", [128, T], F32)
        kp = sb("kp", [128, T], F32)
        bb = sb("bb", [128, T], F32)
        Gp = [sb("Gp%d" % i, [128, 1 + T], F32) for i in range(2)]
        gi = sb("gi", [128, T], F32)
        ge = sb("ge", [128, T], F32)
        E1 = [sb("E1_%d" % i, [128, T], F32) for i in range(2)]
        E2 = sb("E2", [128, T], F32)
        r_lw, r_aa, r_kx, r_sq, r_kp, r_bb, r_gi, r_ge, r_E2 = (Res() for _ in range(9))
        r_Gp = [Res(), Res()]
        r_E1 = [Res(), Res()]
        ARz = [sb("ARz%d" % i, [128, 2, T], BF16) for i in range(4)]
        Bt = sb("Bt", [128, 2, T], BF16)
        Kt = sb("Kt", [128, 2, T], BF16)
        Bh = sb("Bh", [128, 2, T], F32)
        Kh = sb("Kh", [128, 2, T], F32)
        prodb = sb("prodb", [128, 2, T], BF16)
        r_AR, r_Bt, r_Kt, r_Bh, r_Kh, r_prodb = ([Res(), Res()] for _ in range(6))
        Hs = sb("Hs", [128, 2, 64], F32)
        Hb = sb("Hb", [128, 2, 64], BF16)
        r_Hs, r_Hb = Res(), Res()
        yrT = sb("yrT", [128, 2, S], BF16)
        r_yrT = Res()
        tok = sb("tok", [64, 2, 4, 2, 64], F32)
        r_tok = [Res(), Res()]
        scs = sb("scs", [64, 4, 4, 64], F32)
        r_scs = [Res(), Res()]
        nns = sb("nns", [64, 4, 64], F32)
        r_nns = Res()
        pws = [sb("pws%d" % i, [64, 4, 2, 64], F32) for i in range(5)]
        r_pws = [Res() for _ in range(5)]
        U = [sb("U%d" % i, [64, 4, 64], F32) for i in range(2)]
        r_U = [Res(), Res()]
        yv = sb("yv", [64, 4, 64], F32)
        ysq = sb("ysq", [64, 4, 64], F32)
        yo = sb("yo", [64, 256], F32)
        st = sb("st", [64, 8, 4, 1], F32)
        r_yv, r_ysq, r_yo, r_st = Res(), Res(), Res(), Res()
        banks = [_ps(es, nc, "R_b%d" % i, [128, 512], F32) for i in range(8)]
        rb = [PRes() for _ in range(8)]

        sch.dma("sp", mu[:, :], d["r_mu"][l], wacc=[r_par])
        sch.dma("sp", pv[:, 0, :], d["r_w0"][l], wacc=[r_par])
        sch.dma("sp", pv[:, 1, :], d["r_a0"][l], wacc=[r_par])
        sch.dma("sp", pv[:, 2, :], d["r_k_k"][l], wacc=[r_par])
        sch.dma("sp", pv[:, 3, :], d["r_k_a"][l], wacc=[r_par])
        sch.dma("sp", pv[:, 4, :], d["r_r_k"][l], wacc=[r_par])
        r_lup = Res()
        sch.op("pool", lambda e: e.memset(lup_f[:, :, :], 0.0), writes=[r_lup])
        sch.dma("sp", lup_f[0:32, 0, :], d["r_w_up"][l], writes=[r_lup])
        sch.dma("sp", lup_f[0:32, 1, :], d["r_a_up"][l], writes=[r_lup])
        sch.dma("sp", lup_f[0:64, 2, :], d["r_g_up"][l], writes=[r_lup])
        sch.dma("sp", lng[:, :], d["r_ln_g"][l], wacc=[r_par])
        sch.dma("sp", lnb[:, :], d["r_ln_b"][l], wacc=[r_par])
        sch.op("dve", lambda e: e.tensor_scalar(out=omu[:, :], in0=mu[:, :], scalar1=-1.0, scalar2=1.0,
                                                op0=ALU.mult, op1=ALU.add), reads=[r_par], writes=[r_par])
        sch.op("dve", lambda e: e.tensor_scalar(out=pv[:, 5, :], in0=pv[:, 3, :], scalar1=-1.0, scalar2=1.0,
                                                op0=ALU.mult, op1=ALU.add), reads=[r_par], writes=[r_par])
        sch.op("dve", lambda e: e.tensor_copy(out=lup[:, :, :], in_=lup_f[:, :, :]), reads=[r_par, r_lup], writes=[r_par])
        sch.op("pool", lambda e: e.memset(Hs[:, :, :], 0.0), writes=[r_Hs])
        sch.op("pool", lambda e: e.memset(Hb[:, :, :], 0.0), writes=[r_Hb])
        for pr in range(2):
            sch.op("pool", lambda e, pr=pr: e.memset(Gp[pr][:, :], 0.0), writes=[r_Gp[pr]])
        for h in range(4):
            zs_ = slice((1 - h % 2) * 64, (1 - h % 2) * 64 + 64)
            sch.op("pool", lambda e, h=h, zs_=zs_: e.memset(ARz[h][zs_, :, :], 0.0), wacc=[r_AR[h // 2]])

        def v3(ap2):
            return ap2.rearrange("p (c t) -> p c t", t=L)

        dests = [(pm_r, 0, "r0"), (pm_r, 1, "r1"), (pm_k, 0, "k0"), (pm_k, 1, "k1"),
                 (pm_v, 0, "v0"), (pm_v, 1, "v1"), (None, 0, "l")]
        for blk in range(S // T):
            t0 = blk * T
            for rc in range(7):
                if blk == 0:
                    sch.op("pool", lambda e: e.memset(praw[:, 0:1], 0.0), writes=[r_praw])
                    sch.dma("sp", praw[:, 1:1 + T], d["rT"][rc * 128:(rc + 1) * 128, 0:T],
                            reads=[C.r_rT], wacc=[r_praw])
                else:
                    sch.dma("sp", praw[:, :], d["rT"][rc * 128:(rc + 1) * 128, t0 - 1:t0 + T],
                            reads=[C.r_rT], writes=[r_praw])
                dt_, pi, key = dests[rc]
                o = pm_l[:, :] if dt_ is None else dt_[:, pi, :]
                sch.op("pool", lambda e, rc=rc: e.tensor_scalar(out=tmp[:, :], in0=praw[:, 1:1 + T],
                                                                scalar1=omu[:, rc:rc + 1], scalar2=0.0, op0=ALU.mult, op1=ALU.add),
                       reads=[r_praw, r_par], writes=[r_tmp])
                sch.op("dve", lambda e, rc=rc, o=o: e.scalar_tensor_tensor(
                    out=o, in0=praw[:, 0:T], scalar=mu[:, rc:rc + 1], in1=tmp[:, :], op0=ALU.mult, op1=ALU.add),
                    reads=[r_praw, r_tmp, r_par], writes=[r_pm[key]])
            sch.op("act", lambda e: e.activation(out=lin[0:32, :], in_=pm_l[0:32, :], func=AF.Tanh),
                   reads=[r_pm["l"]], wacc=[r_lin])
            sch.op("act", lambda e: e.copy(out=lin[32:64, :], in_=pm_l[32:64, :]), reads=[r_pm["l"]], wacc=[r_lin])
            sch.op("act", lambda e: e.activation(out=lin[64:128, :], in_=pm_l[64:128, :], func=AF.Sigmoid),
                   reads=[r_pm["l"]], wacc=[r_lin])
            sch.dma("sp", lin_a[:, :], lin[32:64, :], reads=[r_lin], writes=[r_lina])
            sch.dma("sp", lin_g[:, :], lin[64:128, :], reads=[r_lin], writes=[r_ling])
            pb7 = banks[7]
            for pr in range(2):
                rk, rr_, rvv = r_pm["k%d" % pr], r_pm["r%d" % pr], r_pm["v%d" % pr]
                pcs = slice(pr * 128, (pr + 1) * 128)
                for hf in range(T // 512):
                    hs = slice(hf * 512, (hf + 1) * 512)
                    sch.op("pe", lambda e, hs=hs, pcs=pcs: e.matmul(out=pb7[:, :], lhsT=lup[0:32, 0, pcs],
                                                                    rhs=lin[0:32, hs], start=True, stop=True),
                           reads=[r_lin, r_par], writes=[rb[7]])
                    sch.op("act", lambda e, hs=hs, pr=pr: e.activation(out=lw[:, hs], in_=pb7[:, :], func=AF.Sigmoid,
                                                                       bias=pv[:, 0, pr:pr + 1], scale=1.0),
                           reads=[rb[7], r_par], wacc=[r_lw])
                    sch.op("pe", lambda e, hs=hs, pcs=pcs: e.matmul(out=pb7[:, :], lhsT=lup[0:32, 1, pcs],
                                                                    rhs=lin_a[:, hs], start=True, stop=True),
                           reads=[r_lina, r_par], writes=[rb[7]])
                    sch.op("act", lambda e, hs=hs, pr=pr: e.activation(out=aa[:, hs], in_=pb7[:, :], func=AF.Sigmoid,
                                                                       bias=pv[:, 1, pr:pr + 1], scale=1.0),
                           reads=[rb[7], r_par], wacc=[r_aa])
                sch.op("pool", lambda e: e.tensor_scalar(out=lw[:, :], in0=lw[:, :], scalar1=NEG, scalar2=0.0,
                                                         op0=ALU.mult, op1=ALU.add), reads=[r_lw], writes=[r_lw])
                sch.op("dve", lambda e, pr=pr: e.tensor_scalar(out=kx[:, :], in0=pm_k[:, pr, :],
                                                               scalar1=pv[:, 2, pr:pr + 1], scalar2=None,
                                                               op0=ALU.mult), reads=[rk, r_par], writes=[r_kx])
                sch.op("pool", lambda e: e.tensor_tensor(out=sq[:, :], in0=kx[:, :], in1=kx[:, :], op=ALU.mult),
                       reads=[r_kx], writes=[r_sq])
                for hf in range(T // 512):
                    hs = slice(hf * 512, (hf + 1) * 512)
                    sch.op("pe", lambda e, hs=hs: e.matmul(out=pb7[:, :], lhsT=C.blkf[:, :], rhs=sq[:, hs],
                                                           start=True, stop=True), reads=[r_sq], writes=[rb[7]])
                    sch.op("act", lambda e, hs=hs: e.activation(out=tmp[:, hs], in_=pb7[:, :], func=AF.Sqrt),
                           reads=[rb[7]], wacc=[r_tmp])
                sch.op("dve", lambda e: e.tensor_scalar(out=tmp[:, :], in0=tmp[:, :], scalar1=1e-12, scalar2=None,
                                                        op0=ALU.max), reads=[r_tmp], writes=[r_tmp])
                sch.op("dve", lambda e: e.reciprocal(out=tmp[:, :], in_=tmp[:, :]), reads=[r_tmp], writes=[r_tmp])
                sch.op("dve", lambda e: e.tensor_tensor(out=kx[:, :], in0=kx[:, :], in1=tmp[:, :], op=ALU.mult),
                       reads=[r_kx, r_tmp], writes=[r_kx])
                sch.op("pool", lambda e, pr=pr: e.tensor_scalar(out=kp[:, :], in0=aa[:, :], scalar1=pv[:, 3, pr:pr + 1],
                                                                scalar2=pv[:, 5, pr:pr + 1], op0=ALU.mult, op1=ALU.add),
                       reads=[r_aa, r_par], writes=[r_kp])
                sch.op("pool", lambda e, pr=pr: e.tensor_tensor(out=kp[:, :], in0=kp[:, :], in1=pm_k[:, pr, :],
                                                                op=ALU.mult), reads=[r_kp, rk], writes=[r_kp])
                sch.op("pool", lambda e: e.tensor_tensor(out=bb[:, :], in0=kx[:, :], in1=aa[:, :], op=ALU.mult),
                       reads=[r_kx, r_aa], writes=[r_bb])
                sch.op("dve", lambda e, pr=pr: e.scalar_tensor_tensor(
                    out=prodb[:, pr, :], in0=pm_r[:, pr, :], scalar=pv[:, 4, pr:pr + 1], in1=kp[:, :],
                    op0=ALU.mult, op1=ALU.mult), reads=[rr_, r_kp, r_par], writes=[r_prodb[pr]])
                G = Gp[pr]
                sch.op("dve", lambda e, G=G: e.tensor_copy(out=G[:, 0:1], in_=G[:, T:T + 1]),
                       reads=[r_Gp[pr]], writes=[r_Gp[pr]])
                sch.op("dve", lambda e, G=G: e.tensor_tensor_scan(out=G[:, 1:1 + T], data0=lw[:, :], data1=lw[:, :],
                                                                  initial=G[:, 0:1], op0=ALU.add, op1=ALU.min),
                       reads=[r_lw, r_Gp[pr]], writes=[r_Gp[pr]])
                base = v3(G[:, 0:T])[:, :, 0:1].broadcast_to([128, NCH, L])
                sch.op("dve", lambda e, G=G, base=base: e.tensor_tensor(out=v3(gi[:, :]), in0=v3(G[:, 1:1 + T]),
                                                                        in1=base, op=ALU.subtract),
                       reads=[r_Gp[pr]], writes=[r_gi])
                sch.op("pool", lambda e: e.tensor_tensor(out=ge[:, :], in0=gi[:, :], in1=lw[:, :], op=ALU.subtract),
                       reads=[r_gi, r_lw], writes=[r_ge])
                e1 = E1[pr]
                sch.op("act", lambda e, e1=e1: e.activation(out=e1[:, :], in_=gi[:, :], func=AF.Exp),
                       reads=[r_gi], writes=[r_E1[pr]])
                sch.op("act", lambda e: e.activation(out=E2[:, :], in_=ge[:, :], func=AF.Exp),
                       reads=[r_ge], writes=[r_E2])
                for hf in range(2):
                    ps_ = slice(hf * 64, hf * 64 + 64)
                    hq = 2 * pr + hf
                    sch.op("dve", lambda e, hq=hq, ps_=ps_: e.scalar_tensor_tensor(
                        out=ARz[hq][ps_, 0, :], in0=kx[ps_, :], scalar=-1.0, in1=E2[ps_, :], op0=ALU.mult, op1=ALU.mult),
                        reads=[r_kx, r_E2], wacc=[r_AR[pr]])
                    sch.op("pool", lambda e, hq=hq, ps_=ps_, pr=pr, e1=e1: e.tensor_tensor(
                        out=ARz[hq][ps_, 1, :], in0=pm_r[ps_, pr, :], in1=e1[ps_, :], op=ALU.mult),
                        reads=[rr_, r_E1[pr]], wacc=[r_AR[pr]])
                sch.op("act", lambda e: e.activation(out=E2[:, :], in_=gi[:, :], func=AF.Exp, scale=-1.0),
                       reads=[r_gi, r_AR[pr]], writes=[r_E2])
                sch.op("dve", lambda e, pr=pr: e.tensor_tensor(out=Bt[:, pr, :], in0=bb[:, :], in1=E2[:, :],
                                                               op=ALU.mult), reads=[r_bb, r_E2], writes=[r_Bt[pr]])
                sch.op("pool", lambda e, pr=pr: e.tensor_tensor(out=Kt[:, pr, :], in0=kp[:, :], in1=E2[:, :],
                                                                op=ALU.mult), reads=[r_kp, r_E2], writes=[r_Kt[pr]])
                gend = v3(gi[:, :])[:, :, L - 1:L].broadcast_to([128, NCH, L])
                sch.op("dve", lambda e, gend=gend: e.tensor_tensor(out=v3(ge[:, :]), in0=gend, in1=v3(gi[:, :]),
                                                                   op=ALU.subtract), reads=[r_gi], writes=[r_ge])
                sch.op("act", lambda e: e.activation(out=ge[:, :], in_=ge[:, :], func=AF.Exp),
                       reads=[r_ge], writes=[r_ge])
                sch.op("dve", lambda e, pr=pr: e.tensor_tensor(out=Bh[:, pr, :], in0=bb[:, :], in1=ge[:, :],
                                                               op=ALU.mult), reads=[r_bb, r_ge], writes=[r_Bh[pr]])
                sch.op("pool", lambda e, pr=pr: e.tensor_tensor(out=Kh[:, pr, :], in0=kp[:, :], in1=ge[:, :],
                                                                op=ALU.mult), reads=[r_kp, r_ge], writes=[r_Kh[pr]])
            for cc in range(NCH):
                cs = slice(cc * L, (cc + 1) * L)
                gcs = slice(t0 + cc * L, t0 + (cc + 1) * L)
                for pr in range(2):
                    tpp = banks[0]
                    for j, (src, rs) in enumerate(((Bh, r_Bh[pr]), (Kh, r_Kh[pr]), (pm_v, r_pm["v%d" % pr]))):
                        sch.op("pe", lambda e, j=j, src=src, pr=pr, cs=cs: e.transpose(
                            out=tpp[0:64, j * 128:(j + 1) * 128], in_=src[:, pr, cs], identity=C.identf[:, :]),
                            reads=[rs], writes=[rb[0]])
                    sch.op("act", lambda e, pr=pr: e.copy(
                        out=tok[:, pr, 0:3, :, :],
                        in_=banks[0][0:64, 0:384].rearrange("p (j h e) -> p j h e", j=3, h=2)),
                        reads=[rb[0]], writes=[r_tok[pr]])
                for h in range(4):
                    pr, pb = h // 2, (h % 2) * 64
                    bk = banks[1 + pr]
                    o = (h % 2) * 256
                    rhs_ar = ARz[h][:, :, cs]
                    sch.op("pe", lambda e, bk=bk, o=o, pr=pr, rhs_ar=rhs_ar, cs=cs: e.matmul(
                        out=bk[0:64, o:o + 128], lhsT=Bt[:, pr, cs], rhs=rhs_ar, start=True, stop=True),
                        reads=[r_Bt[pr], r_AR[pr]], writes=[rb[1 + pr]])
                    sch.op("pe", lambda e, bk=bk, o=o, pr=pr, rhs_ar=rhs_ar, cs=cs: e.matmul(
                        out=bk[0:64, o + 128:o + 256], lhsT=Kt[:, pr, cs], rhs=rhs_ar, start=True, stop=True),
                        reads=[r_Kt[pr], r_AR[pr]], writes=[rb[1 + pr]])
                    sch.op("pe", lambda e, h=h, pr=pr, cs=cs: e.matmul(
                        out=banks[4][0:64, h * 64:(h + 1) * 64], lhsT=ARz[h][:, 0, cs],
                        rhs=Bt[:, pr, cs], start=True, stop=True),
                        reads=[r_Bt[pr], r_AR[pr]], writes=[rb[4]])
                for pr in range(2):
                    sch.op("dve", lambda e, pr=pr: e.tensor_tensor(
                        out=scs[:, 2 * pr:2 * pr + 2, :, :].rearrange("p h b t -> p (h b t)"),
                        in0=banks[1 + pr][0:64, :], in1=C.rwmask[:, :], op=ALU.mult),
                        reads=[rb[1 + pr]], writes=[r_scs[pr]])
                sch.op("dve", lambda e: e.tensor_tensor(out=nns[:, :, :].rearrange("p h t -> p (h t)"),
                                                        in0=banks[4][0:64, 0:256], in1=C.nmask[:, :], op=ALU.mult),
                       reads=[rb[4]], writes=[r_nns])

                def Mlev(lev, h):
                    return scs[:, h, 0, :] if lev == 0 else pws[lev - 1][:, h, 1, :]

                def Nlev(lev, h):
                    return nns[:, h, :] if lev == 0 else pws[lev - 1][:, h, 0, :]

                def Rlev(lev, h):
                    return [r_scs[h // 2], r_nns] if lev == 0 else [r_pws[lev - 1]]
                for lev in range(1, 6):
                    for h in range(4):
                        if lev < 5:
                            sch.op("pe", lambda e, lev=lev, h=h: e.matmul(
                                out=banks[3][0:64, h * 128:h * 128 + 64], lhsT=Mlev(lev - 1, h), rhs=Nlev(lev - 1, h),
                                start=True, stop=True), reads=Rlev(lev - 1, h), writes=[rb[3]])
                        sch.op("pe", lambda e, lev=lev, h=h: e.matmul(
                            out=banks[3][0:64, h * 128 + 64:h * 128 + 128], lhsT=Nlev(lev - 1, h), rhs=Mlev(lev - 1, h),
                            start=True, stop=True), reads=Rlev(lev - 1, h), writes=[rb[3]])
                    evac(C, lev, pws[lev - 1][:, :, :, :].rearrange("p h b t -> p (h b t)"), banks[3][0:64, :],
                         [rb[3]], writes=[r_pws[lev - 1]])
                wq = banks[5]
                for h in range(4):
                    pr, pb, hf = h // 2, (h % 2) * 64, h % 2
                    sch.op("pe", lambda e, h=h, pr=pr, cs=cs: e.matmul(
                        out=wq[0:64, h * 64:(h + 1) * 64], lhsT=ARz[h][:, 0, cs], rhs=Hb[:, pr, :],
                        start=True, stop=False), reads=[r_AR[pr], r_Hb], writes=[rb[5]])
                    sch.op("pe", lambda e, h=h, pr=pr, hf=hf: e.matmul(
                        out=wq[0:64, h * 64:(h + 1) * 64], lhsT=scs[:, h, 2, :], rhs=tok[:, pr, 2, hf, :],
                        start=False, stop=True), reads=[r_scs[pr], r_tok[pr]], writes=[rb[5]])
                cur = 0
                sch.op("dve", lambda e: e.tensor_copy(out=U[0][:, :, :].rearrange("p h e -> p (h e)"),
                                                      in_=wq[0:64, 0:256]), reads=[rb[5]], writes=[r_U[0]])
                for lev in range(6):
                    for h in range(4):
                        sch.op("pe", lambda e, lev=lev, h=h, cur=cur: e.matmul(
                            out=wq[0:64, h * 64:(h + 1) * 64], lhsT=Mlev(lev, h), rhs=U[cur][:, h, :],
                            start=True, stop=True), reads=Rlev(lev, h) + [r_U[cur]], writes=[rb[5]])
                    sch.op("dve", lambda e, cur=cur: e.tensor_tensor(
                        out=U[1 - cur][:, :, :].rearrange("p h e -> p (h e)"), in0=wq[0:64, 0:256],
                        in1=U[cur][:, :, :].rearrange("p h e -> p (h e)"), op=ALU.add),
                        reads=[rb[5], r_U[cur]], writes=[r_U[1 - cur]])
                    cur = 1 - cur
                Uf, rUf = U[cur], r_U[cur]
                b6 = banks[6]
                for h in range(4):
                    pr, pb, hf = h // 2, (h % 2) * 64, h % 2
                    o = b6[0:64, h * 64:(h + 1) * 64]
                    sch.op("pe", lambda e, o=o, h=h, pr=pr, cs=cs: e.matmul(
                        out=o, lhsT=ARz[h][:, 1, cs], rhs=Hb[:, pr, :], start=True, stop=False),
                        reads=[r_AR[pr], r_Hb], writes=[rb[6]])
                    sch.op("pe", lambda e, o=o, h=h, Uf=Uf: e.matmul(
                        out=o, lhsT=scs[:, h, 1, :], rhs=Uf[:, h, :], start=False, stop=False),
                        reads=[r_scs[pr], rUf], writes=[rb[6]])
                    sch.op("pe", lambda e, o=o, h=h, pr=pr, hf=hf: e.matmul(
                        out=o, lhsT=scs[:, h, 3, :], rhs=tok[:, pr, 2, hf, :], start=False, stop=True),
                        reads=[r_scs[pr], r_tok[pr]], writes=[rb[6]])
                for h in range(4):
                    pr, pb, hf = h // 2, (h % 2) * 64, h % 2
                    o = b6[:, 256 + h * 64:256 + (h + 1) * 64]
                    sch.op("pe", lambda e, o=o, h=h, pr=pr, hf=hf, Uf=Uf: e.matmul(
                        out=o, lhsT=tok[:, pr, 0, :, :].rearrange("p a e -> p (a e)"), rhs=Uf[:, h, :],
                        start=True, stop=False), reads=[r_tok[pr], rUf], writes=[rb[6]])
                    sch.op("pe", lambda e, o=o, pr=pr, hf=hf: e.matmul(
                        out=o, lhsT=tok[:, pr, 1, :, :].rearrange("p a e -> p (a e)"), rhs=tok[:, pr, 2, hf, :],
                        start=False, stop=True), reads=[r_tok[pr]], writes=[rb[6]])
                sch.op("act", lambda e: e.copy(out=yv[:, :, :].rearrange("p h e -> p (h e)"), in_=b6[0:64, 0:256]),
                       reads=[rb[6]], writes=[r_yv])
                for h in range(4):
                    pr, pb = h // 2, (h % 2) * 64
                    sch.op("dve", lambda e, h=h, pr=pr, pb=pb, cc=cc: e.scalar_tensor_tensor(
                        out=Hs[pb:pb + 64, pr, :], in0=Hs[pb:pb + 64, pr, :],
                        scalar=E1[pr][pb:pb + 64, cc * L + L - 1:cc * L + L],
                        in1=b6[pb:pb + 64, 256 + h * 64:256 + (h + 1) * 64], op0=ALU.mult, op1=ALU.add),
                        reads=[r_Hs, r_E1[pr], rb[6]], wacc=[r_Hs])
                sch.op("act", lambda e: e.copy(out=Hb[:, :, :], in_=Hs[:, :, :]), reads=[r_Hs], writes=[r_Hb])
                b4 = banks[4]
                sch.op("pe", lambda e, cs=cs: e.matmul(out=b4[0:64, 256:512], lhsT=lin_g[:, cs], rhs=lup[0:64, 2, :],
                                                       start=True, stop=True), reads=[r_ling, r_par, r_nns], writes=[rb[4]])
                sch.op("dve", lambda e: e.tensor_reduce(out=st[:, 0, :, :], in_=yv[:, :, :], axis=AX.X, op=ALU.add),
                       reads=[r_yv], writes=[r_st])
                sch.op("pool", lambda e: e.tensor_tensor(out=ysq[:, :, :], in0=yv[:, :, :], in1=yv[:, :, :], op=ALU.mult),
                       reads=[r_yv], writes=[r_ysq])
                sch.op("dve", lambda e: e.tensor_reduce(out=st[:, 1, :, :], in_=ysq[:, :, :], axis=AX.X, op=ALU.add),
                       reads=[r_ysq], writes=[r_st])
                sch.op("dve", lambda e: e.tensor_scalar(out=st[:, 2, :, :], in0=st[:, 0, :, :], scalar1=1.0 / 64,
                                                        scalar2=None, op0=ALU.mult), reads=[r_st], writes=[r_st])
                sch.op("dve", lambda e: e.tensor_tensor(out=st[:, 3, :, :], in0=st[:, 2, :, :], in1=st[:, 2, :, :],
                                                        op=ALU.mult), reads=[r_st], writes=[r_st])
                sch.op("dve", lambda e: e.scalar_tensor_tensor(out=st[:, 4, :, :], in0=st[:, 1, :, :], scalar=1.0 / 64,
                                                               in1=st[:, 3, :, :], op0=ALU.mult, op1=ALU.subtract),
                       reads=[r_st], writes=[r_st])
                sch.op("dve", lambda e: e.tensor_scalar(out=st[:, 4, :, :], in0=st[:, 4, :, :], scalar1=64e-5,
                                                        scalar2=None, op0=ALU.add), reads=[r_st], writes=[r_st])
                sch.op("act", lambda e: e.activation(out=st[:, 5, :, :], in_=st[:, 4, :, :], func=AF.Sqrt),
                       reads=[r_st], writes=[r_st])
                sch.op("dve", lambda e: e.reciprocal(out=st[:, 6, :, :], in_=st[:, 5, :, :]), reads=[r_st], writes=[r_st])
                sch.op("dve", lambda e: e.tensor_tensor(out=ysq[:, :, :], in0=yv[:, :, :],
                                                        in1=st[:, 2, :, :].broadcast_to([64, 4, 64]), op=ALU.subtract),
                       reads=[r_yv, r_st], writes=[r_ysq])
                sch.op("dve", lambda e: e.tensor_tensor(out=ysq[:, :, :], in0=ysq[:, :, :],
                                                        in1=st[:, 6, :, :].broadcast_to([64, 4, 64]), op=ALU.mult),
                       reads=[r_ysq, r_st], writes=[r_ysq])
                y2 = ysq[:, :, :].rearrange("p h e -> p (h e)")
                sch.op("pool", lambda e: e.tensor_tensor(out=y2, in0=y2, in1=lng[:, :], op=ALU.mult),
                       reads=[r_ysq, r_par], writes=[r_ysq])
                sch.op("pool", lambda e: e.tensor_tensor(out=y2, in0=y2, in1=lnb[:, :], op=ALU.add),
                       reads=[r_ysq, r_par], writes=[r_ysq])
                for pr in range(2):
                    sch.op("pe", lambda e, pr=pr, cs=cs: e.matmul(out=b4[0:64, 2 * pr:2 * pr + 2], lhsT=prodb[:, pr, cs],
                                                                  rhs=C.sel2b[:, :], start=True, stop=True),
                           reads=[r_prodb[pr], r_nns], writes=[rb[4]])
                sch.op("act", lambda e: e.copy(out=st[:, 7, :, 0], in_=b4[0:64, 0:4]), reads=[rb[4]], writes=[r_st])
                for pr in range(2):
                    sch.op("dve", lambda e, pr=pr: e.tensor_tensor(
                        out=yv[:, 2 * pr:2 * pr + 2, :], in0=tok[:, pr, 2, :, :],
                        in1=st[:, 7, 2 * pr:2 * pr + 2, :].broadcast_to([64, 2, 64]), op=ALU.mult),
                        reads=[r_tok[pr], r_st], wacc=[r_yv])
                sch.op("pool", lambda e: e.tensor_tensor(out=ysq[:, :, :], in0=ysq[:, :, :], in1=yv[:, :, :], op=ALU.add),
                       reads=[r_ysq, r_yv], writes=[r_ysq])
                sch.op("dve", lambda e: e.tensor_tensor(out=yo[:, :], in0=y2, in1=b4[0:64, 256:512], op=ALU.mult),
                       reads=[r_ysq, rb[4]], writes=[r_yo])
                for j in range(2):
                    sch.op("pe", lambda e, j=j: e.transpose(out=banks[0][:, 384 + j * 64:384 + (j + 1) * 64],
                                                            in_=yo[:, j * 128:(j + 1) * 128],
                                                            identity=C.identf[0:64, 0:64]),
                           reads=[r_yo], writes=[rb[0]])
                sch.op("act", lambda e, gcs=gcs: e.copy(out=yrT[:, :, gcs],
                                                        in_=banks[0][:, 384:512].rearrange("p (j t) -> p j t", j=2)),
                       reads=[rb[0]], wacc=[r_yrT])
        for j in range(2):
            sch.dma("sp", d["mixT"][256 + j * 128:256 + (j + 1) * 128, :], yrT[:, j, :], reads=[r_yrT],
                    wacc=[C.r_mixT])


DOFF_ = (2, 4, 6, 9, 12, 15)


def phase_C(C, l, bg=None, bg_every=5):
    nc, sch, d = C.nc, C.sch, C.d
    QB = 512
    lam_init = 0.8 - 0.6 * math.exp(-0.3 * l)
    with ExitStack() as es:
        def sb(name, shape, dt):
            return _sb(es, nc, "C_" + name, shape, dt)
        lq = sb("lq", [128, 4, 64], F32)
        lt = sb("lt", [128, 2, 64], F32)
        ls = sb("ls", [128, 4], F32)
        nlam = sb("nlam", [128, 1], F32)
        ag = sb("ag", [128, 4], F32)
        r_par = Res()
        NHB = 1 if bg is not None else 2
        qT = [[sb("qT%d_%d" % (i, c), [128, S], BF16) for c in range(2)] for i in range(NHB)]
        kT = [sb("kT%d" % i, [128, S], BF16) for i in range(NHB)]
        vt = [sb("vt%d" % i, [128, NT, 128], BF16) for i in range(NHB)]
        r_q, r_k, r_v = [Res(), Res()], [Res(), Res()], [Res(), Res()]
        NB = 3
        pt = [sb("pt%d" % i, [128, QB], BF16) for i in range(NB)]
        r_pt = [Res() for _ in range(NB)]
        rl = sb("rl", [128, QB], F32)
        oc = [sb("oc%d" % i, [128, QB], F32) for i in range(2)]
        od = sb("od", [128, QB], F32)
        sq = sb("sq", [128, QB], F32)
        rs = sb("rs", [128, QB], F32)
        ya = sb("ya", [128, QB], BF16)
        r_rl, r_od, r_sq, r_rs, r_ya = Res(), Res(), Res(), Res(), Res()
        r_oc = [Res(), Res()]
        p_st = [_ps(es, nc, "C_pst%d" % i, [128, 512], F32) for i in range(3)]
        rp_st = [PRes() for _ in range(3)]
        p_os = [_ps(es, nc, "C_po%d" % i, [128, 512], F32) for i in range(2)]
        p_ls = [_ps(es, nc, "C_pl%d" % i, [128, 512], F32) for i in range(2)]
        p_ss = _ps(es, nc, "C_pss", [128, 512], F32)
        rp_os, rp_ls, rp_ss = [PRes(), PRes()], [PRes(), PRes()], PRes()
        deferred = []

        for i, nm in enumerate(("a_lq1", "a_lk1", "a_lq2", "a_lk2")):
            sch.dma("sp", lq[:, i, :], d[nm][l], wacc=[r_par])
        sch.dma("sp", ag[:, :], d["a_norm_g"][l], wacc=[r_par])
        for i in range(2):
            sch.op("dve", lambda e, i=i: e.tensor_tensor(out=lt[:, i, :], in0=lq[:, 2 * i, :], in1=lq[:, 2 * i + 1, :],
                                                         op=ALU.mult), reads=[r_par], writes=[r_par])
        sch.op("dve", lambda e: e.tensor_reduce(out=ls[:, 0:2], in_=lt[:, :, :], axis=AX.X, op=ALU.add),
               reads=[r_par], writes=[r_par])
        sch.op("act", lambda e: e.activation(out=ls[:, 2:4], in_=ls[:, 0:2], func=AF.Exp), reads=[r_par], writes=[r_par])
        sch.op("dve", lambda e: e.tensor_tensor(out=nlam[:, :], in0=ls[:, 3:4], in1=ls[:, 2:3], op=ALU.subtract),
               reads=[r_par], writes=[r_par])
        sch.op("dve", lambda e: e.tensor_scalar(out=nlam[:, :], in0=nlam[:, :], scalar1=-lam_init, scalar2=None,
                                                op0=ALU.add), reads=[r_par], writes=[r_par])
        sch.op("dve", lambda e: e.tensor_scalar(out=ag[:, :], in0=ag[:, :], scalar1=1.0 - lam_init, scalar2=None,
                                                op0=ALU.mult), reads=[r_par], writes=[r_par])
        ist = 0
        ipt = 0
        for h in range(4):
            hb = h % NHB
            for c in range(2):
                if h < NHB:
                    sch.op("pool", lambda e, hb=hb, c=c: e.memset(qT[hb][c][(1 - c) * 64:(1 - c) * 64 + 64, :], 0.0),
                           wacc=[r_q[hb]])
                sch.dma("sp", qT[hb][c][c * 64:c * 64 + 64, :], d["aqkT"][h * 128 + c * 64:h * 128 + c * 64 + 64, :],
                        reads=[C.r_aqkT], wacc=[r_q[hb]])
            sch.dma("sp", kT[hb][:, :], d["aqkT"][512 + h * 128:512 + (h + 1) * 128, :], reads=[C.r_aqkT],
                    writes=[r_k[hb]])
            sch.dma("sp", vt[hb][:, :, :], d["av"][:, h * 128:(h + 1) * 128].rearrange("(t p) e -> p t e", p=128),
                    reads=[C.r_av], writes=[r_v[hb]])
            items = [(j, c, kt) for j in range(S // QB) for c in range(2) for kt in range(4 * (j + 1))]

            def front(it, idx):
                j, c, kt = it
                i0 = max(0, kt - 4 * j)
                n0 = i0 * 128
                ps, rps = p_st[idx % NB], rp_st[idx % NB]
                p_, rp_ = pt[idx % NB], r_pt[idx % NB]
                sch.op("pe", lambda e: e.matmul(
                    out=ps[:, n0:QB], lhsT=kT[hb][:, kt * 128:(kt + 1) * 128],
                    rhs=qT[hb][c][:, j * QB + n0:(j + 1) * QB], start=True, stop=True),
                    reads=[r_k[hb], r_q[hb]], writes=[rps])
                sch.op("act", lambda e: e.activation(out=p_[:, n0:QB], in_=ps[:, n0:QB], func=AF.Exp, scale=0.125),
                       reads=[rps], writes=[rp_])
                if kt >= 4 * j:
                    eng = "dve"
                    sch.op(eng, lambda e: e.tensor_tensor(out=p_[:, n0:n0 + 128], in0=p_[:, n0:n0 + 128],
                                                          in1=C.trib[:, :], op=ALU.mult), reads=[rp_], writes=[rp_])

            def back(it, idx):
                j, c, kt = it
                nk = 4 * (j + 1)
                gpar = (2 * j + c) % 2
                p_o, rp_o = p_os[gpar], rp_os[gpar]
                p_l, rp_l = p_ls[gpar], rp_ls[gpar]
                i0 = max(0, kt - 4 * j)
                n0 = i0 * 128
                p_, rp_ = pt[idx % NB], r_pt[idx % NB]
                sch.op("pe", lambda e: e.matmul(out=p_o[:, n0:QB], lhsT=vt[hb][:, kt, :], rhs=p_[:, n0:QB],
                                                start=(kt == 0), stop=(kt == nk - 1)),
                       reads=[r_v[hb], rp_], writes=[rp_o])
                sch.op("pe", lambda e: e.matmul(out=p_l[:, n0:QB], lhsT=C.onesb[:, :], rhs=p_[:, n0:QB],
                                                start=(kt == 0), stop=(kt == nk - 1)), reads=[rp_], writes=[rp_l])
                if kt != nk - 1:
                    return
                def d0():
                    sch.op("act", lambda e: e.activation(out=rl[:, :], in_=p_l[:, :], func=AF.Ln),
                           reads=[rp_l], writes=[r_rl])
                    sch.op("act", lambda e: e.activation(out=rl[:, :], in_=rl[:, :], func=AF.Exp, scale=-1.0),
                           reads=[r_rl], writes=[r_rl])

                def d0b():
                    sch.op("dve", lambda e: e.tensor_tensor(out=oc[c][:, :], in0=p_o[:, :], in1=rl[:, :],
                                                            op=ALU.mult), reads=[rp_o, r_rl], writes=[r_oc[c]])
                deferred.extend([[DOFF_[0], d0], [DOFF_[1], d0b]])
                if c == 0:
                    return
                qs = slice(j * QB, (j + 1) * QB)

                def d1():
                    sch.op("dve", lambda e: e.scalar_tensor_tensor(out=od[:, :], in0=oc[1][:, :], scalar=nlam[:, 0:1],
                                                                   in1=oc[0][:, :], op0=ALU.mult, op1=ALU.add),
                           reads=[r_oc[0], r_oc[1], r_par], writes=[r_od])
                    sch.op("pool", lambda e: e.tensor_tensor(out=sq[:, :], in0=od[:, :], in1=od[:, :], op=ALU.mult),
                           reads=[r_od], writes=[r_sq])

                def d2():
                    sch.op("pe", lambda e: e.matmul(out=p_ss[:, :], lhsT=C.onesf[:, :], rhs=sq[:, :], start=True,
                                                    stop=True), reads=[r_sq], writes=[rp_ss])
                    sch.op("dve", lambda e: e.tensor_scalar(out=rs[:, :], in0=p_ss[:, :], scalar1=1.0 / 128,
                                                            scalar2=1e-5, op0=ALU.mult, op1=ALU.add),
                           reads=[rp_ss], writes=[r_rs])

                def d3():
                    sch.op("act", lambda e: e.activation(out=rs[:, :], in_=rs[:, :], func=AF.Ln),
                           reads=[r_rs], writes=[r_rs])
                    sch.op("act", lambda e: e.activation(out=rs[:, :], in_=rs[:, :], func=AF.Exp, scale=-0.5),
                           reads=[r_rs], writes=[r_rs])

                def d4():
                    sch.op("dve", lambda e: e.scalar_tensor_tensor(out=ya[:, :], in0=od[:, :], scalar=ag[:, h:h + 1],
                                                                   in1=rs[:, :], op0=ALU.mult, op1=ALU.mult),
                           reads=[r_od, r_rs, r_par], writes=[r_ya])
                    sch.dma("sp", d["mixT"][512 + h * 128:512 + (h + 1) * 128, qs], ya[:, :], reads=[r_ya],
                            wacc=[C.r_mixT])
                deferred.extend([[DOFF_[2], d1], [DOFF_[3], d2], [DOFF_[4], d3], [DOFF_[5], d4]])

            def run_deferred(force=False):
                for e_ in list(deferred):
                    e_[0] -= 1
                    if force or e_[0] <= 0:
                        deferred.remove(e_)
                        e_[1]()

            LOOK = 2
            for i in range(min(LOOK, len(items))):
                front(items[i], ist + i)
            for i in range(len(items)):
                if i + LOOK < len(items):
                    front(items[i + LOOK], ist + i + LOOK)
                back(items[i], ist + i)
                run_deferred()
                if bg is not None and i % bg_every == bg_every // 2:
                    for _ in range(globals().get("BGN_", 1)):
                        next(bg, None)
            ist += len(items)
            while deferred:
                run_deferred(force=True)
        if bg is not None:
            for _ in bg:
                pass


PARAM_SHAPES = {
    "norm1_g": [DEPTH, 128, 8], "norm2_g": [DEPTH, 128, 8], "final_gb": [128, D],
    "w_in": [DEPTH, D, IN_PROJ], "w_out": [DEPTH, D, D], "w_ff_up": [DEPTH, D, DFF], "w_ff_down": [DEPTH, DFF, D],
    "m_conv_w": [DEPTH, 128, 4, 4], "m_conv_b": [DEPTH, 128, 4], "m_b_i": [DEPTH, 4, 1], "m_b_f": [DEPTH, 4, 1],
    "m_norm_g": [DEPTH, 128, 256],
    "r_mu": [DEPTH, 128, 7], "r_w0": [DEPTH, 128, 2], "r_a0": [DEPTH, 128, 2], "r_k_k": [DEPTH, 128, 2],
    "r_k_a": [DEPTH, 128, 2], "r_r_k": [DEPTH, 128, 2], "r_w_up": [DEPTH, 32, 256], "r_a_up": [DEPTH, 32, 256],
    "r_g_up": [DEPTH, 64, 256], "r_ln_g": [DEPTH, 64, 256], "r_ln_b": [DEPTH, 64, 256],
    "a_lq1": [DEPTH, 128, 64], "a_lk1": [DEPTH, 128, 64], "a_lq2": [DEPTH, 128, 64], "a_lk2": [DEPTH, 128, 64],
    "a_norm_g": [DEPTH, 128, 4],
    "c_ident": [128, 128], "c_tri": [128, 128], "c_rwmask": [64, 512], "c_nmask": [64, 256],
    "c_sel4": [4, 2, 128], "c_blk": [128, 128], "c_sel2": [128, 2],
}
SCRATCH = {
    "mqkT": ([512, S], F32), "mgT": ([8, S], F32), "mvo": ([S, 512], F32), "rT": ([896, S], F32),
    "aqkT": ([1024, S], BF16), "av": ([S, 512], BF16), "mixT": ([1024, S], BF16), "xres": ([S, D], F32),
    "rpb": ([1024, 15 * 512], BF16), "rpf": ([1024, 8 * 512], F32),
}


BIGW = ("w_in", "w_out", "w_ff_up", "w_ff_down")


def build(phases=None, debug=(), ext_in=(), limit=None):
    nc = bass.Bass("TRN2", target_bir_lowering=False)
    C = Ctx()
    C.nc = nc
    d = {}
    d["x"] = nc.dram_tensor("x", [S, D], F32, kind="ExternalInput").ap()
    need_big = phases is None or any(ph in ("A", "O") for ph, _ in phases)
    for k, shp in PARAM_SHAPES.items():
        if k in BIGW and not need_big:
            continue
        d[k] = nc.dram_tensor(k, shp, F32, kind="ExternalInput").ap()
    for k, (shp, dt) in SCRATCH.items():
        kind = "ExternalOutput" if k in debug else "Internal"
        if k in ext_in:
            kind = "ExternalInput"
        d[k] = nc.dram_tensor(k, shp, dt, kind=kind).ap()
    d["y"] = nc.dram_tensor("y", [S, D], F32, kind="ExternalOutput").ap()
    C.d = d
    for k in ("mqkT", "mgT", "mvo", "rT", "aqkT", "av", "mixT", "xres", "x", "y", "rpb", "rpf"):
        setattr(C, "r_" + k, Res(k))
    with ExitStack() as es:
        sch = Sched(nc, es)
        sch.limit = limit
        C.sch = sch
        C.identf = _sb(es, nc, "identf", [128, 128], F32)
        C.identb = _sb(es, nc, "identb", [128, 128], BF16)
        C.trif = _sb(es, nc, "trif", [128, 128], F32)
        C.trib = _sb(es, nc, "trib", [128, 128], BF16)
        C.rwmask = _sb(es, nc, "rwmask", [64, 512], F32)
        C.nmask = _sb(es, nc, "nmask", [64, 256], F32)
        C.sel4 = _sb(es, nc, "sel4", [4, 2, 128], F32)
        C.blkf = _sb(es, nc, "blkf", [128, 128], F32)
        C.sel2f = _sb(es, nc, "sel2f", [128, 2], F32)
        C.sel2b = _sb(es, nc, "sel2b", [128, 2], BF16)
        C.onesf = _sb(es, nc, "onesf", [128, 128], F32)
        C.onesb = _sb(es, nc, "onesb", [128, 128], BF16)
        rc = Res()
        for t, k in ((C.identf, "c_ident"), (C.trif, "c_tri"), (C.rwmask, "c_rwmask"), (C.nmask, "c_nmask"),
                     (C.blkf, "c_blk"), (C.sel2f, "c_sel2")):
            sch.dma("sp", t[:, :], d[k], wacc=[rc])
        sch.dma("sp", C.sel4[:, :, :], d["c_sel4"], wacc=[rc])
        sch.op("dve", lambda e: e.tensor_copy(out=C.identb[:, :], in_=C.identf[:, :]), reads=[rc], wacc=[rc])
        sch.op("dve", lambda e: e.tensor_copy(out=C.trib[:, :], in_=C.trif[:, :]), reads=[rc], wacc=[rc])
        sch.op("dve", lambda e: e.tensor_copy(out=C.sel2b[:, :], in_=C.sel2f[:, :]), reads=[rc], wacc=[rc])
        sch.op("pool", lambda e: e.memset(C.onesf[:, :], 1.0), wacc=[rc])
        sch.op("pool", lambda e: e.memset(C.onesb[:, :], 1.0), wacc=[rc])
        for eng in ("pe", "dve", "act", "pool"):
            sch._deps(eng, [rc], [])
        if phases is None:
            phases = []
            for l in range(DEPTH):
                phases += [("A", l), ("M", l), ("R", l), ("C", l), ("O", l)]
        phases_all = list(phases)
        for (ph, l) in phases:
            sch.barrier()
            X, rX = (d["x"], C.r_x) if l == 0 else (d["xres"], C.r_xres)
            if ph == "A":
                phase_A(C, l, X, rX)
            elif ph == "M":
                phase_M(C, l)
            elif ph == "R":
                phase_R2(C, l)
            elif ph == "C":
                if getattr(build, "preload", True) and (phases_all.count(("O", l)) > 0):
                    wes = ExitStack()
                    C.ow = OWeights(C, wes, l)
                    C.ow_es = wes
                    phase_C(C, l, bg=C.ow.gen())
                else:
                    C.ow = None
                    phase_C(C, l)
            elif ph == "RC":
                phase_RC(C, l)
            elif ph == "CP":
                wes = ExitStack()
                gen = phase_R2(C, l, mode="prep", es_ext=wes)
                phase_C(C, l, bg=gen, bg_every=1)
                wes.close()
            elif ph == "R3":
                phase_R2(C, l, mode="chunks")
            elif ph == "RP":
                wes = ExitStack()
                for _ in phase_R2(C, l, mode="prep", es_ext=wes):
                    pass
                wes.close()
            elif ph == "O":
                last = (l == DEPTH - 1)
                XO, rXO = (d["y"], C.r_y) if last else (d["xres"], C.r_xres)
                ow = getattr(C, "ow", None)
                phase_O(C, l, X, rX, XO, rXO, last, W=ow)
                if ow is not None:
                    C.ow_es.close()
                    C.ow = None
        sch.limit = None
        sch.finish()
        C.nops = sch.nops
    build.last_nops = sch.nops
    return nc


def host_params(inp):
    f = lambda a: np.ascontiguousarray(np.asarray(a, dtype=np.float32))
    L = DEPTH
    p = {}
    p["norm1_g"] = f(np.asarray(inp["norm1_g"]).reshape(L, 8, 128).transpose(0, 2, 1))
    p["norm2_g"] = f(np.asarray(inp["norm2_g"]).reshape(L, 8, 128).transpose(0, 2, 1))
    p["final_gb"] = f(np.broadcast_to(np.asarray(inp["final_g"]).reshape(1, D), (128, D)))
    for k in ("w_in", "w_out", "w_ff_up", "w_ff_down", "r_w_up", "r_a_up", "r_g_up"):
        p[k] = f(inp[k])
    p["m_conv_w"] = f(np.asarray(inp["m_conv_w"]).reshape(L, 4, 4, 128).transpose(0, 3, 2, 1))
    p["m_conv_b"] = f(np.asarray(inp["m_conv_b"]).reshape(L, 4, 128).transpose(0, 2, 1))
    p["m_b_i"] = f(np.asarray(inp["m_b_i"]).reshape(L, 4, 1))
    p["m_b_f"] = f(np.asarray(inp["m_b_f"]).reshape(L, 4, 1))
    p["m_norm_g"] = f(np.broadcast_to(np.asarray(inp["m_norm_g"]).reshape(L, 1, 256), (L, 128, 256)))
    p["r_mu"] = f(np.asarray(inp["r_mu"]).reshape(L, 7, 128).transpose(0, 2, 1))
    for k in ("r_w0", "r_a0", "r_k_k", "r_k_a", "r_r_k"):
        p[k] = f(np.asarray(inp[k]).reshape(L, 2, 128).transpose(0, 2, 1))
    p["r_ln_g"] = f(np.broadcast_to(np.asarray(inp["r_ln_g"]).reshape(L, 1, 256), (L, 64, 256)))
    p["r_ln_b"] = f(np.broadcast_to(np.asarray(inp["r_ln_b"]).reshape(L, 1, 256), (L, 64, 256)))
    for k in ("a_lq1", "a_lk1", "a_lq2", "a_lk2"):
        p[k] = f(np.broadcast_to(np.asarray(inp[k]).reshape(L, 1, 64), (L, 128, 64)))
    p["a_norm_g"] = f(np.asarray(inp["a_norm_g"]).reshape(L, 4, 128).transpose(0, 2, 1))
    pp = np.arange(128)[:, None]
    nn = np.arange(128)[None, :]
    p["c_ident"] = f(pp == nn)
    p["c_tri"] = f(pp <= nn)
    p64, n64 = np.arange(64)[:, None], np.arange(64)[None, :]
    strict, incl = f(p64 < n64), f(p64 <= n64)
    one = np.concatenate([strict, incl, strict, incl], axis=1)
    p["c_rwmask"] = f(np.concatenate([one, one], axis=1))
    p["c_nmask"] = f(np.tile(f(n64 < p64), (1, 4)))
    sel4 = np.zeros((4, 2, 128), np.float32)
    for h in range(4):
        sel4[h, h // 2, (h % 2) * 64:(h % 2) * 64 + 64] = 1.0
    p["c_sel4"] = sel4
    p["c_blk"] = f((pp // 64) == (nn // 64))
    p["c_sel2"] = f((np.arange(128)[:, None] // 64) == np.arange(2)[None, :])
    return p


_NC_CACHE = {}


def kernel(**inputs):
    x = np.asarray(inputs["x"], dtype=np.float32)
    B = x.shape[0]
    p = host_params(inputs)
    if "full" not in _NC_CACHE:
        _NC_CACHE["full"] = build()
    nc = _NC_CACHE["full"]
    in_maps = []
    for b in range(B):
        m = dict(p)
        m["x"] = np.ascontiguousarray(x[b])
        in_maps.append(m)
    res = run_bass_kernel_spmd(nc, in_maps, core_ids=list(range(B)))
    return np.stack([np.asarray(r["y"]) for r in res.results], axis=0).astype(np.float32)


class _NullCtx:
    def __init__(self, v):
        self.v = v

    def __enter__(self):
        return self.v

    def __exit__(self, *a):
        return False


def phase_R2(C, l, mode="all", es_ext=None):
    nc, sch, d = C.nc, C.sch, C.d
    DO_PREP = mode in ("all", "prep")
    SLACK = globals().get("SLACK_", 2) if mode == "prep" else 0
    DO_CHUNKS = mode in ("all", "chunks")
    T = 512
    L = 64
    NCH = T // L
    NBLK = S // T
    NEG = -0.6065306597126334
    with (ExitStack() if es_ext is None else _NullCtx(es_ext)) as es:
        def sb(name, shape, dt):
            return _sb(es, nc, "R_" + name, shape, dt)
        mu = sb("mu", [128, 7], F32)
        omu = sb("omu", [128, 7], F32)
        pv = sb("pv", [128, 6, 2], F32)
        lup_f = sb("lupf", [64, 3, 256], F32)
        lup = sb("lup", [64, 3, 256], BF16)
        lng = sb("lng", [64, 256], F32)
        lnb = sb("lnb", [64, 256], F32)
        r_par, r_lup = Res(), Res()
        r_praw = Res()
        r_lin, r_lina = Res(), Res()
        r_tmp = Res()
        PT = []
        r_Gp = [Res(), Res()]
        if DO_PREP:
            praw = sb("praw", [128, 1 + T], F32)
            pm_r = sb("pm_r", [128, 2, T], F32)
            pm_k = sb("pm_k", [128, 2, T], F32)
            pm_l = sb("pm_l", [128, T], F32)
            lin = sb("lin", [128, T], BF16)
            lin_a = sb("lin_a", [32, T], BF16)
            tmp = sb("tmp", [128, T], F32)
            for i in range(2):
                t_ = Ctx()
                for nm in ("tmp", "lw", "aa", "kx", "sq", "kp", "bb", "gi", "ge", "E2"):
                    setattr(t_, nm, sb("%s_p%d" % (nm, i), [128, T], F32))
                    setattr(t_, "r_" + nm, Res())
                PT.append(t_)
            Gp = [sb("Gp%d" % i, [128, 1 + T], F32) for i in range(2)]
        B = []
        for i in range(2):
            o = Ctx()
            o.bbf = sb("Bbf%d" % i, [128, 15, T], BF16)
            o.bf32 = sb("Bf32_%d" % i, [128, 8, T], F32)
            o.ARz = [o.bbf[:, 2 * h:2 * h + 2, :] for h in range(4)]
            o.Bt = o.bbf[:, 8:10, :]
            o.Kt = o.bbf[:, 10:12, :]
            o.prodb = o.bbf[:, 12:14, :]
            o.lin_g = o.bbf[0:64, 14, :]
            o.Bh = o.bf32[:, 0:2, :]
            o.Kh = o.bf32[:, 2:4, :]
            o.pm_v = o.bf32[:, 4:6, :]
            o.E1 = [o.bf32[:, 6 + p, :] for p in range(2)]
            o.r_AR, o.r_Bt, o.r_Kt, o.r_Bh, o.r_Kh, o.r_v, o.r_prodb, o.r_E1 = ([Res(), Res()] for _ in range(8))
            o.r_ling = Res()
            o.all_bf = o.r_AR + o.r_Bt + o.r_Kt + o.r_prodb + [o.r_ling]
            o.all_f32 = o.r_Bh + o.r_Kh + o.r_v + o.r_E1
            B.append(o)
        NK = 3
        K_ = []
        for i in range(NK if DO_CHUNKS else 0):
            o = Ctx()
            o.tok = sb("tok%d" % i, [64, 2, 4, 2, 64], BF16)
            o.scs = sb("scs%d" % i, [64, 4, 4, 64], BF16)
            o.nns = sb("nns%d" % i, [64, 4, 64], BF16)
            o.pws = [sb("pws%d_%d" % (i, j), [64, 4, 2, 64], BF16) for j in range(2)]
            o.Q = [sb("Q%d_%d" % (i, j), [64, 4, 64], BF16) for j in range(2)]
            o.r_tok = [Res(), Res()]
            o.r_scs = [Res(), Res()]
            o.r_nns = Res()
            o.r_pws = [Res() for _ in range(2)]
            o.r_Q = [Res(), Res()]
            K_.append(o)
        r_Hs, r_Hb = Res(), Res()
        r_yrT = Res()
        r_Wt, r_Ut = Res(), Res()
        r_yv = [Res(), Res()]
        r_bvv, r_st7, r_gsb = [Res(), Res()], [Res(), Res()], [Res(), Res()]
        r_yo2 = [Res(), Res()]
        r_ysq, r_bv, r_yo, r_st = Res(), Res(), Res(), Res()
        if DO_CHUNKS:
            Hs = sb("Hs", [128, 2, 64], F32)
            Hb = sb("Hb", [128, 2, 64], BF16)
            yrT = sb("yrT", [128, 2, S], BF16)
            Wt = sb("Wt", [64, 4, 64], BF16)
            Ut = sb("Ut", [64, 4, 64], BF16)
            yv = [sb("yv%d" % i, [64, 4, 64], F32) for i in range(2)]
            ysq = sb("ysq", [64, 4, 64], F32)
            bv = [sb("bv%d" % i, [64, 4, 64], F32) for i in range(2)]
            st7 = [sb("st7_%d" % i, [64, 4, 1], F32) for i in range(2)]
            gsb = [sb("gsb%d" % i, [64, 256], F32) for i in range(2)]
            yo = [sb("yo%d" % i, [64, 256], F32) for i in range(2)]
            st = sb("st", [64, 8, 4, 1], F32)
            banks = [_ps(es, nc, "R_b%d" % i, [128, 512], F32) for i in range(8)]
            rb = [PRes() for _ in range(8)]
        else:
            pb_only = _ps(es, nc, "R_pb", [128, 512], F32)
            banks = [None] * 7 + [pb_only]
            rb = [None] * 7 + [PRes()]

        sch.dma("sp", mu[:, :], d["r_mu"][l], wacc=[r_par])
        for i, nm in enumerate(("r_w0", "r_a0", "r_k_k", "r_k_a", "r_r_k")):
            sch.dma("sp", pv[:, i, :], d[nm][l], wacc=[r_par])
        sch.op("pool", lambda e: e.memset(lup_f[:, :, :], 0.0), writes=[r_lup])
        sch.dma("sp", lup_f[0:32, 0, :], d["r_w_up"][l], writes=[r_lup])
        sch.dma("sp", lup_f[0:32, 1, :], d["r_a_up"][l], writes=[r_lup])
        sch.dma("sp", lup_f[0:64, 2, :], d["r_g_up"][l], writes=[r_lup])
        sch.dma("sp", lng[:, :], d["r_ln_g"][l], wacc=[r_par])
        sch.dma("sp", lnb[:, :], d["r_ln_b"][l], wacc=[r_par])
        sch.op("dve", lambda e: e.tensor_scalar(out=omu[:, :], in0=mu[:, :], scalar1=-1.0, scalar2=1.0,
                                                op0=ALU.mult, op1=ALU.add), reads=[r_par], writes=[r_par])
        sch.op("dve", lambda e: e.tensor_scalar(out=pv[:, 5, :], in0=pv[:, 3, :], scalar1=-1.0, scalar2=1.0,
                                                op0=ALU.mult, op1=ALU.add), reads=[r_par], writes=[r_par])
        sch.op("dve", lambda e: e.tensor_copy(out=lup[:, :, :], in_=lup_f[:, :, :]), reads=[r_par, r_lup],
               writes=[r_par])
        if DO_CHUNKS:
            sch.op("pool", lambda e: e.memset(Hs[:, :, :], 0.0), writes=[r_Hs])
            sch.op("pool", lambda e: e.memset(Hb[:, :, :], 0.0), writes=[r_Hb])
        if DO_PREP:
            for pr in range(2):
                sch.op("pool", lambda e, pr=pr: e.memset(Gp[pr][:, :], 0.0), writes=[r_Gp[pr]])
            for i in range(2):
                for h in range(4):
                    zs_ = slice((1 - h % 2) * 64, (1 - h % 2) * 64 + 64)
                    sch.op("pool", lambda e, i=i, h=h, zs_=zs_: e.memset(B[i].ARz[h][zs_, :, :], 0.0),
                           wacc=[B[i].r_AR[h // 2]])
            sch.op("pool", lambda e: e.memset(praw[:, 0:1], 0.0), writes=[r_praw])

        def v3(ap2):
            return ap2.rearrange("p (c t) -> p c t", t=L)

        def prep_gen(blk):
            o = B[blk % 2]
            t0 = blk * T
            dests = [(pm_r, 0), (pm_r, 1), (pm_k, 0), (pm_k, 1), (o.pm_v, 0), (o.pm_v, 1), (None, 0)]
            r_pmr, r_pmk = [Res(), Res()], [Res(), Res()]
            r_pml = Res()
            rr_list = [r_pmr[0], r_pmr[1], r_pmk[0], r_pmk[1], o.r_v[0], o.r_v[1], r_pml]
            for rc in range(7):
                if blk == 0:
                    sch.dma("sp", praw[:, 1:1 + T], d["rT"][rc * 128:(rc + 1) * 128, 0:T],
                            reads=[C.r_rT], wacc=[r_praw])
                else:
                    sch.dma("sp", praw[:, :], d["rT"][rc * 128:(rc + 1) * 128, t0 - 1:t0 + T],
                            reads=[C.r_rT], writes=[r_praw])
                dt_, pi = dests[rc]
                ot = pm_l[:, :] if dt_ is None else dt_[:, pi, :]
                sch.op("pool", lambda e, rc=rc: e.tensor_scalar(out=tmp[:, :], in0=praw[:, 1:1 + T],
                                                                scalar1=omu[:, rc:rc + 1], scalar2=0.0, op0=ALU.mult, op1=ALU.add),
                       reads=[r_praw, r_par], writes=[r_tmp])
                yield
                sch.op("dve", lambda e, rc=rc, ot=ot: e.scalar_tensor_tensor(
                    out=ot, in0=praw[:, 0:T], scalar=mu[:, rc:rc + 1], in1=tmp[:, :], op0=ALU.mult, op1=ALU.add),
                    reads=[r_praw, r_tmp, r_par], writes=[rr_list[rc]])
                yield
            for _ in range(SLACK):
                yield
            sch.op("act", lambda e: e.activation(out=lin[0:32, :], in_=pm_l[0:32, :], func=AF.Tanh),
                   reads=[r_pml], wacc=[r_lin])
            for _ in range(SLACK):
                yield
            sch.op("act", lambda e: e.copy(out=lin[32:64, :], in_=pm_l[32:64, :]), reads=[r_pml], wacc=[r_lin])
            for _ in range(SLACK):
                yield
            sch.op("act", lambda e: e.activation(out=lin[64:128, :], in_=pm_l[64:128, :], func=AF.Sigmoid),
                   reads=[r_pml], wacc=[r_lin])
            sch.dma("sp", lin_a[:, :], lin[32:64, :], reads=[r_lin], writes=[r_lina])
            sch.dma("sp", o.lin_g[:, :], lin[64:128, :], reads=[r_lin], writes=[o.r_ling])
            yield
            pb7 = banks[7]

            def pair_gen(pr):
                P_ = PT[pr]
                tmp, lw, aa, kx, sq, kp, bb, gi, ge, E2 = (P_.tmp, P_.lw, P_.aa, P_.kx, P_.sq, P_.kp, P_.bb, P_.gi,
                                                          P_.ge, P_.E2)
                r_tmp, r_lw, r_aa, r_kx, r_sq, r_kp, r_bb, r_gi, r_ge, r_E2 = (
                    P_.r_tmp, P_.r_lw, P_.r_aa, P_.r_kx, P_.r_sq, P_.r_kp, P_.r_bb, P_.r_gi, P_.r_ge, P_.r_E2)
                rk, rr_ = r_pmk[pr], r_pmr[pr]
                pcs = slice(pr * 128, (pr + 1) * 128)
                sch.op("pe", lambda e, pcs=pcs: e.matmul(out=pb7[:, 0:T], lhsT=lup[0:32, 0, pcs], rhs=lin[0:32, :],
                                                         start=True, stop=True), reads=[r_lin, r_par], writes=[rb[7]])
                for _ in range(SLACK):
                    yield
                sch.op("act", lambda e, pr=pr: e.activation(out=lw[:, :], in_=pb7[:, 0:T], func=AF.Sigmoid,
                                                            bias=pv[:, 0, pr:pr + 1], scale=1.0),
                       reads=[rb[7], r_par], writes=[r_lw])
                yield
                sch.op("pe", lambda e, pcs=pcs: e.matmul(out=pb7[:, 0:T], lhsT=lup[0:32, 1, pcs], rhs=lin_a[:, :],
                                                         start=True, stop=True), reads=[r_lina, r_par], writes=[rb[7]])
                for _ in range(SLACK):
                    yield
                sch.op("act", lambda e, pr=pr: e.activation(out=aa[:, :], in_=pb7[:, 0:T], func=AF.Sigmoid,
                                                            bias=pv[:, 1, pr:pr + 1], scale=1.0),
                       reads=[rb[7], r_par], writes=[r_aa])
                yield
                sch.op("pool", lambda e: e.tensor_scalar(out=lw[:, :], in0=lw[:, :], scalar1=NEG, scalar2=0.0,
                                                         op0=ALU.mult, op1=ALU.add), reads=[r_lw], writes=[r_lw])
                sch.op("dve", lambda e, pr=pr: e.tensor_scalar(out=kx[:, :], in0=pm_k[:, pr, :],
                                                               scalar1=pv[:, 2, pr:pr + 1], scalar2=None,
                                                               op0=ALU.mult), reads=[rk, r_par], writes=[r_kx])
                yield
                sch.op("pool", lambda e: e.tensor_tensor(out=sq[:, :], in0=kx[:, :], in1=kx[:, :], op=ALU.mult),
                       reads=[r_kx], writes=[r_sq])
                sch.op("pe", lambda e: e.matmul(out=pb7[:, 0:T], lhsT=C.blkf[:, :], rhs=sq[:, :],
                                                start=True, stop=True), reads=[r_sq], writes=[rb[7]])
                for _ in range(SLACK):
                    yield
                sch.op("act", lambda e: e.activation(out=tmp[:, :], in_=pb7[:, 0:T], func=AF.Sqrt),
                       reads=[rb[7]], writes=[r_tmp])
                yield
                sch.op("dve", lambda e: e.tensor_scalar(out=tmp[:, :], in0=tmp[:, :], scalar1=1e-12, scalar2=None,
                                                        op0=ALU.max), reads=[r_tmp], writes=[r_tmp])
                yield
                sch.op("dve", lambda e: e.reciprocal(out=tmp[:, :], in_=tmp[:, :]), reads=[r_tmp], writes=[r_tmp])
                yield
                sch.op("dve", lambda e: e.tensor_tensor(out=kx[:, :], in0=kx[:, :], in1=tmp[:, :], op=ALU.mult),
                       reads=[r_kx, r_tmp], writes=[r_kx])
                yield
                sch.op("pool", lambda e, pr=pr: e.tensor_scalar(out=kp[:, :], in0=aa[:, :], scalar1=pv[:, 3, pr:pr + 1],
                                                                scalar2=pv[:, 5, pr:pr + 1], op0=ALU.mult, op1=ALU.add),
                       reads=[r_aa, r_par], writes=[r_kp])
                yield
                sch.op("pool", lambda e, pr=pr: e.tensor_tensor(out=kp[:, :], in0=kp[:, :], in1=pm_k[:, pr, :],
                                                                op=ALU.mult), reads=[r_kp, rk], writes=[r_kp])
                yield
                sch.op("pool", lambda e: e.tensor_tensor(out=bb[:, :], in0=kx[:, :], in1=aa[:, :], op=ALU.mult),
                       reads=[r_kx, r_aa], writes=[r_bb])
                yield
                sch.op("dve", lambda e, pr=pr: e.scalar_tensor_tensor(
                    out=o.prodb[:, pr, :], in0=pm_r[:, pr, :], scalar=pv[:, 4, pr:pr + 1], in1=kp[:, :],
                    op0=ALU.mult, op1=ALU.mult), reads=[rr_, r_kp, r_par], writes=[o.r_prodb[pr]])
                yield
                G = Gp[pr]
                sch.op("dve", lambda e, G=G: e.tensor_copy(out=G[:, 0:1], in_=G[:, T:T + 1]),
                       reads=[r_Gp[pr]], writes=[r_Gp[pr]])
                sch.op("dve", lambda e, G=G: e.tensor_tensor_scan(out=G[:, 1:1 + T], data0=lw[:, :], data1=lw[:, :],
                                                                  initial=G[:, 0:1], op0=ALU.add, op1=ALU.min),
                       reads=[r_lw, r_Gp[pr]], writes=[r_Gp[pr]])
                yield
                base = v3(G[:, 0:T])[:, :, 0:1].broadcast_to([128, NCH, L])
                sch.op("dve", lambda e, G=G, base=base: e.tensor_tensor(out=v3(gi[:, :]), in0=v3(G[:, 1:1 + T]),
                                                                        in1=base, op=ALU.subtract),
                       reads=[r_Gp[pr]], writes=[r_gi])
                yield
                sch.op("pool", lambda e: e.tensor_tensor(out=ge[:, :], in0=gi[:, :], in1=lw[:, :], op=ALU.subtract),
                       reads=[r_gi, r_lw], writes=[r_ge])
                e1 = o.E1[pr]
                for _ in range(SLACK):
                    yield
                sch.op("act", lambda e, e1=e1: e.activation(out=e1[:, :], in_=gi[:, :], func=AF.Exp),
                       reads=[r_gi], writes=[o.r_E1[pr]])
                yield
                for _ in range(SLACK):
                    yield
                sch.op("act", lambda e: e.activation(out=E2[:, :], in_=ge[:, :], func=AF.Exp),
                       reads=[r_ge], writes=[r_E2])
                yield
                for hf in range(2):
                    ps_ = slice(hf * 64, hf * 64 + 64)
                    hq = 2 * pr + hf
                    sch.op("dve", lambda e, hq=hq, ps_=ps_: e.scalar_tensor_tensor(
                        out=o.ARz[hq][ps_, 0, :], in0=kx[ps_, :], scalar=-1.0, in1=E2[ps_, :],
                        op0=ALU.mult, op1=ALU.mult), reads=[r_kx, r_E2], wacc=[o.r_AR[pr]])
                    sch.op("pool", lambda e, hq=hq, ps_=ps_, pr=pr, e1=e1: e.tensor_tensor(
                        out=o.ARz[hq][ps_, 1, :], in0=pm_r[ps_, pr, :], in1=e1[ps_, :], op=ALU.mult),
                        reads=[rr_, o.r_E1[pr]], wacc=[o.r_AR[pr]])
                    yield
                for _ in range(SLACK):
                    yield
                sch.op("act", lambda e: e.activation(out=E2[:, :], in_=gi[:, :], func=AF.Exp, scale=-1.0),
                       reads=[r_gi], writes=[r_E2])
                yield
                sch.op("dve", lambda e, pr=pr: e.tensor_tensor(out=o.Bt[:, pr, :], in0=bb[:, :], in1=E2[:, :],
                                                               op=ALU.mult), reads=[r_bb, r_E2], writes=[o.r_Bt[pr]])
                sch.op("pool", lambda e, pr=pr: e.tensor_tensor(out=o.Kt[:, pr, :], in0=kp[:, :], in1=E2[:, :],
                                                                op=ALU.mult), reads=[r_kp, r_E2], writes=[o.r_Kt[pr]])
                yield
                gend = v3(gi[:, :])[:, :, L - 1:L].broadcast_to([128, NCH, L])
                sch.op("dve", lambda e, gend=gend: e.tensor_tensor(out=v3(ge[:, :]), in0=gend, in1=v3(gi[:, :]),
                                                                   op=ALU.subtract), reads=[r_gi], writes=[r_ge])
                yield
                for _ in range(SLACK):
                    yield
                sch.op("act", lambda e: e.activation(out=ge[:, :], in_=ge[:, :], func=AF.Exp),
                       reads=[r_ge], writes=[r_ge])
                yield
                sch.op("dve", lambda e, pr=pr: e.tensor_tensor(out=o.Bh[:, pr, :], in0=bb[:, :], in1=ge[:, :],
                                                               op=ALU.mult), reads=[r_bb, r_ge], writes=[o.r_Bh[pr]])
                sch.op("pool", lambda e, pr=pr: e.tensor_tensor(out=o.Kh[:, pr, :], in0=kp[:, :], in1=ge[:, :],
                                                                op=ALU.mult), reads=[r_kp, r_ge], writes=[o.r_Kh[pr]])
                yield

            gens = [pair_gen(0), pair_gen(1)]
            while gens:
                for g in list(gens):
                    try:
                        next(g)
                        yield
                    except StopIteration:
                        gens.remove(g)

        def flat(t3):
            return t3[:, :, :].rearrange("p a t -> p (a t)")

        def store_block(blk):
            o = B[blk % 2]
            rows = slice(blk * 128, (blk + 1) * 128)
            sch.dma("sp", d["rpb"][rows, :], flat(o.bbf), reads=o.all_bf, wacc=[C.r_rpb])
            sch.dma("sp", d["rpf"][rows, :], flat(o.bf32), reads=o.all_f32, wacc=[C.r_rpf])

        def load_block(blk):
            o = B[blk % 2]
            rows = slice(blk * 128, (blk + 1) * 128)
            sch.dma("sp", flat(o.bbf), d["rpb"][rows, :], reads=[C.r_rpb], writes=o.all_bf)
            sch.dma("sp", flat(o.bf32), d["rpf"][rows, :], reads=[C.r_rpf], writes=o.all_f32)

        if mode == "prep":
            def prep_all():
                for blk in range(NBLK):
                    yield from prep_gen(blk)
                    store_block(blk)
                    yield
            return prep_all()

        def pre_gen(gc):
            blk, cc = divmod(gc, NCH)
            o = B[blk % 2]
            k = K_[gc % NK]
            cs = slice(cc * L, (cc + 1) * L)
            for pr in range(2):
                for j, (src, rs) in enumerate(((o.Bh, o.r_Bh[pr]), (o.Kh, o.r_Kh[pr]), (o.pm_v, o.r_v[pr]))):
                    sch.op("pe", lambda e, j=j, src=src, pr=pr: e.transpose(
                        out=banks[0][0:64, j * 128:(j + 1) * 128], in_=src[:, pr, cs], identity=C.identf[:, :]),
                        reads=[rs], writes=[rb[0]])
                sch.op("act", lambda e, pr=pr: e.copy(
                    out=k.tok[:, pr, 0:3, :, :],
                    in_=banks[0][0:64, 0:384].rearrange("p (j h e) -> p j h e", j=3, h=2)),
                    reads=[rb[0]], writes=[k.r_tok[pr]])
                yield
            for h in range(4):
                pr = h // 2
                bk = banks[1 + pr]
                oo = (h % 2) * 256
                rhs_ar = o.ARz[h][:, :, cs]
                sch.op("pe", lambda e, bk=bk, oo=oo, pr=pr, rhs_ar=rhs_ar: e.matmul(
                    out=bk[0:64, oo:oo + 128], lhsT=o.Bt[:, pr, cs], rhs=rhs_ar, start=True, stop=True),
                    reads=[o.r_Bt[pr], o.r_AR[pr]], writes=[rb[1 + pr]])
                sch.op("pe", lambda e, bk=bk, oo=oo, pr=pr, rhs_ar=rhs_ar: e.matmul(
                    out=bk[0:64, oo + 128:oo + 256], lhsT=o.Kt[:, pr, cs], rhs=rhs_ar, start=True, stop=True),
                    reads=[o.r_Kt[pr], o.r_AR[pr]], writes=[rb[1 + pr]])
                sch.op("pe", lambda e, h=h, pr=pr: e.matmul(
                    out=banks[4][0:64, h * 64:(h + 1) * 64], lhsT=o.ARz[h][:, 0, cs],
                    rhs=o.Bt[:, pr, cs], start=True, stop=True),
                    reads=[o.r_Bt[pr], o.r_AR[pr]], writes=[rb[4]])
                if h % 2 == 1:
                    sch.op("dve", lambda e, pr=pr: e.tensor_tensor(
                        out=k.scs[:, 2 * pr:2 * pr + 2, :, :].rearrange("p h b t -> p (h b t)"),
                        in0=banks[1 + pr][0:64, :], in1=C.rwmask[:, :], op=ALU.mult),
                        reads=[rb[1 + pr]], writes=[k.r_scs[pr]])
            sch.op("dve", lambda e: e.tensor_tensor(out=k.nns[:, :, :].rearrange("p h t -> p (h t)"),
                                                    in0=banks[4][0:64, 0:256], in1=C.nmask[:, :], op=ALU.mult),
                   reads=[rb[4]], writes=[k.r_nns])
            yield

            def Mlev(lev, h):
                return k.scs[:, h, 0, :] if lev == 0 else k.pws[(lev - 1) % 2][:, h, 1, :]

            def Nlev(lev, h):
                return k.nns[:, h, :] if lev == 0 else k.pws[(lev - 1) % 2][:, h, 0, :]

            def Rlev(lev, h):
                return [k.r_scs[h // 2], k.r_nns] if lev == 0 else [k.r_pws[(lev - 1) % 2]]
            sch.op("pool", lambda e: e.tensor_tensor(
                out=k.Q[0][:, :, :], in0=k.scs[:, :, 0, :],
                in1=C.identf[0:64, 0:64].unsqueeze(1).broadcast_to([64, 4, 64]), op=ALU.add),
                reads=[k.r_scs[0], k.r_scs[1]], writes=[k.r_Q[0]])
            yield
            qi = 0
            for lev in range(1, 6):
                for h in range(4):
                    sch.op("pe", lambda e, lev=lev, h=h: e.matmul(
                        out=banks[3][0:64, h * 128:h * 128 + 64], lhsT=Mlev(lev - 1, h), rhs=Nlev(lev - 1, h),
                        start=True, stop=True), reads=Rlev(lev - 1, h), writes=[rb[3]])
                    if lev < 5:
                        sch.op("pe", lambda e, lev=lev, h=h: e.matmul(
                            out=banks[3][0:64, h * 128 + 64:h * 128 + 128], lhsT=Nlev(lev - 1, h),
                            rhs=Mlev(lev - 1, h), start=True, stop=True), reads=Rlev(lev - 1, h), writes=[rb[3]])
                if lev < 5:
                    evac(C, 0, k.pws[(lev - 1) % 2][:, :, :, :].rearrange("p h b t -> p (h b t)"),
                         banks[3][0:64, :], [rb[3]], writes=[k.r_pws[(lev - 1) % 2]])
                    nsrc = lambda h, lev=lev: k.pws[(lev - 1) % 2][:, h, 0, :]
                    rn = [k.r_pws[(lev - 1) % 2]]
                else:
                    sch.op("act", lambda e: e.copy(
                        out=k.nns[:, :, :], in_=banks[3][0:64, :].rearrange("p (h b t) -> p h b t", h=4, b=2)[:, :, 0, :]),
                        reads=[rb[3]], writes=[k.r_nns])
                    nsrc = lambda h: k.nns[:, h, :]
                    rn = [k.r_nns]
                yield
                for h in range(4):
                    sch.op("pe", lambda e, h=h, qi=qi: e.matmul(
                        out=banks[5][0:64, 256 + h * 64:256 + (h + 1) * 64], lhsT=C.identb[0:64, 0:64],
                        rhs=k.Q[qi][:, h, :], start=True, stop=False), reads=[k.r_Q[qi]], writes=[rb[5]])
                    sch.op("pe", lambda e, h=h, qi=qi, nsrc=nsrc: e.matmul(
                        out=banks[5][0:64, 256 + h * 64:256 + (h + 1) * 64], lhsT=nsrc(h), rhs=k.Q[qi][:, h, :],
                        start=False, stop=True), reads=rn + [k.r_Q[qi]], writes=[rb[5]])
                sch.op("act", lambda e, qi=qi: e.copy(
                    out=k.Q[1 - qi][:, :, :].rearrange("p h e -> p (h e)"), in_=banks[5][0:64, 256:512]),
                    reads=[rb[5]], writes=[k.r_Q[1 - qi]])
                qi = 1 - qi
                yield
            k.qfin = qi

        def epi_early(gc):
            blk, cc = divmod(gc, NCH)
            o = B[blk % 2]
            k = K_[gc % NK]
            cs = slice(cc * L, (cc + 1) * L)
            b7 = banks[7]
            i2 = gc % 2
            sch.op("pe", lambda e: e.matmul(out=b7[0:64, 0:256], lhsT=o.lin_g[:, cs], rhs=lup[0:64, 2, :],
                                            start=True, stop=True), reads=[o.r_ling, r_par], writes=[rb[7]])
            for pr in range(2):
                sch.op("pe", lambda e, pr=pr: e.matmul(out=b7[0:64, 256 + 2 * pr:256 + 2 * pr + 2],
                                                       lhsT=o.prodb[:, pr, cs], rhs=C.sel2b[:, :],
                                                       start=True, stop=True), reads=[o.r_prodb[pr]], writes=[rb[7]])
            sch.op("act", lambda e: e.copy(out=gsb[i2][:, :], in_=b7[0:64, 0:256]), reads=[rb[7]], writes=[r_gsb[i2]])
            sch.op("act", lambda e: e.copy(out=st7[i2][:, :, 0], in_=b7[0:64, 256:260]), reads=[rb[7]],
                   writes=[r_st7[i2]])
            for pr in range(2):
                sch.op("pool", lambda e, pr=pr: e.tensor_tensor(
                    out=bv[i2][:, 2 * pr:2 * pr + 2, :], in0=k.tok[:, pr, 2, :, :],
                    in1=st7[i2][:, 2 * pr:2 * pr + 2, :].broadcast_to([64, 2, 64]), op=ALU.mult),
                    reads=[k.r_tok[pr], r_st7[i2]], wacc=[r_bvv[i2]])

        def epi_gen(gc):
            gcs = slice(gc * L, (gc + 1) * L)
            i2 = gc % 2
            y_, ry_ = yv[i2], r_yv[i2]
            b7 = banks[7]
            sch.op("dve", lambda e: e.tensor_reduce(out=st[:, 0, :, :], in_=y_[:, :, :], axis=AX.X, op=ALU.add),
                   reads=[ry_], writes=[r_st])
            sch.op("pool", lambda e: e.tensor_tensor(out=ysq[:, :, :], in0=y_[:, :, :], in1=y_[:, :, :], op=ALU.mult),
                   reads=[ry_], writes=[r_ysq])
            yield
            sch.op("dve", lambda e: e.tensor_reduce(out=st[:, 1, :, :], in_=ysq[:, :, :], axis=AX.X, op=ALU.add),
                   reads=[r_ysq], writes=[r_st])
            yield
            sch.op("pool", lambda e: e.tensor_scalar(out=st[:, 2, :, :], in0=st[:, 0, :, :], scalar1=1.0 / 64,
                                                     scalar2=0.0, op0=ALU.mult, op1=ALU.add), reads=[r_st], writes=[r_st])
            sch.op("pool", lambda e: e.tensor_tensor(out=st[:, 3, :, :], in0=st[:, 2, :, :], in1=st[:, 2, :, :],
                                                     op=ALU.mult), reads=[r_st], writes=[r_st])
            yield
            sch.op("pool", lambda e: e.tensor_scalar(out=st[:, 4, :, :], in0=st[:, 1, :, :], scalar1=1.0 / 64,
                                                     scalar2=64e-5, op0=ALU.mult, op1=ALU.add), reads=[r_st],
                   writes=[r_st])
            sch.op("pool", lambda e: e.tensor_tensor(out=st[:, 4, :, :], in0=st[:, 4, :, :], in1=st[:, 3, :, :],
                                                     op=ALU.subtract), reads=[r_st], writes=[r_st])
            yield
            sch.op("act", lambda e: e.activation(out=st[:, 5, :, :], in_=st[:, 4, :, :], func=AF.Ln),
                   reads=[r_st], writes=[r_st])
            sch.op("act", lambda e: e.activation(out=st[:, 6, :, :], in_=st[:, 5, :, :], func=AF.Exp, scale=-0.5),
                   reads=[r_st], writes=[r_st])
            yield
            sch.op("dve", lambda e: e.tensor_tensor(out=ysq[:, :, :], in0=y_[:, :, :],
                                                    in1=st[:, 2, :, :].broadcast_to([64, 4, 64]), op=ALU.subtract),
                   reads=[ry_, r_st], writes=[r_ysq])
            yield
            sch.op("dve", lambda e: e.tensor_tensor(out=ysq[:, :, :], in0=ysq[:, :, :],
                                                    in1=st[:, 6, :, :].broadcast_to([64, 4, 64]), op=ALU.mult),
                   reads=[r_ysq, r_st], writes=[r_ysq])
            yield
            y2 = ysq[:, :, :].rearrange("p h e -> p (h e)")
            sch.op("pool", lambda e: e.tensor_tensor(out=y2, in0=y2, in1=lng[:, :], op=ALU.mult),
                   reads=[r_ysq, r_par], writes=[r_ysq])
            yield
            sch.op("pool", lambda e: e.tensor_tensor(out=y2, in0=y2, in1=lnb[:, :], op=ALU.add),
                   reads=[r_ysq, r_par], writes=[r_ysq])
            yield
            sch.op("pool", lambda e: e.tensor_tensor(out=ysq[:, :, :], in0=ysq[:, :, :], in1=bv[i2][:, :, :],
                                                     op=ALU.add), reads=[r_ysq, r_bvv[i2]], writes=[r_ysq])
            yield
            sch.op("pool", lambda e: e.tensor_tensor(out=yo[i2][:, :], in0=y2, in1=gsb[i2][:, :], op=ALU.mult),
                   reads=[r_ysq, r_gsb[i2]], writes=[r_yo2[i2]])
            yield

        def epi_tail(gc):
            gcs = slice(gc * L, (gc + 1) * L)
            i2 = gc % 2
            b7 = banks[7]
            for j in range(2):
                sch.op("pe", lambda e, j=j: e.transpose(out=b7[:, 384 + j * 64:384 + (j + 1) * 64],
                                                        in_=yo[i2][:, j * 128:(j + 1) * 128],
                                                        identity=C.identf[0:64, 0:64]), reads=[r_yo2[i2]],
                       writes=[rb[7]])
            sch.op("act", lambda e: e.copy(out=yrT[:, :, gcs], in_=b7[:, 384:512].rearrange("p (j t) -> p j t", j=2)),
                   reads=[rb[7]], wacc=[r_yrT])
            yield

        hi = []
        lo = []
        pq = []
        rr = [0]

        def pull(q, idx):
            try:
                next(q[idx][1])
                return True
            except StopIteration:
                q.pop(idx)
                return False

        def pump(n):
            for _ in range(n):
                if hi:
                    hm = globals().get("HIMODE_", 2)
                    if hm == 0:
                        rr[0] = (rr[0] + 1) % len(hi)
                        pull(hi, rr[0])
                    elif hm == 1:
                        pull(hi, 0)
                    else:
                        rr[0] = (rr[0] + 1) % 3
                        pull(hi, 0 if (rr[0] < 2 or len(hi) < 2) else len(hi) - 1)
                if lo:
                    pull(lo, 0)
                for _ in range(globals().get('PQN_', 1)):
                    if pq:
                        pull(pq, 0)

        def drain_tag(q, pred):
            i = 0
            while i < len(q):
                if pred(q[i][0]):
                    while pull(q, i):
                        pass
                else:
                    i += 1

        def chain(gc):
            blk, cc = divmod(gc, NCH)
            o = B[blk % 2]
            k = K_[gc % NK]
            cs = slice(cc * L, (cc + 1) * L)
            Q = k.Q[k.qfin]
            rQ = k.r_Q[k.qfin]
            wq = banks[5]
            for h in range(4):
                pr, hf = h // 2, h % 2
                sch.op("pe", lambda e, h=h, pr=pr: e.matmul(
                    out=wq[0:64, h * 64:(h + 1) * 64], lhsT=o.ARz[h][:, 0, cs], rhs=Hb[:, pr, :],
                    start=True, stop=False), reads=[o.r_AR[pr], r_Hb], writes=[rb[5]])
                sch.op("pe", lambda e, h=h, pr=pr, hf=hf: e.matmul(
                    out=wq[0:64, h * 64:(h + 1) * 64], lhsT=k.scs[:, h, 2, :], rhs=k.tok[:, pr, 2, hf, :],
                    start=False, stop=True), reads=[k.r_scs[pr], k.r_tok[pr]], writes=[rb[5]])
            sch.op("act", lambda e: e.copy(out=Wt[:, :, :].rearrange("p h e -> p (h e)"), in_=wq[0:64, 0:256]),
                   reads=[rb[5]], writes=[r_Wt])
            pump(globals().get('PUMPS_', (3, 4, 5, 0))[0])
            for h in range(4):
                sch.op("pe", lambda e, h=h: e.matmul(out=wq[0:64, h * 64:(h + 1) * 64], lhsT=Q[:, h, :],
                                                     rhs=Wt[:, h, :], start=True, stop=True),
                       reads=[rQ, r_Wt], writes=[rb[5]])
            sch.op("act", lambda e: e.copy(out=Ut[:, :, :].rearrange("p h e -> p (h e)"), in_=wq[0:64, 0:256]),
                   reads=[rb[5]], writes=[r_Ut])
            pump(globals().get('PUMPS_', (3, 4, 5, 0))[1])
            b6 = banks[6]
            for h in range(4):
                pr, hf = h // 2, h % 2
                oo = b6[:, 256 + h * 64:256 + (h + 1) * 64]
                sch.op("pe", lambda e, oo=oo, h=h, pr=pr: e.matmul(
                    out=oo, lhsT=k.tok[:, pr, 0, :, :].rearrange("p a e -> p (a e)"), rhs=Ut[:, h, :],
                    start=True, stop=False), reads=[k.r_tok[pr], r_Ut], writes=[rb[6]])
                sch.op("pe", lambda e, oo=oo, pr=pr, hf=hf: e.matmul(
                    out=oo, lhsT=k.tok[:, pr, 1, :, :].rearrange("p a e -> p (a e)"), rhs=k.tok[:, pr, 2, hf, :],
                    start=False, stop=True), reads=[k.r_tok[pr]], writes=[rb[6]])
            for h in range(4):
                pr, hf = h // 2, h % 2
                oo = b6[0:64, h * 64:(h + 1) * 64]
                sch.op("pe", lambda e, oo=oo, h=h, pr=pr: e.matmul(
                    out=oo, lhsT=o.ARz[h][:, 1, cs], rhs=Hb[:, pr, :], start=True, stop=False),
                    reads=[o.r_AR[pr], r_Hb], writes=[rb[6]])
                sch.op("pe", lambda e, oo=oo, h=h, pr=pr: e.matmul(
                    out=oo, lhsT=k.scs[:, h, 1, :], rhs=Ut[:, h, :], start=False, stop=False),
                    reads=[k.r_scs[pr], r_Ut], writes=[rb[6]])
                sch.op("pe", lambda e, oo=oo, h=h, pr=pr, hf=hf: e.matmul(
                    out=oo, lhsT=k.scs[:, h, 3, :], rhs=k.tok[:, pr, 2, hf, :], start=False, stop=True),
                    reads=[k.r_scs[pr], k.r_tok[pr]], writes=[rb[6]])
            for h in range(4):
                pr, pb = h // 2, (h % 2) * 64
                sch.op("dve", lambda e, h=h, pr=pr, pb=pb: e.scalar_tensor_tensor(
                    out=Hs[pb:pb + 64, pr, :], in0=Hs[pb:pb + 64, pr, :],
                    scalar=o.E1[pr][pb:pb + 64, cc * L + L - 1:cc * L + L],
                    in1=b6[pb:pb + 64, 256 + h * 64:256 + (h + 1) * 64], op0=ALU.mult, op1=ALU.add),
                    reads=[r_Hs, o.r_E1[pr], rb[6]], wacc=[r_Hs])
            sch.op("act", lambda e: e.copy(out=Hb[:, :, :], in_=Hs[:, :, :]), reads=[r_Hs], writes=[r_Hb])
            y_, ry_ = yv[gc % 2], r_yv[gc % 2]
            sch.op("act", lambda e: e.copy(out=y_[:, :, :].rearrange("p h e -> p (h e)"), in_=b6[0:64, 0:256]),
                   reads=[rb[6]], writes=[ry_])
            pump(globals().get('PUMPS_', (3, 4, 5, 0))[2])

        NG = NBLK * NCH
        if mode == "chunks":
            load_block(0)
        else:
            for _ in prep_gen(0):
                pass
        for _ in pre_gen(0):
            pass
        hi.append((1, pre_gen(1)))
        for gc in range(NG):
            blk, cc = divmod(gc, NCH)
            drain_tag(lo, lambda t: t[0] == "epi" and t[1] <= gc - 2)
            if gc + 2 < NG:
                if (gc + 2) // NCH != (gc + 1) // NCH or (gc + 2) % NCH == 0:
                    drain_tag(pq, lambda t: True)
                hi.append((gc + 2, pre_gen(gc + 2)))
            chain(gc)
            epi_early(gc)
            lo.append((("epi", gc), epi_gen(gc)))
            if cc == 0 and blk + 1 < NBLK:
                if mode == "chunks":
                    load_block(blk + 1)
                else:
                    pq.append((("prep", blk + 1), prep_gen(blk + 1)))
            pump(globals().get('PUMPS_', (3, 4, 5, 0))[3])
            drain_tag(hi, lambda t: t == gc + 1)
            if gc >= 1:
                drain_tag(lo, lambda t: t[0] == "epi" and t[1] <= gc - 1)
                for _ in epi_tail(gc - 1):
                    pass
        drain_tag(lo, lambda t: True)
        for _ in epi_tail(NG - 1):
            pass
        for j in range(2):
            sch.dma("sp", d["mixT"][256 + j * 128:256 + (j + 1) * 128, :], yrT[:, j, :], reads=[r_yrT],
                    wacc=[C.r_mixT])


def phase_RC(C, l):
    nc, sch, d = C.nc, C.sch, C.d
    T = 256
    L = 64
    NCH = T // L
    NBLK = S // T
    NEG = -0.6065306597126334
    with ExitStack() as es:
        def sb(name, shape, dt):
            return _sb(es, nc, "R_" + name, shape, dt)
        mu = sb("mu", [128, 7], F32)
        omu = sb("omu", [128, 7], F32)
        pv = sb("pv", [128, 6, 2], F32)
        lup_f = sb("lupf", [64, 3, 256], F32)
        lup = sb("lup", [64, 3, 256], BF16)
        lng = sb("lng", [64, 256], F32)
        lnb = sb("lnb", [64, 256], F32)
        r_par, r_lup = Res(), Res()
        praw = sb("praw", [128, 1 + T], F32)
        r_praw = Res()
        pm_r = sb("pm_r", [128, 2, T], F32)
        pm_k = sb("pm_k", [128, 2, T], F32)
        pm_l = sb("pm_l", [128, T], F32)
        lin = sb("lin", [128, T], BF16)
        lin_a = sb("lin_a", [32, T], BF16)
        r_lin, r_lina = Res(), Res()
        tmp = sb("tmp", [128, T], F32)
        lw = sb("lw", [128, T], F32)
        aa = sb("aa", [128, T], F32)
        kx = sb("kx", [128, T], F32)
        sq = sb("sq", [128, T], F32)
        kp = sb("kp", [128, T], F32)
        bb = sb("bb", [128, T], F32)
        gi = sb("gi", [128, T], F32)
        ge = sb("ge", [128, T], F32)
        E2 = sb("E2", [128, T], F32)
        r_tmp, r_lw, r_aa, r_kx, r_sq, r_kp, r_bb, r_gi, r_ge, r_E2 = (Res() for _ in range(10))
        Gp = [sb("Gp%d" % i, [128, 1 + T], F32) for i in range(2)]
        r_Gp = [Res(), Res()]
        B = []
        for i in range(2):
            o = Ctx()
            o.ARz = [sb("ARz%d_%d" % (i, h), [128, 2, T], BF16) for h in range(4)]
            o.Bt = sb("Bt%d" % i, [128, 2, T], BF16)
            o.Kt = sb("Kt%d" % i, [128, 2, T], BF16)
            o.Bh = sb("Bh%d" % i, [128, 2, T], F32)
            o.Kh = sb("Kh%d" % i, [128, 2, T], F32)
            o.pm_v = sb("pmv%d" % i, [128, 2, T], F32)
            o.prodb = sb("prodb%d" % i, [128, 2, T], BF16)
            o.lin_g = sb("ling%d" % i, [64, T], BF16)
            o.E1 = [sb("E1_%d_%d" % (i, p), [128, T], F32) for p in range(2)]
            o.r_AR, o.r_Bt, o.r_Kt, o.r_Bh, o.r_Kh, o.r_v, o.r_prodb, o.r_E1 = ([Res(), Res()] for _ in range(8))
            o.r_ling = Res()
            B.append(o)
        NK = 3
        K_ = []
        for i in range(NK):
            o = Ctx()
            o.tok = sb("tok%d" % i, [64, 2, 4, 2, 64], BF16)
            o.scs = sb("scs%d" % i, [64, 4, 4, 64], BF16)
            o.nns = sb("nns%d" % i, [64, 4, 64], BF16)
            o.pws = [sb("pws%d_%d" % (i, j), [64, 4, 2, 64], BF16) for j in range(2)]
            o.Q = [sb("Q%d_%d" % (i, j), [64, 4, 64], BF16) for j in range(2)]
            o.r_tok = [Res(), Res()]
            o.r_scs = [Res(), Res()]
            o.r_nns = Res()
            o.r_pws = [Res() for _ in range(2)]
            o.r_Q = [Res(), Res()]
            K_.append(o)
        Hs = sb("Hs", [128, 2, 64], F32)
        Hb = sb("Hb", [128, 2, 64], BF16)
        r_Hs, r_Hb = Res(), Res()
        yrT = sb("yrT", [128, 2, S], BF16)
        r_yrT = Res()
        Wt = sb("Wt", [64, 4, 64], BF16)
        Ut = sb("Ut", [64, 4, 64], BF16)
        r_Wt, r_Ut = Res(), Res()
        yv = [sb("yv%d" % i, [64, 4, 64], F32) for i in range(2)]
        r_yv = [Res(), Res()]
        ysq = sb("ysq", [64, 4, 64], F32)
        bv = [sb("bv%d" % i, [64, 4, 64], F32) for i in range(2)]
        st7 = [sb("st7_%d" % i, [64, 4, 1], F32) for i in range(2)]
        gsb = [sb("gsb%d" % i, [64, 256], F32) for i in range(2)]
        r_bvv, r_st7, r_gsb = [Res(), Res()], [Res(), Res()], [Res(), Res()]
        yo = [sb("yo%d" % i, [64, 256], F32) for i in range(2)]
        r_yo2 = [Res(), Res()]
        st = sb("st", [64, 8, 4, 1], F32)
        r_ysq, r_bv, r_yo, r_st = Res(), Res(), Res(), Res()
        phys = [_ps(es, nc, "RC_b%d" % i, [128, 512], F32) for i in range(8)]
        prs = [PRes() for _ in range(8)]
        amap = [0, 1, 1, 2, 3, 4, 4, 0]
        banks = [phys[i] for i in amap]
        rb = [prs[i] for i in amap]
        bkq, rbq = phys[3], prs[3]

        sch.dma("sp", mu[:, :], d["r_mu"][l], wacc=[r_par])
        for i, nm in enumerate(("r_w0", "r_a0", "r_k_k", "r_k_a", "r_r_k")):
            sch.dma("sp", pv[:, i, :], d[nm][l], wacc=[r_par])
        sch.op("pool", lambda e: e.memset(lup_f[:, :, :], 0.0), writes=[r_lup])
        sch.dma("sp", lup_f[0:32, 0, :], d["r_w_up"][l], writes=[r_lup])
        sch.dma("sp", lup_f[0:32, 1, :], d["r_a_up"][l], writes=[r_lup])
        sch.dma("sp", lup_f[0:64, 2, :], d["r_g_up"][l], writes=[r_lup])
        sch.dma("sp", lng[:, :], d["r_ln_g"][l], wacc=[r_par])
        sch.dma("sp", lnb[:, :], d["r_ln_b"][l], wacc=[r_par])
        sch.op("dve", lambda e: e.tensor_scalar(out=omu[:, :], in0=mu[:, :], scalar1=-1.0, scalar2=1.0,
                                                op0=ALU.mult, op1=ALU.add), reads=[r_par], writes=[r_par])
        sch.op("dve", lambda e: e.tensor_scalar(out=pv[:, 5, :], in0=pv[:, 3, :], scalar1=-1.0, scalar2=1.0,
                                                op0=ALU.mult, op1=ALU.add), reads=[r_par], writes=[r_par])
        sch.op("dve", lambda e: e.tensor_copy(out=lup[:, :, :], in_=lup_f[:, :, :]), reads=[r_par, r_lup],
               writes=[r_par])
        sch.op("pool", lambda e: e.memset(Hs[:, :, :], 0.0), writes=[r_Hs])
        sch.op("pool", lambda e: e.memset(Hb[:, :, :], 0.0), writes=[r_Hb])
        for pr in range(2):
            sch.op("pool", lambda e, pr=pr: e.memset(Gp[pr][:, :], 0.0), writes=[r_Gp[pr]])
        for i in range(2):
            for h in range(4):
                zs_ = slice((1 - h % 2) * 64, (1 - h % 2) * 64 + 64)
                sch.op("pool", lambda e, i=i, h=h, zs_=zs_: e.memset(B[i].ARz[h][zs_, :, :], 0.0),
                       wacc=[B[i].r_AR[h // 2]])
        sch.op("pool", lambda e: e.memset(praw[:, 0:1], 0.0), writes=[r_praw])

        def v3(ap2):
            return ap2.rearrange("p (c t) -> p c t", t=L)

        def prep_gen(blk):
            o = B[blk % 2]
            t0 = blk * T
            dests = [(pm_r, 0), (pm_r, 1), (pm_k, 0), (pm_k, 1), (o.pm_v, 0), (o.pm_v, 1), (None, 0)]
            r_pmr, r_pmk = [Res(), Res()], [Res(), Res()]
            r_pml = Res()
            rr_list = [r_pmr[0], r_pmr[1], r_pmk[0], r_pmk[1], o.r_v[0], o.r_v[1], r_pml]
            for rc in range(7):
                if blk == 0:
                    sch.dma("sp", praw[:, 1:1 + T], d["rT"][rc * 128:(rc + 1) * 128, 0:T],
                            reads=[C.r_rT], wacc=[r_praw])
                else:
                    sch.dma("sp", praw[:, :], d["rT"][rc * 128:(rc + 1) * 128, t0 - 1:t0 + T],
                            reads=[C.r_rT], writes=[r_praw])
                dt_, pi = dests[rc]
                ot = pm_l[:, :] if dt_ is None else dt_[:, pi, :]
                sch.op("pool", lambda e, rc=rc: e.tensor_scalar(out=tmp[:, :], in0=praw[:, 1:1 + T],
                                                                scalar1=omu[:, rc:rc + 1], scalar2=0.0, op0=ALU.mult, op1=ALU.add),
                       reads=[r_praw, r_par], writes=[r_tmp])
                yield
                sch.op("dve", lambda e, rc=rc, ot=ot: e.scalar_tensor_tensor(
                    out=ot, in0=praw[:, 0:T], scalar=mu[:, rc:rc + 1], in1=tmp[:, :], op0=ALU.mult, op1=ALU.add),
                    reads=[r_praw, r_tmp, r_par], writes=[rr_list[rc]])
                yield
            sch.op("act", lambda e: e.activation(out=lin[0:32, :], in_=pm_l[0:32, :], func=AF.Tanh),
                   reads=[r_pml], wacc=[r_lin])
            sch.op("act", lambda e: e.copy(out=lin[32:64, :], in_=pm_l[32:64, :]), reads=[r_pml], wacc=[r_lin])
            sch.op("act", lambda e: e.activation(out=lin[64:128, :], in_=pm_l[64:128, :], func=AF.Sigmoid),
                   reads=[r_pml], wacc=[r_lin])
            sch.dma("sp", lin_a[:, :], lin[32:64, :], reads=[r_lin], writes=[r_lina])
            sch.dma("sp", o.lin_g[:, :], lin[64:128, :], reads=[r_lin], writes=[o.r_ling])
            yield
            pb7 = banks[7]
            for pr in range(2):
                rk, rr_ = r_pmk[pr], r_pmr[pr]
                pcs = slice(pr * 128, (pr + 1) * 128)
                sch.op("pe", lambda e, pcs=pcs: e.matmul(out=pb7[:, 0:T], lhsT=lup[0:32, 0, pcs], rhs=lin[0:32, :],
                                                         start=True, stop=True), reads=[r_lin, r_par], writes=[rb[7]])
                sch.op("act", lambda e, pr=pr: e.activation(out=lw[:, :], in_=pb7[:, 0:T], func=AF.Sigmoid,
                                                            bias=pv[:, 0, pr:pr + 1], scale=1.0),
                       reads=[rb[7], r_par], writes=[r_lw])
                yield
                sch.op("pe", lambda e, pcs=pcs: e.matmul(out=pb7[:, 0:T], lhsT=lup[0:32, 1, pcs], rhs=lin_a[:, :],
                                                         start=True, stop=True), reads=[r_lina, r_par], writes=[rb[7]])
                sch.op("act", lambda e, pr=pr: e.activation(out=aa[:, :], in_=pb7[:, 0:T], func=AF.Sigmoid,
                                                            bias=pv[:, 1, pr:pr + 1], scale=1.0),
                       reads=[rb[7], r_par], writes=[r_aa])
                yield
                sch.op("pool", lambda e: e.tensor_scalar(out=lw[:, :], in0=lw[:, :], scalar1=NEG, scalar2=0.0,
                                                         op0=ALU.mult, op1=ALU.add), reads=[r_lw], writes=[r_lw])
                sch.op("dve", lambda e, pr=pr: e.tensor_scalar(out=kx[:, :], in0=pm_k[:, pr, :],
                                                               scalar1=pv[:, 2, pr:pr + 1], scalar2=None,
                                                               op0=ALU.mult), reads=[rk, r_par], writes=[r_kx])
                yield
                sch.op("pool", lambda e: e.tensor_tensor(out=sq[:, :], in0=kx[:, :], in1=kx[:, :], op=ALU.mult),
                       reads=[r_kx], writes=[r_sq])
                sch.op("pe", lambda e: e.matmul(out=pb7[:, 0:T], lhsT=C.blkf[:, :], rhs=sq[:, :],
                                                start=True, stop=True), reads=[r_sq], writes=[rb[7]])
                sch.op("act", lambda e: e.activation(out=tmp[:, :], in_=pb7[:, 0:T], func=AF.Sqrt),
                       reads=[rb[7]], writes=[r_tmp])
                yield
                sch.op("dve", lambda e: e.tensor_scalar(out=tmp[:, :], in0=tmp[:, :], scalar1=1e-12, scalar2=None,
                                                        op0=ALU.max), reads=[r_tmp], writes=[r_tmp])
                yield
                sch.op("dve", lambda e: e.reciprocal(out=tmp[:, :], in_=tmp[:, :]), reads=[r_tmp], writes=[r_tmp])
                yield
                sch.op("dve", lambda e: e.tensor_tensor(out=kx[:, :], in0=kx[:, :], in1=tmp[:, :], op=ALU.mult),
                       reads=[r_kx, r_tmp], writes=[r_kx])
                yield
                sch.op("pool", lambda e, pr=pr: e.tensor_scalar(out=kp[:, :], in0=aa[:, :], scalar1=pv[:, 3, pr:pr + 1],
                                                                scalar2=pv[:, 5, pr:pr + 1], op0=ALU.mult, op1=ALU.add),
                       reads=[r_aa, r_par], writes=[r_kp])
                yield
                sch.op("pool", lambda e, pr=pr: e.tensor_tensor(out=kp[:, :], in0=kp[:, :], in1=pm_k[:, pr, :],
                                                                op=ALU.mult), reads=[r_kp, rk], writes=[r_kp])
                yield
                sch.op("pool", lambda e: e.tensor_tensor(out=bb[:, :], in0=kx[:, :], in1=aa[:, :], op=ALU.mult),
                       reads=[r_kx, r_aa], writes=[r_bb])
                yield
                sch.op("dve", lambda e, pr=pr: e.scalar_tensor_tensor(
                    out=o.prodb[:, pr, :], in0=pm_r[:, pr, :], scalar=pv[:, 4, pr:pr + 1], in1=kp[:, :],
                    op0=ALU.mult, op1=ALU.mult), reads=[rr_, r_kp, r_par], writes=[o.r_prodb[pr]])
                yield
                G = Gp[pr]
                sch.op("dve", lambda e, G=G: e.tensor_copy(out=G[:, 0:1], in_=G[:, T:T + 1]),
                       reads=[r_Gp[pr]], writes=[r_Gp[pr]])
                sch.op("dve", lambda e, G=G: e.tensor_tensor_scan(out=G[:, 1:1 + T], data0=lw[:, :], data1=lw[:, :],
                                                                  initial=G[:, 0:1], op0=ALU.add, op1=ALU.min),
                       reads=[r_lw, r_Gp[pr]], writes=[r_Gp[pr]])
                yield
                base = v3(G[:, 0:T])[:, :, 0:1].broadcast_to([128, NCH, L])
                sch.op("dve", lambda e, G=G, base=base: e.tensor_tensor(out=v3(gi[:, :]), in0=v3(G[:, 1:1 + T]),
                                                                        in1=base, op=ALU.subtract),
                       reads=[r_Gp[pr]], writes=[r_gi])
                yield
                sch.op("pool", lambda e: e.tensor_tensor(out=ge[:, :], in0=gi[:, :], in1=lw[:, :], op=ALU.subtract),
                       reads=[r_gi, r_lw], writes=[r_ge])
                e1 = o.E1[pr]
                sch.op("act", lambda e, e1=e1: e.activation(out=e1[:, :], in_=gi[:, :], func=AF.Exp),
                       reads=[r_gi], writes=[o.r_E1[pr]])
                yield
                sch.op("act", lambda e: e.activation(out=E2[:, :], in_=ge[:, :], func=AF.Exp),
                       reads=[r_ge], writes=[r_E2])
                yield
                for hf in range(2):
                    ps_ = slice(hf * 64, hf * 64 + 64)
                    hq = 2 * pr + hf
                    sch.op("dve", lambda e, hq=hq, ps_=ps_: e.scalar_tensor_tensor(
                        out=o.ARz[hq][ps_, 0, :], in0=kx[ps_, :], scalar=-1.0, in1=E2[ps_, :],
                        op0=ALU.mult, op1=ALU.mult), reads=[r_kx, r_E2], wacc=[o.r_AR[pr]])
                    sch.op("pool", lambda e, hq=hq, ps_=ps_, pr=pr, e1=e1: e.tensor_tensor(
                        out=o.ARz[hq][ps_, 1, :], in0=pm_r[ps_, pr, :], in1=e1[ps_, :], op=ALU.mult),
                        reads=[rr_, o.r_E1[pr]], wacc=[o.r_AR[pr]])
                    yield
                sch.op("act", lambda e: e.activation(out=E2[:, :], in_=gi[:, :], func=AF.Exp, scale=-1.0),
                       reads=[r_gi], writes=[r_E2])
                yield
                sch.op("dve", lambda e, pr=pr: e.tensor_tensor(out=o.Bt[:, pr, :], in0=bb[:, :], in1=E2[:, :],
                                                               op=ALU.mult), reads=[r_bb, r_E2], writes=[o.r_Bt[pr]])
                sch.op("pool", lambda e, pr=pr: e.tensor_tensor(out=o.Kt[:, pr, :], in0=kp[:, :], in1=E2[:, :],
                                                                op=ALU.mult), reads=[r_kp, r_E2], writes=[o.r_Kt[pr]])
                yield
                gend = v3(gi[:, :])[:, :, L - 1:L].broadcast_to([128, NCH, L])
                sch.op("dve", lambda e, gend=gend: e.tensor_tensor(out=v3(ge[:, :]), in0=gend, in1=v3(gi[:, :]),
                                                                   op=ALU.subtract), reads=[r_gi], writes=[r_ge])
                yield
                sch.op("act", lambda e: e.activation(out=ge[:, :], in_=ge[:, :], func=AF.Exp),
                       reads=[r_ge], writes=[r_ge])
                yield
                sch.op("dve", lambda e, pr=pr: e.tensor_tensor(out=o.Bh[:, pr, :], in0=bb[:, :], in1=ge[:, :],
                                                               op=ALU.mult), reads=[r_bb, r_ge], writes=[o.r_Bh[pr]])
                sch.op("pool", lambda e, pr=pr: e.tensor_tensor(out=o.Kh[:, pr, :], in0=kp[:, :], in1=ge[:, :],
                                                                op=ALU.mult), reads=[r_kp, r_ge], writes=[o.r_Kh[pr]])
                yield

        def pre_gen(gc):
            blk, cc = divmod(gc, NCH)
            o = B[blk % 2]
            k = K_[gc % NK]
            cs = slice(cc * L, (cc + 1) * L)
            for pr in range(2):
                for j, (src, rs) in enumerate(((o.Bh, o.r_Bh[pr]), (o.Kh, o.r_Kh[pr]), (o.pm_v, o.r_v[pr]))):
                    sch.op("pe", lambda e, j=j, src=src, pr=pr: e.transpose(
                        out=banks[0][0:64, j * 128:(j + 1) * 128], in_=src[:, pr, cs], identity=C.identf[:, :]),
                        reads=[rs], writes=[rb[0]])
                sch.op("act", lambda e, pr=pr: e.copy(
                    out=k.tok[:, pr, 0:3, :, :],
                    in_=banks[0][0:64, 0:384].rearrange("p (j h e) -> p j h e", j=3, h=2)),
                    reads=[rb[0]], writes=[k.r_tok[pr]])
                yield
            for h in range(4):
                pr = h // 2
                bk = banks[1 + pr]
                oo = (h % 2) * 256
                rhs_ar = o.ARz[h][:, :, cs]
                sch.op("pe", lambda e, bk=bk, oo=oo, pr=pr, rhs_ar=rhs_ar: e.matmul(
                    out=bk[0:64, oo:oo + 128], lhsT=o.Bt[:, pr, cs], rhs=rhs_ar, start=True, stop=True),
                    reads=[o.r_Bt[pr], o.r_AR[pr]], writes=[rb[1 + pr]])
                sch.op("pe", lambda e, bk=bk, oo=oo, pr=pr, rhs_ar=rhs_ar: e.matmul(
                    out=bk[0:64, oo + 128:oo + 256], lhsT=o.Kt[:, pr, cs], rhs=rhs_ar, start=True, stop=True),
                    reads=[o.r_Kt[pr], o.r_AR[pr]], writes=[rb[1 + pr]])
                sch.op("pe", lambda e, h=h, pr=pr: e.matmul(
                    out=banks[4][0:64, h * 64:(h + 1) * 64], lhsT=o.ARz[h][:, 0, cs],
                    rhs=o.Bt[:, pr, cs], start=True, stop=True),
                    reads=[o.r_Bt[pr], o.r_AR[pr]], writes=[rb[4]])
                if h % 2 == 1:
                    sch.op("dve", lambda e, pr=pr: e.tensor_tensor(
                        out=k.scs[:, 2 * pr:2 * pr + 2, :, :].rearrange("p h b t -> p (h b t)"),
                        in0=banks[1 + pr][0:64, :], in1=C.rwmask[:, :], op=ALU.mult),
                        reads=[rb[1 + pr]], writes=[k.r_scs[pr]])
            sch.op("dve", lambda e: e.tensor_tensor(out=k.nns[:, :, :].rearrange("p h t -> p (h t)"),
                                                    in0=banks[4][0:64, 0:256], in1=C.nmask[:, :], op=ALU.mult),
                   reads=[rb[4]], writes=[k.r_nns])
            yield

            def Mlev(lev, h):
                return k.scs[:, h, 0, :] if lev == 0 else k.pws[(lev - 1) % 2][:, h, 1, :]

            def Nlev(lev, h):
                return k.nns[:, h, :] if lev == 0 else k.pws[(lev - 1) % 2][:, h, 0, :]

            def Rlev(lev, h):
                return [k.r_scs[h // 2], k.r_nns] if lev == 0 else [k.r_pws[(lev - 1) % 2]]
            sch.op("pool", lambda e: e.tensor_tensor(
                out=k.Q[0][:, :, :], in0=k.scs[:, :, 0, :],
                in1=C.identf[0:64, 0:64].unsqueeze(1).broadcast_to([64, 4, 64]), op=ALU.add),
                reads=[k.r_scs[0], k.r_scs[1]], writes=[k.r_Q[0]])
            yield
            qi = 0
            for lev in range(1, 6):
                for h in range(4):
                    sch.op("pe", lambda e, lev=lev, h=h: e.matmul(
                        out=banks[3][0:64, h * 128:h * 128 + 64], lhsT=Mlev(lev - 1, h), rhs=Nlev(lev - 1, h),
                        start=True, stop=True), reads=Rlev(lev - 1, h), writes=[rb[3]])
                    if lev < 5:
                        sch.op("pe", lambda e, lev=lev, h=h: e.matmul(
                            out=banks[3][0:64, h * 128 + 64:h * 128 + 128], lhsT=Nlev(lev - 1, h),
                            rhs=Mlev(lev - 1, h), start=True, stop=True), reads=Rlev(lev - 1, h), writes=[rb[3]])
                if lev < 5:
                    evac(C, lev, k.pws[(lev - 1) % 2][:, :, :, :].rearrange("p h b t -> p (h b t)"),
                         banks[3][0:64, :], [rb[3]], writes=[k.r_pws[(lev - 1) % 2]])
                    nsrc = lambda h, lev=lev: k.pws[(lev - 1) % 2][:, h, 0, :]
                    rn = [k.r_pws[(lev - 1) % 2]]
                else:
                    sch.op("act", lambda e: e.copy(
                        out=k.nns[:, :, :], in_=banks[3][0:64, :].rearrange("p (h b t) -> p h b t", h=4, b=2)[:, :, 0, :]),
                        reads=[rb[3]], writes=[k.r_nns])
                    nsrc = lambda h: k.nns[:, h, :]
                    rn = [k.r_nns]
                yield
                for h in range(4):
                    sch.op("pe", lambda e, h=h, qi=qi, nsrc=nsrc: e.matmul(
                        out=bkq[0:64, 256 + h * 64:256 + (h + 1) * 64], lhsT=nsrc(h), rhs=k.Q[qi][:, h, :],
                        start=True, stop=True), reads=rn + [k.r_Q[qi]], writes=[rbq])
                sch.op("dve", lambda e, qi=qi: e.tensor_tensor(
                    out=k.Q[1 - qi][:, :, :].rearrange("p h e -> p (h e)"), in0=bkq[0:64, 256:512],
                    in1=k.Q[qi][:, :, :].rearrange("p h e -> p (h e)"), op=ALU.add),
                    reads=[rbq, k.r_Q[qi]], writes=[k.r_Q[1 - qi]])
                qi = 1 - qi
                yield
            k.qfin = qi

        def epi_early(gc):
            blk, cc = divmod(gc, NCH)
            o = B[blk % 2]
            k = K_[gc % NK]
            cs = slice(cc * L, (cc + 1) * L)
            b7 = banks[7]
            i2 = gc % 2
            sch.op("pe", lambda e: e.matmul(out=b7[0:64, 0:256], lhsT=o.lin_g[:, cs], rhs=lup[0:64, 2, :],
                                            start=True, stop=True), reads=[o.r_ling, r_par], writes=[rb[7]])
            for pr in range(2):
                sch.op("pe", lambda e, pr=pr: e.matmul(out=b7[0:64, 256 + 2 * pr:256 + 2 * pr + 2],
                                                       lhsT=o.prodb[:, pr, cs], rhs=C.sel2b[:, :],
                                                       start=True, stop=True), reads=[o.r_prodb[pr]], writes=[rb[7]])
            sch.op("act", lambda e: e.copy(out=gsb[i2][:, :], in_=b7[0:64, 0:256]), reads=[rb[7]], writes=[r_gsb[i2]])
            sch.op("act", lambda e: e.copy(out=st7[i2][:, :, 0], in_=b7[0:64, 256:260]), reads=[rb[7]],
                   writes=[r_st7[i2]])
            for pr in range(2):
                sch.op("pool", lambda e, pr=pr: e.tensor_tensor(
                    out=bv[i2][:, 2 * pr:2 * pr + 2, :], in0=k.tok[:, pr, 2, :, :],
                    in1=st7[i2][:, 2 * pr:2 * pr + 2, :].broadcast_to([64, 2, 64]), op=ALU.mult),
                    reads=[k.r_tok[pr], r_st7[i2]], wacc=[r_bvv[i2]])

        def epi_gen(gc):
            gcs = slice(gc * L, (gc + 1) * L)
            i2 = gc % 2
            y_, ry_ = yv[i2], r_yv[i2]
            b7 = banks[7]
            sch.op("dve", lambda e: e.tensor_reduce(out=st[:, 0, :, :], in_=y_[:, :, :], axis=AX.X, op=ALU.add),
                   reads=[ry_], writes=[r_st])
            sch.op("pool", lambda e: e.tensor_tensor(out=ysq[:, :, :], in0=y_[:, :, :], in1=y_[:, :, :], op=ALU.mult),
                   reads=[ry_], writes=[r_ysq])
            yield
            sch.op("dve", lambda e: e.tensor_reduce(out=st[:, 1, :, :], in_=ysq[:, :, :], axis=AX.X, op=ALU.add),
                   reads=[r_ysq], writes=[r_st])
            yield
            sch.op("dve", lambda e: e.tensor_scalar(out=st[:, 2, :, :], in0=st[:, 0, :, :], scalar1=1.0 / 64,
                                                    scalar2=None, op0=ALU.mult), reads=[r_st], writes=[r_st])
            sch.op("dve", lambda e: e.tensor_tensor(out=st[:, 3, :, :], in0=st[:, 2, :, :], in1=st[:, 2, :, :],
                                                    op=ALU.mult), reads=[r_st], writes=[r_st])
            yield
            sch.op("dve", lambda e: e.scalar_tensor_tensor(out=st[:, 4, :, :], in0=st[:, 1, :, :], scalar=1.0 / 64,
                                                           in1=st[:, 3, :, :], op0=ALU.mult, op1=ALU.subtract),
                   reads=[r_st], writes=[r_st])
            sch.op("dve", lambda e: e.tensor_scalar(out=st[:, 4, :, :], in0=st[:, 4, :, :], scalar1=64e-5,
                                                    scalar2=None, op0=ALU.add), reads=[r_st], writes=[r_st])
            yield
            sch.op("act", lambda e: e.activation(out=st[:, 5, :, :], in_=st[:, 4, :, :], func=AF.Sqrt),
                   reads=[r_st], writes=[r_st])
            sch.op("dve", lambda e: e.reciprocal(out=st[:, 6, :, :], in_=st[:, 5, :, :]), reads=[r_st], writes=[r_st])
            yield
            sch.op("dve", lambda e: e.tensor_tensor(out=ysq[:, :, :], in0=y_[:, :, :],
                                                    in1=st[:, 2, :, :].broadcast_to([64, 4, 64]), op=ALU.subtract),
                   reads=[ry_, r_st], writes=[r_ysq])
            yield
            sch.op("dve", lambda e: e.tensor_tensor(out=ysq[:, :, :], in0=ysq[:, :, :],
                                                    in1=st[:, 6, :, :].broadcast_to([64, 4, 64]), op=ALU.mult),
                   reads=[r_ysq, r_st], writes=[r_ysq])
            yield
            y2 = ysq[:, :, :].rearrange("p h e -> p (h e)")
            sch.op("pool", lambda e: e.tensor_tensor(out=y2, in0=y2, in1=lng[:, :], op=ALU.mult),
                   reads=[r_ysq, r_par], writes=[r_ysq])
            yield
            sch.op("pool", lambda e: e.tensor_tensor(out=y2, in0=y2, in1=lnb[:, :], op=ALU.add),
                   reads=[r_ysq, r_par], writes=[r_ysq])
            yield
            sch.op("pool", lambda e: e.tensor_tensor(out=ysq[:, :, :], in0=ysq[:, :, :], in1=bv[i2][:, :, :],
                                                     op=ALU.add), reads=[r_ysq, r_bvv[i2]], writes=[r_ysq])
            yield
            sch.op("dve", lambda e: e.tensor_tensor(out=yo[i2][:, :], in0=y2, in1=gsb[i2][:, :], op=ALU.mult),
                   reads=[r_ysq, r_gsb[i2]], writes=[r_yo2[i2]])
            yield

        def epi_tail(gc):
            gcs = slice(gc * L, (gc + 1) * L)
            i2 = gc % 2
            b7 = banks[7]
            for j in range(2):
                sch.op("pe", lambda e, j=j: e.transpose(out=b7[:, 384 + j * 64:384 + (j + 1) * 64],
                                                        in_=yo[i2][:, j * 128:(j + 1) * 128],
                                                        identity=C.identf[0:64, 0:64]), reads=[r_yo2[i2]],
                       writes=[rb[7]])
            sch.op("act", lambda e: e.copy(out=yrT[:, :, gcs], in_=b7[:, 384:512].rearrange("p (j t) -> p j t", j=2)),
                   reads=[rb[7]], wacc=[r_yrT])
            yield

        QB = 256
        lam_init = 0.8 - 0.6 * math.exp(-0.3 * l)

        def csb(name, shape, dt):
            return _sb(es, nc, "C_" + name, shape, dt)
        c_lq = csb("lq", [128, 4, 64], F32)
        c_lt = csb("lt", [128, 2, 64], F32)
        c_ls = csb("ls", [128, 4], F32)
        nlam = csb("nlam", [128, 1], F32)
        ag = csb("ag", [128, 4], F32)
        rc_par = Res()
        qTc = [csb("qT%d" % c, [128, S], BF16) for c in range(2)]
        kTc = csb("kT", [128, S], BF16)
        vtc = csb("vt", [128, NT, 128], BF16)
        rc_q, rc_k, rc_v = Res(), Res(), Res()
        ptc = [csb("pt%d" % i, [128, QB], BF16) for i in range(2)]
        rc_pt = [Res(), Res()]
        c_rl = csb("rl", [128, QB], F32)
        c_oc = [csb("oc%d" % i, [128, QB], F32) for i in range(2)]
        c_od = csb("od", [128, QB], F32)
        c_sq = csb("sq", [128, QB], F32)
        c_rs = csb("rs", [128, QB], F32)
        c_ya = csb("ya", [128, QB], BF16)
        rc_rl, rc_od, rc_sq, rc_rs, rc_ya = Res(), Res(), Res(), Res(), Res()
        rc_oc = [Res(), Res()]
        p_st = [phys[5], phys[6]]
        rp_st = [prs[5], prs[6]]
        p_o = phys[7][:, 0:QB]
        p_l = phys[7][:, 256:256 + QB]
        rp_ol = prs[7]
        p_ss = phys[5][:, 256:256 + QB]
        rp_ss = prs[5]

        def c_gen():
            for i, nm in enumerate(("a_lq1", "a_lk1", "a_lq2", "a_lk2")):
                sch.dma("sp", c_lq[:, i, :], d[nm][l], wacc=[rc_par])
            sch.dma("sp", ag[:, :], d["a_norm_g"][l], wacc=[rc_par])
            for i in range(2):
                sch.op("dve", lambda e, i=i: e.tensor_tensor(out=c_lt[:, i, :], in0=c_lq[:, 2 * i, :],
                                                             in1=c_lq[:, 2 * i + 1, :], op=ALU.mult),
                       reads=[rc_par], writes=[rc_par])
            sch.op("dve", lambda e: e.tensor_reduce(out=c_ls[:, 0:2], in_=c_lt[:, :, :], axis=AX.X, op=ALU.add),
                   reads=[rc_par], writes=[rc_par])
            sch.op("act", lambda e: e.activation(out=c_ls[:, 2:4], in_=c_ls[:, 0:2], func=AF.Exp),
                   reads=[rc_par], writes=[rc_par])
            sch.op("dve", lambda e: e.tensor_tensor(out=nlam[:, :], in0=c_ls[:, 3:4], in1=c_ls[:, 2:3], op=ALU.subtract),
                   reads=[rc_par], writes=[rc_par])
            sch.op("dve", lambda e: e.tensor_scalar(out=nlam[:, :], in0=nlam[:, :], scalar1=-lam_init, scalar2=None,
                                                    op0=ALU.add), reads=[rc_par], writes=[rc_par])
            sch.op("dve", lambda e: e.tensor_scalar(out=ag[:, :], in0=ag[:, :], scalar1=1.0 - lam_init, scalar2=None,
                                                    op0=ALU.mult), reads=[rc_par], writes=[rc_par])
            yield
            idx0 = 0
            for h in range(4):
                for c in range(2):
                    if h == 0:
                        sch.op("pool", lambda e, c=c: e.memset(qTc[c][(1 - c) * 64:(1 - c) * 64 + 64, :], 0.0),
                               wacc=[rc_q])
                    sch.dma("sp", qTc[c][c * 64:c * 64 + 64, :], d["aqkT"][h * 128 + c * 64:h * 128 + c * 64 + 64, :],
                            reads=[C.r_aqkT], wacc=[rc_q])
                sch.dma("sp", kTc[:, :], d["aqkT"][512 + h * 128:512 + (h + 1) * 128, :], reads=[C.r_aqkT],
                        writes=[rc_k])
                sch.dma("sp", vtc[:, :, :], d["av"][:, h * 128:(h + 1) * 128].rearrange("(t p) e -> p t e", p=128),
                        reads=[C.r_av], writes=[rc_v])
                yield
                NQ = QB // 128
                items = [(j, c, kt) for j in range(S // QB) for c in range(2) for kt in range(NQ * (j + 1))]

                def front(it, idx):
                    j, c, kt = it
                    n0 = max(0, kt - NQ * j) * 128
                    ps, rps = p_st[idx % 2], rp_st[idx % 2]
                    p_, rp_ = ptc[idx % 2], rc_pt[idx % 2]
                    sch.op("pe", lambda e: e.matmul(
                        out=ps[:, n0:QB], lhsT=kTc[:, kt * 128:(kt + 1) * 128],
                        rhs=qTc[c][:, j * QB + n0:(j + 1) * QB], start=True, stop=True),
                        reads=[rc_k, rc_q], writes=[rps])
                    sch.op("act", lambda e: e.activation(out=p_[:, n0:QB], in_=ps[:, n0:QB], func=AF.Exp, scale=0.125),
                           reads=[rps], writes=[rp_])
                    if kt >= NQ * j:
                        sch.op("pool", lambda e: e.tensor_tensor(out=p_[:, n0:n0 + 128], in0=p_[:, n0:n0 + 128],
                                                                 in1=C.trib[:, :], op=ALU.mult), reads=[rp_], writes=[rp_])

                def back(it, idx):
                    j, c, kt = it
                    nk = NQ * (j + 1)
                    n0 = max(0, kt - NQ * j) * 128
                    p_, rp_ = ptc[idx % 2], rc_pt[idx % 2]
                    sch.op("pe", lambda e: e.matmul(out=p_o[:, n0:QB], lhsT=vtc[:, kt, :], rhs=p_[:, n0:QB],
                                                    start=(kt == 0), stop=(kt == nk - 1)),
                           reads=[rc_v, rp_], writes=[rp_ol])
                    sch.op("pe", lambda e: e.matmul(out=p_l[:, n0:QB], lhsT=C.onesb[:, :], rhs=p_[:, n0:QB],
                                                    start=False, stop=(kt == nk - 1), skip_group_check=True),
                           reads=[rp_], writes=[rp_ol])
                    if kt != nk - 1:
                        return
                    sch.op("act", lambda e: e.activation(out=c_rl[:, :], in_=p_l, func=AF.Ln),
                           reads=[rp_ol], writes=[rc_rl])
                    sch.op("act", lambda e: e.activation(out=c_rl[:, :], in_=c_rl[:, :], func=AF.Exp, scale=-1.0),
                           reads=[rc_rl], writes=[rc_rl])
                    sch.op("dve", lambda e: e.tensor_tensor(out=c_oc[c][:, :], in0=p_o, in1=c_rl[:, :], op=ALU.mult),
                           reads=[rp_ol, rc_rl], writes=[rc_oc[c]])
                    if c == 0:
                        return
                    qs = slice(j * QB, (j + 1) * QB)
                    sch.op("dve", lambda e: e.scalar_tensor_tensor(out=c_od[:, :], in0=c_oc[1][:, :], scalar=nlam[:, 0:1],
                                                                   in1=c_oc[0][:, :], op0=ALU.mult, op1=ALU.add),
                           reads=[rc_oc[0], rc_oc[1], rc_par], writes=[rc_od])
                    sch.op("pool", lambda e: e.tensor_tensor(out=c_sq[:, :], in0=c_od[:, :], in1=c_od[:, :], op=ALU.mult),
                           reads=[rc_od], writes=[rc_sq])
                    sch.op("pe", lambda e: e.matmul(out=p_ss, lhsT=C.onesf[:, :], rhs=c_sq[:, :], start=True, stop=True),
                           reads=[rc_sq], writes=[rp_ss])
                    sch.op("dve", lambda e: e.tensor_scalar(out=c_rs[:, :], in0=p_ss, scalar1=1.0 / 128, scalar2=1e-5,
                                                            op0=ALU.mult, op1=ALU.add), reads=[rp_ss], writes=[rc_rs])
                    sch.op("act", lambda e: e.activation(out=c_rs[:, :], in_=c_rs[:, :], func=AF.Ln),
                           reads=[rc_rs], writes=[rc_rs])
                    sch.op("act", lambda e: e.activation(out=c_rs[:, :], in_=c_rs[:, :], func=AF.Exp, scale=-0.5),
                           reads=[rc_rs], writes=[rc_rs])
                    sch.op("dve", lambda e: e.scalar_tensor_tensor(out=c_ya[:, :], in0=c_od[:, :], scalar=ag[:, h:h + 1],
                                                                   in1=c_rs[:, :], op0=ALU.mult, op1=ALU.mult),
                           reads=[rc_od, rc_rs, rc_par], writes=[rc_ya])
                    sch.dma("sp", d["mixT"][512 + h * 128:512 + (h + 1) * 128, qs], c_ya[:, :], reads=[rc_ya],
                            wacc=[C.r_mixT])

                front(items[0], idx0)
                for i in range(len(items)):
                    if i + 1 < len(items):
                        front(items[i + 1], idx0 + i + 1)
                    back(items[i], idx0 + i)
                    yield
                idx0 += len(items)

        hi = []
        lo = []
        pq = []
        cq = [("attn", c_gen())] if not globals().get("NO_ATTN", False) else []
        CRATE = globals().get("CRATE_", 2)
        rr = [0]

        def pull(q, idx):
            try:
                next(q[idx][1])
                return True
            except StopIteration:
                q.pop(idx)
                return False

        def pump(n):
            for _ in range(n):
                if hi:
                    rr[0] = (rr[0] + 1) % len(hi)
                    pull(hi, rr[0])
                if lo:
                    pull(lo, 0)
                if pq:
                    pull(pq, 0)
                for _ in range(CRATE):
                    if cq:
                        pull(cq, 0)

        def drain_tag(q, pred):
            i = 0
            while i < len(q):
                if pred(q[i][0]):
                    while pull(q, i):
                        pass
                else:
                    i += 1

        def chain(gc):
            blk, cc = divmod(gc, NCH)
            o = B[blk % 2]
            k = K_[gc % NK]
            cs = slice(cc * L, (cc + 1) * L)
            Q = k.Q[k.qfin]
            rQ = k.r_Q[k.qfin]
            wq = banks[5]
            for h in range(4):
                pr, hf = h // 2, h % 2
                sch.op("pe", lambda e, h=h, pr=pr: e.matmul(
                    out=wq[0:64, h * 64:(h + 1) * 64], lhsT=o.ARz[h][:, 0, cs], rhs=Hb[:, pr, :],
                    start=True, stop=False), reads=[o.r_AR[pr], r_Hb], writes=[rb[5]])
                sch.op("pe", lambda e, h=h, pr=pr, hf=hf: e.matmul(
                    out=wq[0:64, h * 64:(h + 1) * 64], lhsT=k.scs[:, h, 2, :], rhs=k.tok[:, pr, 2, hf, :],
                    start=False, stop=True), reads=[k.r_scs[pr], k.r_tok[pr]], writes=[rb[5]])
            sch.op("dve", lambda e: e.tensor_copy(out=Wt[:, :, :].rearrange("p h e -> p (h e)"), in_=wq[0:64, 0:256]),
                   reads=[rb[5]], writes=[r_Wt])
            pump(6)
            for h in range(4):
                sch.op("pe", lambda e, h=h: e.matmul(out=wq[0:64, h * 64:(h + 1) * 64], lhsT=Q[:, h, :],
                                                     rhs=Wt[:, h, :], start=True, stop=True),
                       reads=[rQ, r_Wt], writes=[rb[5]])
            sch.op("dve", lambda e: e.tensor_copy(out=Ut[:, :, :].rearrange("p h e -> p (h e)"), in_=wq[0:64, 0:256]),
                   reads=[rb[5]], writes=[r_Ut])
            pump(6)
            b6 = banks[6]
            for h in range(4):
                pr, hf = h // 2, h % 2
                oo = b6[:, 256 + h * 64:256 + (h + 1) * 64]
                sch.op("pe", lambda e, oo=oo, h=h, pr=pr: e.matmul(
                    out=oo, lhsT=k.tok[:, pr, 0, :, :].rearrange("p a e -> p (a e)"), rhs=Ut[:, h, :],
                    start=True, stop=False), reads=[k.r_tok[pr], r_Ut], writes=[rb[6]])
                sch.op("pe", lambda e, oo=oo, pr=pr, hf=hf: e.matmul(
                    out=oo, lhsT=k.tok[:, pr, 1, :, :].rearrange("p a e -> p (a e)"), rhs=k.tok[:, pr, 2, hf, :],
                    start=False, stop=True), reads=[k.r_tok[pr]], writes=[rb[6]])
            for h in range(4):
                pr, hf = h // 2, h % 2
                oo = b6[0:64, h * 64:(h + 1) * 64]
                sch.op("pe", lambda e, oo=oo, h=h, pr=pr: e.matmul(
                    out=oo, lhsT=o.ARz[h][:, 1, cs], rhs=Hb[:, pr, :], start=True, stop=False),
                    reads=[o.r_AR[pr], r_Hb], writes=[rb[6]])
                sch.op("pe", lambda e, oo=oo, h=h, pr=pr: e.matmul(
                    out=oo, lhsT=k.scs[:, h, 1, :], rhs=Ut[:, h, :], start=False, stop=False),
                    reads=[k.r_scs[pr], r_Ut], writes=[rb[6]])
                sch.op("pe", lambda e, oo=oo, h=h, pr=pr, hf=hf: e.matmul(
                    out=oo, lhsT=k.scs[:, h, 3, :], rhs=k.tok[:, pr, 2, hf, :], start=False, stop=True),
                    reads=[k.r_scs[pr], k.r_tok[pr]], writes=[rb[6]])
            for h in range(4):
                pr, pb = h // 2, (h % 2) * 64
                sch.op("dve", lambda e, h=h, pr=pr, pb=pb: e.scalar_tensor_tensor(
                    out=Hs[pb:pb + 64, pr, :], in0=Hs[pb:pb + 64, pr, :],
                    scalar=o.E1[pr][pb:pb + 64, cc * L + L - 1:cc * L + L],
                    in1=b6[pb:pb + 64, 256 + h * 64:256 + (h + 1) * 64], op0=ALU.mult, op1=ALU.add),
                    reads=[r_Hs, o.r_E1[pr], rb[6]], wacc=[r_Hs])
            sch.op("act", lambda e: e.copy(out=Hb[:, :, :], in_=Hs[:, :, :]), reads=[r_Hs], writes=[r_Hb])
            y_, ry_ = yv[gc % 2], r_yv[gc % 2]
            sch.op("act", lambda e: e.copy(out=y_[:, :, :].rearrange("p h e -> p (h e)"), in_=b6[0:64, 0:256]),
                   reads=[rb[6]], writes=[ry_])
            pump(6)

        NG = NBLK * NCH
        for _ in prep_gen(0):
            pass
        for _ in pre_gen(0):
            pass
        hi.append((1, pre_gen(1)))
        for gc in range(NG):
            blk, cc = divmod(gc, NCH)
            drain_tag(lo, lambda t: t[0] == "epi" and t[1] <= gc - 2)
            if gc + 2 < NG:
                if (gc + 2) // NCH != (gc + 1) // NCH or (gc + 2) % NCH == 0:
                    drain_tag(pq, lambda t: True)
                hi.append((gc + 2, pre_gen(gc + 2)))
            chain(gc)
            epi_early(gc)
            lo.append((("epi", gc), epi_gen(gc)))
            if cc == 0 and blk + 1 < NBLK:
                pq.append((("prep", blk + 1), prep_gen(blk + 1)))
            pump(6)
            drain_tag(hi, lambda t: t == gc + 1)
            if gc >= 1:
                drain_tag(lo, lambda t: t[0] == "epi" and t[1] <= gc - 1)
                for _ in epi_tail(gc - 1):
                    pass
        drain_tag(lo, lambda t: True)
        for _ in epi_tail(NG - 1):
            pass
        drain_tag(cq, lambda t: True)
        for j in range(2):
            sch.dma("sp", d["mixT"][256 + j * 128:256 + (j + 1) * 128, :], yrT[:, j, :], reads=[r_yrT],
                    wacc=[C.r_mixT])
```

```python
import math
from contextlib import ExitStack

import numpy as np
import concourse.bass as bass
import concourse.mybir as mybir
from concourse.bass_utils import run_bass_kernel_spmd

F32 = mybir.dt.float32
BF16 = mybir.dt.bfloat16
ALU = mybir.AluOpType
AF = mybir.ActivationFunctionType
AX = mybir.AxisListType

D = 1024
S = 4096
DEPTH = 2
NT = S // 128
IN_PROJ = 3464
DFF = 4096
NDS = 8
EPS = 1e-6


class Res:
    __slots__ = ("w", "r", "f", "name", "excl")

    def __init__(self, name="", excl=False):
        self.f = {}
        self.w = {}
        self.r = {}
        self.name = name
        self.excl = excl


def PRes():
    return Res("psum", True)


class Sched:
    def __init__(self, nc, es):
        self.nc = nc
        self.eng = {"pe": nc.tensor, "dve": nc.vector, "act": nc.scalar,
                    "pool": nc.gpsimd, "sp": nc.sync}
        self.semobj = {}
        self.cnt = {}
        for k in self.eng:
            self.semobj[k] = es.enter_context(nc.semaphore("s_" + k))
            self.cnt[k] = 0
        self.seen = {k: {} for k in self.eng}
        self.dcnt = {}
        self.dnext = {}
        for k in ("sp", "act", "pool"):
            self.dnext[k] = 0
            for i in range(NDS):
                key = "d_%s%d" % (k, i)
                self.semobj[key] = es.enter_context(nc.semaphore(key))
                self.dcnt[key] = 0

    def _wait(self, eng, key, val):
        if eng == "pe" and key == "pe":
            return
        if self.seen[eng].get(key, 0) >= val:
            return
        self.eng[eng].wait_ge(self.semobj[key], val)
        self.seen[eng][key] = val

    def _deps(self, eng, reads, writes, wacc=()):
        deps = {}
        for w in wacc:
            for k, v in w.r.items():
                if deps.get(k, 0) < v:
                    deps[k] = v
            for k, v in w.f.items():
                if deps.get(k, 0) < v:
                    deps[k] = v
        for r in reads:
            for k, v in r.w.items():
                if deps.get(k, 0) < v:
                    deps[k] = v
        for w in writes:
            for k, v in w.w.items():
                if deps.get(k, 0) < v:
                    deps[k] = v
            for k, v in w.r.items():
                if deps.get(k, 0) < v:
                    deps[k] = v
        for k, v in deps.items():
            self._wait(eng, k, v)

    def _mark(self, key, val, reads, writes, wacc=()):
        for r in reads:
            if r.r.get(key, 0) < val:
                r.r[key] = val
        for w in wacc:
            if w.w.get(key, 0) < val:
                w.w[key] = val
        for w in writes:
            w.w = {key: val}
            w.f = {key: val}
            w.r = {}

    limit = None
    nops = 0

    def op(self, eng, fn, reads=(), writes=(), wacc=()):
        self.nops += 1
        if self.limit is not None and self.nops > self.limit:
            return
        if any(r.excl for r in reads):
            writes = list(writes) + [r for r in reads if r.excl]
            reads = [r for r in reads if not r.excl]
        self._deps(eng, reads, writes, wacc)
        inst = fn(self.eng[eng])
        self.cnt[eng] += 1
        v = self.cnt[eng]
        inst.then_inc(self.semobj[eng], 1)
        self._mark(eng, v, reads, writes, wacc)

    def dma(self, eng, out, in_, reads=(), writes=(), wacc=(), **kw):
        self.nops += 1
        if self.limit is not None and self.nops > self.limit:
            return
        i = self.dnext[eng]
        self.dnext[eng] = (i + 1) % NDS
        key = "d_%s%d" % (eng, i)
        prev = self.dcnt[key]
        if prev > 0:
            self._wait(eng, key, prev)
        self._deps(eng, reads, writes, wacc)
        inst = self.eng[eng].dma_start(out=out, in_=in_, **kw)
        val = prev + 16
        self.dcnt[key] = val
        inst.then_inc(self.semobj[key], 16)
        self._mark(key, val, reads, writes, wacc)

    def barrier(self):
        for e in self.eng:
            for k, v in self.dcnt.items():
                if v > 0:
                    self._wait(e, k, v)
            for k in self.eng:
                if self.cnt[k] > 0:
                    self._wait(e, k, self.cnt[k])

    def finish(self):
        for k, v in self.dcnt.items():
            if v > 0:
                self._wait("sp", k, v)
        for k in self.eng:
            if k != "sp" and self.cnt[k] > 0:
                self._wait("sp", k, self.cnt[k])


class Ctx:
    pass


_UID = [0]


def _sb(es, nc, name, shape, dt):
    _UID[0] += 1
    return es.enter_context(nc.sbuf_tensor("%s_%d" % (name, _UID[0]), shape, dt))


def _ps(es, nc, name, shape, dt):
    _UID[0] += 1
    return es.enter_context(nc.psum_tensor("%s_%d" % (name, _UID[0]), shape, dt))


def load_w_bf16(C, dst, dst_res, src2d, nk, ncols, stage, stage_res, cnt0=0):
    sch = C.sch
    step = stage.shape[2]
    i = cnt0
    for kc in range(nk):
        for c0 in range(0, ncols, step):
            n = min(step, ncols - c0)
            j = i % stage.shape[1]
            sch.dma("sp", stage[:, j, 0:n], src2d[kc * 128:(kc + 1) * 128, c0:c0 + n],
                    writes=[stage_res[j]])
            eng = ("dve", "pool", "act")[i % 3]
            if eng == "act":
                sch.op("act", lambda e, j=j, n=n, kc=kc, c0=c0: e.copy(
                    out=dst[:, kc, c0:c0 + n], in_=stage[:, j, 0:n]),
                    reads=[stage_res[j]], wacc=[dst_res[kc]])
            else:
                sch.op(eng, lambda e, j=j, n=n, kc=kc, c0=c0: e.tensor_copy(
                    out=dst[:, kc, c0:c0 + n], in_=stage[:, j, 0:n]),
                    reads=[stage_res[j]], wacc=[dst_res[kc]])
            i += 1
    return i


def load_w_gen(C, dst, dst_res, src2d, nk, ncols, stage, stage_res, cnt):
    sch = C.sch
    step = stage.shape[2]
    nslot = stage.shape[1]
    for kc in range(nk):
        for c0 in range(0, ncols, step):
            n = min(step, ncols - c0)
            i = cnt[0]
            cnt[0] += 1
            j = i % nslot
            sch.dma("sp", stage[:, j, 0:n], src2d[kc * 128:(kc + 1) * 128, c0:c0 + n], writes=[stage_res[j]])
            eng = ("dve", "pool")[i % 2]
            sch.op(eng, lambda e, j=j, n=n, kc=kc, c0=c0: e.tensor_copy(
                out=dst[:, kc, c0:c0 + n], in_=stage[:, j, 0:n]), reads=[stage_res[j]], wacc=[dst_res[kc]])
            yield


class OWeights:
    def __init__(self, C, es, l):
        nc = C.nc
        self.wout = _sb(es, nc, "O_wout", [128, 8, D], BF16)
        self.wup = _sb(es, nc, "O_wup", [128, 8, DFF], BF16)
        self.wdn = _sb(es, nc, "O_wdn", [128, 32, D], BF16)
        self.wout_r = [Res() for _ in range(8)]
        self.wup_r = [Res() for _ in range(8)]
        self.wdn_r = [Res() for _ in range(32)]
        self.stage = _sb(es, nc, "O_stage", [128, 2, 512], F32)
        self.stage_r = [Res() for _ in range(2)]
        self.l = l
        self.C = C

    def gen(self):
        C, d, l = self.C, self.C.d, self.l
        cnt = [0]
        yield from load_w_gen(C, self.wout, self.wout_r, d["w_out"][l], 8, D, self.stage, self.stage_r, cnt)
        yield from load_w_gen(C, self.wup, self.wup_r, d["w_ff_up"][l], 8, DFF, self.stage, self.stage_r, cnt)
        yield from load_w_gen(C, self.wdn, self.wdn_r, d["w_ff_down"][l], 32, D, self.stage, self.stage_r, cnt)


def rms_rstd(C, ss, ms, sd, rstd, r_ss, r_tmp, r_rstd, n, eps):
    sch = C.sch
    sch.op("dve", lambda e: e.tensor_scalar(out=ms, in0=ss, scalar1=1.0 / n, scalar2=eps,
                                            op0=ALU.mult, op1=ALU.add),
           reads=[r_ss], writes=[r_tmp])
    sch.op("act", lambda e: e.activation(out=sd, in_=ms, func=AF.Ln),
           reads=[r_tmp], writes=[r_tmp])
    sch.op("act", lambda e: e.activation(out=rstd, in_=sd, func=AF.Exp, scale=-0.5), reads=[r_tmp], writes=[r_rstd])


def norm_transpose(C, xt, nsub, g, hn, hT, tps, R):
    sch = C.sch
    for s in range(nsub):
        sch.op("act", lambda e, s=s: e.activation(out=R["junk_t"][:, :], in_=xt[:, s, :], func=AF.Square,
                                                  accum_out=R["ss_t"][:, s:s + 1]),
               reads=[R["xt"]], writes=[R["junk"], R["ss"]])
    rms_rstd(C, R["ss_t"][:, 0:nsub], R["ms_t"][:, 0:nsub], R["sd_t"][:, 0:nsub], R["rstd_t"][:, 0:nsub],
             R["ss"], R["tmp"], R["rstd"], D, EPS)
    for s in range(nsub):
        eng = "dve" if s % 2 == 0 else "pool"
        sch.op(eng, lambda e, s=s: e.tensor_scalar(out=hn[:, s, :], in0=xt[:, s, :],
                                                   scalar1=R["rstd_t"][:, s:s + 1], scalar2=0.0, op0=ALU.mult, op1=ALU.add),
               reads=[R["xt"], R["rstd"]], writes=[R["hn"]])
    for kc in range(8):
        tp, rtp = tps[kc % 2]
        for s in range(nsub):
            sch.op("pe", lambda e, s=s, kc=kc, tp=tp: e.transpose(
                out=tp[:, s * 128:(s + 1) * 128], in_=hn[:, s, kc * 128:(kc + 1) * 128], identity=C.identb[:, :]),
                reads=[R["hn"]], writes=[rtp])
        if kc % 2 == 0:
            sch.op("dve", lambda e, kc=kc, tp=tp: e.tensor_scalar(
                out=hT[:, kc, :], in0=tp[:, 0:nsub * 128], scalar1=g[:, kc:kc + 1], scalar2=None, op0=ALU.mult),
                reads=[rtp, R["g"]], writes=[R["hT"][kc]])
        else:
            sch.op("act", lambda e, kc=kc, tp=tp: e.activation(
                out=hT[:, kc, :], in_=tp[:, 0:nsub * 128], func=AF.Copy, scale=g[:, kc:kc + 1]),
                reads=[rtp, R["g"]], writes=[R["hT"][kc]])


def evac(C, i, out, in_, reads, writes=(), wacc=()):
    if i % 2 == 0:
        C.sch.op("act", lambda e: e.copy(out=out, in_=in_), reads=reads, writes=writes, wacc=wacc)
    else:
        C.sch.op("dve", lambda e: e.tensor_copy(out=out, in_=in_), reads=reads, writes=writes, wacc=wacc)


def phase_A(C, l, X, rX):
    nc, sch, d = C.nc, C.sch, C.d
    TB = 512
    with ExitStack() as es:
        win = _sb(es, nc, "A_win", [128, 8, IN_PROJ], BF16)
        win_r = [Res() for _ in range(8)]
        stage = _sb(es, nc, "A_stage", [128, 4, 1732], F32)
        stage_r = [Res() for _ in range(4)]
        g1 = _sb(es, nc, "A_g1", [128, 8], F32)
        xts = [_sb(es, nc, "A_xt%d" % i, [128, 4, D], F32) for i in range(2)]
        xr = [Res(), Res()]
        hn = _sb(es, nc, "A_hn", [128, 4, D], BF16)
        hT = _sb(es, nc, "A_hT", [128, 8, TB], BF16)
        R = {"junk": Res(), "ss": Res(), "tmp": Res(), "rstd": Res(), "hn": Res(), "g": Res(),
             "hT": [Res() for _ in range(8)]}
        R["junk_t"] = _sb(es, nc, "A_junk", [128, D], BF16)
        R["ss_t"] = _sb(es, nc, "A_ss", [128, 4], F32)
        R["ms_t"] = _sb(es, nc, "A_ms", [128, 4], F32)
        R["sd_t"] = _sb(es, nc, "A_sd", [128, 4], F32)
        R["rstd_t"] = _sb(es, nc, "A_rstd", [128, 4], F32)
        st_mqk = _sb(es, nc, "A_smqk", [128, 4, TB], F32)
        st_g = _sb(es, nc, "A_sg", [8, TB], F32)
        st_r = _sb(es, nc, "A_sr", [128, 7, TB], F32)
        st_aqk = _sb(es, nc, "A_saqk", [128, 8, TB], BF16)
        st_mvo = _sb(es, nc, "A_smvo", [128, 4, 512], F32)
        st_av = _sb(es, nc, "A_sav", [128, 4, 512], BF16)
        r_mqk, r_g, r_r, r_aqk, r_mvo, r_av = Res(), Res(), Res(), Res(), Res(), Res()
        tps = [(_ps(es, nc, "A_tp%d" % i, [128, 1024], BF16), PRes()) for i in range(2)]
        mms = [(_ps(es, nc, "A_mm%d" % i, [128, 512], F32), PRes()) for i in range(4)]

        def load_x(blk):
            sch.dma("sp", xts[blk % 2][:, :, :],
                    X[blk * TB:(blk + 1) * TB, :].rearrange("(s p) d -> p s d", p=128),
                    reads=[rX], writes=[xr[blk % 2]])

        load_x(0)
        sch.dma("sp", g1[:, :], d["norm1_g"][l], writes=[R["g"]])
        load_w_bf16(C, win, win_r, d["w_in"][l], 8, IN_PROJ, stage, stage_r)

        fm = []
        for i in range(4):
            fm.append((i * 128, 128, st_mqk, i, r_mqk))
        fm.append((1024, 8, st_g, None, r_g))
        for i in range(7):
            fm.append((1032 + i * 128, 128, st_r, i, r_r))
        for i in range(8):
            fm.append((1928 + i * 128, 128, st_aqk, i, r_aqk))
        mmi = 0
        for blk in range(S // TB):
            xt = xts[blk % 2]
            R["xt"] = xr[blk % 2]
            norm_transpose(C, xt, 4, g1, hn, hT, tps, R)
            if blk + 1 < S // TB:
                load_x(blk + 1)
            t0 = blk * TB
            for (c0, n, stt, idx, rs) in fm:
                ps, rps = mms[mmi % 4]
                mmi += 1
                for kc in range(8):
                    sch.op("pe", lambda e, kc=kc, c0=c0, n=n, ps=ps: e.matmul(
                        out=ps[0:n, :], lhsT=win[:, kc, c0:c0 + n], rhs=hT[:, kc, :],
                        start=(kc == 0), stop=(kc == 7)),
                        reads=[win_r[kc], R["hT"][kc]], writes=[rps])
                o = stt[0:n, :] if idx is None else stt[:, idx, :]
                evac(C, mmi, o, ps[0:n, :], [rps], wacc=[rs])
            sch.dma("sp", d["mqkT"][:, t0:t0 + TB].rearrange("(c p) t -> p c t", p=128), st_mqk[:, :, :],
                    reads=[r_mqk], wacc=[C.r_mqkT])
            sch.dma("sp", d["mgT"][:, t0:t0 + TB], st_g[:, :], reads=[r_g], wacc=[C.r_mgT])
            sch.dma("sp", d["rT"][:, t0:t0 + TB].rearrange("(c p) t -> p c t", p=128), st_r[:, :, :],
                    reads=[r_r], wacc=[C.r_rT])
            sch.dma("sp", d["aqkT"][:, t0:t0 + TB].rearrange("(c p) t -> p c t", p=128), st_aqk[:, :, :],
                    reads=[r_aqk], wacc=[C.r_aqkT])
            for (c0, stt, rs) in ((512, st_mvo, r_mvo), (2952, st_av, r_av)):
                for s in range(4):
                    ps, rps = mms[mmi % 4]
                    mmi += 1
                    for kc in range(8):
                        sch.op("pe", lambda e, kc=kc, c0=c0, s=s, ps=ps: e.matmul(
                            out=ps[:, :], lhsT=hT[:, kc, s * 128:(s + 1) * 128], rhs=win[:, kc, c0:c0 + 512],
                            start=(kc == 0), stop=(kc == 7)),
                            reads=[win_r[kc], R["hT"][kc]], writes=[rps])
                    evac(C, mmi, stt[:, s, :], ps[:, :], [rps], wacc=[rs])
            sch.dma("sp", d["mvo"][t0:t0 + TB, :].rearrange("(s p) c -> p s c", p=128), st_mvo[:, :, :],
                    reads=[r_mvo], wacc=[C.r_mvo])
            sch.dma("sp", d["av"][t0:t0 + TB, :].rearrange("(s p) c -> p s c", p=128), st_av[:, :, :],
                    reads=[r_av], wacc=[C.r_av])


def phase_O(C, l, X, rX, XOUT, rXOUT, final, W=None):
    nc, sch, d = C.nc, C.sch, C.d
    TB = 256
    NS = 2
    with ExitStack() as es:
        if W is None:
            wout = _sb(es, nc, "O_wout", [128, 8, D], BF16)
            wup = _sb(es, nc, "O_wup", [128, 8, DFF], BF16)
            wdn = _sb(es, nc, "O_wdn", [128, 32, D], BF16)
            wout_r = [Res() for _ in range(8)]
            wup_r = [Res() for _ in range(8)]
            wdn_r = [Res() for _ in range(32)]
            stage = _sb(es, nc, "O_stage", [128, 2, 512], F32)
            stage_r = [Res(), Res()]
        else:
            wout, wup, wdn = W.wout, W.wup, W.wdn
            wout_r, wup_r, wdn_r = W.wout_r, W.wup_r, W.wdn_r
        g2 = _sb(es, nc, "O_g2", [128, 8], F32)
        gf = _sb(es, nc, "O_gf", [128, 8], F32)
        gfb = _sb(es, nc, "O_gfb", [128, D], F32)
        mixt = _sb(es, nc, "O_mixt", [128, 8, TB], BF16)
        r_mixt = Res()
        xt = _sb(es, nc, "O_xt", [128, NS, D], F32)
        x1 = _sb(es, nc, "O_x1", [128, NS, D], F32)
        r_x1 = Res()
        hn = _sb(es, nc, "O_hn", [128, NS, D], BF16)
        hT = _sb(es, nc, "O_hT", [128, 8, TB], BF16)
        rr = [_sb(es, nc, "O_rr%d" % i, [128, TB], F32) for i in range(2)]
        rr_r = [Res(), Res()]
        u = _sb(es, nc, "O_u", [128, 32, TB], BF16)
        u_r = [Res() for _ in range(32)]
        R = {"junk": Res(), "ss": Res(), "tmp": Res(), "rstd": Res(), "hn": Res(), "g": Res(),
             "hT": [Res() for _ in range(8)], "xt": Res()}
        R["junk_t"] = _sb(es, nc, "O_junk", [128, D], BF16)
        R["ss_t"] = _sb(es, nc, "O_ss", [128, 4], F32)
        R["ms_t"] = _sb(es, nc, "O_ms", [128, 4], F32)
        R["sd_t"] = _sb(es, nc, "O_sd", [128, 4], F32)
        R["rstd_t"] = _sb(es, nc, "O_rstd", [128, 4], F32)
        r_xt = Res()
        r_gf = Res()
        tps = [(_ps(es, nc, "O_tp%d" % i, [128, 1024], BF16), PRes()) for i in range(2)]
        mma = [(_ps(es, nc, "O_mma%d" % i, [128, 512], F32), PRes()) for i in range(3)]
        mmb = [(_ps(es, nc, "O_mmb%d" % i, [128, 512], F32), PRes()) for i in range(3)]

        def load_block(blk):
            t0 = blk * TB
            sch.dma("sp", xt[:, :, :], X[t0:t0 + TB, :].rearrange("(s p) d -> p s d", p=128),
                    reads=[rX], writes=[r_xt])
            sch.dma("sp", mixt[:, :, :], d["mixT"][:, t0:t0 + TB].rearrange("(c p) t -> p c t", p=128),
                    reads=[C.r_mixT], writes=[r_mixt])

        load_block(0)
        sch.dma("sp", g2[:, :], d["norm2_g"][l], writes=[R["g"]])
        if final:
            sch.dma("sp", gfb[:, :], d["final_gb"], writes=[r_gf])
        if W is None:
            i = load_w_bf16(C, wout, wout_r, d["w_out"][l], 8, D, stage, stage_r)
            i = load_w_bf16(C, wup, wup_r, d["w_ff_up"][l], 8, DFF, stage, stage_r, i)
            load_w_bf16(C, wdn, wdn_r, d["w_ff_down"][l], 32, D, stage, stage_r, i)

        ia = 0
        ib = 0
        for blk in range(S // TB):
            t0 = blk * TB
            for s in range(NS):
                for hf in range(2):
                    ps, rps = mma[ia % 3]
                    ia += 1
                    for kc in range(8):
                        sch.op("pe", lambda e, kc=kc, s=s, hf=hf, ps=ps: e.matmul(
                            out=ps[:, :], lhsT=mixt[:, kc, s * 128:(s + 1) * 128],
                            rhs=wout[:, kc, hf * 512:(hf + 1) * 512], start=(kc == 0), stop=(kc == 7)),
                            reads=[r_mixt, wout_r[kc]], writes=[rps])
                    sch.op("dve", lambda e, s=s, hf=hf, ps=ps: e.tensor_tensor(
                        out=x1[:, s, hf * 512:(hf + 1) * 512], in0=ps[:, :], in1=xt[:, s, hf * 512:(hf + 1) * 512],
                        op=ALU.add), reads=[rps, r_xt], wacc=[r_x1])
            if blk + 1 < S // TB:
                load_block(blk + 1)
            R["xt"] = r_x1
            norm_transpose(C, x1, NS, g2, hn, hT, tps, R)
            for j in range(32):
                ps, rps = mmb[ib % 3]
                ib += 1
                for kc in range(8):
                    sch.op("pe", lambda e, kc=kc, j=j, ps=ps: e.matmul(
                        out=ps[:, 0:TB], lhsT=wup[:, kc, j * 128:(j + 1) * 128], rhs=hT[:, kc, :],
                        start=(kc == 0), stop=(kc == 7)),
                        reads=[wup_r[kc], R["hT"][kc]], writes=[rps])
                rt, rtr = rr[j % 2], rr_r[j % 2]
                sch.op("act", lambda e, ps=ps, rt=rt: e.activation(out=rt[:, :], in_=ps[:, 0:TB], func=AF.Relu),
                       reads=[rps], writes=[rtr])
                eng = "pool" if j % 2 == 0 else "dve"
                sch.op(eng, lambda e, j=j, rt=rt: e.tensor_tensor(out=u[:, j, :], in0=rt[:, :], in1=rt[:, :],
                                                                  op=ALU.mult),
                       reads=[rtr], writes=[u_r[j]])
            for s in range(NS):
                for hf in range(2):
                    ps, rps = mma[ia % 3]
                    ia += 1
                    for j in range(32):
                        sch.op("pe", lambda e, j=j, s=s, hf=hf, ps=ps: e.matmul(
                            out=ps[:, :], lhsT=u[:, j, s * 128:(s + 1) * 128],
                            rhs=wdn[:, j, hf * 512:(hf + 1) * 512], start=(j == 0), stop=(j == 31)),
                            reads=[u_r[j], wdn_r[j]], writes=[rps])
                    sch.op("dve", lambda e, s=s, hf=hf, ps=ps: e.tensor_tensor(
                        out=x1[:, s, hf * 512:(hf + 1) * 512], in0=ps[:, :], in1=x1[:, s, hf * 512:(hf + 1) * 512],
                        op=ALU.add), reads=[rps, r_x1], wacc=[r_x1])
            if final:
                for s in range(NS):
                    sch.op("act", lambda e, s=s: e.activation(out=R["junk_t"][:, :], in_=x1[:, s, :], func=AF.Square,
                                                              accum_out=R["ss_t"][:, s:s + 1]),
                           reads=[r_x1], writes=[R["junk"], R["ss"]])
                rms_rstd(C, R["ss_t"][:, 0:NS], R["ms_t"][:, 0:NS], R["sd_t"][:, 0:NS], R["rstd_t"][:, 0:NS],
                         R["ss"], R["tmp"], R["rstd"], D, EPS)
                for s in range(NS):
                    sch.op("dve", lambda e, s=s: e.scalar_tensor_tensor(
                        out=x1[:, s, :], in0=x1[:, s, :], scalar=R["rstd_t"][:, s:s + 1], in1=gfb[:, :],
                        op0=ALU.mult, op1=ALU.mult), reads=[r_x1, R["rstd"], r_gf], wacc=[r_x1])
            sch.dma("sp", XOUT[t0:t0 + TB, :].rearrange("(s p) d -> p s d", p=128), x1[:, :, :],
                    reads=[r_x1], wacc=[rXOUT])


def phase_M(C, l):
    nc, sch, d = C.nc, C.sch, C.d
    L = 128
    NC_ = S // L
    with ExitStack() as es:
        gi = _sb(es, nc, "M_gi", [4, S], F32)
        gf = _sb(es, nc, "M_gf", [4, S], F32)
        Fn = _sb(es, nc, "M_Fn", [4, S], F32)
        r_gi, r_gf, r_Fn = Res(), Res(), Res()
        bi = _sb(es, nc, "M_bi", [4, 1], F32)
        bfn = _sb(es, nc, "M_bfn", [4, 1], F32)
        r_b = Res()
        Mc = _sb(es, nc, "M_Mc", [4, NC_], F32)
        Md = _sb(es, nc, "M_Md", [4, NC_], F32)
        ec = _sb(es, nc, "M_ec", [4, NC_], F32)
        r_Mc, r_ec = Res(), Res()
        gtok = _sb(es, nc, "M_gtok", [128, NC_, 8, 1], F32)
        r_gtok = Res()
        eb = _sb(es, nc, "M_eb", [128, 2, NC_], F32)
        r_eb = Res()
        cw = _sb(es, nc, "M_cw", [128, 4, 4], F32)
        cb = _sb(es, nc, "M_cb", [128, 4], F32)
        mg = _sb(es, nc, "M_mg", [128, 256], F32)
        r_par = Res()
        xins = [_sb(es, nc, "M_xin%d" % i, [128, 3 + S], F32) for i in range(2)]
        accs = [_sb(es, nc, "M_acc%d" % i, [128, S], F32) for i in range(2)]
        r_xins, r_accs = [Res(), Res()], [Res(), Res()]
        qz = [_sb(es, nc, "M_qz%d" % i, [128, S], BF16) for i in range(4)]
        kpair = [_sb(es, nc, "M_kp%d" % i, [128, S], BF16) for i in range(2)]
        r_qz = [Res() for _ in range(4)]
        r_kp = [Res() for _ in range(2)]
        ymT = _sb(es, nc, "M_ymT", [128, 2, S], BF16)
        r_ymT = Res()
        vo = [_sb(es, nc, "M_vo%d" % i, [128, 512], F32) for i in range(2)]
        r_vo = [Res(), Res()]
        va = _sb(es, nc, "M_va", [128, 4, 65], BF16)
        r_va = Res()
        sig = _sb(es, nc, "M_sig", [128, 256], F32)
        t2s = [_sb(es, nc, "M_t2_%d" % i, [128, 256], F32) for i in range(2)]
        r_t2s = [Res(), Res()]
        r_sig = Res()
        ktok = _sb(es, nc, "M_ktok", [128, 256], BF16)
        r_ktok = Res()
        at = _sb(es, nc, "M_at", [128, 4, 128], BF16)
        r_at = Res()
        Cs = _sb(es, nc, "M_Cs", [128, 2, 65], F32)
        Cb = _sb(es, nc, "M_Cb", [128, 2, 65], BF16)
        r_Cs, r_Cb = Res(), Res()
        sm = _sb(es, nc, "M_sm", [128, 8, 4, 1], F32)
        r_sm = Res()
        hh = _sb(es, nc, "M_hh", [128, 4, 64], F32)
        sq = _sb(es, nc, "M_sq", [128, 4, 64], F32)
        y1 = _sb(es, nc, "M_y1", [128, 4, 64], F32)
        yms = [_sb(es, nc, "M_ym%d" % i, [128, 256], BF16) for i in range(2)]
        r_yms = [Res(), Res()]
        r_hh, r_sq, r_y1 = Res(), Res(), Res()
        p_g = _ps(es, nc, "M_pg", [128, 512], F32)
        p_kt = _ps(es, nc, "M_pkt", [128, 1024], BF16)
        p_kv = _ps(es, nc, "M_pkv", [128, 512], F32)
        p_at = _ps(es, nc, "M_pat", [128, 4, 128], F32)
        p_nums = [_ps(es, nc, "M_pnum%d" % i, [128, 512], F32) for i in range(2)]
        p_yt = _ps(es, nc, "M_pyt", [128, 1024], BF16)
        rp_g, rp_kt, rp_kv, rp_at, rp_yt = PRes(), PRes(), PRes(), PRes(), PRes()
        rp_nums = [PRes(), PRes()]

        sch.dma("sp", bi[:, :], d["m_b_i"][l], wacc=[r_b])
        sch.dma("sp", bfn[:, :], d["m_b_f"][l], wacc=[r_b])
        sch.dma("sp", cw[:, :, :], d["m_conv_w"][l], wacc=[r_par])
        sch.dma("sp", cb[:, :], d["m_conv_b"][l], wacc=[r_par])
        sch.dma("sp", mg[:, :], d["m_norm_g"][l], wacc=[r_par])
        sch.dma("sp", gi[:, :], d["mgT"][0:4, :], reads=[C.r_mgT], writes=[r_gi])
        sch.dma("sp", gf[:, :], d["mgT"][4:8, :], reads=[C.r_mgT], writes=[r_gf])
        sch.op("dve", lambda e: e.tensor_scalar(out=bfn[:, :], in0=bfn[:, :], scalar1=-1.0, scalar2=None,
                                                op0=ALU.mult), reads=[r_b], writes=[r_b])
        for i in range(2):
            sch.op("pool", lambda e, i=i: e.memset(xins[i][:, 0:3], 0.0), writes=[r_xins[i]])
        def conv(ch):
            xin, acc, r_xin, r_acc = xins[ch % 2], accs[ch % 2], r_xins[ch % 2], r_accs[ch % 2]
            sch.dma("sp", xin[:, 3:3 + S], d["mqkT"][ch * 128:(ch + 1) * 128, :], reads=[C.r_mqkT], writes=[r_xin])
            sch.op("act", lambda e, ch=ch, xin=xin, acc=acc: e.activation(
                out=acc[:, :], in_=xin[:, 3:3 + S], func=AF.Identity, scale=cw[:, ch, 3:4], bias=cb[:, ch:ch + 1]),
                reads=[r_xin, r_par], writes=[r_acc])
            for j in range(3):
                sch.op("dve", lambda e, ch=ch, j=j: e.scalar_tensor_tensor(
                    out=acc[:, :], in0=xin[:, j:j + S], scalar=cw[:, ch, j:j + 1], in1=acc[:, :],
                    op0=ALU.mult, op1=ALU.add), reads=[r_xin, r_acc, r_par], writes=[r_acc])
            if ch < 2:
                for hf in range(2):
                    hq = 2 * ch + hf
                    ps_ = slice(hf * 64, hf * 64 + 64)
                    zs_ = slice((1 - hf) * 64, (1 - hf) * 64 + 64)
                    sch.op("pool", lambda e, hq=hq, zs_=zs_: e.memset(qz[hq][zs_, :], 0.0), wacc=[r_qz[hq]])
                    sch.op("act", lambda e, hq=hq, ps_=ps_: e.activation(out=qz[hq][ps_, :], in_=acc[ps_, :],
                                                                         func=AF.Silu),
                           reads=[r_acc], wacc=[r_qz[hq]])
            else:
                sch.op("act", lambda e, ch=ch: e.activation(out=kpair[ch - 2][:, :], in_=acc[:, :], func=AF.Silu),
                       reads=[r_acc], writes=[r_kp[ch - 2]])
        sch.op("dve", lambda e: e.tensor_scalar(out=gi[:, :], in0=gi[:, :], scalar1=bi[:, 0:1], scalar2=None,
                                                op0=ALU.add), reads=[r_gi, r_b], writes=[r_gi])
        sch.op("act", lambda e: e.activation(out=gf[:, :], in_=gf[:, :], func=AF.Exp, bias=bfn[:, 0:1], scale=-1.0),
               reads=[r_gf, r_b], writes=[r_gf])
        sch.op("act", lambda e: e.activation(out=gf[:, :], in_=gf[:, :], func=AF.Ln, bias=1.0, scale=1.0),
               reads=[r_gf], writes=[r_gf])
        conv(0)
        sch.op("dve", lambda e: e.tensor_tensor_scan(out=Fn[:, :], data0=gf[:, :], data1=gf[:, :], initial=0.0,
                                                     op0=ALU.add, op1=ALU.max), reads=[r_gf], writes=[r_Fn])
        sch.op("dve", lambda e: e.tensor_tensor(out=gi[:, :], in0=gi[:, :], in1=Fn[:, :], op=ALU.add),
               reads=[r_gi, r_Fn], writes=[r_gi])
        sch.op("dve", lambda e: e.tensor_tensor_scan(out=gf[:, :], data0=gi[:, :], data1=gi[:, :], initial=0.0,
                                                     op0=ALU.max, op1=ALU.max), reads=[r_gi], writes=[r_gf])
        conv(1)
        U3 = gf[:, :].rearrange("p (c t) -> p c t", t=L)
        sch.op("dve", lambda e: e.tensor_copy(out=Mc[:, :].unsqueeze(2), in_=U3[:, :, L - 1:L]),
               reads=[r_gf], writes=[r_Mc])
        Mbc = Mc[:, :].unsqueeze(2).broadcast_to([4, NC_, L])
        u3 = gi[:, :].rearrange("p (c t) -> p c t", t=L)
        sch.op("dve", lambda e: e.tensor_tensor(out=u3, in0=u3, in1=Mbc, op=ALU.subtract),
               reads=[r_gi, r_Mc], writes=[r_gi])
        sch.op("act", lambda e: e.activation(out=gi[:, :], in_=gi[:, :], func=AF.Exp, bias=math.log(0.125), scale=1.0),
               reads=[r_gi], writes=[r_gi])
        conv(2)
        F3 = Fn[:, :].rearrange("p (c t) -> p c t", t=L)
        sch.op("dve", lambda e: e.tensor_tensor(out=F3, in0=F3, in1=Mbc, op=ALU.subtract),
               reads=[r_Fn, r_Mc], writes=[r_Fn])
        sch.op("act", lambda e: e.activation(out=Fn[:, :], in_=Fn[:, :], func=AF.Exp), reads=[r_Fn], writes=[r_Fn])
        conv(3)
        sch.op("dve", lambda e: e.tensor_tensor(out=Md[:, 1:NC_], in0=Mc[:, 0:NC_ - 1], in1=Mc[:, 1:NC_],
                                                op=ALU.subtract), reads=[r_Mc], wacc=[r_ec])
        sch.op("dve", lambda e: e.tensor_scalar(out=Md[:, 0:1], in0=Mc[:, 0:1], scalar1=-1.0, scalar2=None,
                                                op0=ALU.mult), reads=[r_Mc], wacc=[r_ec])
        sch.op("act", lambda e: e.activation(out=ec[:, :], in_=Md[:, :], func=AF.Exp), reads=[r_ec], writes=[r_ec])
        pg3 = p_g[:, 0:NC_ * 8].rearrange("p (c g) -> p c g", g=8)
        for c in range(NC_):
            sch.op("pe", lambda e, c=c: e.transpose(out=pg3[:, c, 0:4], in_=gi[:, c * L:(c + 1) * L],
                                                    identity=C.identf[0:4, 0:4]), reads=[r_gi], writes=[rp_g])
            sch.op("pe", lambda e, c=c: e.transpose(out=pg3[:, c, 4:8], in_=Fn[:, c * L:(c + 1) * L],
                                                    identity=C.identf[0:4, 0:4]), reads=[r_Fn], writes=[rp_g])
        sch.op("dve", lambda e: e.tensor_copy(out=gtok[:, :, :, 0], in_=pg3), reads=[rp_g], writes=[r_gtok])
        for pr in range(2):
            sch.op("pe", lambda e, pr=pr: e.matmul(out=p_g[:, 256 + pr * NC_:256 + (pr + 1) * NC_],
                                                   lhsT=C.sel4[:, pr, :], rhs=ec[:, :], start=True, stop=True),
                   reads=[r_ec, r_gtok], writes=[rp_g])
        sch.op("dve", lambda e: e.tensor_copy(out=eb[:, :, :],
                                              in_=p_g[:, 256:256 + 2 * NC_].rearrange("p (a c) -> p a c", a=2)),
               reads=[rp_g], writes=[r_eb])
        sch.op("pool", lambda e: e.memset(Cs[:, :, :], 0.0), writes=[r_Cs])
        num3s = [pn[:, 0:260].rearrange("p (h e) -> p h e", e=65) for pn in p_nums]
        kv3 = p_kv[:, 0:260].rearrange("p (a e) -> p a e", a=2)
        def epilogue(c):
            ym, r_ym = yms[c % 2], r_yms[c % 2]
            t2, r_t2 = t2s[c % 2], r_t2s[c % 2]
            num3, rp_num = num3s[c % 2], rp_nums[c % 2]
            sch.op("act", lambda e: e.activation(out=sm[:, 0, :, :], in_=num3[:, :, 64:65], func=AF.Abs),
                   reads=[rp_num], writes=[r_sm])
            sch.op("dve", lambda e, c=c: e.tensor_tensor(out=sm[:, 1, :, :], in0=sm[:, 0, :, :],
                                                         in1=gtok[:, c, 4:8, :], op=ALU.max),
                   reads=[r_sm, r_gtok], writes=[r_sm])
            sch.op("dve", lambda e: e.reciprocal(out=sm[:, 2, :, :], in_=sm[:, 1, :, :]), reads=[r_sm], writes=[r_sm])
            sch.op("dve", lambda e: e.tensor_tensor(out=hh[:, :, :], in0=num3[:, :, 0:64],
                                                    in1=sm[:, 2, :, :].broadcast_to([128, 4, 64]), op=ALU.mult),
                   reads=[rp_num, r_sm], writes=[r_hh])
            sch.op("dve", lambda e: e.tensor_tensor(out=sq[:, :, :], in0=hh[:, :, :], in1=hh[:, :, :], op=ALU.mult),
                   reads=[r_hh], writes=[r_sq])
            sch.op("dve", lambda e: e.tensor_reduce(out=sm[:, 3, :, :], in_=sq[:, :, :], axis=AX.X, op=ALU.add),
                   reads=[r_sq], writes=[r_sm])
            rms_rstd(C, sm[:, 3, :, 0], sm[:, 4, :, 0], sm[:, 5, :, 0], sm[:, 6, :, 0], r_sm, r_sm, r_sm, 64, 1e-6)
            sch.op("dve", lambda e: e.tensor_tensor(out=y1[:, :, :], in0=hh[:, :, :],
                                                    in1=sm[:, 6, :, :].broadcast_to([128, 4, 64]), op=ALU.mult),
                   reads=[r_hh, r_sm], writes=[r_y1])
            sch.op("dve", lambda e: e.tensor_tensor(out=ym[:, :], in0=y1[:, :, :].rearrange("p h e -> p (h e)"),
                                                    in1=t2[:, :], op=ALU.mult), reads=[r_y1, r_t2], writes=[r_ym])

        def y_tail(c):
            ym, r_ym = yms[c % 2], r_yms[c % 2]
            cs = slice(c * L, (c + 1) * L)
            for j in range(2):
                sch.op("pe", lambda e, j=j: e.transpose(out=p_yt[:, j * 128:(j + 1) * 128],
                                                        in_=ym[:, j * 128:(j + 1) * 128], identity=C.identb[:, :]),
                       reads=[r_ym], writes=[rp_yt])
            sch.op("act", lambda e: e.copy(out=ymT[:, :, cs], in_=p_yt[:, 0:256].rearrange("p (j t) -> p j t", j=2)),
                   reads=[rp_yt], wacc=[r_ymT])

        for c in range(NC_):
            cs = slice(c * L, (c + 1) * L)
            vt, rv = vo[c % 2], r_vo[c % 2]
            t2, r_t2 = t2s[c % 2], r_t2s[c % 2]
            num3, rp_num = num3s[c % 2], rp_nums[c % 2]
            sch.dma("sp", vt[:, :], d["mvo"][c * L:(c + 1) * L, :], reads=[C.r_mvo], writes=[rv])
            for h in range(4):
                eng = "dve"
                sch.op(eng, lambda e, h=h, vt=vt, c=c: e.tensor_scalar(
                    out=va[:, h, 0:64], in0=vt[:, h * 64:(h + 1) * 64], scalar1=gtok[:, c, h, :], scalar2=0.0,
                    op0=ALU.mult, op1=ALU.add), reads=[rv, r_gtok], wacc=[r_va])
            sch.op("pool", lambda e, c=c: e.tensor_copy(out=va[:, :, 64:65], in_=gtok[:, c, 0:4, :]),
                   reads=[r_gtok], wacc=[r_va])
            sch.op("act", lambda e, vt=vt: e.activation(out=sig[:, :], in_=vt[:, 256:512], func=AF.Exp, scale=-1.0),
                   reads=[rv], writes=[r_sig])
            sch.op("act", lambda e: e.activation(out=sig[:, :], in_=sig[:, :], func=AF.Ln, bias=1.0, scale=1.0),
                   reads=[r_sig], writes=[r_sig])
            sch.op("act", lambda e: e.activation(out=sig[:, :], in_=sig[:, :], func=AF.Exp, scale=-1.0),
                   reads=[r_sig], writes=[r_sig])
            sch.op("pool", lambda e: e.tensor_tensor(out=t2[:, :], in0=sig[:, :], in1=mg[:, :], op=ALU.mult),
                   reads=[r_sig, r_par], writes=[r_t2])
            for pr in range(2):
                sch.op("pe", lambda e, pr=pr, cs=cs: e.transpose(out=p_kt[:, pr * 128:(pr + 1) * 128],
                                                                 in_=kpair[pr][:, cs], identity=C.identb[:, :]),
                       reads=[r_kp[pr]], writes=[rp_kt])
            sch.op("act", lambda e: e.copy(out=ktok[:, :], in_=p_kt[:, 0:256]), reads=[rp_kt], writes=[r_ktok])
            for pr in range(2):
                sch.op("pe", lambda e, pr=pr: e.matmul(out=kv3[:, pr, :], lhsT=ktok[:, pr * 128:(pr + 1) * 128],
                                                       rhs=va[:, 2 * pr:2 * pr + 2, :], start=True, stop=True),
                       reads=[r_ktok, r_va], writes=[rp_kv])
            for h in range(4):
                pr, pb = h // 2, (h % 2) * 64
                sch.op("pe", lambda e, h=h, pr=pr, pb=pb, cs=cs: e.matmul(
                    out=p_at[:, h, :], lhsT=kpair[pr][:, cs], rhs=qz[h][:, cs],
                    start=True, stop=True), reads=[r_kp[pr], r_qz[h]], writes=[rp_at])
            sch.op("dve", lambda e: e.tensor_tensor(out=at[:, :, :], in0=p_at[:, :, :],
                                                    in1=C.trif[:, :].unsqueeze(1).broadcast_to([128, 4, 128]),
                                                    op=ALU.mult), reads=[rp_at], writes=[r_at])
            for pr in range(2):
                sch.op("dve", lambda e, pr=pr, c=c: e.tensor_scalar(
                    out=Cb[:, pr, :], in0=Cs[:, pr, :], scalar1=eb[:, pr, c:c + 1], scalar2=None, op0=ALU.mult),
                    reads=[r_Cs, r_eb], wacc=[r_Cb])
            for h in range(4):
                pr, pb = h // 2, (h % 2) * 64
                sch.op("pe", lambda e, h=h: e.matmul(out=num3[:, h, :], lhsT=at[:, h, :], rhs=va[:, h, :],
                                                     start=True, stop=False), reads=[r_at, r_va], writes=[rp_num])
                sch.op("pe", lambda e, h=h, pr=pr, pb=pb, cs=cs: e.matmul(
                    out=num3[:, h, :], lhsT=qz[h][:, cs], rhs=Cb[:, pr, :],
                    start=False, stop=True), reads=[r_qz[h], r_Cb], writes=[rp_num])
            for pr in range(2):
                for hf in range(2):
                    pb = hf * 64
                    sch.op("dve", lambda e, pr=pr, hf=hf, pb=pb, c=c: e.scalar_tensor_tensor(
                        out=Cs[pb:pb + 64, pr, :], in0=Cs[pb:pb + 64, pr, :], scalar=eb[pb:pb + 64, pr, c:c + 1],
                        in1=kv3[pb:pb + 64, pr, hf * 65:(hf + 1) * 65], op0=ALU.mult, op1=ALU.add),
                        reads=[r_Cs, r_eb, rp_kv, r_Cb], wacc=[r_Cs])
            if c >= 1:
                epilogue(c - 1)
            if c >= 2:
                y_tail(c - 2)
        epilogue(NC_ - 1)
        y_tail(NC_ - 2)
        y_tail(NC_ - 1)
        for j in range(2):
            sch.dma("sp", d["mixT"][j * 128:(j + 1) * 128, :], ymT[:, j, :], reads=[r_ymT], wacc=[C.r_mixT])


def phase_R(C, l):
    nc, sch, d = C.nc, C.sch, C.d
    T = 1024
    L = 64
    NCH = T // L
    NEG = -0.6065306597126334
    with ExitStack() as es:
        def sb(name, shape, dt):
            return _sb(es, nc, "R_" + name, shape, dt)
        mu = sb("mu", [128, 7], F32)
        omu = sb("omu", [128, 7], F32)
        pv = sb("pv", [128, 6, 2], F32)
        lup_f = sb("lupf", [64, 3, 256], F32)
        lup = sb("lup", [64, 3, 256], BF16)
        lin_a = sb("lin_a", [32, T], BF16)
        lin_g = sb("lin_g", [64, T], BF16)
        r_lina, r_ling = Res(), Res()
        lng = sb("lng", [64, 256], F32)
        lnb = sb("lnb", [64, 256], F32)
        r_par = Res()
        praw = sb("praw", [128, 1 + T], F32)
        r_praw = Res()
        pm_r = sb("pm_r", [128, 2, T], F32)
        pm_k = sb("pm_k", [128, 2, T], F32)
        pm_v = sb("pm_v", [128, 2, T], F32)
        pm_l = sb("pm_l", [128, T], F32)
        r_pm = {k: Res() for k in ("r0", "r1", "k0", "k1", "v0", "v1", "l")}
        lin = sb("lin", [128, T], BF16)
        r_lin = Res()
        tmp = sb("tmp", [128, T], F32)
        r_tmp = Res()
        lw = sb("lw", [128, T], F32)
        aa = sb("aa", [128, T], F32)
        kx = sb("kx", [128, T], F32)
        sq = sb("sq", [128, T], F32)
        kp = sb("kp", [128, T], F32)
        bb = sb("bb", [128, T], F32)
        Gp = [sb("Gp%d" % i, [128, 1 + T], F32) for i in range(2)]
        gi = sb("gi", [128, T], F32)
        ge = sb("ge", [128, T], F32)
        E1 = [sb("E1_%d" % i, [128, T], F32) for i in range(2)]
        E2 = sb("E2", [128, T], F32)
        r_lw, r_aa, r_kx, r_sq, r_kp, r_bb, r_gi, r_ge, r_E2 = (Res() for _ in range(9))
        r_Gp = [Res(), Res()]
        r_E1 = [Res(), Res()]
        ARz = [sb("ARz%d" % i, [128, 2, T], BF16) for i in range(4)]
        Bt = sb("Bt", [128, 2, T], BF16)
        Kt = sb("Kt", [128, 2, T], BF16)
        Bh = sb("Bh", [128, 2, T], F32)
        Kh = sb("Kh", [128, 2, T], F32)
        prodb = sb("prodb", [128, 2, T], BF16)
        r_AR, r_Bt, r_Kt, r_Bh, r_Kh, r_prodb = ([Res(), Res()] for _ in range(6))
        Hs = sb("Hs", [128, 2, 64], F32)
        Hb = sb("Hb", [128, 2, 64], BF16)
        r_Hs, r_Hb = Res(), Res()
        yrT = sb("yrT", [128, 2, S], BF16)
        r_yrT = Res()
        tok = sb("tok", [64, 2, 4, 2, 64], F32)
        r_tok = [Res(), Res()]
        scs = sb("scs", [64, 4, 4, 64], F32)
        r_scs = [Res(), Res()]
        nns = sb("nns", [64, 4, 64], F32)
        r_nns = Res()
        pws = [sb("pws%d" % i, [64, 4, 2, 64], F32) for i in range(5)]
        r_pws = [Res() for _ in range(5)]
        U = [sb("U%d" % i, [64, 4, 64], F32) for i in range(2)]
        r_U = [Res(), Res()]
        yv = sb("yv", [64, 4, 64], F32)
        ysq = sb("ysq", [64, 4, 64], F32)
        yo = sb("yo", [64, 256], F32)
        st = sb("st", [64, 8, 4, 1], F32)
        r_yv, r_ysq, r_yo, r_st = Res(), Res(), Res(), Res()
        banks = [_ps(es, nc, "R_b%d" % i, [128, 512], F32) for i in range(8)]
        rb = [PRes() for _ in range(8)]

        sch.dma("sp", mu[:, :], d["r_mu"][l], wacc=[r_par])
        sch.dma("sp", pv[:, 0, :], d["r_w0"][l], wacc=[r_par])
        sch.dma("sp", pv[:, 1, :], d["r_a0"][l], wacc=[r_par])
        sch.dma("sp", pv[:, 2, :], d["r_k_k"][l], wacc=[r_par])
        sch.dma("sp", pv[:, 3, :], d["r_k_a"][l], wacc=[r_par])
        sch.dma("sp", pv[:, 4, :], d["r_r_k"][l], wacc=[r_par])
        r_lup = Res()
        sch.op("pool", lambda e: e.memset(lup_f[:, :, :], 0.0), writes=[r_lup])
        sch.dma("sp", lup_f[0:32, 0, :], d["r_w_up"][l], writes=[r_lup])
        sch.dma("sp", lup_f[0:32, 1, :], d["r_a_up"][l], writes=[r_lup])
        sch.dma("sp", lup_f[0:64, 2, :], d["r_g_up"][l], writes=[r_lup])
        sch.dma("sp", lng[:, :], d["r_ln_g"][l], wacc=[r_par])
        sch.dma("sp", lnb[:, :], d["r_ln_b"][l], wacc=[r_par])
        sch.op("dve", lambda e: e.tensor_scalar(out=omu[:, :], in0=mu[:, :], scalar1=-1.0, scalar2=1.0,
                                                op0=ALU.mult, op1=ALU.add), reads=[r_par], writes=[r_par])
        sch.op("dve", lambda e: e.tensor_scalar(out=pv[:, 5, :], in0=pv[:, 3, :], scalar1=-1.0, scalar2=1.0,
                                                op0=ALU.mult, op1=ALU.add), reads=[r_par], writes=[r_par])
        sch.op("dve", lambda e: e.tensor_copy(out=lup[:, :, :], in_=lup_f[:, :, :]), reads=[r_par, r_lup], writes=[r_par])
        sch.op("pool", lambda e: e.memset(Hs[:, :, :], 0.0), writes=[r_Hs])
        sch.op("pool", lambda e: e.memset(Hb[:, :, :], 0.0), writes=[r_Hb])
        for pr in range(2):
            sch.op("pool", lambda e, pr=pr: e.memset(Gp[pr][:, :], 0.0), writes=[r_Gp[pr]])
        for h in range(4):
            zs_ = slice((1 - h % 2) * 64, (1 - h % 2) * 64 + 64)
            sch.op("pool", lambda e, h=h, zs_=zs_: e.memset(ARz[h][zs_, :, :], 0.0), wacc=[r_AR[h // 2]])

        def v3(ap2):
            return ap2.rearrange("p (c t) -> p c t", t=L)

        dests = [(pm_r, 0, "r0"), (pm_r, 1, "r1"), (pm_k, 0, "k0"), (pm_k, 1, "k1"),
                 (pm_v, 0, "v0"), (pm_v, 1, "v1"), (None, 0, "l")]
        for blk in range(S // T):
            t0 = blk * T
            for rc in range(7):
                if blk == 0:
                    sch.op("pool", lambda e: e.memset(praw[:, 0:1], 0.0), writes=[r_praw])
                    sch.dma("sp", praw[:, 1:1 + T], d["rT"][rc * 128:(rc + 1) * 128, 0:T],
                            reads=[C.r_rT], wacc=[r_praw])
                else:
                    sch.dma("sp", praw[:, :], d["rT"][rc * 128:(rc + 1) * 128, t0 - 1:t0 + T],
                            reads=[C.r_rT], writes=[r_praw])
                dt_, pi, key = dests[rc]
                o = pm_l[:, :] if dt_ is None else dt_[:, pi, :]
                sch.op("pool", lambda e, rc=rc: e.tensor_scalar(out=tmp[:, :], in0=praw[:, 1:1 + T],
                                                                scalar1=omu[:, rc:rc + 1], scalar2=0.0, op0=ALU.mult, op1=ALU.add),
                       reads=[r_praw, r_par], writes=[r_tmp])
                sch.op("dve", lambda e, rc=rc, o=o: e.scalar_tensor_tensor(
                    out=o, in0=praw[:, 0:T], scalar=mu[:, rc:rc + 1], in1=tmp[:, :], op0=ALU.mult, op1=ALU.add),
                    reads=[r_praw, r_tmp, r_par], writes=[r_pm[key]])
            sch.op("act", lambda e: e.activation(out=lin[0:32, :], in_=pm_l[0:32, :], func=AF.Tanh),
                   reads=[r_pm["l"]], wacc=[r_lin])
            sch.op("act", lambda e: e.copy(out=lin[32:64, :], in_=pm_l[32:64, :]), reads=[r_pm["l"]], wacc=[r_lin])
            sch.op("act", lambda e: e.activation(out=lin[64:128, :], in_=pm_l[64:128, :], func=AF.Sigmoid),
                   reads=[r_pm["l"]], wacc=[r_lin])
            sch.dma("sp", lin_a[:, :], lin[32:64, :], reads=[r_lin], writes=[r_lina])
            sch.dma("sp", lin_g[:, :], lin[64:128, :], reads=[r_lin], writes=[r_ling])
            pb7 = banks[7]
            for pr in range(2):
                rk, rr_, rvv = r_pm["k%d" % pr], r_pm["r%d" % pr], r_pm["v%d" % pr]
                pcs = slice(pr * 128, (pr + 1) * 128)
                for hf in range(T // 512):
                    hs = slice(hf * 512, (hf + 1) * 512)
                    sch.op("pe", lambda e, hs=hs, pcs=pcs: e.matmul(out=pb7[:, :], lhsT=lup[0:32, 0, pcs],
                                                                    rhs=lin[0:32, hs], start=True, stop=True),
                           reads=[r_lin, r_par], writes=[rb[7]])
                    sch.op("act", lambda e, hs=hs, pr=pr: e.activation(out=lw[:, hs], in_=pb7[:, :], func=AF.Sigmoid,
                                                                       bias=pv[:, 0, pr:pr + 1], scale=1.0),
                           reads=[rb[7], r_par], wacc=[r_lw])
                    sch.op("pe", lambda e, hs=hs, pcs=pcs: e.matmul(out=pb7[:, :], lhsT=lup[0:32, 1, pcs],
                                                                    rhs=lin_a[:, hs], start=True, stop=True),
                           reads=[r_lina, r_par], writes=[rb[7]])
                    sch.op("act", lambda e, hs=hs, pr=pr: e.activation(out=aa[:, hs], in_=pb7[:, :], func=AF.Sigmoid,
                                                                       bias=pv[:, 1, pr:pr + 1], scale=1.0),
                           reads=[rb[7], r_par], wacc=[r_aa])
                sch.op("pool", lambda e: e.tensor_scalar(out=lw[:, :], in0=lw[:, :], scalar1=NEG, scalar2=0.0,
                                                         op0=ALU.mult, op1=ALU.add), reads=[r_lw], writes=[r_lw])
                sch.op("dve", lambda e, pr=pr: e.tensor_scalar(out=kx[:, :], in0=pm_k[:, pr, :],
                                                               scalar1=pv[:, 2, pr:pr + 1], scalar2=None,
                                                               op0=ALU.mult), reads=[rk, r_par], writes=[r_kx])
                sch.op("pool", lambda e: e.tensor_tensor(out=sq[:, :], in0=kx[:, :], in1=kx[:, :], op=ALU.mult),
                       reads=[r_kx], writes=[r_sq])
                for hf in range(T // 512):
                    hs = slice(hf * 512, (hf + 1) * 512)
                    sch.op("pe", lambda e, hs=hs: e.matmul(out=pb7[:, :], lhsT=C.blkf[:, :], rhs=sq[:, hs],
                                                           start=True, stop=True), reads=[r_sq], writes=[rb[7]])
                    sch.op("act", lambda e, hs=hs: e.activation(out=tmp[:, hs], in_=pb7[:, :], func=AF.Sqrt),
                           reads=[rb[7]], wacc=[r_tmp])
                sch.op("dve", lambda e: e.tensor_scalar(out=tmp[:, :], in0=tmp[:, :], scalar1=1e-12, scalar2=None,
                                                        op0=ALU.max), reads=[r_tmp], writes=[r_tmp])
                sch.op("dve", lambda e: e.reciprocal(out=tmp[:, :], in_=tmp[:, :]), reads=[r_tmp], writes=[r_tmp])
                sch.op("dve", lambda e: e.tensor_tensor(out=kx[:, :], in0=kx[:, :], in1=tmp[:, :], op=ALU.mult),
                       reads=[r_kx, r_tmp], writes=[r_kx])
                sch.op("pool", lambda e, pr=pr: e.tensor_scalar(out=kp[:, :], in0=aa[:, :], scalar1=pv[:, 3, pr:pr + 1],
                                                                scalar2=pv[:, 5, pr:pr + 1], op0=ALU.mult, op1=ALU.add),
                       reads=[r_aa, r_par], writes=[r_kp])
                sch.op("pool", lambda e, pr=pr: e.tensor_tensor(out=kp[:, :], in0=kp[:, :], in1=pm_k[:, pr, :],
                                                                op=ALU.mult), reads=[r_kp, rk], writes=[r_kp])
                sch.op("pool", lambda e: e.tensor_tensor(out=bb[:, :], in0=kx[:, :], in1=aa[:, :], op=ALU.mult),
                       reads=[r_kx, r_aa], writes=[r_bb])
                sch.op("dve", lambda e, pr=pr: e.scalar_tensor_tensor(
                    out=prodb[:, pr, :], in0=pm_r[:, pr, :], scalar=pv[:, 4, pr:pr + 1], in1=kp[:, :],
                    op0=ALU.mult, op1=ALU.mult), reads=[rr_, r_kp, r_par], writes=[r_prodb[pr]])
                G = Gp[pr]
                sch.op("dve", lambda e, G=G: e.tensor_copy(out=G[:, 0:1], in_=G[:, T:T + 1]),
                       reads=[r_Gp[pr]], writes=[r_Gp[pr]])
                sch.op("dve", lambda e, G=G: e.tensor_tensor_scan(out=G[:, 1:1 + T], data0=lw[:, :], data1=lw[:, :],
                                                                  initial=G[:, 0:1], op0=ALU.add, op1=ALU.min),
                       reads=[r_lw, r_Gp[pr]], writes=[r_Gp[pr]])
                base = v3(G[:, 0:T])[:, :, 0:1].broadcast_to([128, NCH, L])
                sch.op("dve", lambda e, G=G, base=base: e.tensor_tensor(out=v3(gi[:, :]), in0=v3(G[:, 1:1 + T]),
                                                                        in1=base, op=ALU.subtract),
                       reads=[r_Gp[pr]], writes=[r_gi])
                sch.op("pool", lambda e: e.tensor_tensor(out=ge[:, :], in0=gi[:, :], in1=lw[:, :], op=ALU.subtract),
                       reads=[r_gi, r_lw], writes=[r_ge])
                e1 = E1[pr]
                sch.op("act", lambda e, e1=e1: e.activation(out=e1[:, :], in_=gi[:, :], func=AF.Exp),
                       reads=[r_gi], writes=[r_E1[pr]])
                sch.op("act", lambda e: e.activation(out=E2[:, :], in_=ge[:, :], func=AF.Exp),
                       reads=[r_ge], writes=[r_E2])
                for hf in range(2):
                    ps_ = slice(hf * 64, hf * 64 + 64)
                    hq = 2 * pr + hf
                    sch.op("dve", lambda e, hq=hq, ps_=ps_: e.scalar_tensor_tensor(
                        out=ARz[hq][ps_, 0, :], in0=kx[ps_, :], scalar=-1.0, in1=E2[ps_, :], op0=ALU.mult, op1=ALU.mult),
                        reads=[r_kx, r_E2], wacc=[r_AR[pr]])
                    sch.op("pool", lambda e, hq=hq, ps_=ps_, pr=pr, e1=e1: e.tensor_tensor(
                        out=ARz[hq][ps_, 1, :], in0=pm_r[ps_, pr, :], in1=e1[ps_, :], op=ALU.mult),
                        reads=[rr_, r_E1[pr]], wacc=[r_AR[pr]])
                sch.op("act", lambda e: e.activation(out=E2[:, :], in_=gi[:, :], func=AF.Exp, scale=-1.0),
                       reads=[r_gi, r_AR[pr]], writes=[r_E2])
                sch.op("dve", lambda e, pr=pr: e.tensor_tensor(out=Bt[:, pr, :], in0=bb[:, :], in1=E2[:, :],
                                                               op=ALU.mult), reads=[r_bb, r_E2], writes=[r_Bt[pr]])
                sch.op("pool", lambda e, pr=pr: e.tensor_tensor(out=Kt[:, pr, :], in0=kp[:, :], in1=E2[:, :],
                                                                op=ALU.mult), reads=[r_kp, r_E2], writes=[r_Kt[pr]])
                gend = v3(gi[:, :])[:, :, L - 1:L].broadcast_to([128, NCH, L])
                sch.op("dve", lambda e, gend=gend: e.tensor_tensor(out=v3(ge[:, :]), in0=gend, in1=v3(gi[:, :]),
                                                                   op=ALU.subtract), reads=[r_gi], writes=[r_ge])
                sch.op("act", lambda e: e.activation(out=ge[:, :], in_=ge[:, :], func=AF.Exp),
                       reads=[r_ge], writes=[r_ge])
                sch.op("dve", lambda e, pr=pr: e.tensor_tensor(out=Bh[:, pr, :], in0=bb[:, :], in1=ge[:, :],
                                                               op=ALU.mult), reads=[r_bb, r_ge], writes=[r_Bh[pr]])
                sch.op("pool", lambda e, pr=pr: e.tensor_tensor(out=Kh[:, pr, :], in0=kp[:, :], in1=ge[:, :],
                                                                op=ALU.mult), reads=[r_kp, r_ge], writes=[r_Kh[pr]])
            for cc in range(NCH):
                cs = slice(cc * L, (cc + 1) * L)
                gcs = slice(t0 + cc * L, t0 + (cc + 1) * L)
                for pr in range(2):
                    tpp = banks[0]
                    for j, (src, rs) in enumerate(((Bh, r_Bh[pr]), (Kh, r_Kh[pr]), (pm_v, r_pm["v%d" % pr]))):
                        sch.op("pe", lambda e, j=j, src=src, pr=pr, cs=cs: e.transpose(
                            out=tpp[0:64, j * 128:(j + 1) * 128], in_=src[:, pr, cs], identity=C.identf[:, :]),
                            reads=[rs], writes=[rb[0]])
                    sch.op("act", lambda e, pr=pr: e.copy(
                        out=tok[:, pr, 0:3, :, :],
                        in_=banks[0][0:64, 0:384].rearrange("p (j h e) -> p j h e", j=3, h=2)),
                        reads=[rb[0]], writes=[r_tok[pr]])
                for h in range(4):
                    pr, pb = h // 2, (h % 2) * 64
                    bk = banks[1 + pr]
                    o = (h % 2) * 256
                    rhs_ar = ARz[h][:, :, cs]
                    sch.op("pe", lambda e, bk=bk, o=o, pr=pr, rhs_ar=rhs_ar, cs=cs: e.matmul(
                        out=bk[0:64, o:o + 128], lhsT=Bt[:, pr, cs], rhs=rhs_ar, start=True, stop=True),
                        reads=[r_Bt[pr], r_AR[pr]], writes=[rb[1 + pr]])
                    sch.op("pe", lambda e, bk=bk, o=o, pr=pr, rhs_ar=rhs_ar, cs=cs: e.matmul(
                        out=bk[0:64, o + 128:o + 256], lhsT=Kt[:, pr, cs], rhs=rhs_ar, start=True, stop=True),
                        reads=[r_Kt[pr], r_AR[pr]], writes=[rb[1 + pr]])
                    sch.op("pe", lambda e, h=h, pr=pr, cs=cs: e.matmul(
                        out=banks[4][0:64, h * 64:(h + 1) * 64], lhsT=ARz[h][:, 0, cs],
                        rhs=Bt[:, pr, cs], start=True, stop=True),
                        reads=[r_Bt[pr], r_AR[pr]], writes=[rb[4]])
                for pr in range(2):
                    sch.op("dve", lambda e, pr=pr: e.tensor_tensor(
                        out=scs[:, 2 * pr:2 * pr + 2, :, :].rearrange("p h b t -> p (h b t)"),
                        in0=banks[1 + pr][0:64, :], in1=C.rwmask[:, :], op=ALU.mult),
                        reads=[rb[1 + pr]], writes=[r_scs[pr]])
                sch.op("dve", lambda e: e.tensor_tensor(out=nns[:, :, :].rearrange("p h t -> p (h t)"),
                                                        in0=banks[4][0:64, 0:256], in1=C.nmask[:, :], op=ALU.mult),
                       reads=[rb[4]], writes=[r_nns])

                def Mlev(lev, h):
                    return scs[:, h, 0, :] if lev == 0 else pws[lev - 1][:, h, 1, :]

                def Nlev(lev, h):
                    return nns[:, h, :] if lev == 0 else pws[lev - 1][:, h, 0, :]

                def Rlev(lev, h):
                    return [r_scs[h // 2], r_nns] if lev == 0 else [r_pws[lev - 1]]
                for lev in range(1, 6):
                    for h in range(4):
                        if lev < 5:
                            sch.op("pe", lambda e, lev=lev, h=h: e.matmul(
                                out=banks[3][0:64, h * 128:h * 128 + 64], lhsT=Mlev(lev - 1, h), rhs=Nlev(lev - 1, h),
                                start=True, stop=True), reads=Rlev(lev - 1, h), writes=[rb[3]])
                        sch.op("pe", lambda e, lev=lev, h=h: e.matmul(
                            out=banks[3][0:64, h * 128 + 64:h * 128 + 128], lhsT=Nlev(lev - 1, h), rhs=Mlev(lev - 1, h),
                            start=True, stop=True), reads=Rlev(lev - 1, h), writes=[rb[3]])
                    evac(C, lev, pws[lev - 1][:, :, :, :].rearrange("p h b t -> p (h b t)"), banks[3][0:64, :],
                         [rb[3]], writes=[r_pws[lev - 1]])
                wq = banks[5]
                for h in range(4):
                    pr, pb, hf = h // 2, (h % 2) * 64, h % 2
                    sch.op("pe", lambda e, h=h, pr=pr, cs=cs: e.matmul(
                        out=wq[0:64, h * 64:(h + 1) * 64], lhsT=ARz[h][:, 0, cs], rhs=Hb[:, pr, :],
                        start=True, stop=False), reads=[r_AR[pr], r_Hb], writes=[rb[5]])
                    sch.op("pe", lambda e, h=h, pr=pr, hf=hf: e.matmul(
                        out=wq[0:64, h * 64:(h + 1) * 64], lhsT=scs[:, h, 2, :], rhs=tok[:, pr, 2, hf, :],
                        start=False, stop=True), reads=[r_scs[pr], r_tok[pr]], writes=[rb[5]])
                cur = 0
                sch.op("dve", lambda e: e.tensor_copy(out=U[0][:, :, :].rearrange("p h e -> p (h e)"),
                                                      in_=wq[0:64, 0:256]), reads=[rb[5]], writes=[r_U[0]])
                for lev in range(6):
                    for h in range(4):
                        sch.op("pe", lambda e, lev=lev, h=h, cur=cur: e.matmul(
                            out=wq[0:64, h * 64:(h + 1) * 64], lhsT=Mlev(lev, h), rhs=U[cur][:, h, :],
                            start=True, stop=True), reads=Rlev(lev, h) + [r_U[cur]], writes=[rb[5]])
                    sch.op("dve", lambda e, cur=cur: e.tensor_tensor(
                        out=U[1 - cur][:, :, :].rearrange("p h e -> p (h e)"), in0=wq[0:64, 0:256],
                        in1=U[cur][:, :, :].rearrange("p h e -> p (h e)"), op=ALU.add),
                        reads=[rb[5], r_U[cur]], writes=[r_U[1 - cur]])
                    cur = 1 - cur
                Uf, rUf = U[cur], r_U[cur]
                b6 = banks[6]
                for h in range(4):
                    pr, pb, hf = h // 2, (h % 2) * 64, h % 2
                    o = b6[0:64, h * 64:(h + 1) * 64]
                    sch.op("pe", lambda e, o=o, h=h, pr=pr, cs=cs: e.matmul(
                        out=o, lhsT=ARz[h][:, 1, cs], rhs=Hb[:, pr, :], start=True, stop=False),
                        reads=[r_AR[pr], r_Hb], writes=[rb[6]])
                    sch.op("pe", lambda e, o=o, h=h, Uf=Uf: e.matmul(
                        out=o, lhsT=scs[:, h, 1, :], rhs=Uf[:, h, :], start=False, stop=False),
                        reads=[r_scs[pr], rUf], writes=[rb[6]])
                    sch.op("pe", lambda e, o=o, h=h, pr=pr, hf=hf: e.matmul(
                        out=o, lhsT=scs[:, h, 3, :], rhs=tok[:, pr, 2, hf, :], start=False, stop=True),
                        reads=[r_scs[pr], r_tok[pr]], writes=[rb[6]])
                for h in range(4):
                    pr, pb, hf = h // 2, (h % 2) * 64, h % 2
                    o = b6[:, 256 + h * 64:256 + (h + 1) * 64]
                    sch.op("pe", lambda e, o=o, h=h, pr=pr, hf=hf, Uf=Uf: e.matmul(
                        out=o, lhsT=tok[:, pr, 0, :, :].rearrange("p a e -> p (a e)"), rhs=Uf[:, h, :],
                        start=True, stop=False), reads=[r_tok[pr], rUf], writes=[rb[6]])
                    sch.op("pe", lambda e, o=o, pr=pr, hf=hf: e.matmul(
                        out=o, lhsT=tok[:, pr, 1, :, :].rearrange("p a e -> p (a e)"), rhs=tok[:, pr, 2, hf, :],
                        start=False, stop=True), reads=[r_tok[pr]], writes=[rb[6]])
                sch.op("act", lambda e: e.copy(out=yv[:, :, :].rearrange("p h e -> p (h e)"), in_=b6[0:64, 0:256]),
                       reads=[rb[6]], writes=[r_yv])
                for h in range(4):
                    pr, pb = h // 2, (h % 2) * 64
                    sch.op("dve", lambda e, h=h, pr=pr, pb=pb, cc=cc: e.scalar_tensor_tensor(
                        out=Hs[pb:pb + 64, pr, :], in0=Hs[pb:pb + 64, pr, :],
                        scalar=E1[pr][pb:pb + 64, cc * L + L - 1:cc * L + L],
                        in1=b6[pb:pb + 64, 256 + h * 64:256 + (h + 1) * 64], op0=ALU.mult, op1=ALU.add),
                        reads=[r_Hs, r_E1[pr], rb[6]], wacc=[r_Hs])
                sch.op("act", lambda e: e.copy(out=Hb[:, :, :], in_=Hs[:, :, :]), reads=[r_Hs], writes=[r_Hb])
                b4 = banks[4]
                sch.op("pe", lambda e, cs=cs: e.matmul(out=b4[0:64, 256:512], lhsT=lin_g[:, cs], rhs=lup[0:64, 2, :],
                                                       start=True, stop=True), reads=[r_ling, r_par, r_nns], writes=[rb[4]])
                sch.op("dve", lambda e: e.tensor_reduce(out=st[:, 0, :, :], in_=yv[:, :, :], axis=AX.X, op=ALU.add),
                       reads=[r_yv], writes=[r_st])
                sch.op("pool", lambda e: e.tensor_tensor(out=ysq[:, :, :], in0=yv[:, :, :], in1=yv[:, :, :], op=ALU.mult),
                       reads=[r_yv], writes=[r_ysq])
                sch.op("dve", lambda e: e.tensor_reduce(out=st[:, 1, :, :], in_=ysq[:, :, :], axis=AX.X, op=ALU.add),
                       reads=[r_ysq], writes=[r_st])
                sch.op("dve", lambda e: e.tensor_scalar(out=st[:, 2, :, :], in0=st[:, 0, :, :], scalar1=1.0 / 64,
                                                        scalar2=None, op0=ALU.mult), reads=[r_st], writes=[r_st])
                sch.op("dve", lambda e: e.tensor_tensor(out=st[:, 3, :, :], in0=st[:, 2, :, :], in1=st[:, 2, :, :],
                                                        op=ALU.mult), reads=[r_st], writes=[r_st])
                sch.op("dve", lambda e: e.scalar_tensor_tensor(out=st[:, 4, :, :], in0=st[:, 1, :, :], scalar=1.0 / 64,
                                                               in1=st[:, 3, :, :], op0=ALU.mult, op1=ALU.subtract),
                       reads=[r_st], writes=[r_st])
                sch.op("dve", lambda e: e.tensor_scalar(out=st[:, 4, :, :], in0=st[:, 4, :, :], scalar1=64e-5,
                                                        scalar2=None, op0=ALU.add), reads=[r_st], writes=[r_st])
                sch.op("act", lambda e: e.activation(out=st[:, 5, :, :], in_=st[:, 4, :, :], func=AF.Sqrt),
                       reads=[r_st], writes=[r_st])
                sch.op("dve", lambda e: e.reciprocal(out=st[:, 6, :, :], in_=st[:, 5, :, :]), reads=[r_st], writes=[r_st])
                sch.op("dve", lambda e: e.tensor_tensor(out=ysq[:, :, :], in0=yv[:, :, :],
                                                        in1=st[:, 2, :, :].broadcast_to([64, 4, 64]), op=ALU.subtract),
                       reads=[r_yv, r_st], writes=[r_ysq])
                sch.op("dve", lambda e: e.tensor_tensor(out=ysq[:, :, :], in0=ysq[:, :, :],
                                                        in1=st[:, 6, :, :].broadcast_to([64, 4, 64]), op=ALU.mult),
                       reads=[r_ysq, r_st], writes=[r_ysq])
                y2 = ysq[:, :, :].rearrange("p h e -> p (h e)")
                sch.op("pool", lambda e: e.tensor_tensor(out=y2, in0=y2, in1=lng[:, :], op=ALU.mult),
                       reads=[r_ysq, r_par], writes=[r_ysq])
                sch.op("pool", lambda e: e.tensor_tensor(out=y2, in0=y2, in1=lnb[:, :], op=ALU.add),
                       reads=[r_ysq, r_par], writes=[r_ysq])
                for pr in range(2):
                    sch.op("pe", lambda e, pr=pr, cs=cs: e.matmul(out=b4[0:64, 2 * pr:2 * pr + 2], lhsT=prodb[:, pr, cs],
                                                                  rhs=C.sel2b[:, :], start=True, stop=True),
                           reads=[r_prodb[pr], r_nns], writes=[rb[4]])
                sch.op("act", lambda e: e.copy(out=st[:, 7, :, 0], in_=b4[0:64, 0:4]), reads=[rb[4]], writes=[r_st])
                for pr in range(2):
                    sch.op("dve", lambda e, pr=pr: e.tensor_tensor(
                        out=yv[:, 2 * pr:2 * pr + 2, :], in0=tok[:, pr, 2, :, :],
                        in1=st[:, 7, 2 * pr:2 * pr + 2, :].broadcast_to([64, 2, 64]), op=ALU.mult),
                        reads=[r_tok[pr], r_st], wacc=[r_yv])
                sch.op("pool", lambda e: e.tensor_tensor(out=ysq[:, :, :], in0=ysq[:, :, :], in1=yv[:, :, :], op=ALU.add),
                       reads=[r_ysq, r_yv], writes=[r_ysq])
                sch.op("dve", lambda e: e.tensor_tensor(out=yo[:, :], in0=y2, in1=b4[0:64, 256:512], op=ALU.mult),
                       reads=[r_ysq, rb[4]], writes=[r_yo])
                for j in range(2):
                    sch.op("pe", lambda e, j=j: e.transpose(out=banks[0][:, 384 + j * 64:384 + (j + 1) * 64],
                                                            in_=yo[:, j * 128:(j + 1) * 128],
                                                            identity=C.identf[0:64, 0:64]),
                           reads=[r_yo], writes=[rb[0]])
                sch.op("act", lambda e, gcs=gcs: e.copy(out=yrT[:, :, gcs],
                                                        in_=banks[0][:, 384:512].rearrange("p (j t) -> p j t", j=2)),
                       reads=[rb[0]], wacc=[r_yrT])
        for j in range(2):
            sch.dma("sp", d["mixT"][256 + j * 128:256 + (j + 1) * 128, :], yrT[:, j, :], reads=[r_yrT],
                    wacc=[C.r_mixT])


DOFF_ = (2, 4, 6, 9, 12, 15)


def phase_C(C, l, bg=None, bg_every=5):
    nc, sch, d = C.nc, C.sch, C.d
    QB = 512
    lam_init = 0.8 - 0.6 * math.exp(-0.3 * l)
    with ExitStack() as es:
        def sb(name, shape, dt):
            return _sb(es, nc, "C_" + name, shape, dt)
        lq = sb("lq", [128, 4, 64], F32)
        lt = sb("lt", [128, 2, 64], F32)
        ls = sb("ls", [128, 4], F32)
        nlam = sb("nlam", [128, 1], F32)
        ag = sb("ag", [128, 4], F32)
        r_par = Res()
        NHB = 1 if bg is not None else 2
        qT = [[sb("qT%d_%d" % (i, c), [128, S], BF16) for c in range(2)] for i in range(NHB)]
        kT = [sb("kT%d" % i, [128, S], BF16) for i in range(NHB)]
        vt = [sb("vt%d" % i, [128, NT, 128], BF16) for i in range(NHB)]
        r_q, r_k, r_v = [Res(), Res()], [Res(), Res()], [Res(), Res()]
        NB = 3
        pt = [sb("pt%d" % i, [128, QB], BF16) for i in range(NB)]
        r_pt = [Res() for _ in range(NB)]
        rl = sb("rl", [128, QB], F32)
        oc = [sb("oc%d" % i, [128, QB], F32) for i in range(2)]
        od = sb("od", [128, QB], F32)
        sq = sb("sq", [128, QB], F32)
        rs = sb("rs", [128, QB], F32)
        ya = sb("ya", [128, QB], BF16)
        r_rl, r_od, r_sq, r_rs, r_ya = Res(), Res(), Res(), Res(), Res()
        r_oc = [Res(), Res()]
        p_st = [_ps(es, nc, "C_pst%d" % i, [128, 512], F32) for i in range(3)]
        rp_st = [PRes() for _ in range(3)]
        p_os = [_ps(es, nc, "C_po%d" % i, [128, 512], F32) for i in range(2)]
        p_ls = [_ps(es, nc, "C_pl%d" % i, [128, 512], F32) for i in range(2)]
        p_ss = _ps(es, nc, "C_pss", [128, 512], F32)
        rp_os, rp_ls, rp_ss = [PRes(), PRes()], [PRes(), PRes()], PRes()
        deferred = []

        for i, nm in enumerate(("a_lq1", "a_lk1", "a_lq2", "a_lk2")):
            sch.dma("sp", lq[:, i, :], d[nm][l], wacc=[r_par])
        sch.dma("sp", ag[:, :], d["a_norm_g"][l], wacc=[r_par])
        for i in range(2):
            sch.op("dve", lambda e, i=i: e.tensor_tensor(out=lt[:, i, :], in0=lq[:, 2 * i, :], in1=lq[:, 2 * i + 1, :],
                                                         op=ALU.mult), reads=[r_par], writes=[r_par])
        sch.op("dve", lambda e: e.tensor_reduce(out=ls[:, 0:2], in_=lt[:, :, :], axis=AX.X, op=ALU.add),
               reads=[r_par], writes=[r_par])
        sch.op("act", lambda e: e.activation(out=ls[:, 2:4], in_=ls[:, 0:2], func=AF.Exp), reads=[r_par], writes=[r_par])
        sch.op("dve", lambda e: e.tensor_tensor(out=nlam[:, :], in0=ls[:, 3:4], in1=ls[:, 2:3], op=ALU.subtract),
               reads=[r_par], writes=[r_par])
        sch.op("dve", lambda e: e.tensor_scalar(out=nlam[:, :], in0=nlam[:, :], scalar1=-lam_init, scalar2=None,
                                                op0=ALU.add), reads=[r_par], writes=[r_par])
        sch.op("dve", lambda e: e.tensor_scalar(out=ag[:, :], in0=ag[:, :], scalar1=1.0 - lam_init, scalar2=None,
                                                op0=ALU.mult), reads=[r_par], writes=[r_par])
        ist = 0
        ipt = 0
        for h in range(4):
            hb = h % NHB
            for c in range(2):
                if h < NHB:
                    sch.op("pool", lambda e, hb=hb, c=c: e.memset(qT[hb][c][(1 - c) * 64:(1 - c) * 64 + 64, :], 0.0),
                           wacc=[r_q[hb]])
                sch.dma("sp", qT[hb][c][c * 64:c * 64 + 64, :], d["aqkT"][h * 128 + c * 64:h * 128 + c * 64 + 64, :],
                        reads=[C.r_aqkT], wacc=[r_q[hb]])
            sch.dma("sp", kT[hb][:, :], d["aqkT"][512 + h * 128:512 + (h + 1) * 128, :], reads=[C.r_aqkT],
                    writes=[r_k[hb]])
            sch.dma("sp", vt[hb][:, :, :], d["av"][:, h * 128:(h + 1) * 128].rearrange("(t p) e -> p t e", p=128),
                    reads=[C.r_av], writes=[r_v[hb]])
            items = [(j, c, kt) for j in range(S // QB) for c in range(2) for kt in range(4 * (j + 1))]

            def front(it, idx):
                j, c, kt = it
                i0 = max(0, kt - 4 * j)
                n0 = i0 * 128
                ps, rps = p_st[idx % NB], rp_st[idx % NB]
                p_, rp_ = pt[idx % NB], r_pt[idx % NB]
                sch.op("pe", lambda e: e.matmul(
                    out=ps[:, n0:QB], lhsT=kT[hb][:, kt * 128:(kt + 1) * 128],
                    rhs=qT[hb][c][:, j * QB + n0:(j + 1) * QB], start=True, stop=True),
                    reads=[r_k[hb], r_q[hb]], writes=[rps])
                sch.op("act", lambda e: e.activation(out=p_[:, n0:QB], in_=ps[:, n0:QB], func=AF.Exp, scale=0.125),
                       reads=[rps], writes=[rp_])
                if kt >= 4 * j:
                    eng = "dve"
                    sch.op(eng, lambda e: e.tensor_tensor(out=p_[:, n0:n0 + 128], in0=p_[:, n0:n0 + 128],
                                                          in1=C.trib[:, :], op=ALU.mult), reads=[rp_], writes=[rp_])

            def back(it, idx):
                j, c, kt = it
                nk = 4 * (j + 1)
                gpar = (2 * j + c) % 2
                p_o, rp_o = p_os[gpar], rp_os[gpar]
                p_l, rp_l = p_ls[gpar], rp_ls[gpar]
                i0 = max(0, kt - 4 * j)
                n0 = i0 * 128
                p_, rp_ = pt[idx % NB], r_pt[idx % NB]
                sch.op("pe", lambda e: e.matmul(out=p_o[:, n0:QB], lhsT=vt[hb][:, kt, :], rhs=p_[:, n0:QB],
                                                start=(kt == 0), stop=(kt == nk - 1)),
                       reads=[r_v[hb], rp_], writes=[rp_o])
                sch.op("pe", lambda e: e.matmul(out=p_l[:, n0:QB], lhsT=C.onesb[:, :], rhs=p_[:, n0:QB],
                                                start=(kt == 0), stop=(kt == nk - 1)), reads=[rp_], writes=[rp_l])
                if kt != nk - 1:
                    return
                def d0():
                    sch.op("act", lambda e: e.activation(out=rl[:, :], in_=p_l[:, :], func=AF.Ln),
                           reads=[rp_l], writes=[r_rl])
                    sch.op("act", lambda e: e.activation(out=rl[:, :], in_=rl[:, :], func=AF.Exp, scale=-1.0),
                           reads=[r_rl], writes=[r_rl])

                def d0b():
                    sch.op("dve", lambda e: e.tensor_tensor(out=oc[c][:, :], in0=p_o[:, :], in1=rl[:, :],
                                                            op=ALU.mult), reads=[rp_o, r_rl], writes=[r_oc[c]])
                deferred.extend([[DOFF_[0], d0], [DOFF_[1], d0b]])
                if c == 0:
                    return
                qs = slice(j * QB, (j + 1) * QB)

                def d1():
                    sch.op("dve", lambda e: e.scalar_tensor_tensor(out=od[:, :], in0=oc[1][:, :], scalar=nlam[:, 0:1],
                                                                   in1=oc[0][:, :], op0=ALU.mult, op1=ALU.add),
                           reads=[r_oc[0], r_oc[1], r_par], writes=[r_od])
                    sch.op("pool", lambda e: e.tensor_tensor(out=sq[:, :], in0=od[:, :], in1=od[:, :], op=ALU.mult),
                           reads=[r_od], writes=[r_sq])

                def d2():
                    sch.op("pe", lambda e: e.matmul(out=p_ss[:, :], lhsT=C.onesf[:, :], rhs=sq[:, :], start=True,
                                                    stop=True), reads=[r_sq], writes=[rp_ss])
                    sch.op("dve", lambda e: e.tensor_scalar(out=rs[:, :], in0=p_ss[:, :], scalar1=1.0 / 128,
                                                            scalar2=1e-5, op0=ALU.mult, op1=ALU.add),
                           reads=[rp_ss], writes=[r_rs])

                def d3():
                    sch.op("act", lambda e: e.activation(out=rs[:, :], in_=rs[:, :], func=AF.Ln),
                           reads=[r_rs], writes=[r_rs])
                    sch.op("act", lambda e: e.activation(out=rs[:, :], in_=rs[:, :], func=AF.Exp, scale=-0.5),
                           reads=[r_rs], writes=[r_rs])

                def d4():
                    sch.op("dve", lambda e: e.scalar_tensor_tensor(out=ya[:, :], in0=od[:, :], scalar=ag[:, h:h + 1],
                                                                   in1=rs[:, :], op0=ALU.mult, op1=ALU.mult),
                           reads=[r_od, r_rs, r_par], writes=[r_ya])
                    sch.dma("sp", d["mixT"][512 + h * 128:512 + (h + 1) * 128, qs], ya[:, :], reads=[r_ya],
                            wacc=[C.r_mixT])
                deferred.extend([[DOFF_[2], d1], [DOFF_[3], d2], [DOFF_[4], d3], [DOFF_[5], d4]])

            def run_deferred(force=False):
                for e_ in list(deferred):
                    e_[0] -= 1
                    if force or e_[0] <= 0:
                        deferred.remove(e_)
                        e_[1]()

            LOOK = 2
            for i in range(min(LOOK, len(items))):
                front(items[i], ist + i)
            for i in range(len(items)):
                if i + LOOK < len(items):
                    front(items[i + LOOK], ist + i + LOOK)
                back(items[i], ist + i)
                run_deferred()
                if bg is not None and i % bg_every == bg_every // 2:
                    for _ in range(globals().get("BGN_", 1)):
                        next(bg, None)
            ist += len(items)
            while deferred:
                run_deferred(force=True)
        if bg is not None:
            for _ in bg:
                pass


PARAM_SHAPES = {
    "norm1_g": [DEPTH, 128, 8], "norm2_g": [DEPTH, 128, 8], "final_gb": [128, D],
    "w_in": [DEPTH, D, IN_PROJ], "w_out": [DEPTH, D, D], "w_ff_up": [DEPTH, D, DFF], "w_ff_down": [DEPTH, DFF, D],
    "m_conv_w": [DEPTH, 128, 4, 4], "m_conv_b": [DEPTH, 128, 4], "m_b_i": [DEPTH, 4, 1], "m_b_f": [DEPTH, 4, 1],
    "m_norm_g": [DEPTH, 128, 256],
    "r_mu": [DEPTH, 128, 7], "r_w0": [DEPTH, 128, 2], "r_a0": [DEPTH, 128, 2], "r_k_k": [DEPTH, 128, 2],
    "r_k_a": [DEPTH, 128, 2], "r_r_k": [DEPTH, 128, 2], "r_w_up": [DEPTH, 32, 256], "r_a_up": [DEPTH, 32, 256],
    "r_g_up": [DEPTH, 64, 256], "r_ln_g": [DEPTH, 64, 256], "r_ln_b": [DEPTH, 64, 256],
    "a_lq1": [DEPTH, 128, 64], "a_lk1": [DEPTH, 128, 64], "a_lq2": [DEPTH, 128, 64], "a_lk2": [DEPTH, 128, 64],
    "a_norm_g": [DEPTH, 128, 4],
    "c_ident": [128, 128], "c_tri": [128, 128], "c_rwmask": [64, 512], "c_nmask": [64, 256],
    "c_sel4": [4, 2, 128], "c_blk": [128, 128], "c_sel2": [128, 2],
}
SCRATCH = {
    "mqkT": ([512, S], F32), "mgT": ([8, S], F32), "mvo": ([S, 512], F32), "rT": ([896, S], F32),
    "aqkT": ([1024, S], BF16), "av": ([S, 512], BF16), "mixT": ([1024, S], BF16), "xres": ([S, D], F32),
    "rpb": ([1024, 15 * 512], BF16), "rpf": ([1024, 8 * 512], F32),
}


BIGW = ("w_in", "w_out", "w_ff_up", "w_ff_down")


def build(phases=None, debug=(), ext_in=(), limit=None):
    nc = bass.Bass("TRN2", target_bir_lowering=False)
    C = Ctx()
    C.nc = nc
    d = {}
    d["x"] = nc.dram_tensor("x", [S, D], F32, kind="ExternalInput").ap()
    need_big = phases is None or any(ph in ("A", "O") for ph, _ in phases)
    for k, shp in PARAM_SHAPES.items():
        if k in BIGW and not need_big:
            continue
        d[k] = nc.dram_tensor(k, shp, F32, kind="ExternalInput").ap()
    for k, (shp, dt) in SCRATCH.items():
        kind = "ExternalOutput" if k in debug else "Internal"
        if k in ext_in:
            kind = "ExternalInput"
        d[k] = nc.dram_tensor(k, shp, dt, kind=kind).ap()
    d["y"] = nc.dram_tensor("y", [S, D], F32, kind="ExternalOutput").ap()
    C.d = d
    for k in ("mqkT", "mgT", "mvo", "rT", "aqkT", "av", "mixT", "xres", "x", "y", "rpb", "rpf"):
        setattr(C, "r_" + k, Res(k))
    with ExitStack() as es:
        sch = Sched(nc, es)
        sch.limit = limit
        C.sch = sch
        C.identf = _sb(es, nc, "identf", [128, 128], F32)
        C.identb = _sb(es, nc, "identb", [128, 128], BF16)
        C.trif = _sb(es, nc, "trif", [128, 128], F32)
        C.trib = _sb(es, nc, "trib", [128, 128], BF16)
        C.rwmask = _sb(es, nc, "rwmask", [64, 512], F32)
        C.nmask = _sb(es, nc, "nmask", [64, 256], F32)
        C.sel4 = _sb(es, nc, "sel4", [4, 2, 128], F32)
        C.blkf = _sb(es, nc, "blkf", [128, 128], F32)
        C.sel2f = _sb(es, nc, "sel2f", [128, 2], F32)
        C.sel2b = _sb(es, nc, "sel2b", [128, 2], BF16)
        C.onesf = _sb(es, nc, "onesf", [128, 128], F32)
        C.onesb = _sb(es, nc, "onesb", [128, 128], BF16)
        rc = Res()
        for t, k in ((C.identf, "c_ident"), (C.trif, "c_tri"), (C.rwmask, "c_rwmask"), (C.nmask, "c_nmask"),
                     (C.blkf, "c_blk"), (C.sel2f, "c_sel2")):
            sch.dma("sp", t[:, :], d[k], wacc=[rc])
        sch.dma("sp", C.sel4[:, :, :], d["c_sel4"], wacc=[rc])
        sch.op("dve", lambda e: e.tensor_copy(out=C.identb[:, :], in_=C.identf[:, :]), reads=[rc], wacc=[rc])
        sch.op("dve", lambda e: e.tensor_copy(out=C.trib[:, :], in_=C.trif[:, :]), reads=[rc], wacc=[rc])
        sch.op("dve", lambda e: e.tensor_copy(out=C.sel2b[:, :], in_=C.sel2f[:, :]), reads=[rc], wacc=[rc])
        sch.op("pool", lambda e: e.memset(C.onesf[:, :], 1.0), wacc=[rc])
        sch.op("pool", lambda e: e.memset(C.onesb[:, :], 1.0), wacc=[rc])
        for eng in ("pe", "dve", "act", "pool"):
            sch._deps(eng, [rc], [])
        if phases is None:
            phases = []
            for l in range(DEPTH):
                phases += [("A", l), ("M", l), ("R", l), ("C", l), ("O", l)]
        phases_all = list(phases)
        for (ph, l) in phases:
            sch.barrier()
            X, rX = (d["x"], C.r_x) if l == 0 else (d["xres"], C.r_xres)
            if ph == "A":
                phase_A(C, l, X, rX)
            elif ph == "M":
                phase_M(C, l)
            elif ph == "R":
                phase_R2(C, l)
            elif ph == "C":
                if getattr(build, "preload", True) and (phases_all.count(("O", l)) > 0):
                    wes = ExitStack()
                    C.ow = OWeights(C, wes, l)
                    C.ow_es = wes
                    phase_C(C, l, bg=C.ow.gen())
                else:
                    C.ow = None
                    phase_C(C, l)
            elif ph == "RC":
                phase_RC(C, l)
            elif ph == "CP":
                wes = ExitStack()
                gen = phase_R2(C, l, mode="prep", es_ext=wes)
                phase_C(C, l, bg=gen, bg_every=1)
                wes.close()
            elif ph == "R3":
                phase_R2(C, l, mode="chunks")
            elif ph == "RP":
                wes = ExitStack()
                for _ in phase_R2(C, l, mode="prep", es_ext=wes):
                    pass
                wes.close()
            elif ph == "O":
                last = (l == DEPTH - 1)
                XO, rXO = (d["y"], C.r_y) if last else (d["xres"], C.r_xres)
                ow = getattr(C, "ow", None)
                phase_O(C, l, X, rX, XO, rXO, last, W=ow)
                if ow is not None:
                    C.ow_es.close()
                    C.ow = None
        sch.limit = None
        sch.finish()
        C.nops = sch.nops
    build.last_nops = sch.nops
    return nc


def host_params(inp):
    f = lambda a: np.ascontiguousarray(np.asarray(a, dtype=np.float32))
    L = DEPTH
    p = {}
    p["norm1_g"] = f(np.asarray(inp["norm1_g"]).reshape(L, 8, 128).transpose(0, 2, 1))
    p["norm2_g"] = f(np.asarray(inp["norm2_g"]).reshape(L, 8, 128).transpose(0, 2, 1))
    p["final_gb"] = f(np.broadcast_to(np.asarray(inp["final_g"]).reshape(1, D), (128, D)))
    for k in ("w_in", "w_out", "w_ff_up", "w_ff_down", "r_w_up", "r_a_up", "r_g_up"):
        p[k] = f(inp[k])
    p["m_conv_w"] = f(np.asarray(inp["m_conv_w"]).reshape(L, 4, 4, 128).transpose(0, 3, 2, 1))
    p["m_conv_b"] = f(np.asarray(inp["m_conv_b"]).reshape(L, 4, 128).transpose(0, 2, 1))
    p["m_b_i"] = f(np.asarray(inp["m_b_i"]).reshape(L, 4, 1))
    p["m_b_f"] = f(np.asarray(inp["m_b_f"]).reshape(L, 4, 1))
    p["m_norm_g"] = f(np.broadcast_to(np.asarray(inp["m_norm_g"]).reshape(L, 1, 256), (L, 128, 256)))
    p["r_mu"] = f(np.asarray(inp["r_mu"]).reshape(L, 7, 128).transpose(0, 2, 1))
    for k in ("r_w0", "r_a0", "r_k_k", "r_k_a", "r_r_k"):
        p[k] = f(np.asarray(inp[k]).reshape(L, 2, 128).transpose(0, 2, 1))
    p["r_ln_g"] = f(np.broadcast_to(np.asarray(inp["r_ln_g"]).reshape(L, 1, 256), (L, 64, 256)))
    p["r_ln_b"] = f(np.broadcast_to(np.asarray(inp["r_ln_b"]).reshape(L, 1, 256), (L, 64, 256)))
    for k in ("a_lq1", "a_lk1", "a_lq2", "a_lk2"):
        p[k] = f(np.broadcast_to(np.asarray(inp[k]).reshape(L, 1, 64), (L, 128, 64)))
    p["a_norm_g"] = f(np.asarray(inp["a_norm_g"]).reshape(L, 4, 128).transpose(0, 2, 1))
    pp = np.arange(128)[:, None]
    nn = np.arange(128)[None, :]
    p["c_ident"] = f(pp == nn)
    p["c_tri"] = f(pp <= nn)
    p64, n64 = np.arange(64)[:, None], np.arange(64)[None, :]
    strict, incl = f(p64 < n64), f(p64 <= n64)
    one = np.concatenate([strict, incl, strict, incl], axis=1)
    p["c_rwmask"] = f(np.concatenate([one, one], axis=1))
    p["c_nmask"] = f(np.tile(f(n64 < p64), (1, 4)))
    sel4 = np.zeros((4, 2, 128), np.float32)
    for h in range(4):
        sel4[h, h // 2, (h % 2) * 64:(h % 2) * 64 + 64] = 1.0
    p["c_sel4"] = sel4
    p["c_blk"] = f((pp // 64) == (nn // 64))
    p["c_sel2"] = f((np.arange(128)[:, None] // 64) == np.arange(2)[None, :])
    return p


_NC_CACHE = {}


def kernel(**inputs):
    x = np.asarray(inputs["x"], dtype=np.float32)
    B = x.shape[0]
    p = host_params(inputs)
    if "full" not in _NC_CACHE:
        _NC_CACHE["full"] = build()
    nc = _NC_CACHE["full"]
    in_maps = []
    for b in range(B):
        m = dict(p)
        m["x"] = np.ascontiguousarray(x[b])
        in_maps.append(m)
    res = run_bass_kernel_spmd(nc, in_maps, core_ids=list(range(B)))
    return np.stack([np.asarray(r["y"]) for r in res.results], axis=0).astype(np.float32)


class _NullCtx:
    def __init__(self, v):
        self.v = v

    def __enter__(self):
        return self.v

    def __exit__(self, *a):
        return False


def phase_R2(C, l, mode="all", es_ext=None):
    nc, sch, d = C.nc, C.sch, C.d
    DO_PREP = mode in ("all", "prep")
    SLACK = globals().get("SLACK_", 2) if mode == "prep" else 0
    DO_CHUNKS = mode in ("all", "chunks")
    T = 512
    L = 64
    NCH = T // L
    NBLK = S // T
    NEG = -0.6065306597126334
    with (ExitStack() if es_ext is None else _NullCtx(es_ext)) as es:
        def sb(name, shape, dt):
            return _sb(es, nc, "R_" + name, shape, dt)
        mu = sb("mu", [128, 7], F32)
        omu = sb("omu", [128, 7], F32)
        pv = sb("pv", [128, 6, 2], F32)
        lup_f = sb("lupf", [64, 3, 256], F32)
        lup = sb("lup", [64, 3, 256], BF16)
        lng = sb("lng", [64, 256], F32)
        lnb = sb("lnb", [64, 256], F32)
        r_par, r_lup = Res(), Res()
        r_praw = Res()
        r_lin, r_lina = Res(), Res()
        r_tmp = Res()
        PT = []
        r_Gp = [Res(), Res()]
        if DO_PREP:
            praw = sb("praw", [128, 1 + T], F32)
            pm_r = sb("pm_r", [128, 2, T], F32)
            pm_k = sb("pm_k", [128, 2, T], F32)
            pm_l = sb("pm_l", [128, T], F32)
            lin = sb("lin", [128, T], BF16)
            lin_a = sb("lin_a", [32, T], BF16)
            tmp = sb("tmp", [128, T], F32)
            for i in range(2):
                t_ = Ctx()
                for nm in ("tmp", "lw", "aa", "kx", "sq", "kp", "bb", "gi", "ge", "E2"):
                    setattr(t_, nm, sb("%s_p%d" % (nm, i), [128, T], F32))
                    setattr(t_, "r_" + nm, Res())
                PT.append(t_)
            Gp = [sb("Gp%d" % i, [128, 1 + T], F32) for i in range(2)]
        B = []
        for i in range(2):
            o = Ctx()
            o.bbf = sb("Bbf%d" % i, [128, 15, T], BF16)
            o.bf32 = sb("Bf32_%d" % i, [128, 8, T], F32)
            o.ARz = [o.bbf[:, 2 * h:2 * h + 2, :] for h in range(4)]
            o.Bt = o.bbf[:, 8:10, :]
            o.Kt = o.bbf[:, 10:12, :]
            o.prodb = o.bbf[:, 12:14, :]
            o.lin_g = o.bbf[0:64, 14, :]
            o.Bh = o.bf32[:, 0:2, :]
            o.Kh = o.bf32[:, 2:4, :]
            o.pm_v = o.bf32[:, 4:6, :]
            o.E1 = [o.bf32[:, 6 + p, :] for p in range(2)]
            o.r_AR, o.r_Bt, o.r_Kt, o.r_Bh, o.r_Kh, o.r_v, o.r_prodb, o.r_E1 = ([Res(), Res()] for _ in range(8))
            o.r_ling = Res()
            o.all_bf = o.r_AR + o.r_Bt + o.r_Kt + o.r_prodb + [o.r_ling]
            o.all_f32 = o.r_Bh + o.r_Kh + o.r_v + o.r_E1
            B.append(o)
        NK = 3
        K_ = []
        for i in range(NK if DO_CHUNKS else 0):
            o = Ctx()
            o.tok = sb("tok%d" % i, [64, 2, 4, 2, 64], BF16)
            o.scs = sb("scs%d" % i, [64, 4, 4, 64], BF16)
            o.nns = sb("nns%d" % i, [64, 4, 64], BF16)
            o.pws = [sb("pws%d_%d" % (i, j), [64, 4, 2, 64], BF16) for j in range(2)]
            o.Q = [sb("Q%d_%d" % (i, j), [64, 4, 64], BF16) for j in range(2)]
            o.r_tok = [Res(), Res()]
            o.r_scs = [Res(), Res()]
            o.r_nns = Res()
            o.r_pws = [Res() for _ in range(2)]
            o.r_Q = [Res(), Res()]
            K_.append(o)
        r_Hs, r_Hb = Res(), Res()
        r_yrT = Res()
        r_Wt, r_Ut = Res(), Res()
        r_yv = [Res(), Res()]
        r_bvv, r_st7, r_gsb = [Res(), Res()], [Res(), Res()], [Res(), Res()]
        r_yo2 = [Res(), Res()]
        r_ysq, r_bv, r_yo, r_st = Res(), Res(), Res(), Res()
        if DO_CHUNKS:
            Hs = sb("Hs", [128, 2, 64], F32)
            Hb = sb("Hb", [128, 2, 64], BF16)
            yrT = sb("yrT", [128, 2, S], BF16)
            Wt = sb("Wt", [64, 4, 64], BF16)
            Ut = sb("Ut", [64, 4, 64], BF16)
            yv = [sb("yv%d" % i, [64, 4, 64], F32) for i in range(2)]
            ysq = sb("ysq", [64, 4, 64], F32)
            bv = [sb("bv%d" % i, [64, 4, 64], F32) for i in range(2)]
            st7 = [sb("st7_%d" % i, [64, 4, 1], F32) for i in range(2)]
            gsb = [sb("gsb%d" % i, [64, 256], F32) for i in range(2)]
            yo = [sb("yo%d" % i, [64, 256], F32) for i in range(2)]
            st = sb("st", [64, 8, 4, 1], F32)
            banks = [_ps(es, nc, "R_b%d" % i, [128, 512], F32) for i in range(8)]
            rb = [PRes() for _ in range(8)]
        else:
            pb_only = _ps(es, nc, "R_pb", [128, 512], F32)
            banks = [None] * 7 + [pb_only]
            rb = [None] * 7 + [PRes()]

        sch.dma("sp", mu[:, :], d["r_mu"][l], wacc=[r_par])
        for i, nm in enumerate(("r_w0", "r_a0", "r_k_k", "r_k_a", "r_r_k")):
            sch.dma("sp", pv[:, i, :], d[nm][l], wacc=[r_par])
        sch.op("pool", lambda e: e.memset(lup_f[:, :, :], 0.0), writes=[r_lup])
        sch.dma("sp", lup_f[0:32, 0, :], d["r_w_up"][l], writes=[r_lup])
        sch.dma("sp", lup_f[0:32, 1, :], d["r_a_up"][l], writes=[r_lup])
        sch.dma("sp", lup_f[0:64, 2, :], d["r_g_up"][l], writes=[r_lup])
        sch.dma("sp", lng[:, :], d["r_ln_g"][l], wacc=[r_par])
        sch.dma("sp", lnb[:, :], d["r_ln_b"][l], wacc=[r_par])
        sch.op("dve", lambda e: e.tensor_scalar(out=omu[:, :], in0=mu[:, :], scalar1=-1.0, scalar2=1.0,
                                                op0=ALU.mult, op1=ALU.add), reads=[r_par], writes=[r_par])
        sch.op("dve", lambda e: e.tensor_scalar(out=pv[:, 5, :], in0=pv[:, 3, :], scalar1=-1.0, scalar2=1.0,
                                                op0=ALU.mult, op1=ALU.add), reads=[r_par], writes=[r_par])
        sch.op("dve", lambda e: e.tensor_copy(out=lup[:, :, :], in_=lup_f[:, :, :]), reads=[r_par, r_lup],
               writes=[r_par])
        if DO_CHUNKS:
            sch.op("pool", lambda e: e.memset(Hs[:, :, :], 0.0), writes=[r_Hs])
            sch.op("pool", lambda e: e.memset(Hb[:, :, :], 0.0), writes=[r_Hb])
        if DO_PREP:
            for pr in range(2):
                sch.op("pool", lambda e, pr=pr: e.memset(Gp[pr][:, :], 0.0), writes=[r_Gp[pr]])
            for i in range(2):
                for h in range(4):
                    zs_ = slice((1 - h % 2) * 64, (1 - h % 2) * 64 + 64)
                    sch.op("pool", lambda e, i=i, h=h, zs_=zs_: e.memset(B[i].ARz[h][zs_, :, :], 0.0),
                           wacc=[B[i].r_AR[h // 2]])
            sch.op("pool", lambda e: e.memset(praw[:, 0:1], 0.0), writes=[r_praw])

        def v3(ap2):
            return ap2.rearrange("p (c t) -> p c t", t=L)

        def prep_gen(blk):
            o = B[blk % 2]
            t0 = blk * T
            dests = [(pm_r, 0), (pm_r, 1), (pm_k, 0), (pm_k, 1), (o.pm_v, 0), (o.pm_v, 1), (None, 0)]
            r_pmr, r_pmk = [Res(), Res()], [Res(), Res()]
            r_pml = Res()
            rr_list = [r_pmr[0], r_pmr[1], r_pmk[0], r_pmk[1], o.r_v[0], o.r_v[1], r_pml]
            for rc in range(7):
                if blk == 0:
                    sch.dma("sp", praw[:, 1:1 + T], d["rT"][rc * 128:(rc + 1) * 128, 0:T],
                            reads=[C.r_rT], wacc=[r_praw])
                else:
                    sch.dma("sp", praw[:, :], d["rT"][rc * 128:(rc + 1) * 128, t0 - 1:t0 + T],
                            reads=[C.r_rT], writes=[r_praw])
                dt_, pi = dests[rc]
                ot = pm_l[:, :] if dt_ is None else dt_[:, pi, :]
                sch.op("pool", lambda e, rc=rc: e.tensor_scalar(out=tmp[:, :], in0=praw[:, 1:1 + T],
                                                                scalar1=omu[:, rc:rc + 1], scalar2=0.0, op0=ALU.mult, op1=ALU.add),
                       reads=[r_praw, r_par], writes=[r_tmp])
                yield
                sch.op("dve", lambda e, rc=rc, ot=ot: e.scalar_tensor_tensor(
                    out=ot, in0=praw[:, 0:T], scalar=mu[:, rc:rc + 1], in1=tmp[:, :], op0=ALU.mult, op1=ALU.add),
                    reads=[r_praw, r_tmp, r_par], writes=[rr_list[rc]])
                yield
            for _ in range(SLACK):
                yield
            sch.op("act", lambda e: e.activation(out=lin[0:32, :], in_=pm_l[0:32, :], func=AF.Tanh),
                   reads=[r_pml], wacc=[r_lin])
            for _ in range(SLACK):
                yield
            sch.op("act", lambda e: e.copy(out=lin[32:64, :], in_=pm_l[32:64, :]), reads=[r_pml], wacc=[r_lin])
            for _ in range(SLACK):
                yield
            sch.op("act", lambda e: e.activation(out=lin[64:128, :], in_=pm_l[64:128, :], func=AF.Sigmoid),
                   reads=[r_pml], wacc=[r_lin])
            sch.dma("sp", lin_a[:, :], lin[32:64, :], reads=[r_lin], writes=[r_lina])
            sch.dma("sp", o.lin_g[:, :], lin[64:128, :], reads=[r_lin], writes=[o.r_ling])
            yield
            pb7 = banks[7]

            def pair_gen(pr):
                P_ = PT[pr]
                tmp, lw, aa, kx, sq, kp, bb, gi, ge, E2 = (P_.tmp, P_.lw, P_.aa, P_.kx, P_.sq, P_.kp, P_.bb, P_.gi,
                                                          P_.ge, P_.E2)
                r_tmp, r_lw, r_aa, r_kx, r_sq, r_kp, r_bb, r_gi, r_ge, r_E2 = (
                    P_.r_tmp, P_.r_lw, P_.r_aa, P_.r_kx, P_.r_sq, P_.r_kp, P_.r_bb, P_.r_gi, P_.r_ge, P_.r_E2)
                rk, rr_ = r_pmk[pr], r_pmr[pr]
                pcs = slice(pr * 128, (pr + 1) * 128)
                sch.op("pe", lambda e, pcs=pcs: e.matmul(out=pb7[:, 0:T], lhsT=lup[0:32, 0, pcs], rhs=lin[0:32, :],
                                                         start=True, stop=True), reads=[r_lin, r_par], writes=[rb[7]])
                for _ in range(SLACK):
                    yield
                sch.op("act", lambda e, pr=pr: e.activation(out=lw[:, :], in_=pb7[:, 0:T], func=AF.Sigmoid,
                                                            bias=pv[:, 0, pr:pr + 1], scale=1.0),
                       reads=[rb[7], r_par], writes=[r_lw])
                yield
                sch.op("pe", lambda e, pcs=pcs: e.matmul(out=pb7[:, 0:T], lhsT=lup[0:32, 1, pcs], rhs=lin_a[:, :],
                                                         start=True, stop=True), reads=[r_lina, r_par], writes=[rb[7]])
                for _ in range(SLACK):
                    yield
                sch.op("act", lambda e, pr=pr: e.activation(out=aa[:, :], in_=pb7[:, 0:T], func=AF.Sigmoid,
                                                            bias=pv[:, 1, pr:pr + 1], scale=1.0),
                       reads=[rb[7], r_par], writes=[r_aa])
                yield
                sch.op("pool", lambda e: e.tensor_scalar(out=lw[:, :], in0=lw[:, :], scalar1=NEG, scalar2=0.0,
                                                         op0=ALU.mult, op1=ALU.add), reads=[r_lw], writes=[r_lw])
                sch.op("dve", lambda e, pr=pr: e.tensor_scalar(out=kx[:, :], in0=pm_k[:, pr, :],
                                                               scalar1=pv[:, 2, pr:pr + 1], scalar2=None,
                                                               op0=ALU.mult), reads=[rk, r_par], writes=[r_kx])
                yield
                sch.op("pool", lambda e: e.tensor_tensor(out=sq[:, :], in0=kx[:, :], in1=kx[:, :], op=ALU.mult),
                       reads=[r_kx], writes=[r_sq])
                sch.op("pe", lambda e: e.matmul(out=pb7[:, 0:T], lhsT=C.blkf[:, :], rhs=sq[:, :],
                                                start=True, stop=True), reads=[r_sq], writes=[rb[7]])
                for _ in range(SLACK):
                    yield
                sch.op("act", lambda e: e.activation(out=tmp[:, :], in_=pb7[:, 0:T], func=AF.Sqrt),
                       reads=[rb[7]], writes=[r_tmp])
                yield
                sch.op("dve", lambda e: e.tensor_scalar(out=tmp[:, :], in0=tmp[:, :], scalar1=1e-12, scalar2=None,
                                                        op0=ALU.max), reads=[r_tmp], writes=[r_tmp])
                yield
                sch.op("dve", lambda e: e.reciprocal(out=tmp[:, :], in_=tmp[:, :]), reads=[r_tmp], writes=[r_tmp])
                yield
                sch.op("dve", lambda e: e.tensor_tensor(out=kx[:, :], in0=kx[:, :], in1=tmp[:, :], op=ALU.mult),
                       reads=[r_kx, r_tmp], writes=[r_kx])
                yield
                sch.op("pool", lambda e, pr=pr: e.tensor_scalar(out=kp[:, :], in0=aa[:, :], scalar1=pv[:, 3, pr:pr + 1],
                                                                scalar2=pv[:, 5, pr:pr + 1], op0=ALU.mult, op1=ALU.add),
                       reads=[r_aa, r_par], writes=[r_kp])
                yield
                sch.op("pool", lambda e, pr=pr: e.tensor_tensor(out=kp[:, :], in0=kp[:, :], in1=pm_k[:, pr, :],
                                                                op=ALU.mult), reads=[r_kp, rk], writes=[r_kp])
                yield
                sch.op("pool", lambda e: e.tensor_tensor(out=bb[:, :], in0=kx[:, :], in1=aa[:, :], op=ALU.mult),
                       reads=[r_kx, r_aa], writes=[r_bb])
                yield
                sch.op("dve", lambda e, pr=pr: e.scalar_tensor_tensor(
                    out=o.prodb[:, pr, :], in0=pm_r[:, pr, :], scalar=pv[:, 4, pr:pr + 1], in1=kp[:, :],
                    op0=ALU.mult, op1=ALU.mult), reads=[rr_, r_kp, r_par], writes=[o.r_prodb[pr]])
                yield
                G = Gp[pr]
                sch.op("dve", lambda e, G=G: e.tensor_copy(out=G[:, 0:1], in_=G[:, T:T + 1]),
                       reads=[r_Gp[pr]], writes=[r_Gp[pr]])
                sch.op("dve", lambda e, G=G: e.tensor_tensor_scan(out=G[:, 1:1 + T], data0=lw[:, :], data1=lw[:, :],
                                                                  initial=G[:, 0:1], op0=ALU.add, op1=ALU.min),
                       reads=[r_lw, r_Gp[pr]], writes=[r_Gp[pr]])
                yield
                base = v3(G[:, 0:T])[:, :, 0:1].broadcast_to([128, NCH, L])
                sch.op("dve", lambda e, G=G, base=base: e.tensor_tensor(out=v3(gi[:, :]), in0=v3(G[:, 1:1 + T]),
                                                                        in1=base, op=ALU.subtract),
                       reads=[r_Gp[pr]], writes=[r_gi])
                yield
                sch.op("pool", lambda e: e.tensor_tensor(out=ge[:, :], in0=gi[:, :], in1=lw[:, :], op=ALU.subtract),
                       reads=[r_gi, r_lw], writes=[r_ge])
                e1 = o.E1[pr]
                for _ in range(SLACK):
                    yield
                sch.op("act", lambda e, e1=e1: e.activation(out=e1[:, :], in_=gi[:, :], func=AF.Exp),
                       reads=[r_gi], writes=[o.r_E1[pr]])
                yield
                for _ in range(SLACK):
                    yield
                sch.op("act", lambda e: e.activation(out=E2[:, :], in_=ge[:, :], func=AF.Exp),
                       reads=[r_ge], writes=[r_E2])
                yield
                for hf in range(2):
                    ps_ = slice(hf * 64, hf * 64 + 64)
                    hq = 2 * pr + hf
                    sch.op("dve", lambda e, hq=hq, ps_=ps_: e.scalar_tensor_tensor(
                        out=o.ARz[hq][ps_, 0, :], in0=kx[ps_, :], scalar=-1.0, in1=E2[ps_, :],
                        op0=ALU.mult, op1=ALU.mult), reads=[r_kx, r_E2], wacc=[o.r_AR[pr]])
                    sch.op("pool", lambda e, hq=hq, ps_=ps_, pr=pr, e1=e1: e.tensor_tensor(
                        out=o.ARz[hq][ps_, 1, :], in0=pm_r[ps_, pr, :], in1=e1[ps_, :], op=ALU.mult),
                        reads=[rr_, o.r_E1[pr]], wacc=[o.r_AR[pr]])
                    yield
                for _ in range(SLACK):
                    yield
                sch.op("act", lambda e: e.activation(out=E2[:, :], in_=gi[:, :], func=AF.Exp, scale=-1.0),
                       reads=[r_gi], writes=[r_E2])
                yield
                sch.op("dve", lambda e, pr=pr: e.tensor_tensor(out=o.Bt[:, pr, :], in0=bb[:, :], in1=E2[:, :],
                                                               op=ALU.mult), reads=[r_bb, r_E2], writes=[o.r_Bt[pr]])
                sch.op("pool", lambda e, pr=pr: e.tensor_tensor(out=o.Kt[:, pr, :], in0=kp[:, :], in1=E2[:, :],
                                                                op=ALU.mult), reads=[r_kp, r_E2], writes=[o.r_Kt[pr]])
                yield
                gend = v3(gi[:, :])[:, :, L - 1:L].broadcast_to([128, NCH, L])
                sch.op("dve", lambda e, gend=gend: e.tensor_tensor(out=v3(ge[:, :]), in0=gend, in1=v3(gi[:, :]),
                                                                   op=ALU.subtract), reads=[r_gi], writes=[r_ge])
                yield
                for _ in range(SLACK):
                    yield
                sch.op("act", lambda e: e.activation(out=ge[:, :], in_=ge[:, :], func=AF.Exp),
                       reads=[r_ge], writes=[r_ge])
                yield
                sch.op("dve", lambda e, pr=pr: e.tensor_tensor(out=o.Bh[:, pr, :], in0=bb[:, :], in1=ge[:, :],
                                                               op=ALU.mult), reads=[r_bb, r_ge], writes=[o.r_Bh[pr]])
                sch.op("pool", lambda e, pr=pr: e.tensor_tensor(out=o.Kh[:, pr, :], in0=kp[:, :], in1=ge[:, :],
                                                                op=ALU.mult), reads=[r_kp, r_ge], writes=[o.r_Kh[pr]])
                yield

            gens = [pair_gen(0), pair_gen(1)]
            while gens:
                for g in list(gens):
                    try:
                        next(g)
                        yield
                    except StopIteration:
                        gens.remove(g)

        def flat(t3):
            return t3[:, :, :].rearrange("p a t -> p (a t)")

        def store_block(blk):
            o = B[blk % 2]
            rows = slice(blk * 128, (blk + 1) * 128)
            sch.dma("sp", d["rpb"][rows, :], flat(o.bbf), reads=o.all_bf, wacc=[C.r_rpb])
            sch.dma("sp", d["rpf"][rows, :], flat(o.bf32), reads=o.all_f32, wacc=[C.r_rpf])

        def load_block(blk):
            o = B[blk % 2]
            rows = slice(blk * 128, (blk + 1) * 128)
            sch.dma("sp", flat(o.bbf), d["rpb"][rows, :], reads=[C.r_rpb], writes=o.all_bf)
            sch.dma("sp", flat(o.bf32), d["rpf"][rows, :], reads=[C.r_rpf], writes=o.all_f32)

        if mode == "prep":
            def prep_all():
                for blk in range(NBLK):
                    yield from prep_gen(blk)
                    store_block(blk)
                    yield
            return prep_all()

        def pre_gen(gc):
            blk, cc = divmod(gc, NCH)
            o = B[blk % 2]
            k = K_[gc % NK]
            cs = slice(cc * L, (cc + 1) * L)
            for pr in range(2):
                for j, (src, rs) in enumerate(((o.Bh, o.r_Bh[pr]), (o.Kh, o.r_Kh[pr]), (o.pm_v, o.r_v[pr]))):
                    sch.op("pe", lambda e, j=j, src=src, pr=pr: e.transpose(
                        out=banks[0][0:64, j * 128:(j + 1) * 128], in_=src[:, pr, cs], identity=C.identf[:, :]),
                        reads=[rs], writes=[rb[0]])
                sch.op("act", lambda e, pr=pr: e.copy(
                    out=k.tok[:, pr, 0:3, :, :],
                    in_=banks[0][0:64, 0:384].rearrange("p (j h e) -> p j h e", j=3, h=2)),
                    reads=[rb[0]], writes=[k.r_tok[pr]])
                yield
            for h in range(4):
                pr = h // 2
                bk = banks[1 + pr]
                oo = (h % 2) * 256
                rhs_ar = o.ARz[h][:, :, cs]
                sch.op("pe", lambda e, bk=bk, oo=oo, pr=pr, rhs_ar=rhs_ar: e.matmul(
                    out=bk[0:64, oo:oo + 128], lhsT=o.Bt[:, pr, cs], rhs=rhs_ar, start=True, stop=True),
                    reads=[o.r_Bt[pr], o.r_AR[pr]], writes=[rb[1 + pr]])
                sch.op("pe", lambda e, bk=bk, oo=oo, pr=pr, rhs_ar=rhs_ar: e.matmul(
                    out=bk[0:64, oo + 128:oo + 256], lhsT=o.Kt[:, pr, cs], rhs=rhs_ar, start=True, stop=True),
                    reads=[o.r_Kt[pr], o.r_AR[pr]], writes=[rb[1 + pr]])
                sch.op("pe", lambda e, h=h, pr=pr: e.matmul(
                    out=banks[4][0:64, h * 64:(h + 1) * 64], lhsT=o.ARz[h][:, 0, cs],
                    rhs=o.Bt[:, pr, cs], start=True, stop=True),
                    reads=[o.r_Bt[pr], o.r_AR[pr]], writes=[rb[4]])
                if h % 2 == 1:
                    sch.op("dve", lambda e, pr=pr: e.tensor_tensor(
                        out=k.scs[:, 2 * pr:2 * pr + 2, :, :].rearrange("p h b t -> p (h b t)"),
                        in0=banks[1 + pr][0:64, :], in1=C.rwmask[:, :], op=ALU.mult),
                        reads=[rb[1 + pr]], writes=[k.r_scs[pr]])
            sch.op("dve", lambda e: e.tensor_tensor(out=k.nns[:, :, :].rearrange("p h t -> p (h t)"),
                                                    in0=banks[4][0:64, 0:256], in1=C.nmask[:, :], op=ALU.mult),
                   reads=[rb[4]], writes=[k.r_nns])
            yield

            def Mlev(lev, h):
                return k.scs[:, h, 0, :] if lev == 0 else k.pws[(lev - 1) % 2][:, h, 1, :]

            def Nlev(lev, h):
                return k.nns[:, h, :] if lev == 0 else k.pws[(lev - 1) % 2][:, h, 0, :]

            def Rlev(lev, h):
                return [k.r_scs[h // 2], k.r_nns] if lev == 0 else [k.r_pws[(lev - 1) % 2]]
            sch.op("pool", lambda e: e.tensor_tensor(
                out=k.Q[0][:, :, :], in0=k.scs[:, :, 0, :],
                in1=C.identf[0:64, 0:64].unsqueeze(1).broadcast_to([64, 4, 64]), op=ALU.add),
                reads=[k.r_scs[0], k.r_scs[1]], writes=[k.r_Q[0]])
            yield
            qi = 0
            for lev in range(1, 6):
                for h in range(4):
                    sch.op("pe", lambda e, lev=lev, h=h: e.matmul(
                        out=banks[3][0:64, h * 128:h * 128 + 64], lhsT=Mlev(lev - 1, h), rhs=Nlev(lev - 1, h),
                        start=True, stop=True), reads=Rlev(lev - 1, h), writes=[rb[3]])
                    if lev < 5:
                        sch.op("pe", lambda e, lev=lev, h=h: e.matmul(
                            out=banks[3][0:64, h * 128 + 64:h * 128 + 128], lhsT=Nlev(lev - 1, h),
                            rhs=Mlev(lev - 1, h), start=True, stop=True), reads=Rlev(lev - 1, h), writes=[rb[3]])
                if lev < 5:
                    evac(C, 0, k.pws[(lev - 1) % 2][:, :, :, :].rearrange("p h b t -> p (h b t)"),
                         banks[3][0:64, :], [rb[3]], writes=[k.r_pws[(lev - 1) % 2]])
                    nsrc = lambda h, lev=lev: k.pws[(lev - 1) % 2][:, h, 0, :]
                    rn = [k.r_pws[(lev - 1) % 2]]
                else:
                    sch.op("act", lambda e: e.copy(
                        out=k.nns[:, :, :], in_=banks[3][0:64, :].rearrange("p (h b t) -> p h b t", h=4, b=2)[:, :, 0, :]),
                        reads=[rb[3]], writes=[k.r_nns])
                    nsrc = lambda h: k.nns[:, h, :]
                    rn = [k.r_nns]
                yield
                for h in range(4):
                    sch.op("pe", lambda e, h=h, qi=qi: e.matmul(
                        out=banks[5][0:64, 256 + h * 64:256 + (h + 1) * 64], lhsT=C.identb[0:64, 0:64],
                        rhs=k.Q[qi][:, h, :], start=True, stop=False), reads=[k.r_Q[qi]], writes=[rb[5]])
                    sch.op("pe", lambda e, h=h, qi=qi, nsrc=nsrc: e.matmul(
                        out=banks[5][0:64, 256 + h * 64:256 + (h + 1) * 64], lhsT=nsrc(h), rhs=k.Q[qi][:, h, :],
                        start=False, stop=True), reads=rn + [k.r_Q[qi]], writes=[rb[5]])
                sch.op("act", lambda e, qi=qi: e.copy(
                    out=k.Q[1 - qi][:, :, :].rearrange("p h e -> p (h e)"), in_=banks[5][0:64, 256:512]),
                    reads=[rb[5]], writes=[k.r_Q[1 - qi]])
                qi = 1 - qi
                yield
            k.qfin = qi

        def epi_early(gc):
            blk, cc = divmod(gc, NCH)
            o = B[blk % 2]
            k = K_[gc % NK]
            cs = slice(cc * L, (cc + 1) * L)
            b7 = banks[7]
            i2 = gc % 2
            sch.op("pe", lambda e: e.matmul(out=b7[0:64, 0:256], lhsT=o.lin_g[:, cs], rhs=lup[0:64, 2, :],
                                            start=True, stop=True), reads=[o.r_ling, r_par], writes=[rb[7]])
            for pr in range(2):
                sch.op("pe", lambda e, pr=pr: e.matmul(out=b7[0:64, 256 + 2 * pr:256 + 2 * pr + 2],
                                                       lhsT=o.prodb[:, pr, cs], rhs=C.sel2b[:, :],
                                                       start=True, stop=True), reads=[o.r_prodb[pr]], writes=[rb[7]])
            sch.op("act", lambda e: e.copy(out=gsb[i2][:, :], in_=b7[0:64, 0:256]), reads=[rb[7]], writes=[r_gsb[i2]])
            sch.op("act", lambda e: e.copy(out=st7[i2][:, :, 0], in_=b7[0:64, 256:260]), reads=[rb[7]],
                   writes=[r_st7[i2]])
            for pr in range(2):
                sch.op("pool", lambda e, pr=pr: e.tensor_tensor(
                    out=bv[i2][:, 2 * pr:2 * pr + 2, :], in0=k.tok[:, pr, 2, :, :],
                    in1=st7[i2][:, 2 * pr:2 * pr + 2, :].broadcast_to([64, 2, 64]), op=ALU.mult),
                    reads=[k.r_tok[pr], r_st7[i2]], wacc=[r_bvv[i2]])

        def epi_gen(gc):
            gcs = slice(gc * L, (gc + 1) * L)
            i2 = gc % 2
            y_, ry_ = yv[i2], r_yv[i2]
            b7 = banks[7]
            sch.op("dve", lambda e: e.tensor_reduce(out=st[:, 0, :, :], in_=y_[:, :, :], axis=AX.X, op=ALU.add),
                   reads=[ry_], writes=[r_st])
            sch.op("pool", lambda e: e.tensor_tensor(out=ysq[:, :, :], in0=y_[:, :, :], in1=y_[:, :, :], op=ALU.mult),
                   reads=[ry_], writes=[r_ysq])
            yield
            sch.op("dve", lambda e: e.tensor_reduce(out=st[:, 1, :, :], in_=ysq[:, :, :], axis=AX.X, op=ALU.add),
                   reads=[r_ysq], writes=[r_st])
            yield
            sch.op("pool", lambda e: e.tensor_scalar(out=st[:, 2, :, :], in0=st[:, 0, :, :], scalar1=1.0 / 64,
                                                     scalar2=0.0, op0=ALU.mult, op1=ALU.add), reads=[r_st], writes=[r_st])
            sch.op("pool", lambda e: e.tensor_tensor(out=st[:, 3, :, :], in0=st[:, 2, :, :], in1=st[:, 2, :, :],
                                                     op=ALU.mult), reads=[r_st], writes=[r_st])
            yield
            sch.op("pool", lambda e: e.tensor_scalar(out=st[:, 4, :, :], in0=st[:, 1, :, :], scalar1=1.0 / 64,
                                                     scalar2=64e-5, op0=ALU.mult, op1=ALU.add), reads=[r_st],
                   writes=[r_st])
            sch.op("pool", lambda e: e.tensor_tensor(out=st[:, 4, :, :], in0=st[:, 4, :, :], in1=st[:, 3, :, :],
                                                     op=ALU.subtract), reads=[r_st], writes=[r_st])
            yield
            sch.op("act", lambda e: e.activation(out=st[:, 5, :, :], in_=st[:, 4, :, :], func=AF.Ln),
                   reads=[r_st], writes=[r_st])
            sch.op("act", lambda e: e.activation(out=st[:, 6, :, :], in_=st[:, 5, :, :], func=AF.Exp, scale=-0.5),
                   reads=[r_st], writes=[r_st])
            yield
            sch.op("dve", lambda e: e.tensor_tensor(out=ysq[:, :, :], in0=y_[:, :, :],
                                                    in1=st[:, 2, :, :].broadcast_to([64, 4, 64]), op=ALU.subtract),
                   reads=[ry_, r_st], writes=[r_ysq])
            yield
            sch.op("dve", lambda e: e.tensor_tensor(out=ysq[:, :, :], in0=ysq[:, :, :],
                                                    in1=st[:, 6, :, :].broadcast_to([64, 4, 64]), op=ALU.mult),
                   reads=[r_ysq, r_st], writes=[r_ysq])
            yield
            y2 = ysq[:, :, :].rearrange("p h e -> p (h e)")
            sch.op("pool", lambda e: e.tensor_tensor(out=y2, in0=y2, in1=lng[:, :], op=ALU.mult),
                   reads=[r_ysq, r_par], writes=[r_ysq])
            yield
            sch.op("pool", lambda e: e.tensor_tensor(out=y2, in0=y2, in1=lnb[:, :], op=ALU.add),
                   reads=[r_ysq, r_par], writes=[r_ysq])
            yield
            sch.op("pool", lambda e: e.tensor_tensor(out=ysq[:, :, :], in0=ysq[:, :, :], in1=bv[i2][:, :, :],
                                                     op=ALU.add), reads=[r_ysq, r_bvv[i2]], writes=[r_ysq])
            yield
            sch.op("pool", lambda e: e.tensor_tensor(out=yo[i2][:, :], in0=y2, in1=gsb[i2][:, :], op=ALU.mult),
                   reads=[r_ysq, r_gsb[i2]], writes=[r_yo2[i2]])
            yield

        def epi_tail(gc):
            gcs = slice(gc * L, (gc + 1) * L)
            i2 = gc % 2
            b7 = banks[7]
            for j in range(2):
                sch.op("pe", lambda e, j=j: e.transpose(out=b7[:, 384 + j * 64:384 + (j + 1) * 64],
                                                        in_=yo[i2][:, j * 128:(j + 1) * 128],
                                                        identity=C.identf[0:64, 0:64]), reads=[r_yo2[i2]],
                       writes=[rb[7]])
            sch.op("act", lambda e: e.copy(out=yrT[:, :, gcs], in_=b7[:, 384:512].rearrange("p (j t) -> p j t", j=2)),
                   reads=[rb[7]], wacc=[r_yrT])
            yield

        hi = []
        lo = []
        pq = []
        rr = [0]
        pqc = [0]
        PQX = globals().get("PQX_", 0)

        def pull(q, idx):
            try:
                next(q[idx][1])
                return True
            except StopIteration:
                q.pop(idx)
                return False

        def pump(n):
            for _ in range(n):
                if hi:
                    hm = globals().get("HIMODE_", 2)
                    if hm == 0:
                        rr[0] = (rr[0] + 1) % len(hi)
                        pull(hi, rr[0])
                    elif hm == 1:
                        pull(hi, 0)
                    else:
                        rr[0] = (rr[0] + 1) % 3
                        pull(hi, 0 if (rr[0] < 2 or len(hi) < 2) else len(hi) - 1)
                if lo:
                    pull(lo, 0)
                pqc[0] += 1
                npq = 1 + (1 if (PQX and pqc[0] % PQX == 0) else 0)
                PQS = globals().get("PQS_", 4)
                if PQS and pqc[0] % PQS == 0:
                    npq = 0
                for _q in range(npq):
                    if pq:
                        pull(pq, 0)

        def drain_tag(q, pred):
            i = 0
            while i < len(q):
                if pred(q[i][0]):
                    while pull(q, i):
                        pass
                else:
                    i += 1

        def chain(gc):
            blk, cc = divmod(gc, NCH)
            o = B[blk % 2]
            k = K_[gc % NK]
            cs = slice(cc * L, (cc + 1) * L)
            Q = k.Q[k.qfin]
            rQ = k.r_Q[k.qfin]
            wq = banks[5]
            for h in range(4):
                pr, hf = h // 2, h % 2
                sch.op("pe", lambda e, h=h, pr=pr: e.matmul(
                    out=wq[0:64, h * 64:(h + 1) * 64], lhsT=o.ARz[h][:, 0, cs], rhs=Hb[:, pr, :],
                    start=True, stop=False), reads=[o.r_AR[pr], r_Hb], writes=[rb[5]])
                sch.op("pe", lambda e, h=h, pr=pr, hf=hf: e.matmul(
                    out=wq[0:64, h * 64:(h + 1) * 64], lhsT=k.scs[:, h, 2, :], rhs=k.tok[:, pr, 2, hf, :],
                    start=False, stop=True), reads=[k.r_scs[pr], k.r_tok[pr]], writes=[rb[5]])
            sch.op("act", lambda e: e.copy(out=Wt[:, :, :].rearrange("p h e -> p (h e)"), in_=wq[0:64, 0:256]),
                   reads=[rb[5]], writes=[r_Wt])
            pump(globals().get('PUMPS_', (3, 4, 5, 0))[0])
            for h in range(4):
                sch.op("pe", lambda e, h=h: e.matmul(out=wq[0:64, h * 64:(h + 1) * 64], lhsT=Q[:, h, :],
                                                     rhs=Wt[:, h, :], start=True, stop=True),
                       reads=[rQ, r_Wt], writes=[rb[5]])
            sch.op("act", lambda e: e.copy(out=Ut[:, :, :].rearrange("p h e -> p (h e)"), in_=wq[0:64, 0:256]),
                   reads=[rb[5]], writes=[r_Ut])
            pump(globals().get('PUMPS_', (3, 4, 5, 0))[1])
            b6 = banks[6]
            for h in range(4):
                pr, hf = h // 2, h % 2
                oo = b6[:, 256 + h * 64:256 + (h + 1) * 64]
                sch.op("pe", lambda e, oo=oo, h=h, pr=pr: e.matmul(
                    out=oo, lhsT=k.tok[:, pr, 0, :, :].rearrange("p a e -> p (a e)"), rhs=Ut[:, h, :],
                    start=True, stop=False), reads=[k.r_tok[pr], r_Ut], writes=[rb[6]])
                sch.op("pe", lambda e, oo=oo, pr=pr, hf=hf: e.matmul(
                    out=oo, lhsT=k.tok[:, pr, 1, :, :].rearrange("p a e -> p (a e)"), rhs=k.tok[:, pr, 2, hf, :],
                    start=False, stop=True), reads=[k.r_tok[pr]], writes=[rb[6]])
            for h in range(4):
                pr, hf = h // 2, h % 2
                oo = b6[0:64, h * 64:(h + 1) * 64]
                sch.op("pe", lambda e, oo=oo, h=h, pr=pr: e.matmul(
                    out=oo, lhsT=o.ARz[h][:, 1, cs], rhs=Hb[:, pr, :], start=True, stop=False),
                    reads=[o.r_AR[pr], r_Hb], writes=[rb[6]])
                sch.op("pe", lambda e, oo=oo, h=h, pr=pr: e.matmul(
                    out=oo, lhsT=k.scs[:, h, 1, :], rhs=Ut[:, h, :], start=False, stop=False),
                    reads=[k.r_scs[pr], r_Ut], writes=[rb[6]])
                sch.op("pe", lambda e, oo=oo, h=h, pr=pr, hf=hf: e.matmul(
                    out=oo, lhsT=k.scs[:, h, 3, :], rhs=k.tok[:, pr, 2, hf, :], start=False, stop=True),
                    reads=[k.r_scs[pr], k.r_tok[pr]], writes=[rb[6]])
            for h in range(4):
                pr, pb = h // 2, (h % 2) * 64
                sch.op("dve", lambda e, h=h, pr=pr, pb=pb: e.scalar_tensor_tensor(
                    out=Hs[pb:pb + 64, pr, :], in0=Hs[pb:pb + 64, pr, :],
                    scalar=o.E1[pr][pb:pb + 64, cc * L + L - 1:cc * L + L],
                    in1=b6[pb:pb + 64, 256 + h * 64:256 + (h + 1) * 64], op0=ALU.mult, op1=ALU.add),
                    reads=[r_Hs, o.r_E1[pr], rb[6]], wacc=[r_Hs])
            sch.op("act", lambda e: e.copy(out=Hb[:, :, :], in_=Hs[:, :, :]), reads=[r_Hs], writes=[r_Hb])
            y_, ry_ = yv[gc % 2], r_yv[gc % 2]
            sch.op("act", lambda e: e.copy(out=y_[:, :, :].rearrange("p h e -> p (h e)"), in_=b6[0:64, 0:256]),
                   reads=[rb[6]], writes=[ry_])
            pump(globals().get('PUMPS_', (3, 4, 5, 0))[2])

        NG = NBLK * NCH
        if mode == "chunks":
            load_block(0)
        else:
            for _ in prep_gen(0):
                pass
        for _ in pre_gen(0):
            pass
        hi.append((1, pre_gen(1)))
        for gc in range(NG):
            blk, cc = divmod(gc, NCH)
            drain_tag(lo, lambda t: t[0] == "epi" and t[1] <= gc - 2)
            if gc + 2 < NG:
                if (gc + 2) // NCH != (gc + 1) // NCH or (gc + 2) % NCH == 0:
                    drain_tag(pq, lambda t: True)
                hi.append((gc + 2, pre_gen(gc + 2)))
            chain(gc)
            epi_early(gc)
            lo.append((("epi", gc), epi_gen(gc)))
            if cc == 0 and blk + 1 < NBLK:
                if mode == "chunks":
                    load_block(blk + 1)
                else:
                    pq.append((("prep", blk + 1), prep_gen(blk + 1)))
            pump(globals().get('PUMPS_', (3, 4, 5, 0))[3])
            drain_tag(hi, lambda t: t == gc + 1)
            if gc >= 1:
                drain_tag(lo, lambda t: t[0] == "epi" and t[1] <= gc - 1)
                for _ in epi_tail(gc - 1):
                    pass
        drain_tag(lo, lambda t: True)
        for _ in epi_tail(NG - 1):
            pass
        for j in range(2):
            sch.dma("sp", d["mixT"][256 + j * 128:256 + (j + 1) * 128, :], yrT[:, j, :], reads=[r_yrT],
                    wacc=[C.r_mixT])


def phase_RC(C, l):
    nc, sch, d = C.nc, C.sch, C.d
    T = 256
    L = 64
    NCH = T // L
    NBLK = S // T
    NEG = -0.6065306597126334
    with ExitStack() as es:
        def sb(name, shape, dt):
            return _sb(es, nc, "R_" + name, shape, dt)
        mu = sb("mu", [128, 7], F32)
        omu = sb("omu", [128, 7], F32)
        pv = sb("pv", [128, 6, 2], F32)
        lup_f = sb("lupf", [64, 3, 256], F32)
        lup = sb("lup", [64, 3, 256], BF16)
        lng = sb("lng", [64, 256], F32)
        lnb = sb("lnb", [64, 256], F32)
        r_par, r_lup = Res(), Res()
        praw = sb("praw", [128, 1 + T], F32)
        r_praw = Res()
        pm_r = sb("pm_r", [128, 2, T], F32)
        pm_k = sb("pm_k", [128, 2, T], F32)
        pm_l = sb("pm_l", [128, T], F32)
        lin = sb("lin", [128, T], BF16)
        lin_a = sb("lin_a", [32, T], BF16)
        r_lin, r_lina = Res(), Res()
        tmp = sb("tmp", [128, T], F32)
        lw = sb("lw", [128, T], F32)
        aa = sb("aa", [128, T], F32)
        kx = sb("kx", [128, T], F32)
        sq = sb("sq", [128, T], F32)
        kp = sb("kp", [128, T], F32)
        bb = sb("bb", [128, T], F32)
        gi = sb("gi", [128, T], F32)
        ge = sb("ge", [128, T], F32)
        E2 = sb("E2", [128, T], F32)
        r_tmp, r_lw, r_aa, r_kx, r_sq, r_kp, r_bb, r_gi, r_ge, r_E2 = (Res() for _ in range(10))
        Gp = [sb("Gp%d" % i, [128, 1 + T], F32) for i in range(2)]
        r_Gp = [Res(), Res()]
        B = []
        for i in range(2):
            o = Ctx()
            o.ARz = [sb("ARz%d_%d" % (i, h), [128, 2, T], BF16) for h in range(4)]
            o.Bt = sb("Bt%d" % i, [128, 2, T], BF16)
            o.Kt = sb("Kt%d" % i, [128, 2, T], BF16)
            o.Bh = sb("Bh%d" % i, [128, 2, T], F32)
            o.Kh = sb("Kh%d" % i, [128, 2, T], F32)
            o.pm_v = sb("pmv%d" % i, [128, 2, T], F32)
            o.prodb = sb("prodb%d" % i, [128, 2, T], BF16)
            o.lin_g = sb("ling%d" % i, [64, T], BF16)
            o.E1 = [sb("E1_%d_%d" % (i, p), [128, T], F32) for p in range(2)]
            o.r_AR, o.r_Bt, o.r_Kt, o.r_Bh, o.r_Kh, o.r_v, o.r_prodb, o.r_E1 = ([Res(), Res()] for _ in range(8))
            o.r_ling = Res()
            B.append(o)
        NK = 3
        K_ = []
        for i in range(NK):
            o = Ctx()
            o.tok = sb("tok%d" % i, [64, 2, 4, 2, 64], BF16)
            o.scs = sb("scs%d" % i, [64, 4, 4, 64], BF16)
            o.nns = sb("nns%d" % i, [64, 4, 64], BF16)
            o.pws = [sb("pws%d_%d" % (i, j), [64, 4, 2, 64], BF16) for j in range(2)]
            o.Q = [sb("Q%d_%d" % (i, j), [64, 4, 64], BF16) for j in range(2)]
            o.r_tok = [Res(), Res()]
            o.r_scs = [Res(), Res()]
            o.r_nns = Res()
            o.r_pws = [Res() for _ in range(2)]
            o.r_Q = [Res(), Res()]
            K_.append(o)
        Hs = sb("Hs", [128, 2, 64], F32)
        Hb = sb("Hb", [128, 2, 64], BF16)
        r_Hs, r_Hb = Res(), Res()
        yrT = sb("yrT", [128, 2, S], BF16)
        r_yrT = Res()
        Wt = sb("Wt", [64, 4, 64], BF16)
        Ut = sb("Ut", [64, 4, 64], BF16)
        r_Wt, r_Ut = Res(), Res()
        yv = [sb("yv%d" % i, [64, 4, 64], F32) for i in range(2)]
        r_yv = [Res(), Res()]
        ysq = sb("ysq", [64, 4, 64], F32)
        bv = [sb("bv%d" % i, [64, 4, 64], F32) for i in range(2)]
        st7 = [sb("st7_%d" % i, [64, 4, 1], F32) for i in range(2)]
        gsb = [sb("gsb%d" % i, [64, 256], F32) for i in range(2)]
        r_bvv, r_st7, r_gsb = [Res(), Res()], [Res(), Res()], [Res(), Res()]
        yo = [sb("yo%d" % i, [64, 256], F32) for i in range(2)]
        r_yo2 = [Res(), Res()]
        st = sb("st", [64, 8, 4, 1], F32)
        r_ysq, r_bv, r_yo, r_st = Res(), Res(), Res(), Res()
        phys = [_ps(es, nc, "RC_b%d" % i, [128, 512], F32) for i in range(8)]
        prs = [PRes() for _ in range(8)]
        amap = [0, 1, 1, 2, 3, 4, 4, 0]
        banks = [phys[i] for i in amap]
        rb = [prs[i] for i in amap]
        bkq, rbq = phys[3], prs[3]

        sch.dma("sp", mu[:, :], d["r_mu"][l], wacc=[r_par])
        for i, nm in enumerate(("r_w0", "r_a0", "r_k_k", "r_k_a", "r_r_k")):
            sch.dma("sp", pv[:, i, :], d[nm][l], wacc=[r_par])
        sch.op("pool", lambda e: e.memset(lup_f[:, :, :], 0.0), writes=[r_lup])
        sch.dma("sp", lup_f[0:32, 0, :], d["r_w_up"][l], writes=[r_lup])
        sch.dma("sp", lup_f[0:32, 1, :], d["r_a_up"][l], writes=[r_lup])
        sch.dma("sp", lup_f[0:64, 2, :], d["r_g_up"][l], writes=[r_lup])
        sch.dma("sp", lng[:, :], d["r_ln_g"][l], wacc=[r_par])
        sch.dma("sp", lnb[:, :], d["r_ln_b"][l], wacc=[r_par])
        sch.op("dve", lambda e: e.tensor_scalar(out=omu[:, :], in0=mu[:, :], scalar1=-1.0, scalar2=1.0,
                                                op0=ALU.mult, op1=ALU.add), reads=[r_par], writes=[r_par])
        sch.op("dve", lambda e: e.tensor_scalar(out=pv[:, 5, :], in0=pv[:, 3, :], scalar1=-1.0, scalar2=1.0,
                                                op0=ALU.mult, op1=ALU.add), reads=[r_par], writes=[r_par])
        sch.op("dve", lambda e: e.tensor_copy(out=lup[:, :, :], in_=lup_f[:, :, :]), reads=[r_par, r_lup],
               writes=[r_par])
        sch.op("pool", lambda e: e.memset(Hs[:, :, :], 0.0), writes=[r_Hs])
        sch.op("pool", lambda e: e.memset(Hb[:, :, :], 0.0), writes=[r_Hb])
        for pr in range(2):
            sch.op("pool", lambda e, pr=pr: e.memset(Gp[pr][:, :], 0.0), writes=[r_Gp[pr]])
        for i in range(2):
            for h in range(4):
                zs_ = slice((1 - h % 2) * 64, (1 - h % 2) * 64 + 64)
                sch.op("pool", lambda e, i=i, h=h, zs_=zs_: e.memset(B[i].ARz[h][zs_, :, :], 0.0),
                       wacc=[B[i].r_AR[h // 2]])
        sch.op("pool", lambda e: e.memset(praw[:, 0:1], 0.0), writes=[r_praw])

        def v3(ap2):
            return ap2.rearrange("p (c t) -> p c t", t=L)

        def prep_gen(blk):
            o = B[blk % 2]
            t0 = blk * T
            dests = [(pm_r, 0), (pm_r, 1), (pm_k, 0), (pm_k, 1), (o.pm_v, 0), (o.pm_v, 1), (None, 0)]
            r_pmr, r_pmk = [Res(), Res()], [Res(), Res()]
            r_pml = Res()
            rr_list = [r_pmr[0], r_pmr[1], r_pmk[0], r_pmk[1], o.r_v[0], o.r_v[1], r_pml]
            for rc in range(7):
                if blk == 0:
                    sch.dma("sp", praw[:, 1:1 + T], d["rT"][rc * 128:(rc + 1) * 128, 0:T],
                            reads=[C.r_rT], wacc=[r_praw])
                else:
                    sch.dma("sp", praw[:, :], d["rT"][rc * 128:(rc + 1) * 128, t0 - 1:t0 + T],
                            reads=[C.r_rT], writes=[r_praw])
                dt_, pi = dests[rc]
                ot = pm_l[:, :] if dt_ is None else dt_[:, pi, :]
                sch.op("pool", lambda e, rc=rc: e.tensor_scalar(out=tmp[:, :], in0=praw[:, 1:1 + T],
                                                                scalar1=omu[:, rc:rc + 1], scalar2=0.0, op0=ALU.mult, op1=ALU.add),
                       reads=[r_praw, r_par], writes=[r_tmp])
                yield
                sch.op("dve", lambda e, rc=rc, ot=ot: e.scalar_tensor_tensor(
                    out=ot, in0=praw[:, 0:T], scalar=mu[:, rc:rc + 1], in1=tmp[:, :], op0=ALU.mult, op1=ALU.add),
                    reads=[r_praw, r_tmp, r_par], writes=[rr_list[rc]])
                yield
            sch.op("act", lambda e: e.activation(out=lin[0:32, :], in_=pm_l[0:32, :], func=AF.Tanh),
                   reads=[r_pml], wacc=[r_lin])
            sch.op("act", lambda e: e.copy(out=lin[32:64, :], in_=pm_l[32:64, :]), reads=[r_pml], wacc=[r_lin])
            sch.op("act", lambda e: e.activation(out=lin[64:128, :], in_=pm_l[64:128, :], func=AF.Sigmoid),
                   reads=[r_pml], wacc=[r_lin])
            sch.dma("sp", lin_a[:, :], lin[32:64, :], reads=[r_lin], writes=[r_lina])
            sch.dma("sp", o.lin_g[:, :], lin[64:128, :], reads=[r_lin], writes=[o.r_ling])
            yield
            pb7 = banks[7]
            for pr in range(2):
                rk, rr_ = r_pmk[pr], r_pmr[pr]
                pcs = slice(pr * 128, (pr + 1) * 128)
                sch.op("pe", lambda e, pcs=pcs: e.matmul(out=pb7[:, 0:T], lhsT=lup[0:32, 0, pcs], rhs=lin[0:32, :],
                                                         start=True, stop=True), reads=[r_lin, r_par], writes=[rb[7]])
                sch.op("act", lambda e, pr=pr: e.activation(out=lw[:, :], in_=pb7[:, 0:T], func=AF.Sigmoid,
                                                            bias=pv[:, 0, pr:pr + 1], scale=1.0),
                       reads=[rb[7], r_par], writes=[r_lw])
                yield
                sch.op("pe", lambda e, pcs=pcs: e.matmul(out=pb7[:, 0:T], lhsT=lup[0:32, 1, pcs], rhs=lin_a[:, :],
                                                         start=True, stop=True), reads=[r_lina, r_par], writes=[rb[7]])
                sch.op("act", lambda e, pr=pr: e.activation(out=aa[:, :], in_=pb7[:, 0:T], func=AF.Sigmoid,
                                                            bias=pv[:, 1, pr:pr + 1], scale=1.0),
                       reads=[rb[7], r_par], writes=[r_aa])
                yield
                sch.op("pool", lambda e: e.tensor_scalar(out=lw[:, :], in0=lw[:, :], scalar1=NEG, scalar2=0.0,
                                                         op0=ALU.mult, op1=ALU.add), reads=[r_lw], writes=[r_lw])
                sch.op("dve", lambda e, pr=pr: e.tensor_scalar(out=kx[:, :], in0=pm_k[:, pr, :],
                                                               scalar1=pv[:, 2, pr:pr + 1], scalar2=None,
                                                               op0=ALU.mult), reads=[rk, r_par], writes=[r_kx])
                yield
                sch.op("pool", lambda e: e.tensor_tensor(out=sq[:, :], in0=kx[:, :], in1=kx[:, :], op=ALU.mult),
                       reads=[r_kx], writes=[r_sq])
                sch.op("pe", lambda e: e.matmul(out=pb7[:, 0:T], lhsT=C.blkf[:, :], rhs=sq[:, :],
                                                start=True, stop=True), reads=[r_sq], writes=[rb[7]])
                sch.op("act", lambda e: e.activation(out=tmp[:, :], in_=pb7[:, 0:T], func=AF.Sqrt),
                       reads=[rb[7]], writes=[r_tmp])
                yield
                sch.op("dve", lambda e: e.tensor_scalar(out=tmp[:, :], in0=tmp[:, :], scalar1=1e-12, scalar2=None,
                                                        op0=ALU.max), reads=[r_tmp], writes=[r_tmp])
                yield
                sch.op("dve", lambda e: e.reciprocal(out=tmp[:, :], in_=tmp[:, :]), reads=[r_tmp], writes=[r_tmp])
                yield
                sch.op("dve", lambda e: e.tensor_tensor(out=kx[:, :], in0=kx[:, :], in1=tmp[:, :], op=ALU.mult),
                       reads=[r_kx, r_tmp], writes=[r_kx])
                yield
                sch.op("pool", lambda e, pr=pr: e.tensor_scalar(out=kp[:, :], in0=aa[:, :], scalar1=pv[:, 3, pr:pr + 1],
                                                                scalar2=pv[:, 5, pr:pr + 1], op0=ALU.mult, op1=ALU.add),
                       reads=[r_aa, r_par], writes=[r_kp])
                yield
                sch.op("pool", lambda e, pr=pr: e.tensor_tensor(out=kp[:, :], in0=kp[:, :], in1=pm_k[:, pr, :],
                                                                op=ALU.mult), reads=[r_kp, rk], writes=[r_kp])
                yield
                sch.op("pool", lambda e: e.tensor_tensor(out=bb[:, :], in0=kx[:, :], in1=aa[:, :], op=ALU.mult),
                       reads=[r_kx, r_aa], writes=[r_bb])
                yield
                sch.op("dve", lambda e, pr=pr: e.scalar_tensor_tensor(
                    out=o.prodb[:, pr, :], in0=pm_r[:, pr, :], scalar=pv[:, 4, pr:pr + 1], in1=kp[:, :],
                    op0=ALU.mult, op1=ALU.mult), reads=[rr_, r_kp, r_par], writes=[o.r_prodb[pr]])
                yield
                G = Gp[pr]
                sch.op("dve", lambda e, G=G: e.tensor_copy(out=G[:, 0:1], in_=G[:, T:T + 1]),
                       reads=[r_Gp[pr]], writes=[r_Gp[pr]])
                sch.op("dve", lambda e, G=G: e.tensor_tensor_scan(out=G[:, 1:1 + T], data0=lw[:, :], data1=lw[:, :],
                                                                  initial=G[:, 0:1], op0=ALU.add, op1=ALU.min),
                       reads=[r_lw, r_Gp[pr]], writes=[r_Gp[pr]])
                yield
                base = v3(G[:, 0:T])[:, :, 0:1].broadcast_to([128, NCH, L])
                sch.op("dve", lambda e, G=G, base=base: e.tensor_tensor(out=v3(gi[:, :]), in0=v3(G[:, 1:1 + T]),
                                                                        in1=base, op=ALU.subtract),
                       reads=[r_Gp[pr]], writes=[r_gi])
                yield
                sch.op("pool", lambda e: e.tensor_tensor(out=ge[:, :], in0=gi[:, :], in1=lw[:, :], op=ALU.subtract),
                       reads=[r_gi, r_lw], writes=[r_ge])
                e1 = o.E1[pr]
                sch.op("act", lambda e, e1=e1: e.activation(out=e1[:, :], in_=gi[:, :], func=AF.Exp),
                       reads=[r_gi], writes=[o.r_E1[pr]])
                yield
                sch.op("act", lambda e: e.activation(out=E2[:, :], in_=ge[:, :], func=AF.Exp),
                       reads=[r_ge], writes=[r_E2])
                yield
                for hf in range(2):
                    ps_ = slice(hf * 64, hf * 64 + 64)
                    hq = 2 * pr + hf
                    sch.op("dve", lambda e, hq=hq, ps_=ps_: e.scalar_tensor_tensor(
                        out=o.ARz[hq][ps_, 0, :], in0=kx[ps_, :], scalar=-1.0, in1=E2[ps_, :],
                        op0=ALU.mult, op1=ALU.mult), reads=[r_kx, r_E2], wacc=[o.r_AR[pr]])
                    sch.op("pool", lambda e, hq=hq, ps_=ps_, pr=pr, e1=e1: e.tensor_tensor(
                        out=o.ARz[hq][ps_, 1, :], in0=pm_r[ps_, pr, :], in1=e1[ps_, :], op=ALU.mult),
                        reads=[rr_, o.r_E1[pr]], wacc=[o.r_AR[pr]])
                    yield
                sch.op("act", lambda e: e.activation(out=E2[:, :], in_=gi[:, :], func=AF.Exp, scale=-1.0),
                       reads=[r_gi], writes=[r_E2])
                yield
                sch.op("dve", lambda e, pr=pr: e.tensor_tensor(out=o.Bt[:, pr, :], in0=bb[:, :], in1=E2[:, :],
                                                               op=ALU.mult), reads=[r_bb, r_E2], writes=[o.r_Bt[pr]])
                sch.op("pool", lambda e, pr=pr: e.tensor_tensor(out=o.Kt[:, pr, :], in0=kp[:, :], in1=E2[:, :],
                                                                op=ALU.mult), reads=[r_kp, r_E2], writes=[o.r_Kt[pr]])
                yield
                gend = v3(gi[:, :])[:, :, L - 1:L].broadcast_to([128, NCH, L])
                sch.op("dve", lambda e, gend=gend: e.tensor_tensor(out=v3(ge[:, :]), in0=gend, in1=v3(gi[:, :]),
                                                                   op=ALU.subtract), reads=[r_gi], writes=[r_ge])
                yield
                sch.op("act", lambda e: e.activation(out=ge[:, :], in_=ge[:, :], func=AF.Exp),
                       reads=[r_ge], writes=[r_ge])
                yield
                sch.op("dve", lambda e, pr=pr: e.tensor_tensor(out=o.Bh[:, pr, :], in0=bb[:, :], in1=ge[:, :],
                                                               op=ALU.mult), reads=[r_bb, r_ge], writes=[o.r_Bh[pr]])
                sch.op("pool", lambda e, pr=pr: e.tensor_tensor(out=o.Kh[:, pr, :], in0=kp[:, :], in1=ge[:, :],
                                                                op=ALU.mult), reads=[r_kp, r_ge], writes=[o.r_Kh[pr]])
                yield

        def pre_gen(gc):
            blk, cc = divmod(gc, NCH)
            o = B[blk % 2]
            k = K_[gc % NK]
            cs = slice(cc * L, (cc + 1) * L)
            for pr in range(2):
                for j, (src, rs) in enumerate(((o.Bh, o.r_Bh[pr]), (o.Kh, o.r_Kh[pr]), (o.pm_v, o.r_v[pr]))):
                    sch.op("pe", lambda e, j=j, src=src, pr=pr: e.transpose(
                        out=banks[0][0:64, j * 128:(j + 1) * 128], in_=src[:, pr, cs], identity=C.identf[:, :]),
                        reads=[rs], writes=[rb[0]])
                sch.op("act", lambda e, pr=pr: e.copy(
                    out=k.tok[:, pr, 0:3, :, :],
                    in_=banks[0][0:64, 0:384].rearrange("p (j h e) -> p j h e", j=3, h=2)),
                    reads=[rb[0]], writes=[k.r_tok[pr]])
                yield
            for h in range(4):
                pr = h // 2
                bk = banks[1 + pr]
                oo = (h % 2) * 256
                rhs_ar = o.ARz[h][:, :, cs]
                sch.op("pe", lambda e, bk=bk, oo=oo, pr=pr, rhs_ar=rhs_ar: e.matmul(
                    out=bk[0:64, oo:oo + 128], lhsT=o.Bt[:, pr, cs], rhs=rhs_ar, start=True, stop=True),
                    reads=[o.r_Bt[pr], o.r_AR[pr]], writes=[rb[1 + pr]])
                sch.op("pe", lambda e, bk=bk, oo=oo, pr=pr, rhs_ar=rhs_ar: e.matmul(
                    out=bk[0:64, oo + 128:oo + 256], lhsT=o.Kt[:, pr, cs], rhs=rhs_ar, start=True, stop=True),
                    reads=[o.r_Kt[pr], o.r_AR[pr]], writes=[rb[1 + pr]])
                sch.op("pe", lambda e, h=h, pr=pr: e.matmul(
                    out=banks[4][0:64, h * 64:(h + 1) * 64], lhsT=o.ARz[h][:, 0, cs],
                    rhs=o.Bt[:, pr, cs], start=True, stop=True),
                    reads=[o.r_Bt[pr], o.r_AR[pr]], writes=[rb[4]])
                if h % 2 == 1:
                    sch.op("dve", lambda e, pr=pr: e.tensor_tensor(
                        out=k.scs[:, 2 * pr:2 * pr + 2, :, :].rearrange("p h b t -> p (h b t)"),
                        in0=banks[1 + pr][0:64, :], in1=C.rwmask[:, :], op=ALU.mult),
                        reads=[rb[1 + pr]], writes=[k.r_scs[pr]])
            sch.op("dve", lambda e: e.tensor_tensor(out=k.nns[:, :, :].rearrange("p h t -> p (h t)"),
                                                    in0=banks[4][0:64, 0:256], in1=C.nmask[:, :], op=ALU.mult),
                   reads=[rb[4]], writes=[k.r_nns])
            yield

            def Mlev(lev, h):
                return k.scs[:, h, 0, :] if lev == 0 else k.pws[(lev - 1) % 2][:, h, 1, :]

            def Nlev(lev, h):
                return k.nns[:, h, :] if lev == 0 else k.pws[(lev - 1) % 2][:, h, 0, :]

            def Rlev(lev, h):
                return [k.r_scs[h // 2], k.r_nns] if lev == 0 else [k.r_pws[(lev - 1) % 2]]
            sch.op("pool", lambda e: e.tensor_tensor(
                out=k.Q[0][:, :, :], in0=k.scs[:, :, 0, :],
                in1=C.identf[0:64, 0:64].unsqueeze(1).broadcast_to([64, 4, 64]), op=ALU.add),
                reads=[k.r_scs[0], k.r_scs[1]], writes=[k.r_Q[0]])
            yield
            qi = 0
            for lev in range(1, 6):
                for h in range(4):
                    sch.op("pe", lambda e, lev=lev, h=h: e.matmul(
                        out=banks[3][0:64, h * 128:h * 128 + 64], lhsT=Mlev(lev - 1, h), rhs=Nlev(lev - 1, h),
                        start=True, stop=True), reads=Rlev(lev - 1, h), writes=[rb[3]])
                    if lev < 5:
                        sch.op("pe", lambda e, lev=lev, h=h: e.matmul(
                            out=banks[3][0:64, h * 128 + 64:h * 128 + 128], lhsT=Nlev(lev - 1, h),
                            rhs=Mlev(lev - 1, h), start=True, stop=True), reads=Rlev(lev - 1, h), writes=[rb[3]])
                if lev < 5:
                    evac(C, lev, k.pws[(lev - 1) % 2][:, :, :, :].rearrange("p h b t -> p (h b t)"),
                         banks[3][0:64, :], [rb[3]], writes=[k.r_pws[(lev - 1) % 2]])
                    nsrc = lambda h, lev=lev: k.pws[(lev - 1) % 2][:, h, 0, :]
                    rn = [k.r_pws[(lev - 1) % 2]]
                else:
                    sch.op("act", lambda e: e.copy(
                        out=k.nns[:, :, :], in_=banks[3][0:64, :].rearrange("p (h b t) -> p h b t", h=4, b=2)[:, :, 0, :]),
                        reads=[rb[3]], writes=[k.r_nns])
                    nsrc = lambda h: k.nns[:, h, :]
                    rn = [k.r_nns]
                yield
                for h in range(4):
                    sch.op("pe", lambda e, h=h, qi=qi, nsrc=nsrc: e.matmul(
                        out=bkq[0:64, 256 + h * 64:256 + (h + 1) * 64], lhsT=nsrc(h), rhs=k.Q[qi][:, h, :],
                        start=True, stop=True), reads=rn + [k.r_Q[qi]], writes=[rbq])
                sch.op("dve", lambda e, qi=qi: e.tensor_tensor(
                    out=k.Q[1 - qi][:, :, :].rearrange("p h e -> p (h e)"), in0=bkq[0:64, 256:512],
                    in1=k.Q[qi][:, :, :].rearrange("p h e -> p (h e)"), op=ALU.add),
                    reads=[rbq, k.r_Q[qi]], writes=[k.r_Q[1 - qi]])
                qi = 1 - qi
                yield
            k.qfin = qi

        def epi_early(gc):
            blk, cc = divmod(gc, NCH)
            o = B[blk % 2]
            k = K_[gc % NK]
            cs = slice(cc * L, (cc + 1) * L)
            b7 = banks[7]
            i2 = gc % 2
            sch.op("pe", lambda e: e.matmul(out=b7[0:64, 0:256], lhsT=o.lin_g[:, cs], rhs=lup[0:64, 2, :],
                                            start=True, stop=True), reads=[o.r_ling, r_par], writes=[rb[7]])
            for pr in range(2):
                sch.op("pe", lambda e, pr=pr: e.matmul(out=b7[0:64, 256 + 2 * pr:256 + 2 * pr + 2],
                                                       lhsT=o.prodb[:, pr, cs], rhs=C.sel2b[:, :],
                                                       start=True, stop=True), reads=[o.r_prodb[pr]], writes=[rb[7]])
            sch.op("act", lambda e: e.copy(out=gsb[i2][:, :], in_=b7[0:64, 0:256]), reads=[rb[7]], writes=[r_gsb[i2]])
            sch.op("act", lambda e: e.copy(out=st7[i2][:, :, 0], in_=b7[0:64, 256:260]), reads=[rb[7]],
                   writes=[r_st7[i2]])
            for pr in range(2):
                sch.op("pool", lambda e, pr=pr: e.tensor_tensor(
                    out=bv[i2][:, 2 * pr:2 * pr + 2, :], in0=k.tok[:, pr, 2, :, :],
                    in1=st7[i2][:, 2 * pr:2 * pr + 2, :].broadcast_to([64, 2, 64]), op=ALU.mult),
                    reads=[k.r_tok[pr], r_st7[i2]], wacc=[r_bvv[i2]])

        def epi_gen(gc):
            gcs = slice(gc * L, (gc + 1) * L)
            i2 = gc % 2
            y_, ry_ = yv[i2], r_yv[i2]
            b7 = banks[7]
            sch.op("dve", lambda e: e.tensor_reduce(out=st[:, 0, :, :], in_=y_[:, :, :], axis=AX.X, op=ALU.add),
                   reads=[ry_], writes=[r_st])
            sch.op("pool", lambda e: e.tensor_tensor(out=ysq[:, :, :], in0=y_[:, :, :], in1=y_[:, :, :], op=ALU.mult),
                   reads=[ry_], writes=[r_ysq])
            yield
            sch.op("dve", lambda e: e.tensor_reduce(out=st[:, 1, :, :], in_=ysq[:, :, :], axis=AX.X, op=ALU.add),
                   reads=[r_ysq], writes=[r_st])
            yield
            sch.op("dve", lambda e: e.tensor_scalar(out=st[:, 2, :, :], in0=st[:, 0, :, :], scalar1=1.0 / 64,
                                                    scalar2=None, op0=ALU.mult), reads=[r_st], writes=[r_st])
            sch.op("dve", lambda e: e.tensor_tensor(out=st[:, 3, :, :], in0=st[:, 2, :, :], in1=st[:, 2, :, :],
                                                    op=ALU.mult), reads=[r_st], writes=[r_st])
            yield
            sch.op("dve", lambda e: e.scalar_tensor_tensor(out=st[:, 4, :, :], in0=st[:, 1, :, :], scalar=1.0 / 64,
                                                           in1=st[:, 3, :, :], op0=ALU.mult, op1=ALU.subtract),
                   reads=[r_st], writes=[r_st])
            sch.op("dve", lambda e: e.tensor_scalar(out=st[:, 4, :, :], in0=st[:, 4, :, :], scalar1=64e-5,
                                                    scalar2=None, op0=ALU.add), reads=[r_st], writes=[r_st])
            yield
            sch.op("act", lambda e: e.activation(out=st[:, 5, :, :], in_=st[:, 4, :, :], func=AF.Sqrt),
                   reads=[r_st], writes=[r_st])
            sch.op("dve", lambda e: e.reciprocal(out=st[:, 6, :, :], in_=st[:, 5, :, :]), reads=[r_st], writes=[r_st])
            yield
            sch.op("dve", lambda e: e.tensor_tensor(out=ysq[:, :, :], in0=y_[:, :, :],
                                                    in1=st[:, 2, :, :].broadcast_to([64, 4, 64]), op=ALU.subtract),
                   reads=[ry_, r_st], writes=[r_ysq])
            yield
            sch.op("dve", lambda e: e.tensor_tensor(out=ysq[:, :, :], in0=ysq[:, :, :],
                                                    in1=st[:, 6, :, :].broadcast_to([64, 4, 64]), op=ALU.mult),
                   reads=[r_ysq, r_st], writes=[r_ysq])
            yield
            y2 = ysq[:, :, :].rearrange("p h e -> p (h e)")
            sch.op("pool", lambda e: e.tensor_tensor(out=y2, in0=y2, in1=lng[:, :], op=ALU.mult),
                   reads=[r_ysq, r_par], writes=[r_ysq])
            yield
            sch.op("pool", lambda e: e.tensor_tensor(out=y2, in0=y2, in1=lnb[:, :], op=ALU.add),
                   reads=[r_ysq, r_par], writes=[r_ysq])
            yield
            sch.op("pool", lambda e: e.tensor_tensor(out=ysq[:, :, :], in0=ysq[:, :, :], in1=bv[i2][:, :, :],
                                                     op=ALU.add), reads=[r_ysq, r_bvv[i2]], writes=[r_ysq])
            yield
            sch.op("dve", lambda e: e.tensor_tensor(out=yo[i2][:, :], in0=y2, in1=gsb[i2][:, :], op=ALU.mult),
                   reads=[r_ysq, r_gsb[i2]], writes=[r_yo2[i2]])
            yield

        def epi_tail(gc):
            gcs = slice(gc * L, (gc + 1) * L)
            i2 = gc % 2
            b7 = banks[7]
            for j in range(2):
                sch.op("pe", lambda e, j=j: e.transpose(out=b7[:, 384 + j * 64:384 + (j + 1) * 64],
                                                        in_=yo[i2][:, j * 128:(j + 1) * 128],
                                                        identity=C.identf[0:64, 0:64]), reads=[r_yo2[i2]],
                       writes=[rb[7]])
            sch.op("act", lambda e: e.copy(out=yrT[:, :, gcs], in_=b7[:, 384:512].rearrange("p (j t) -> p j t", j=2)),
                   reads=[rb[7]], wacc=[r_yrT])
            yield

        QB = 256
        lam_init = 0.8 - 0.6 * math.exp(-0.3 * l)

        def csb(name, shape, dt):
            return _sb(es, nc, "C_" + name, shape, dt)
        c_lq = csb("lq", [128, 4, 64], F32)
        c_lt = csb("lt", [128, 2, 64], F32)
        c_ls = csb("ls", [128, 4], F32)
        nlam = csb("nlam", [128, 1], F32)
        ag = csb("ag", [128, 4], F32)
        rc_par = Res()
        qTc = [csb("qT%d" % c, [128, S], BF16) for c in range(2)]
        kTc = csb("kT", [128, S], BF16)
        vtc = csb("vt", [128, NT, 128], BF16)
        rc_q, rc_k, rc_v = Res(), Res(), Res()
        ptc = [csb("pt%d" % i, [128, QB], BF16) for i in range(2)]
        rc_pt = [Res(), Res()]
        c_rl = csb("rl", [128, QB], F32)
        c_oc = [csb("oc%d" % i, [128, QB], F32) for i in range(2)]
        c_od = csb("od", [128, QB], F32)
        c_sq = csb("sq", [128, QB], F32)
        c_rs = csb("rs", [128, QB], F32)
        c_ya = csb("ya", [128, QB], BF16)
        rc_rl, rc_od, rc_sq, rc_rs, rc_ya = Res(), Res(), Res(), Res(), Res()
        rc_oc = [Res(), Res()]
        p_st = [phys[5], phys[6]]
        rp_st = [prs[5], prs[6]]
        p_o = phys[7][:, 0:QB]
        p_l = phys[7][:, 256:256 + QB]
        rp_ol = prs[7]
        p_ss = phys[5][:, 256:256 + QB]
        rp_ss = prs[5]

        def c_gen():
            for i, nm in enumerate(("a_lq1", "a_lk1", "a_lq2", "a_lk2")):
                sch.dma("sp", c_lq[:, i, :], d[nm][l], wacc=[rc_par])
            sch.dma("sp", ag[:, :], d["a_norm_g"][l], wacc=[rc_par])
            for i in range(2):
                sch.op("dve", lambda e, i=i: e.tensor_tensor(out=c_lt[:, i, :], in0=c_lq[:, 2 * i, :],
                                                             in1=c_lq[:, 2 * i + 1, :], op=ALU.mult),
                       reads=[rc_par], writes=[rc_par])
            sch.op("dve", lambda e: e.tensor_reduce(out=c_ls[:, 0:2], in_=c_lt[:, :, :], axis=AX.X, op=ALU.add),
                   reads=[rc_par], writes=[rc_par])
            sch.op("act", lambda e: e.activation(out=c_ls[:, 2:4], in_=c_ls[:, 0:2], func=AF.Exp),
                   reads=[rc_par], writes=[rc_par])
            sch.op("dve", lambda e: e.tensor_tensor(out=nlam[:, :], in0=c_ls[:, 3:4], in1=c_ls[:, 2:3], op=ALU.subtract),
                   reads=[rc_par], writes=[rc_par])
            sch.op("dve", lambda e: e.tensor_scalar(out=nlam[:, :], in0=nlam[:, :], scalar1=-lam_init, scalar2=None,
                                                    op0=ALU.add), reads=[rc_par], writes=[rc_par])
            sch.op("dve", lambda e: e.tensor_scalar(out=ag[:, :], in0=ag[:, :], scalar1=1.0 - lam_init, scalar2=None,
                                                    op0=ALU.mult), reads=[rc_par], writes=[rc_par])
            yield
            idx0 = 0
            for h in range(4):
                for c in range(2):
                    if h == 0:
                        sch.op("pool", lambda e, c=c: e.memset(qTc[c][(1 - c) * 64:(1 - c) * 64 + 64, :], 0.0),
                               wacc=[rc_q])
                    sch.dma("sp", qTc[c][c * 64:c * 64 + 64, :], d["aqkT"][h * 128 + c * 64:h * 128 + c * 64 + 64, :],
                            reads=[C.r_aqkT], wacc=[rc_q])
                sch.dma("sp", kTc[:, :], d["aqkT"][512 + h * 128:512 + (h + 1) * 128, :], reads=[C.r_aqkT],
                        writes=[rc_k])
                sch.dma("sp", vtc[:, :, :], d["av"][:, h * 128:(h + 1) * 128].rearrange("(t p) e -> p t e", p=128),
                        reads=[C.r_av], writes=[rc_v])
                yield
                NQ = QB // 128
                items = [(j, c, kt) for j in range(S // QB) for c in range(2) for kt in range(NQ * (j + 1))]

                def front(it, idx):
                    j, c, kt = it
                    n0 = max(0, kt - NQ * j) * 128
                    ps, rps = p_st[idx % 2], rp_st[idx % 2]
                    p_, rp_ = ptc[idx % 2], rc_pt[idx % 2]
                    sch.op("pe", lambda e: e.matmul(
                        out=ps[:, n0:QB], lhsT=kTc[:, kt * 128:(kt + 1) * 128],
                        rhs=qTc[c][:, j * QB + n0:(j + 1) * QB], start=True, stop=True),
                        reads=[rc_k, rc_q], writes=[rps])
                    sch.op("act", lambda e: e.activation(out=p_[:, n0:QB], in_=ps[:, n0:QB], func=AF.Exp, scale=0.125),
                           reads=[rps], writes=[rp_])
                    if kt >= NQ * j:
                        sch.op("pool", lambda e: e.tensor_tensor(out=p_[:, n0:n0 + 128], in0=p_[:, n0:n0 + 128],
                                                                 in1=C.trib[:, :], op=ALU.mult), reads=[rp_], writes=[rp_])

                def back(it, idx):
                    j, c, kt = it
                    nk = NQ * (j + 1)
                    n0 = max(0, kt - NQ * j) * 128
                    p_, rp_ = ptc[idx % 2], rc_pt[idx % 2]
                    sch.op("pe", lambda e: e.matmul(out=p_o[:, n0:QB], lhsT=vtc[:, kt, :], rhs=p_[:, n0:QB],
                                                    start=(kt == 0), stop=(kt == nk - 1)),
                           reads=[rc_v, rp_], writes=[rp_ol])
                    sch.op("pe", lambda e: e.matmul(out=p_l[:, n0:QB], lhsT=C.onesb[:, :], rhs=p_[:, n0:QB],
                                                    start=False, stop=(kt == nk - 1), skip_group_check=True),
                           reads=[rp_], writes=[rp_ol])
                    if kt != nk - 1:
                        return
                    sch.op("act", lambda e: e.activation(out=c_rl[:, :], in_=p_l, func=AF.Ln),
                           reads=[rp_ol], writes=[rc_rl])
                    sch.op("act", lambda e: e.activation(out=c_rl[:, :], in_=c_rl[:, :], func=AF.Exp, scale=-1.0),
                           reads=[rc_rl], writes=[rc_rl])
                    sch.op("dve", lambda e: e.tensor_tensor(out=c_oc[c][:, :], in0=p_o, in1=c_rl[:, :], op=ALU.mult),
                           reads=[rp_ol, rc_rl], writes=[rc_oc[c]])
                    if c == 0:
                        return
                    qs = slice(j * QB, (j + 1) * QB)
                    sch.op("dve", lambda e: e.scalar_tensor_tensor(out=c_od[:, :], in0=c_oc[1][:, :], scalar=nlam[:, 0:1],
                                                                   in1=c_oc[0][:, :], op0=ALU.mult, op1=ALU.add),
                           reads=[rc_oc[0], rc_oc[1], rc_par], writes=[rc_od])
                    sch.op("pool", lambda e: e.tensor_tensor(out=c_sq[:, :], in0=c_od[:, :], in1=c_od[:, :], op=ALU.mult),
                           reads=[rc_od], writes=[rc_sq])
                    sch.op("pe", lambda e: e.matmul(out=p_ss, lhsT=C.onesf[:, :], rhs=c_sq[:, :], start=True, stop=True),
                           reads=[rc_sq], writes=[rp_ss])
                    sch.op("dve", lambda e: e.tensor_scalar(out=c_rs[:, :], in0=p_ss, scalar1=1.0 / 128, scalar2=1e-5,
                                                            op0=ALU.mult, op1=ALU.add), reads=[rp_ss], writes=[rc_rs])
                    sch.op("act", lambda e: e.activation(out=c_rs[:, :], in_=c_rs[:, :], func=AF.Ln),
                           reads=[rc_rs], writes=[rc_rs])
                    sch.op("act", lambda e: e.activation(out=c_rs[:, :], in_=c_rs[:, :], func=AF.Exp, scale=-0.5),
                           reads=[rc_rs], writes=[rc_rs])
                    sch.op("dve", lambda e: e.scalar_tensor_tensor(out=c_ya[:, :], in0=c_od[:, :], scalar=ag[:, h:h + 1],
                                                                   in1=c_rs[:, :], op0=ALU.mult, op1=ALU.mult),
                           reads=[rc_od, rc_rs, rc_par], writes=[rc_ya])
                    sch.dma("sp", d["mixT"][512 + h * 128:512 + (h + 1) * 128, qs], c_ya[:, :], reads=[rc_ya],
                            wacc=[C.r_mixT])

                front(items[0], idx0)
                for i in range(len(items)):
                    if i + 1 < len(items):
                        front(items[i + 1], idx0 + i + 1)
                    back(items[i], idx0 + i)
                    yield
                idx0 += len(items)

        hi = []
        lo = []
        pq = []
        cq = [("attn", c_gen())] if not globals().get("NO_ATTN", False) else []
        CRATE = globals().get("CRATE_", 2)
        rr = [0]

        def pull(q, idx):
            try:
                next(q[idx][1])
                return True
            except StopIteration:
                q.pop(idx)
                return False

        def pump(n):
            for _ in range(n):
                if hi:
                    rr[0] = (rr[0] + 1) % len(hi)
                    pull(hi, rr[0])
                if lo:
                    pull(lo, 0)
                if pq:
                    pull(pq, 0)
                for _ in range(CRATE):
                    if cq:
                        pull(cq, 0)

        def drain_tag(q, pred):
            i = 0
            while i < len(q):
                if pred(q[i][0]):
                    while pull(q, i):
                        pass
                else:
                    i += 1

        def chain(gc):
            blk, cc = divmod(gc, NCH)
            o = B[blk % 2]
            k = K_[gc % NK]
            cs = slice(cc * L, (cc + 1) * L)
            Q = k.Q[k.qfin]
            rQ = k.r_Q[k.qfin]
            wq = banks[5]
            for h in range(4):
                pr, hf = h // 2, h % 2
                sch.op("pe", lambda e, h=h, pr=pr: e.matmul(
                    out=wq[0:64, h * 64:(h + 1) * 64], lhsT=o.ARz[h][:, 0, cs], rhs=Hb[:, pr, :],
                    start=True, stop=False), reads=[o.r_AR[pr], r_Hb], writes=[rb[5]])
                sch.op("pe", lambda e, h=h, pr=pr, hf=hf: e.matmul(
                    out=wq[0:64, h * 64:(h + 1) * 64], lhsT=k.scs[:, h, 2, :], rhs=k.tok[:, pr, 2, hf, :],
                    start=False, stop=True), reads=[k.r_scs[pr], k.r_tok[pr]], writes=[rb[5]])
            sch.op("dve", lambda e: e.tensor_copy(out=Wt[:, :, :].rearrange("p h e -> p (h e)"), in_=wq[0:64, 0:256]),
                   reads=[rb[5]], writes=[r_Wt])
            pump(6)
            for h in range(4):
                sch.op("pe", lambda e, h=h: e.matmul(out=wq[0:64, h * 64:(h + 1) * 64], lhsT=Q[:, h, :],
                                                     rhs=Wt[:, h, :], start=True, stop=True),
                       reads=[rQ, r_Wt], writes=[rb[5]])
            sch.op("dve", lambda e: e.tensor_copy(out=Ut[:, :, :].rearrange("p h e -> p (h e)"), in_=wq[0:64, 0:256]),
                   reads=[rb[5]], writes=[r_Ut])
            pump(6)
            b6 = banks[6]
            for h in range(4):
                pr, hf = h // 2, h % 2
                oo = b6[:, 256 + h * 64:256 + (h + 1) * 64]
                sch.op("pe", lambda e, oo=oo, h=h, pr=pr: e.matmul(
                    out=oo, lhsT=k.tok[:, pr, 0, :, :].rearrange("p a e -> p (a e)"), rhs=Ut[:, h, :],
                    start=True, stop=False), reads=[k.r_tok[pr], r_Ut], writes=[rb[6]])
                sch.op("pe", lambda e, oo=oo, pr=pr, hf=hf: e.matmul(
                    out=oo, lhsT=k.tok[:, pr, 1, :, :].rearrange("p a e -> p (a e)"), rhs=k.tok[:, pr, 2, hf, :],
                    start=False, stop=True), reads=[k.r_tok[pr]], writes=[rb[6]])
            for h in range(4):
                pr, hf = h // 2, h % 2
                oo = b6[0:64, h * 64:(h + 1) * 64]
                sch.op("pe", lambda e, oo=oo, h=h, pr=pr: e.matmul(
                    out=oo, lhsT=o.ARz[h][:, 1, cs], rhs=Hb[:, pr, :], start=True, stop=False),
                    reads=[o.r_AR[pr], r_Hb], writes=[rb[6]])
                sch.op("pe", lambda e, oo=oo, h=h, pr=pr: e.matmul(
                    out=oo, lhsT=k.scs[:, h, 1, :], rhs=Ut[:, h, :], start=False, stop=False),
                    reads=[k.r_scs[pr], r_Ut], writes=[rb[6]])
                sch.op("pe", lambda e, oo=oo, h=h, pr=pr, hf=hf: e.matmul(
                    out=oo, lhsT=k.scs[:, h, 3, :], rhs=k.tok[:, pr, 2, hf, :], start=False, stop=True),
                    reads=[k.r_scs[pr], k.r_tok[pr]], writes=[rb[6]])
            for h in range(4):
                pr, pb = h // 2, (h % 2) * 64
                sch.op("dve", lambda e, h=h, pr=pr, pb=pb: e.scalar_tensor_tensor(
                    out=Hs[pb:pb + 64, pr, :], in0=Hs[pb:pb + 64, pr, :],
                    scalar=o.E1[pr][pb:pb + 64, cc * L + L - 1:cc * L + L],
                    in1=b6[pb:pb + 64, 256 + h * 64:256 + (h + 1) * 64], op0=ALU.mult, op1=ALU.add),
                    reads=[r_Hs, o.r_E1[pr], rb[6]], wacc=[r_Hs])
            sch.op("act", lambda e: e.copy(out=Hb[:, :, :], in_=Hs[:, :, :]), reads=[r_Hs], writes=[r_Hb])
            y_, ry_ = yv[gc % 2], r_yv[gc % 2]
            sch.op("act", lambda e: e.copy(out=y_[:, :, :].rearrange("p h e -> p (h e)"), in_=b6[0:64, 0:256]),
                   reads=[rb[6]], writes=[ry_])
            pump(6)

        NG = NBLK * NCH
        for _ in prep_gen(0):
            pass
        for _ in pre_gen(0):
            pass
        hi.append((1, pre_gen(1)))
        for gc in range(NG):
            blk, cc = divmod(gc, NCH)
            drain_tag(lo, lambda t: t[0] == "epi" and t[1] <= gc - 2)
            if gc + 2 < NG:
                if (gc + 2) // NCH != (gc + 1) // NCH or (gc + 2) % NCH == 0:
                    drain_tag(pq, lambda t: True)
                hi.append((gc + 2, pre_gen(gc + 2)))
            chain(gc)
            epi_early(gc)
            lo.append((("epi", gc), epi_gen(gc)))
            if cc == 0 and blk + 1 < NBLK:
                pq.append((("prep", blk + 1), prep_gen(blk + 1)))
            pump(6)
            drain_tag(hi, lambda t: t == gc + 1)
            if gc >= 1:
                drain_tag(lo, lambda t: t[0] == "epi" and t[1] <= gc - 1)
                for _ in epi_tail(gc - 1):
                    pass
        drain_tag(lo, lambda t: True)
        for _ in epi_tail(NG - 1):
            pass
        drain_tag(cq, lambda t: True)
        for j in range(2):
            sch.dma("sp", d["mixT"][256 + j * 128:256 + (j + 1) * 128, :], yrT[:, j, :], reads=[r_yrT],
                    wacc=[C.r_mixT])
```

```python
import math
from contextlib import ExitStack

import numpy as np
import concourse.bass as bass
import concourse.mybir as mybir
from concourse.bass_utils import run_bass_kernel_spmd

F32 = mybir.dt.float32
BF16 = mybir.dt.bfloat16
ALU = mybir.AluOpType
AF = mybir.ActivationFunctionType
AX = mybir.AxisListType

D = 1024
S = 4096
DEPTH = 2
NT = S // 128
IN_PROJ = 3464
DFF = 4096
NDS = 8
EPS = 1e-6


class Res:
    __slots__ = ("w", "r", "f", "name", "excl")

    def __init__(self, name="", excl=False):
        self.f = {}
        self.w = {}
        self.r = {}
        self.name = name
        self.excl = excl


def PRes():
    return Res("psum", True)


class Sched:
    def __init__(self, nc, es):
        self.nc = nc
        self.eng = {"pe": nc.tensor, "dve": nc.vector, "act": nc.scalar,
                    "pool": nc.gpsimd, "sp": nc.sync}
        self.semobj = {}
        self.cnt = {}
        for k in self.eng:
            self.semobj[k] = es.enter_context(nc.semaphore("s_" + k))
            self.cnt[k] = 0
        self.seen = {k: {} for k in self.eng}
        self.dcnt = {}
        self.dnext = {}
        for k in ("sp", "act", "pool"):
            self.dnext[k] = 0
            for i in range(NDS):
                key = "d_%s%d" % (k, i)
                self.semobj[key] = es.enter_context(nc.semaphore(key))
                self.dcnt[key] = 0

    def _wait(self, eng, key, val):
        if eng == "pe" and key == "pe":
            return
        if self.seen[eng].get(key, 0) >= val:
            return
        self.eng[eng].wait_ge(self.semobj[key], val)
        self.seen[eng][key] = val

    def _deps(self, eng, reads, writes, wacc=()):
        deps = {}
        for w in wacc:
            for k, v in w.r.items():
                if deps.get(k, 0) < v:
                    deps[k] = v
            for k, v in w.f.items():
                if deps.get(k, 0) < v:
                    deps[k] = v
        for r in reads:
            for k, v in r.w.items():
                if deps.get(k, 0) < v:
                    deps[k] = v
        for w in writes:
            for k, v in w.w.items():
                if deps.get(k, 0) < v:
                    deps[k] = v
            for k, v in w.r.items():
                if deps.get(k, 0) < v:
                    deps[k] = v
        for k, v in deps.items():
            self._wait(eng, k, v)

    def _mark(self, key, val, reads, writes, wacc=()):
        for r in reads:
            if r.r.get(key, 0) < val:
                r.r[key] = val
        for w in wacc:
            if w.w.get(key, 0) < val:
                w.w[key] = val
        for w in writes:
            w.w = {key: val}
            w.f = {key: val}
            w.r = {}

    limit = None
    nops = 0

    def op(self, eng, fn, reads=(), writes=(), wacc=()):
        self.nops += 1
        if self.limit is not None and self.nops > self.limit:
            return
        if any(r.excl for r in reads):
            writes = list(writes) + [r for r in reads if r.excl]
            reads = [r for r in reads if not r.excl]
        self._deps(eng, reads, writes, wacc)
        inst = fn(self.eng[eng])
        self.cnt[eng] += 1
        v = self.cnt[eng]
        inst.then_inc(self.semobj[eng], 1)
        self._mark(eng, v, reads, writes, wacc)

    def dma(self, eng, out, in_, reads=(), writes=(), wacc=(), **kw):
        self.nops += 1
        if self.limit is not None and self.nops > self.limit:
            return
        i = self.dnext[eng]
        self.dnext[eng] = (i + 1) % NDS
        key = "d_%s%d" % (eng, i)
        prev = self.dcnt[key]
        if prev > 0:
            self._wait(eng, key, prev)
        self._deps(eng, reads, writes, wacc)
        inst = self.eng[eng].dma_start(out=out, in_=in_, **kw)
        val = prev + 16
        self.dcnt[key] = val
        inst.then_inc(self.semobj[key], 16)
        self._mark(key, val, reads, writes, wacc)

    def barrier(self):
        for e in self.eng:
            for k, v in self.dcnt.items():
                if v > 0:
                    self._wait(e, k, v)
            for k in self.eng:
                if self.cnt[k] > 0:
                    self._wait(e, k, self.cnt[k])

    def finish(self):
        for k, v in self.dcnt.items():
            if v > 0:
                self._wait("sp", k, v)
        for k in self.eng:
            if k != "sp" and self.cnt[k] > 0:
                self._wait("sp", k, self.cnt[k])


class Ctx:
    pass


_UID = [0]


def _sb(es, nc, name, shape, dt):
    _UID[0] += 1
    return es.enter_context(nc.sbuf_tensor("%s_%d" % (name, _UID[0]), shape, dt))


def _ps(es, nc, name, shape, dt):
    _UID[0] += 1
    return es.enter_context(nc.psum_tensor("%s_%d" % (name, _UID[0]), shape, dt))


def load_w_bf16(C, dst, dst_res, src2d, nk, ncols, stage, stage_res, cnt0=0):
    sch = C.sch
    step = stage.shape[2]
    i = cnt0
    for kc in range(nk):
        for c0 in range(0, ncols, step):
            n = min(step, ncols - c0)
            j = i % stage.shape[1]
            sch.dma("sp", stage[:, j, 0:n], src2d[kc * 128:(kc + 1) * 128, c0:c0 + n],
                    writes=[stage_res[j]])
            eng = ("dve", "pool", "act")[i % 3]
            if eng == "act":
                sch.op("act", lambda e, j=j, n=n, kc=kc, c0=c0: e.copy(
                    out=dst[:, kc, c0:c0 + n], in_=stage[:, j, 0:n]),
                    reads=[stage_res[j]], wacc=[dst_res[kc]])
            else:
                sch.op(eng, lambda e, j=j, n=n, kc=kc, c0=c0: e.tensor_copy(
                    out=dst[:, kc, c0:c0 + n], in_=stage[:, j, 0:n]),
                    reads=[stage_res[j]], wacc=[dst_res[kc]])
            i += 1
    return i


def load_w_gen(C, dst, dst_res, src2d, nk, ncols, stage, stage_res, cnt):
    sch = C.sch
    step = stage.shape[2]
    nslot = stage.shape[1]
    for kc in range(nk):
        for c0 in range(0, ncols, step):
            n = min(step, ncols - c0)
            i = cnt[0]
            cnt[0] += 1
            j = i % nslot
            sch.dma("sp", stage[:, j, 0:n], src2d[kc * 128:(kc + 1) * 128, c0:c0 + n], writes=[stage_res[j]])
            eng = ("dve", "pool")[i % 2]
            sch.op(eng, lambda e, j=j, n=n, kc=kc, c0=c0: e.tensor_copy(
                out=dst[:, kc, c0:c0 + n], in_=stage[:, j, 0:n]), reads=[stage_res[j]], wacc=[dst_res[kc]])
            yield


class OWeights:
    def __init__(self, C, es, l):
        nc = C.nc
        self.wout = _sb(es, nc, "O_wout", [128, 8, D], BF16)
        self.wup = _sb(es, nc, "O_wup", [128, 8, DFF], BF16)
        self.wdn = _sb(es, nc, "O_wdn", [128, 32, D], BF16)
        self.wout_r = [Res() for _ in range(8)]
        self.wup_r = [Res() for _ in range(8)]
        self.wdn_r = [Res() for _ in range(32)]
        self.stage = _sb(es, nc, "O_stage", [128, 2, 512], F32)
        self.stage_r = [Res() for _ in range(2)]
        self.l = l
        self.C = C

    def gen(self):
        C, d, l = self.C, self.C.d, self.l
        cnt = [0]
        yield from load_w_gen(C, self.wout, self.wout_r, d["w_out"][l], 8, D, self.stage, self.stage_r, cnt)
        yield from load_w_gen(C, self.wup, self.wup_r, d["w_ff_up"][l], 8, DFF, self.stage, self.stage_r, cnt)
        yield from load_w_gen(C, self.wdn, self.wdn_r, d["w_ff_down"][l], 32, D, self.stage, self.stage_r, cnt)


def rms_rstd(C, ss, ms, sd, rstd, r_ss, r_tmp, r_rstd, n, eps):
    sch = C.sch
    sch.op("dve", lambda e: e.tensor_scalar(out=ms, in0=ss, scalar1=1.0 / n, scalar2=eps,
                                            op0=ALU.mult, op1=ALU.add),
           reads=[r_ss], writes=[r_tmp])
    sch.op("act", lambda e: e.activation(out=sd, in_=ms, func=AF.Ln),
           reads=[r_tmp], writes=[r_tmp])
    sch.op("act", lambda e: e.activation(out=rstd, in_=sd, func=AF.Exp, scale=-0.5), reads=[r_tmp], writes=[r_rstd])


def norm_transpose(C, xt, nsub, g, hn, hT, tps, R):
    sch = C.sch
    for s in range(nsub):
        sch.op("act", lambda e, s=s: e.activation(out=R["junk_t"][:, :], in_=xt[:, s, :], func=AF.Square,
                                                  accum_out=R["ss_t"][:, s:s + 1]),
               reads=[R["xt"]], writes=[R["junk"], R["ss"]])
    rms_rstd(C, R["ss_t"][:, 0:nsub], R["ms_t"][:, 0:nsub], R["sd_t"][:, 0:nsub], R["rstd_t"][:, 0:nsub],
             R["ss"], R["tmp"], R["rstd"], D, EPS)
    for s in range(nsub):
        eng = "dve" if s % 2 == 0 else "pool"
        sch.op(eng, lambda e, s=s: e.tensor_scalar(out=hn[:, s, :], in0=xt[:, s, :],
                                                   scalar1=R["rstd_t"][:, s:s + 1], scalar2=0.0, op0=ALU.mult, op1=ALU.add),
               reads=[R["xt"], R["rstd"]], writes=[R["hn"]])
    for kc in range(8):
        tp, rtp = tps[kc % 2]
        for s in range(nsub):
            sch.op("pe", lambda e, s=s, kc=kc, tp=tp: e.transpose(
                out=tp[:, s * 128:(s + 1) * 128], in_=hn[:, s, kc * 128:(kc + 1) * 128], identity=C.identb[:, :]),
                reads=[R["hn"]], writes=[rtp])
        if kc % 2 == 0:
            sch.op("dve", lambda e, kc=kc, tp=tp: e.tensor_scalar(
                out=hT[:, kc, :], in0=tp[:, 0:nsub * 128], scalar1=g[:, kc:kc + 1], scalar2=None, op0=ALU.mult),
                reads=[rtp, R["g"]], writes=[R["hT"][kc]])
        else:
            sch.op("act", lambda e, kc=kc, tp=tp: e.activation(
                out=hT[:, kc, :], in_=tp[:, 0:nsub * 128], func=AF.Copy, scale=g[:, kc:kc + 1]),
                reads=[rtp, R["g"]], writes=[R["hT"][kc]])


def evac(C, i, out, in_, reads, writes=(), wacc=()):
    if i % 2 == 0:
        C.sch.op("act", lambda e: e.copy(out=out, in_=in_), reads=reads, writes=writes, wacc=wacc)
    else:
        C.sch.op("dve", lambda e: e.tensor_copy(out=out, in_=in_), reads=reads, writes=writes, wacc=wacc)


def phase_A(C, l, X, rX):
    nc, sch, d = C.nc, C.sch, C.d
    TB = 512
    with ExitStack() as es:
        win = _sb(es, nc, "A_win", [128, 8, IN_PROJ], BF16)
        win_r = [Res() for _ in range(8)]
        stage = _sb(es, nc, "A_stage", [128, 4, 1732], F32)
        stage_r = [Res() for _ in range(4)]
        g1 = _sb(es, nc, "A_g1", [128, 8], F32)
        xts = [_sb(es, nc, "A_xt%d" % i, [128, 4, D], F32) for i in range(2)]
        xr = [Res(), Res()]
        hn = _sb(es, nc, "A_hn", [128, 4, D], BF16)
        hT = _sb(es, nc, "A_hT", [128, 8, TB], BF16)
        R = {"junk": Res(), "ss": Res(), "tmp": Res(), "rstd": Res(), "hn": Res(), "g": Res(),
             "hT": [Res() for _ in range(8)]}
        R["junk_t"] = _sb(es, nc, "A_junk", [128, D], BF16)
        R["ss_t"] = _sb(es, nc, "A_ss", [128, 4], F32)
        R["ms_t"] = _sb(es, nc, "A_ms", [128, 4], F32)
        R["sd_t"] = _sb(es, nc, "A_sd", [128, 4], F32)
        R["rstd_t"] = _sb(es, nc, "A_rstd", [128, 4], F32)
        st_mqk = _sb(es, nc, "A_smqk", [128, 4, TB], F32)
        st_g = _sb(es, nc, "A_sg", [8, TB], F32)
        st_r = _sb(es, nc, "A_sr", [128, 7, TB], F32)
        st_aqk = _sb(es, nc, "A_saqk", [128, 8, TB], BF16)
        st_mvo = _sb(es, nc, "A_smvo", [128, 4, 512], F32)
        st_av = _sb(es, nc, "A_sav", [128, 4, 512], BF16)
        r_mqk, r_g, r_r, r_aqk, r_mvo, r_av = Res(), Res(), Res(), Res(), Res(), Res()
        tps = [(_ps(es, nc, "A_tp%d" % i, [128, 1024], BF16), PRes()) for i in range(2)]
        mms = [(_ps(es, nc, "A_mm%d" % i, [128, 512], F32), PRes()) for i in range(4)]

        def load_x(blk):
            sch.dma("sp", xts[blk % 2][:, :, :],
                    X[blk * TB:(blk + 1) * TB, :].rearrange("(s p) d -> p s d", p=128),
                    reads=[rX], writes=[xr[blk % 2]])

        load_x(0)
        sch.dma("sp", g1[:, :], d["norm1_g"][l], writes=[R["g"]])
        load_w_bf16(C, win, win_r, d["w_in"][l], 8, IN_PROJ, stage, stage_r)

        fm = []
        for i in range(4):
            fm.append((i * 128, 128, st_mqk, i, r_mqk))
        fm.append((1024, 8, st_g, None, r_g))
        for i in range(7):
            fm.append((1032 + i * 128, 128, st_r, i, r_r))
        for i in range(8):
            fm.append((1928 + i * 128, 128, st_aqk, i, r_aqk))
        mmi = 0
        for blk in range(S // TB):
            xt = xts[blk % 2]
            R["xt"] = xr[blk % 2]
            norm_transpose(C, xt, 4, g1, hn, hT, tps, R)
            if blk + 1 < S // TB:
                load_x(blk + 1)
            t0 = blk * TB
            for (c0, n, stt, idx, rs) in fm:
                ps, rps = mms[mmi % 4]
                mmi += 1
                for kc in range(8):
                    sch.op("pe", lambda e, kc=kc, c0=c0, n=n, ps=ps: e.matmul(
                        out=ps[0:n, :], lhsT=win[:, kc, c0:c0 + n], rhs=hT[:, kc, :],
                        start=(kc == 0), stop=(kc == 7)),
                        reads=[win_r[kc], R["hT"][kc]], writes=[rps])
                o = stt[0:n, :] if idx is None else stt[:, idx, :]
                evac(C, mmi, o, ps[0:n, :], [rps], wacc=[rs])
            sch.dma("sp", d["mqkT"][:, t0:t0 + TB].rearrange("(c p) t -> p c t", p=128), st_mqk[:, :, :],
                    reads=[r_mqk], wacc=[C.r_mqkT])
            sch.dma("sp", d["mgT"][:, t0:t0 + TB], st_g[:, :], reads=[r_g], wacc=[C.r_mgT])
            sch.dma("sp", d["rT"][:, t0:t0 + TB].rearrange("(c p) t -> p c t", p=128), st_r[:, :, :],
                    reads=[r_r], wacc=[C.r_rT])
            sch.dma("sp", d["aqkT"][:, t0:t0 + TB].rearrange("(c p) t -> p c t", p=128), st_aqk[:, :, :],
                    reads=[r_aqk], wacc=[C.r_aqkT])
            for (c0, stt, rs) in ((512, st_mvo, r_mvo), (2952, st_av, r_av)):
                for s in range(4):
                    ps, rps = mms[mmi % 4]
                    mmi += 1
                    for kc in range(8):
                        sch.op("pe", lambda e, kc=kc, c0=c0, s=s, ps=ps: e.matmul(
                            out=ps[:, :], lhsT=hT[:, kc, s * 128:(s + 1) * 128], rhs=win[:, kc, c0:c0 + 512],
                            start=(kc == 0), stop=(kc == 7)),
                            reads=[win_r[kc], R["hT"][kc]], writes=[rps])
                    evac(C, mmi, stt[:, s, :], ps[:, :], [rps], wacc=[rs])
            sch.dma("sp", d["mvo"][t0:t0 + TB, :].rearrange("(s p) c -> p s c", p=128), st_mvo[:, :, :],
                    reads=[r_mvo], wacc=[C.r_mvo])
            sch.dma("sp", d["av"][t0:t0 + TB, :].rearrange("(s p) c -> p s c", p=128), st_av[:, :, :],
                    reads=[r_av], wacc=[C.r_av])


def phase_O(C, l, X, rX, XOUT, rXOUT, final, W=None):
    nc, sch, d = C.nc, C.sch, C.d
    TB = 256
    NS = 2
    with ExitStack() as es:
        if W is None:
            wout = _sb(es, nc, "O_wout", [128, 8, D], BF16)
            wup = _sb(es, nc, "O_wup", [128, 8, DFF], BF16)
            wdn = _sb(es, nc, "O_wdn", [128, 32, D], BF16)
            wout_r = [Res() for _ in range(8)]
            wup_r = [Res() for _ in range(8)]
            wdn_r = [Res() for _ in range(32)]
            stage = _sb(es, nc, "O_stage", [128, 2, 512], F32)
            stage_r = [Res(), Res()]
        else:
            wout, wup, wdn = W.wout, W.wup, W.wdn
            wout_r, wup_r, wdn_r = W.wout_r, W.wup_r, W.wdn_r
        g2 = _sb(es, nc, "O_g2", [128, 8], F32)
        gf = _sb(es, nc, "O_gf", [128, 8], F32)
        gfb = _sb(es, nc, "O_gfb", [128, D], F32)
        mixt = _sb(es, nc, "O_mixt", [128, 8, TB], BF16)
        r_mixt = Res()
        xt = _sb(es, nc, "O_xt", [128, NS, D], F32)
        x1 = _sb(es, nc, "O_x1", [128, NS, D], F32)
        r_x1 = Res()
        hn = _sb(es, nc, "O_hn", [128, NS, D], BF16)
        hT = _sb(es, nc, "O_hT", [128, 8, TB], BF16)
        rr = [_sb(es, nc, "O_rr%d" % i, [128, TB], F32) for i in range(2)]
        rr_r = [Res(), Res()]
        u = _sb(es, nc, "O_u", [128, 32, TB], BF16)
        u_r = [Res() for _ in range(32)]
        R = {"junk": Res(), "ss": Res(), "tmp": Res(), "rstd": Res(), "hn": Res(), "g": Res(),
             "hT": [Res() for _ in range(8)], "xt": Res()}
        R["junk_t"] = _sb(es, nc, "O_junk", [128, D], BF16)
        R["ss_t"] = _sb(es, nc, "O_ss", [128, 4], F32)
        R["ms_t"] = _sb(es, nc, "O_ms", [128, 4], F32)
        R["sd_t"] = _sb(es, nc, "O_sd", [128, 4], F32)
        R["rstd_t"] = _sb(es, nc, "O_rstd", [128, 4], F32)
        r_xt = Res()
        r_gf = Res()
        tps = [(_ps(es, nc, "O_tp%d" % i, [128, 1024], BF16), PRes()) for i in range(2)]
        mma = [(_ps(es, nc, "O_mma%d" % i, [128, 512], F32), PRes()) for i in range(3)]
        mmb = [(_ps(es, nc, "O_mmb%d" % i, [128, 512], F32), PRes()) for i in range(3)]

        def load_block(blk):
            t0 = blk * TB
            sch.dma("sp", xt[:, :, :], X[t0:t0 + TB, :].rearrange("(s p) d -> p s d", p=128),
                    reads=[rX], writes=[r_xt])
            sch.dma("sp", mixt[:, :, :], d["mixT"][:, t0:t0 + TB].rearrange("(c p) t -> p c t", p=128),
                    reads=[C.r_mixT], writes=[r_mixt])

        load_block(0)
        sch.dma("sp", g2[:, :], d["norm2_g"][l], writes=[R["g"]])
        if final:
            sch.dma("sp", gfb[:, :], d["final_gb"], writes=[r_gf])
        if W is None:
            i = load_w_bf16(C, wout, wout_r, d["w_out"][l], 8, D, stage, stage_r)
            i = load_w_bf16(C, wup, wup_r, d["w_ff_up"][l], 8, DFF, stage, stage_r, i)
            load_w_bf16(C, wdn, wdn_r, d["w_ff_down"][l], 32, D, stage, stage_r, i)

        ia = 0
        ib = 0
        for blk in range(S // TB):
            t0 = blk * TB
            for s in range(NS):
                for hf in range(2):
                    ps, rps = mma[ia % 3]
                    ia += 1
                    for kc in range(8):
                        sch.op("pe", lambda e, kc=kc, s=s, hf=hf, ps=ps: e.matmul(
                            out=ps[:, :], lhsT=mixt[:, kc, s * 128:(s + 1) * 128],
                            rhs=wout[:, kc, hf * 512:(hf + 1) * 512], start=(kc == 0), stop=(kc == 7)),
                            reads=[r_mixt, wout_r[kc]], writes=[rps])
                    sch.op("dve", lambda e, s=s, hf=hf, ps=ps: e.tensor_tensor(
                        out=x1[:, s, hf * 512:(hf + 1) * 512], in0=ps[:, :], in1=xt[:, s, hf * 512:(hf + 1) * 512],
                        op=ALU.add), reads=[rps, r_xt], wacc=[r_x1])
            if blk + 1 < S // TB:
                load_block(blk + 1)
            R["xt"] = r_x1
            norm_transpose(C, x1, NS, g2, hn, hT, tps, R)
            for j in range(32):
                ps, rps = mmb[ib % 3]
                ib += 1
                for kc in range(8):
                    sch.op("pe", lambda e, kc=kc, j=j, ps=ps: e.matmul(
                        out=ps[:, 0:TB], lhsT=wup[:, kc, j * 128:(j + 1) * 128], rhs=hT[:, kc, :],
                        start=(kc == 0), stop=(kc == 7)),
                        reads=[wup_r[kc], R["hT"][kc]], writes=[rps])
                rt, rtr = rr[j % 2], rr_r[j % 2]
                sch.op("act", lambda e, ps=ps, rt=rt: e.activation(out=rt[:, :], in_=ps[:, 0:TB], func=AF.Relu),
                       reads=[rps], writes=[rtr])
                eng = "pool" if j % 2 == 0 else "dve"
                sch.op(eng, lambda e, j=j, rt=rt: e.tensor_tensor(out=u[:, j, :], in0=rt[:, :], in1=rt[:, :],
                                                                  op=ALU.mult),
                       reads=[rtr], writes=[u_r[j]])
            for s in range(NS):
                for hf in range(2):
                    ps, rps = mma[ia % 3]
                    ia += 1
                    for j in range(32):
                        sch.op("pe", lambda e, j=j, s=s, hf=hf, ps=ps: e.matmul(
                            out=ps[:, :], lhsT=u[:, j, s * 128:(s + 1) * 128],
                            rhs=wdn[:, j, hf * 512:(hf + 1) * 512], start=(j == 0), stop=(j == 31)),
                            reads=[u_r[j], wdn_r[j]], writes=[rps])
                    sch.op("dve", lambda e, s=s, hf=hf, ps=ps: e.tensor_tensor(
                        out=x1[:, s, hf * 512:(hf + 1) * 512], in0=ps[:, :], in1=x1[:, s, hf * 512:(hf + 1) * 512],
                        op=ALU.add), reads=[rps, r_x1], wacc=[r_x1])
            if final:
                for s in range(NS):
                    sch.op("act", lambda e, s=s: e.activation(out=R["junk_t"][:, :], in_=x1[:, s, :], func=AF.Square,
                                                              accum_out=R["ss_t"][:, s:s + 1]),
                           reads=[r_x1], writes=[R["junk"], R["ss"]])
                rms_rstd(C, R["ss_t"][:, 0:NS], R["ms_t"][:, 0:NS], R["sd_t"][:, 0:NS], R["rstd_t"][:, 0:NS],
                         R["ss"], R["tmp"], R["rstd"], D, EPS)
                for s in range(NS):
                    sch.op("dve", lambda e, s=s: e.scalar_tensor_tensor(
                        out=x1[:, s, :], in0=x1[:, s, :], scalar=R["rstd_t"][:, s:s + 1], in1=gfb[:, :],
                        op0=ALU.mult, op1=ALU.mult), reads=[r_x1, R["rstd"], r_gf], wacc=[r_x1])
            sch.dma("sp", XOUT[t0:t0 + TB, :].rearrange("(s p) d -> p s d", p=128), x1[:, :, :],
                    reads=[r_x1], wacc=[rXOUT])


def phase_M(C, l):
    nc, sch, d = C.nc, C.sch, C.d
    L = 128
    NC_ = S // L
    with ExitStack() as es:
        gi = _sb(es, nc, "M_gi", [4, S], F32)
        gf = _sb(es, nc, "M_gf", [4, S], F32)
        Fn = _sb(es, nc, "M_Fn", [4, S], F32)
        r_gi, r_gf, r_Fn = Res(), Res(), Res()
        bi = _sb(es, nc, "M_bi", [4, 1], F32)
        bfn = _sb(es, nc, "M_bfn", [4, 1], F32)
        r_b = Res()
        Mc = _sb(es, nc, "M_Mc", [4, NC_], F32)
        Md = _sb(es, nc, "M_Md", [4, NC_], F32)
        ec = _sb(es, nc, "M_ec", [4, NC_], F32)
        r_Mc, r_ec = Res(), Res()
        gtok = _sb(es, nc, "M_gtok", [128, NC_, 8, 1], F32)
        r_gtok = Res()
        eb = _sb(es, nc, "M_eb", [128, 2, NC_], F32)
        r_eb = Res()
        cw = _sb(es, nc, "M_cw", [128, 4, 4], F32)
        cb = _sb(es, nc, "M_cb", [128, 4], F32)
        mg = _sb(es, nc, "M_mg", [128, 256], F32)
        r_par = Res()
        xins = [_sb(es, nc, "M_xin%d" % i, [128, 3 + S], F32) for i in range(2)]
        accs = [_sb(es, nc, "M_acc%d" % i, [128, S], F32) for i in range(2)]
        r_xins, r_accs = [Res(), Res()], [Res(), Res()]
        qz = [_sb(es, nc, "M_qz%d" % i, [128, S], BF16) for i in range(4)]
        kpair = [_sb(es, nc, "M_kp%d" % i, [128, S], BF16) for i in range(2)]
        r_qz = [Res() for _ in range(4)]
        r_kp = [Res() for _ in range(2)]
        ymT = _sb(es, nc, "M_ymT", [128, 2, S], BF16)
        r_ymT = Res()
        vo = [_sb(es, nc, "M_vo%d" % i, [128, 512], F32) for i in range(2)]
        r_vo = [Res(), Res()]
        va = _sb(es, nc, "M_va", [128, 4, 65], BF16)
        r_va = Res()
        sig = _sb(es, nc, "M_sig", [128, 256], F32)
        t2s = [_sb(es, nc, "M_t2_%d" % i, [128, 256], F32) for i in range(2)]
        r_t2s = [Res(), Res()]
        r_sig = Res()
        ktok = _sb(es, nc, "M_ktok", [128, 256], BF16)
        r_ktok = Res()
        at = _sb(es, nc, "M_at", [128, 4, 128], BF16)
        r_at = Res()
        Cs = _sb(es, nc, "M_Cs", [128, 2, 65], F32)
        Cb = _sb(es, nc, "M_Cb", [128, 2, 65], BF16)
        r_Cs, r_Cb = Res(), Res()
        sm = _sb(es, nc, "M_sm", [128, 8, 4, 1], F32)
        r_sm = Res()
        hh = _sb(es, nc, "M_hh", [128, 4, 64], F32)
        sq = _sb(es, nc, "M_sq", [128, 4, 64], F32)
        y1 = _sb(es, nc, "M_y1", [128, 4, 64], F32)
        yms = [_sb(es, nc, "M_ym%d" % i, [128, 256], BF16) for i in range(2)]
        r_yms = [Res(), Res()]
        r_hh, r_sq, r_y1 = Res(), Res(), Res()
        p_g = _ps(es, nc, "M_pg", [128, 512], F32)
        p_kt = _ps(es, nc, "M_pkt", [128, 1024], BF16)
        p_kv = _ps(es, nc, "M_pkv", [128, 512], F32)
        p_at = _ps(es, nc, "M_pat", [128, 4, 128], F32)
        p_nums = [_ps(es, nc, "M_pnum%d" % i, [128, 512], F32) for i in range(2)]
        p_yt = _ps(es, nc, "M_pyt", [128, 1024], BF16)
        rp_g, rp_kt, rp_kv, rp_at, rp_yt = PRes(), PRes(), PRes(), PRes(), PRes()
        rp_nums = [PRes(), PRes()]

        sch.dma("sp", bi[:, :], d["m_b_i"][l], wacc=[r_b])
        sch.dma("sp", bfn[:, :], d["m_b_f"][l], wacc=[r_b])
        sch.dma("sp", cw[:, :, :], d["m_conv_w"][l], wacc=[r_par])
        sch.dma("sp", cb[:, :], d["m_conv_b"][l], wacc=[r_par])
        sch.dma("sp", mg[:, :], d["m_norm_g"][l], wacc=[r_par])
        sch.dma("sp", gi[:, :], d["mgT"][0:4, :], reads=[C.r_mgT], writes=[r_gi])
        sch.dma("sp", gf[:, :], d["mgT"][4:8, :], reads=[C.r_mgT], writes=[r_gf])
        sch.op("dve", lambda e: e.tensor_scalar(out=bfn[:, :], in0=bfn[:, :], scalar1=-1.0, scalar2=None,
                                                op0=ALU.mult), reads=[r_b], writes=[r_b])
        for i in range(2):
            sch.op("pool", lambda e, i=i: e.memset(xins[i][:, 0:3], 0.0), writes=[r_xins[i]])
        def conv(ch):
            xin, acc, r_xin, r_acc = xins[ch % 2], accs[ch % 2], r_xins[ch % 2], r_accs[ch % 2]
            sch.dma("sp", xin[:, 3:3 + S], d["mqkT"][ch * 128:(ch + 1) * 128, :], reads=[C.r_mqkT], writes=[r_xin])
            sch.op("act", lambda e, ch=ch, xin=xin, acc=acc: e.activation(
                out=acc[:, :], in_=xin[:, 3:3 + S], func=AF.Identity, scale=cw[:, ch, 3:4], bias=cb[:, ch:ch + 1]),
                reads=[r_xin, r_par], writes=[r_acc])
            for j in range(3):
                sch.op("dve", lambda e, ch=ch, j=j: e.scalar_tensor_tensor(
                    out=acc[:, :], in0=xin[:, j:j + S], scalar=cw[:, ch, j:j + 1], in1=acc[:, :],
                    op0=ALU.mult, op1=ALU.add), reads=[r_xin, r_acc, r_par], writes=[r_acc])
            if ch < 2:
                for hf in range(2):
                    hq = 2 * ch + hf
                    ps_ = slice(hf * 64, hf * 64 + 64)
                    zs_ = slice((1 - hf) * 64, (1 - hf) * 64 + 64)
                    sch.op("pool", lambda e, hq=hq, zs_=zs_: e.memset(qz[hq][zs_, :], 0.0), wacc=[r_qz[hq]])
                    sch.op("act", lambda e, hq=hq, ps_=ps_: e.activation(out=qz[hq][ps_, :], in_=acc[ps_, :],
                                                                         func=AF.Silu),
                           reads=[r_acc], wacc=[r_qz[hq]])
            else:
                sch.op("act", lambda e, ch=ch: e.activation(out=kpair[ch - 2][:, :], in_=acc[:, :], func=AF.Silu),
                       reads=[r_acc], writes=[r_kp[ch - 2]])
        sch.op("dve", lambda e: e.tensor_scalar(out=gi[:, :], in0=gi[:, :], scalar1=bi[:, 0:1], scalar2=None,
                                                op0=ALU.add), reads=[r_gi, r_b], writes=[r_gi])
        sch.op("act", lambda e: e.activation(out=gf[:, :], in_=gf[:, :], func=AF.Exp, bias=bfn[:, 0:1], scale=-1.0),
               reads=[r_gf, r_b], writes=[r_gf])
        sch.op("act", lambda e: e.activation(out=gf[:, :], in_=gf[:, :], func=AF.Ln, bias=1.0, scale=1.0),
               reads=[r_gf], writes=[r_gf])
        conv(0)
        sch.op("dve", lambda e: e.tensor_tensor_scan(out=Fn[:, :], data0=gf[:, :], data1=gf[:, :], initial=0.0,
                                                     op0=ALU.add, op1=ALU.max), reads=[r_gf], writes=[r_Fn])
        sch.op("dve", lambda e: e.tensor_tensor(out=gi[:, :], in0=gi[:, :], in1=Fn[:, :], op=ALU.add),
               reads=[r_gi, r_Fn], writes=[r_gi])
        sch.op("dve", lambda e: e.tensor_tensor_scan(out=gf[:, :], data0=gi[:, :], data1=gi[:, :], initial=0.0,
                                                     op0=ALU.max, op1=ALU.max), reads=[r_gi], writes=[r_gf])
        conv(1)
        U3 = gf[:, :].rearrange("p (c t) -> p c t", t=L)
        sch.op("dve", lambda e: e.tensor_copy(out=Mc[:, :].unsqueeze(2), in_=U3[:, :, L - 1:L]),
               reads=[r_gf], writes=[r_Mc])
        Mbc = Mc[:, :].unsqueeze(2).broadcast_to([4, NC_, L])
        u3 = gi[:, :].rearrange("p (c t) -> p c t", t=L)
        sch.op("dve", lambda e: e.tensor_tensor(out=u3, in0=u3, in1=Mbc, op=ALU.subtract),
               reads=[r_gi, r_Mc], writes=[r_gi])
        sch.op("act", lambda e: e.activation(out=gi[:, :], in_=gi[:, :], func=AF.Exp, bias=math.log(0.125), scale=1.0),
               reads=[r_gi], writes=[r_gi])
        conv(2)
        F3 = Fn[:, :].rearrange("p (c t) -> p c t", t=L)
        sch.op("dve", lambda e: e.tensor_tensor(out=F3, in0=F3, in1=Mbc, op=ALU.subtract),
               reads=[r_Fn, r_Mc], writes=[r_Fn])
        sch.op("act", lambda e: e.activation(out=Fn[:, :], in_=Fn[:, :], func=AF.Exp), reads=[r_Fn], writes=[r_Fn])
        conv(3)
        sch.op("dve", lambda e: e.tensor_tensor(out=Md[:, 1:NC_], in0=Mc[:, 0:NC_ - 1], in1=Mc[:, 1:NC_],
                                                op=ALU.subtract), reads=[r_Mc], wacc=[r_ec])
        sch.op("dve", lambda e: e.tensor_scalar(out=Md[:, 0:1], in0=Mc[:, 0:1], scalar1=-1.0, scalar2=None,
                                                op0=ALU.mult), reads=[r_Mc], wacc=[r_ec])
        sch.op("act", lambda e: e.activation(out=ec[:, :], in_=Md[:, :], func=AF.Exp), reads=[r_ec], writes=[r_ec])
        pg3 = p_g[:, 0:NC_ * 8].rearrange("p (c g) -> p c g", g=8)
        for c in range(NC_):
            sch.op("pe", lambda e, c=c: e.transpose(out=pg3[:, c, 0:4], in_=gi[:, c * L:(c + 1) * L],
                                                    identity=C.identf[0:4, 0:4]), reads=[r_gi], writes=[rp_g])
            sch.op("pe", lambda e, c=c: e.transpose(out=pg3[:, c, 4:8], in_=Fn[:, c * L:(c + 1) * L],
                                                    identity=C.identf[0:4, 0:4]), reads=[r_Fn], writes=[rp_g])
        sch.op("dve", lambda e: e.tensor_copy(out=gtok[:, :, :, 0], in_=pg3), reads=[rp_g], writes=[r_gtok])
        for pr in range(2):
            sch.op("pe", lambda e, pr=pr: e.matmul(out=p_g[:, 256 + pr * NC_:256 + (pr + 1) * NC_],
                                                   lhsT=C.sel4[:, pr, :], rhs=ec[:, :], start=True, stop=True),
                   reads=[r_ec, r_gtok], writes=[rp_g])
        sch.op("dve", lambda e: e.tensor_copy(out=eb[:, :, :],
                                              in_=p_g[:, 256:256 + 2 * NC_].rearrange("p (a c) -> p a c", a=2)),
               reads=[rp_g], writes=[r_eb])
        sch.op("pool", lambda e: e.memset(Cs[:, :, :], 0.0), writes=[r_Cs])
        num3s = [pn[:, 0:260].rearrange("p (h e) -> p h e", e=65) for pn in p_nums]
        kv3 = p_kv[:, 0:260].rearrange("p (a e) -> p a e", a=2)
        def epilogue(c):
            ym, r_ym = yms[c % 2], r_yms[c % 2]
            t2, r_t2 = t2s[c % 2], r_t2s[c % 2]
            num3, rp_num = num3s[c % 2], rp_nums[c % 2]
            sch.op("act", lambda e: e.activation(out=sm[:, 0, :, :], in_=num3[:, :, 64:65], func=AF.Abs),
                   reads=[rp_num], writes=[r_sm])
            sch.op("dve", lambda e, c=c: e.tensor_tensor(out=sm[:, 1, :, :], in0=sm[:, 0, :, :],
                                                         in1=gtok[:, c, 4:8, :], op=ALU.max),
                   reads=[r_sm, r_gtok], writes=[r_sm])
            sch.op("dve", lambda e: e.reciprocal(out=sm[:, 2, :, :], in_=sm[:, 1, :, :]), reads=[r_sm], writes=[r_sm])
            sch.op("dve", lambda e: e.tensor_tensor(out=hh[:, :, :], in0=num3[:, :, 0:64],
                                                    in1=sm[:, 2, :, :].broadcast_to([128, 4, 64]), op=ALU.mult),
                   reads=[rp_num, r_sm], writes=[r_hh])
            sch.op("dve", lambda e: e.tensor_tensor(out=sq[:, :, :], in0=hh[:, :, :], in1=hh[:, :, :], op=ALU.mult),
                   reads=[r_hh], writes=[r_sq])
            sch.op("dve", lambda e: e.tensor_reduce(out=sm[:, 3, :, :], in_=sq[:, :, :], axis=AX.X, op=ALU.add),
                   reads=[r_sq], writes=[r_sm])
            rms_rstd(C, sm[:, 3, :, 0], sm[:, 4, :, 0], sm[:, 5, :, 0], sm[:, 6, :, 0], r_sm, r_sm, r_sm, 64, 1e-6)
            sch.op("dve", lambda e: e.tensor_tensor(out=y1[:, :, :], in0=hh[:, :, :],
                                                    in1=sm[:, 6, :, :].broadcast_to([128, 4, 64]), op=ALU.mult),
                   reads=[r_hh, r_sm], writes=[r_y1])
            sch.op("dve", lambda e: e.tensor_tensor(out=ym[:, :], in0=y1[:, :, :].rearrange("p h e -> p (h e)"),
                                                    in1=t2[:, :], op=ALU.mult), reads=[r_y1, r_t2], writes=[r_ym])

        def y_tail(c):
            ym, r_ym = yms[c % 2], r_yms[c % 2]
            cs = slice(c * L, (c + 1) * L)
            for j in range(2):
                sch.op("pe", lambda e, j=j: e.transpose(out=p_yt[:, j * 128:(j + 1) * 128],
                                                        in_=ym[:, j * 128:(j + 1) * 128], identity=C.identb[:, :]),
                       reads=[r_ym], writes=[rp_yt])
            sch.op("act", lambda e: e.copy(out=ymT[:, :, cs], in_=p_yt[:, 0:256].rearrange("p (j t) -> p j t", j=2)),
                   reads=[rp_yt], wacc=[r_ymT])

        for c in range(NC_):
            cs = slice(c * L, (c + 1) * L)
            vt, rv = vo[c % 2], r_vo[c % 2]
            t2, r_t2 = t2s[c % 2], r_t2s[c % 2]
            num3, rp_num = num3s[c % 2], rp_nums[c % 2]
            sch.dma("sp", vt[:, :], d["mvo"][c * L:(c + 1) * L, :], reads=[C.r_mvo], writes=[rv])
            for h in range(4):
                eng = "dve"
                sch.op(eng, lambda e, h=h, vt=vt, c=c: e.tensor_scalar(
                    out=va[:, h, 0:64], in0=vt[:, h * 64:(h + 1) * 64], scalar1=gtok[:, c, h, :], scalar2=0.0,
                    op0=ALU.mult, op1=ALU.add), reads=[rv, r_gtok], wacc=[r_va])
            sch.op("pool", lambda e, c=c: e.tensor_copy(out=va[:, :, 64:65], in_=gtok[:, c, 0:4, :]),
                   reads=[r_gtok], wacc=[r_va])
            sch.op("act", lambda e, vt=vt: e.activation(out=sig[:, :], in_=vt[:, 256:512], func=AF.Exp, scale=-1.0),
                   reads=[rv], writes=[r_sig])
            sch.op("act", lambda e: e.activation(out=sig[:, :], in_=sig[:, :], func=AF.Ln, bias=1.0, scale=1.0),
                   reads=[r_sig], writes=[r_sig])
            sch.op("act", lambda e: e.activation(out=sig[:, :], in_=sig[:, :], func=AF.Exp, scale=-1.0),
                   reads=[r_sig], writes=[r_sig])
            sch.op("pool", lambda e: e.tensor_tensor(out=t2[:, :], in0=sig[:, :], in1=mg[:, :], op=ALU.mult),
                   reads=[r_sig, r_par], writes=[r_t2])
            for pr in range(2):
                sch.op("pe", lambda e, pr=pr, cs=cs: e.transpose(out=p_kt[:, pr * 128:(pr + 1) * 128],
                                                                 in_=kpair[pr][:, cs], identity=C.identb[:, :]),
                       reads=[r_kp[pr]], writes=[rp_kt])
            sch.op("act", lambda e: e.copy(out=ktok[:, :], in_=p_kt[:, 0:256]), reads=[rp_kt], writes=[r_ktok])
            for pr in range(2):
                sch.op("pe", lambda e, pr=pr: e.matmul(out=kv3[:, pr, :], lhsT=ktok[:, pr * 128:(pr + 1) * 128],
                                                       rhs=va[:, 2 * pr:2 * pr + 2, :], start=True, stop=True),
                       reads=[r_ktok, r_va], writes=[rp_kv])
            for h in range(4):
                pr, pb = h // 2, (h % 2) * 64
                sch.op("pe", lambda e, h=h, pr=pr, pb=pb, cs=cs: e.matmul(
                    out=p_at[:, h, :], lhsT=kpair[pr][:, cs], rhs=qz[h][:, cs],
                    start=True, stop=True), reads=[r_kp[pr], r_qz[h]], writes=[rp_at])
            sch.op("dve", lambda e: e.tensor_tensor(out=at[:, :, :], in0=p_at[:, :, :],
                                                    in1=C.trif[:, :].unsqueeze(1).broadcast_to([128, 4, 128]),
                                                    op=ALU.mult), reads=[rp_at], writes=[r_at])
            for pr in range(2):
                sch.op("dve", lambda e, pr=pr, c=c: e.tensor_scalar(
                    out=Cb[:, pr, :], in0=Cs[:, pr, :], scalar1=eb[:, pr, c:c + 1], scalar2=None, op0=ALU.mult),
                    reads=[r_Cs, r_eb], wacc=[r_Cb])
            for h in range(4):
                pr, pb = h // 2, (h % 2) * 64
                sch.op("pe", lambda e, h=h: e.matmul(out=num3[:, h, :], lhsT=at[:, h, :], rhs=va[:, h, :],
                                                     start=True, stop=False), reads=[r_at, r_va], writes=[rp_num])
                sch.op("pe", lambda e, h=h, pr=pr, pb=pb, cs=cs: e.matmul(
                    out=num3[:, h, :], lhsT=qz[h][:, cs], rhs=Cb[:, pr, :],
                    start=False, stop=True), reads=[r_qz[h], r_Cb], writes=[rp_num])
            for pr in range(2):
                for hf in range(2):
                    pb = hf * 64
                    sch.op("dve", lambda e, pr=pr, hf=hf, pb=pb, c=c: e.scalar_tensor_tensor(
                        out=Cs[pb:pb + 64, pr, :], in0=Cs[pb:pb + 64, pr, :], scalar=eb[pb:pb + 64, pr, c:c + 1],
                        in1=kv3[pb:pb + 64, pr, hf * 65:(hf + 1) * 65], op0=ALU.mult, op1=ALU.add),
                        reads=[r_Cs, r_eb, rp_kv, r_Cb], wacc=[r_Cs])
            if c >= 1:
                epilogue(c - 1)
            if c >= 2:
                y_tail(c - 2)
        epilogue(NC_ - 1)
        y_tail(NC_ - 2)
        y_tail(NC_ - 1)
        for j in range(2):
            sch.dma("sp", d["mixT"][j * 128:(j + 1) * 128, :], ymT[:, j, :], reads=[r_ymT], wacc=[C.r_mixT])


def phase_R(C, l):
    nc, sch, d = C.nc, C.sch, C.d
    T = 1024
    L = 64
    NCH = T // L
    NEG = -0.6065306597126334
    with ExitStack() as es:
        def sb(name, shape, dt):
            return _sb(es, nc, "R_" + name, shape, dt)
        mu = sb("mu", [128, 7], F32)
        omu = sb("omu", [128, 7], F32)
        pv = sb("pv", [128, 6, 2], F32)
        lup_f = sb("lupf", [64, 3, 256], F32)
        lup = sb("lup", [64, 3, 256], BF16)
        lin_a = sb("lin_a", [32, T], BF16)
        lin_g = sb("lin_g", [64, T], BF16)
        r_lina, r_ling = Res(), Res()
        lng = sb("lng", [64, 256], F32)
        lnb = sb("lnb", [64, 256], F32)
        r_par = Res()
        praw = sb("praw", [128, 1 + T], F32)
        r_praw = Res()
        pm_r = sb("pm_r", [128, 2, T], F32)
        pm_k = sb("pm_k", [128, 2, T], F32)
        pm_v = sb("pm_v", [128, 2, T], F32)
        pm_l = sb("pm_l", [128, T], F32)
        r_pm = {k: Res() for k in ("r0", "r1", "k0", "k1", "v0", "v1", "l")}
        lin = sb("lin", [128, T], BF16)
        r_lin = Res()
        tmp = sb("tmp", [128, T], F32)
        r_tmp = Res()
        lw = sb("lw", [128, T], F32)
        aa = sb("aa", [128, T], F32)
        kx = sb("kx", [128, T], F32)
        sq = sb("sq", [128, T], F32)
        kp = sb("kp", [128, T], F32)
        bb = sb("bb", [128, T], F32)
        Gp = [sb("Gp%d" % i, [128, 1 + T], F32) for i in range(2)]
        gi = sb("gi", [128, T], F32)
        ge = sb("ge", [128, T], F32)
        E1 = [sb("E1_%d" % i, [128, T], F32) for i in range(2)]
        E2 = sb("E2", [128, T], F32)
        r_lw, r_aa, r_kx, r_sq, r_kp, r_bb, r_gi, r_ge, r_E2 = (Res() for _ in range(9))
        r_Gp = [Res(), Res()]
        r_E1 = [Res(), Res()]
        ARz = [sb("ARz%d" % i, [128, 2, T], BF16) for i in range(4)]
        Bt = sb("Bt", [128, 2, T], BF16)
        Kt = sb("Kt", [128, 2, T], BF16)
        Bh = sb("Bh", [128, 2, T], F32)
        Kh = sb("Kh", [128, 2, T], F32)
        prodb = sb("prodb", [128, 2, T], BF16)
        r_AR, r_Bt, r_Kt, r_Bh, r_Kh, r_prodb = ([Res(), Res()] for _ in range(6))
        Hs = sb("Hs", [128, 2, 64], F32)
        Hb = sb("Hb", [128, 2, 64], BF16)
        r_Hs, r_Hb = Res(), Res()
        yrT = sb("yrT", [128, 2, S], BF16)
        r_yrT = Res()
        tok = sb("tok", [64, 2, 4, 2, 64], F32)
        r_tok = [Res(), Res()]
        scs = sb("scs", [64, 4, 4, 64], F32)
        r_scs = [Res(), Res()]
        nns = sb("nns", [64, 4, 64], F32)
        r_nns = Res()
        pws = [sb("pws%d" % i, [64, 4, 2, 64], F32) for i in range(5)]
        r_pws = [Res() for _ in range(5)]
        U = [sb("U%d" % i, [64, 4, 64], F32) for i in range(2)]
        r_U = [Res(), Res()]
        yv = sb("yv", [64, 4, 64], F32)
        ysq = sb("ysq", [64, 4, 64], F32)
        yo = sb("yo", [64, 256], F32)
        st = sb("st", [64, 8, 4, 1], F32)
        r_yv, r_ysq, r_yo, r_st = Res(), Res(), Res(), Res()
        banks = [_ps(es, nc, "R_b%d" % i, [128, 512], F32) for i in range(8)]
        rb = [PRes() for _ in range(8)]

        sch.dma("sp", mu[:, :], d["r_mu"][l], wacc=[r_par])
        sch.dma("sp", pv[:, 0, :], d["r_w0"][l], wacc=[r_par])
        sch.dma("sp", pv[:, 1, :], d["r_a0"][l], wacc=[r_par])
        sch.dma("sp", pv[:, 2, :], d["r_k_k"][l], wacc=[r_par])
        sch.dma("sp", pv[:, 3, :], d["r_k_a"][l], wacc=[r_par])
        sch.dma("sp", pv[:, 4, :], d["r_r_k"][l], wacc=[r_par])
        r_lup = Res()
        sch.op("pool", lambda e: e.memset(lup_f[:, :, :], 0.0), writes=[r_lup])
        sch.dma("sp", lup_f[0:32, 0, :], d["r_w_up"][l], writes=[r_lup])
        sch.dma("sp", lup_f[0:32, 1, :], d["r_a_up"][l], writes=[r_lup])
        sch.dma("sp", lup_f[0:64, 2, :], d["r_g_up"][l], writes=[r_lup])
        sch.dma("sp", lng[:, :], d["r_ln_g"][l], wacc=[r_par])
        sch.dma("sp", lnb[:, :], d["r_ln_b"][l], wacc=[r_par])
        sch.op("dve", lambda e: e.tensor_scalar(out=omu[:, :], in0=mu[:, :], scalar1=-1.0, scalar2=1.0,
                                                op0=ALU.mult, op1=ALU.add), reads=[r_par], writes=[r_par])
        sch.op("dve", lambda e: e.tensor_scalar(out=pv[:, 5, :], in0=pv[:, 3, :], scalar1=-1.0, scalar2=1.0,
                                                op0=ALU.mult, op1=ALU.add), reads=[r_par], writes=[r_par])
        sch.op("dve", lambda e: e.tensor_copy(out=lup[:, :, :], in_=lup_f[:, :, :]), reads=[r_par, r_lup], writes=[r_par])
        sch.op("pool", lambda e: e.memset(Hs[:, :, :], 0.0), writes=[r_Hs])
        sch.op("pool", lambda e: e.memset(Hb[:, :, :], 0.0), writes=[r_Hb])
        for pr in range(2):
            sch.op("pool", lambda e, pr=pr: e.memset(Gp[pr][:, :], 0.0), writes=[r_Gp[pr]])
        for h in range(4):
            zs_ = slice((1 - h % 2) * 64, (1 - h % 2) * 64 + 64)
            sch.op("pool", lambda e, h=h, zs_=zs_: e.memset(ARz[h][zs_, :, :], 0.0), wacc=[r_AR[h // 2]])

        def v3(ap2):
            return ap2.rearrange("p (c t) -> p c t", t=L)

        dests = [(pm_r, 0, "r0"), (pm_r, 1, "r1"), (pm_k, 0, "k0"), (pm_k, 1, "k1"),
                 (pm_v, 0, "v0"), (pm_v, 1, "v1"), (None, 0, "l")]
        for blk in range(S // T):
            t0 = blk * T
            for rc in range(7):
                if blk == 0:
                    sch.op("pool", lambda e: e.memset(praw[:, 0:1], 0.0), writes=[r_praw])
                    sch.dma("sp", praw[:, 1:1 + T], d["rT"][rc * 128:(rc + 1) * 128, 0:T],
                            reads=[C.r_rT], wacc=[r_praw])
                else:
                    sch.dma("sp", praw[:, :], d["rT"][rc * 128:(rc + 1) * 128, t0 - 1:t0 + T],
                            reads=[C.r_rT], writes=[r_praw])
                dt_, pi, key = dests[rc]
                o = pm_l[:, :] if dt_ is None else dt_[:, pi, :]
                sch.op("pool", lambda e, rc=rc: e.tensor_scalar(out=tmp[:, :], in0=praw[:, 1:1 + T],
                                                                scalar1=omu[:, rc:rc + 1], scalar2=0.0, op0=ALU.mult, op1=ALU.add),
                       reads=[r_praw, r_par], writes=[r_tmp])
                sch.op("dve", lambda e, rc=rc, o=o: e.scalar_tensor_tensor(
                    out=o, in0=praw[:, 0:T], scalar=mu[:, rc:rc + 1], in1=tmp[:, :], op0=ALU.mult, op1=ALU.add),
                    reads=[r_praw, r_tmp, r_par], writes=[r_pm[key]])
            sch.op("act", lambda e: e.activation(out=lin[0:32, :], in_=pm_l[0:32, :], func=AF.Tanh),
                   reads=[r_pm["l"]], wacc=[r_lin])
            sch.op("act", lambda e: e.copy(out=lin[32:64, :], in_=pm_l[32:64, :]), reads=[r_pm["l"]], wacc=[r_lin])
            sch.op("act", lambda e: e.activation(out=lin[64:128, :], in_=pm_l[64:128, :], func=AF.Sigmoid),
                   reads=[r_pm["l"]], wacc=[r_lin])
            sch.dma("sp", lin_a[:, :], lin[32:64, :], reads=[r_lin], writes=[r_lina])
            sch.dma("sp", lin_g[:, :], lin[64:128, :], reads=[r_lin], writes=[r_ling])
            pb7 = banks[7]
            for pr in range(2):
                rk, rr_, rvv = r_pm["k%d" % pr], r_pm["r%d" % pr], r_pm["v%d" % pr]
                pcs = slice(pr * 128, (pr + 1) * 128)
                for hf in range(T // 512):
                    hs = slice(hf * 512, (hf + 1) * 512)
                    sch.op("pe", lambda e, hs=hs, pcs=pcs: e.matmul(out=pb7[:, :], lhsT=lup[0:32, 0, pcs],
                                                                    rhs=lin[0:32, hs], start=True, stop=True),
                           reads=[r_lin, r_par], writes=[rb[7]])
                    sch.op("act", lambda e, hs=hs, pr=pr: e.activation(out=lw[:, hs], in_=pb7[:, :], func=AF.Sigmoid,
                                                                       bias=pv[:, 0, pr:pr + 1], scale=1.0),
                           reads=[rb[7], r_par], wacc=[r_lw])
                    sch.op("pe", lambda e, hs=hs, pcs=pcs: e.matmul(out=pb7[:, :], lhsT=lup[0:32, 1, pcs],
                                                                    rhs=lin_a[:, hs], start=True, stop=True),
                           reads=[r_lina, r_par], writes=[rb[7]])
                    sch.op("act", lambda e, hs=hs, pr=pr: e.activation(out=aa[:, hs], in_=pb7[:, :], func=AF.Sigmoid,
                                                                       bias=pv[:, 1, pr:pr + 1], scale=1.0),
                           reads=[rb[7], r_par], wacc=[r_aa])
                sch.op("pool", lambda e: e.tensor_scalar(out=lw[:, :], in0=lw[:, :], scalar1=NEG, scalar2=0.0,
                                                         op0=ALU.mult, op1=ALU.add), reads=[r_lw], writes=[r_lw])
                sch.op("dve", lambda e, pr=pr: e.tensor_scalar(out=kx[:, :], in0=pm_k[:, pr, :],
                                                               scalar1=pv[:, 2, pr:pr + 1], scalar2=None,
                                                               op0=ALU.mult), reads=[rk, r_par], writes=[r_kx])
                sch.op("pool", lambda e: e.tensor_tensor(out=sq[:, :], in0=kx[:, :], in1=kx[:, :], op=ALU.mult),
                       reads=[r_kx], writes=[r_sq])
                for hf in range(T // 512):
                    hs = slice(hf * 512, (hf + 1) * 512)
                    sch.op("pe", lambda e, hs=hs: e.matmul(out=pb7[:, :], lhsT=C.blkf[:, :], rhs=sq[:, hs],
                                                           start=True, stop=True), reads=[r_sq], writes=[rb[7]])
                    sch.op("act", lambda e, hs=hs: e.activation(out=tmp[:, hs], in_=pb7[:, :], func=AF.Sqrt),
                           reads=[rb[7]], wacc=[r_tmp])
                sch.op("dve", lambda e: e.tensor_scalar(out=tmp[:, :], in0=tmp[:, :], scalar1=1e-12, scalar2=None,
                                                        op0=ALU.max), reads=[r_tmp], writes=[r_tmp])
                sch.op("dve", lambda e: e.reciprocal(out=tmp[:, :], in_=tmp[:, :]), reads=[r_tmp], writes=[r_tmp])
                sch.op("dve", lambda e: e.tensor_tensor(out=kx[:, :], in0=kx[:, :], in1=tmp[:, :], op=ALU.mult),
                       reads=[r_kx, r_tmp], writes=[r_kx])
                sch.op("pool", lambda e, pr=pr: e.tensor_scalar(out=kp[:, :], in0=aa[:, :], scalar1=pv[:, 3, pr:pr + 1],
                                                                scalar2=pv[:, 5, pr:pr + 1], op0=ALU.mult, op1=ALU.add),
                       reads=[r_aa, r_par], writes=[r_kp])
                sch.op("pool", lambda e, pr=pr: e.tensor_tensor(out=kp[:, :], in0=kp[:, :], in1=pm_k[:, pr, :],
                                                                op=ALU.mult), reads=[r_kp, rk], writes=[r_kp])
                sch.op("pool", lambda e: e.tensor_tensor(out=bb[:, :], in0=kx[:, :], in1=aa[:, :], op=ALU.mult),
                       reads=[r_kx, r_aa], writes=[r_bb])
                sch.op("dve", lambda e, pr=pr: e.scalar_tensor_tensor(
                    out=prodb[:, pr, :], in0=pm_r[:, pr, :], scalar=pv[:, 4, pr:pr + 1], in1=kp[:, :],
                    op0=ALU.mult, op1=ALU.mult), reads=[rr_, r_kp, r_par], writes=[r_prodb[pr]])
                G = Gp[pr]
                sch.op("dve", lambda e, G=G: e.tensor_copy(out=G[:, 0:1], in_=G[:, T:T + 1]),
                       reads=[r_Gp[pr]], writes=[r_Gp[pr]])
                sch.op("dve", lambda e, G=G: e.tensor_tensor_scan(out=G[:, 1:1 + T], data0=lw[:, :], data1=lw[:, :],
                                                                  initial=G[:, 0:1], op0=ALU.add, op1=ALU.min),
                       reads=[r_lw, r_Gp[pr]], writes=[r_Gp[pr]])
                base = v3(G[:, 0:T])[:, :, 0:1].broadcast_to([128, NCH, L])
                sch.op("dve", lambda e, G=G, base=base: e.tensor_tensor(out=v3(gi[:, :]), in0=v3(G[:, 1:1 + T]),
                                                                        in1=base, op=ALU.subtract),
                       reads=[r_Gp[pr]], writes=[r_gi])
                sch.op("pool", lambda e: e.tensor_tensor(out=ge[:, :], in0=gi[:, :], in1=lw[:, :], op=ALU.subtract),
                       reads=[r_gi, r_lw], writes=[r_ge])
                e1 = E1[pr]
                sch.op("act", lambda e, e1=e1: e.activation(out=e1[:, :], in_=gi[:, :], func=AF.Exp),
                       reads=[r_gi], writes=[r_E1[pr]])
                sch.op("act", lambda e: e.activation(out=E2[:, :], in_=ge[:, :], func=AF.Exp),
                       reads=[r_ge], writes=[r_E2])
                for hf in range(2):
                    ps_ = slice(hf * 64, hf * 64 + 64)
                    hq = 2 * pr + hf
                    sch.op("dve", lambda e, hq=hq, ps_=ps_: e.scalar_tensor_tensor(
                        out=ARz[hq][ps_, 0, :], in0=kx[ps_, :], scalar=-1.0, in1=E2[ps_, :], op0=ALU.mult, op1=ALU.mult),
                        reads=[r_kx, r_E2], wacc=[r_AR[pr]])
                    sch.op("pool", lambda e, hq=hq, ps_=ps_, pr=pr, e1=e1: e.tensor_tensor(
                        out=ARz[hq][ps_, 1, :], in0=pm_r[ps_, pr, :], in1=e1[ps_, :], op=ALU.mult),
                        reads=[rr_, r_E1[pr]], wacc=[r_AR[pr]])
                sch.op("act", lambda e: e.activation(out=E2[:, :], in_=gi[:, :], func=AF.Exp, scale=-1.0),
                       reads=[r_gi, r_AR[pr]], writes=[r_E2])
                sch.op("dve", lambda e, pr=pr: e.tensor_tensor(out=Bt[:, pr, :], in0=bb[:, :], in1=E2[:, :],
                                                               op=ALU.mult), reads=[r_bb, r_E2], writes=[r_Bt[pr]])
                sch.op("pool", lambda e, pr=pr: e.tensor_tensor(out=Kt[:, pr, :], in0=kp[:, :], in1=E2[:, :],
                                                                op=ALU.mult), reads=[r_kp, r_E2], writes=[r_Kt[pr]])
                gend = v3(gi[:, :])[:, :, L - 1:L].broadcast_to([128, NCH, L])
                sch.op("dve", lambda e, gend=gend: e.tensor_tensor(out=v3(ge[:, :]), in0=gend, in1=v3(gi[:, :]),
                                                                   op=ALU.subtract), reads=[r_gi], writes=[r_ge])
                sch.op("act", lambda e: e.activation(out=ge[:, :], in_=ge[:, :], func=AF.Exp),
                       reads=[r_ge], writes=[r_ge])
                sch.op("dve", lambda e, pr=pr: e.tensor_tensor(out=Bh[:, pr, :], in0=bb[:, :], in1=ge[:, :],
                                                               op=ALU.mult), reads=[r_bb, r_ge], writes=[r_Bh[pr]])
                sch.op("pool", lambda e, pr=pr: e.tensor_tensor(out=Kh[:, pr, :], in0=kp[:, :], in1=ge[:, :],
                                                                op=ALU.mult), reads=[r_kp, r_ge], writes=[r_Kh[pr]])
            for cc in range(NCH):
                cs = slice(cc * L, (cc + 1) * L)
                gcs = slice(t0 + cc * L, t0 + (cc + 1) * L)
                for pr in range(2):
                    tpp = banks[0]
                    for j, (src, rs) in enumerate(((Bh, r_Bh[pr]), (Kh, r_Kh[pr]), (pm_v, r_pm["v%d" % pr]))):
                        sch.op("pe", lambda e, j=j, src=src, pr=pr, cs=cs: e.transpose(
                            out=tpp[0:64, j * 128:(j + 1) * 128], in_=src[:, pr, cs], identity=C.identf[:, :]),
                            reads=[rs], writes=[rb[0]])
                    sch.op("act", lambda e, pr=pr: e.copy(
                        out=tok[:, pr, 0:3, :, :],
                        in_=banks[0][0:64, 0:384].rearrange("p (j h e) -> p j h e", j=3, h=2)),
                        reads=[rb[0]], writes=[r_tok[pr]])
                for h in range(4):
                    pr, pb = h // 2, (h % 2) * 64
                    bk = banks[1 + pr]
                    o = (h % 2) * 256
                    rhs_ar = ARz[h][:, :, cs]
                    sch.op("pe", lambda e, bk=bk, o=o, pr=pr, rhs_ar=rhs_ar, cs=cs: e.matmul(
                        out=bk[0:64, o:o + 128], lhsT=Bt[:, pr, cs], rhs=rhs_ar, start=True, stop=True),
                        reads=[r_Bt[pr], r_AR[pr]], writes=[rb[1 + pr]])
                    sch.op("pe", lambda e, bk=bk, o=o, pr=pr, rhs_ar=rhs_ar, cs=cs: e.matmul(
                        out=bk[0:64, o + 128:o + 256], lhsT=Kt[:, pr, cs], rhs=rhs_ar, start=True, stop=True),
                        reads=[r_Kt[pr], r_AR[pr]], writes=[rb[1 + pr]])
                    sch.op("pe", lambda e, h=h, pr=pr, cs=cs: e.matmul(
                        out=banks[4][0:64, h * 64:(h + 1) * 64], lhsT=ARz[h][:, 0, cs],
                        rhs=Bt[:, pr, cs], start=True, stop=True),
                        reads=[r_Bt[pr], r_AR[pr]], writes=[rb[4]])
                for pr in range(2):
                    sch.op("dve", lambda e, pr=pr: e.tensor_tensor(
                        out=scs[:, 2 * pr:2 * pr + 2, :, :].rearrange("p h b t -> p (h b t)"),
                        in0=banks[1 + pr][0:64, :], in1=C.rwmask[:, :], op=ALU.mult),
                        reads=[rb[1 + pr]], writes=[r_scs[pr]])
                sch.op("dve", lambda e: e.tensor_tensor(out=nns[:, :, :].rearrange("p h t -> p (h t)"),
                                                        in0=banks[4][0:64, 0:256], in1=C.nmask[:, :], op=ALU.mult),
                       reads=[rb[4]], writes=[r_nns])

                def Mlev(lev, h):
                    return scs[:, h, 0, :] if lev == 0 else pws[lev - 1][:, h, 1, :]

                def Nlev(lev, h):
                    return nns[:, h, :] if lev == 0 else pws[lev - 1][:, h, 0, :]

                def Rlev(lev, h):
                    return [r_scs[h // 2], r_nns] if lev == 0 else [r_pws[lev - 1]]
                for lev in range(1, 6):
                    for h in range(4):
                        if lev < 5:
                            sch.op("pe", lambda e, lev=lev, h=h: e.matmul(
                                out=banks[3][0:64, h * 128:h * 128 + 64], lhsT=Mlev(lev - 1, h), rhs=Nlev(lev - 1, h),
                                start=True, stop=True), reads=Rlev(lev - 1, h), writes=[rb[3]])
                        sch.op("pe", lambda e, lev=lev, h=h: e.matmul(
                            out=banks[3][0:64, h * 128 + 64:h * 128 + 128], lhsT=Nlev(lev - 1, h), rhs=Mlev(lev - 1, h),
                            start=True, stop=True), reads=Rlev(lev - 1, h), writes=[rb[3]])
                    evac(C, lev, pws[lev - 1][:, :, :, :].rearrange("p h b t -> p (h b t)"), banks[3][0:64, :],
                         [rb[3]], writes=[r_pws[lev - 1]])
                wq = banks[5]
                for h in range(4):
                    pr, pb, hf = h // 2, (h % 2) * 64, h % 2
                    sch.op("pe", lambda e, h=h, pr=pr, cs=cs: e.matmul(
                        out=wq[0:64, h * 64:(h + 1) * 64], lhsT=ARz[h][:, 0, cs], rhs=Hb[:, pr, :],
                        start=True, stop=False), reads=[r_AR[pr], r_Hb], writes=[rb[5]])
                    sch.op("pe", lambda e, h=h, pr=pr, hf=hf: e.matmul(
                        out=wq[0:64, h * 64:(h + 1) * 64], lhsT=scs[:, h, 2, :], rhs=tok[:, pr, 2, hf, :],
                        start=False, stop=True), reads=[r_scs[pr], r_tok[pr]], writes=[rb[5]])
                cur = 0
                sch.op("dve", lambda e: e.tensor_copy(out=U[0][:, :, :].rearrange("p h e -> p (h e)"),
                                                      in_=wq[0:64, 0:256]), reads=[rb[5]], writes=[r_U[0]])
                for lev in range(6):
                    for h in range(4):
                        sch.op("pe", lambda e, lev=lev, h=h, cur=cur: e.matmul(
                            out=wq[0:64, h * 64:(h + 1) * 64], lhsT=Mlev(lev, h), rhs=U[cur][:, h, :],
                            start=True, stop=True), reads=Rlev(lev, h) + [r_U[cur]], writes=[rb[5]])
                    sch.op("dve", lambda e, cur=cur: e.tensor_tensor(
                        out=U[1 - cur][:, :, :].rearrange("p h e -> p (h e)"), in0=wq[0:64, 0:256],
                        in1=U[cur][:, :, :].rearrange("p h e -> p (h e)"), op=ALU.add),
                        reads=[rb[5], r_U[cur]], writes=[r_U[1 - cur]])
                    cur = 1 - cur
                Uf, rUf = U[cur], r_U[cur]
                b6 = banks[6]
                for h in range(4):
                    pr, pb, hf = h // 2, (h % 2) * 64, h % 2
                    o = b6[0:64, h * 64:(h + 1) * 64]
                    sch.op("pe", lambda e, o=o, h=h, pr=pr, cs=cs: e.matmul(
                        out=o, lhsT=ARz[h][:, 1, cs], rhs=Hb[:, pr, :], start=True, stop=False),
                        reads=[r_AR[pr], r_Hb], writes=[rb[6]])
                    sch.op("pe", lambda e, o=o, h=h, Uf=Uf: e.matmul(
                        out=o, lhsT=scs[:, h, 1, :], rhs=Uf[:, h, :], start=False, stop=False),
                        reads=[r_scs[pr], rUf], writes=[rb[6]])
                    sch.op("pe", lambda e, o=o, h=h, pr=pr, hf=hf: e.matmul(
                        out=o, lhsT=scs[:, h, 3, :], rhs=tok[:, pr, 2, hf, :], start=False, stop=True),
                        reads=[r_scs[pr], r_tok[pr]], writes=[rb[6]])
                for h in range(4):
                    pr, pb, hf = h // 2, (h % 2) * 64, h % 2
                    o = b6[:, 256 + h * 64:256 + (h + 1) * 64]
                    sch.op("pe", lambda e, o=o, h=h, pr=pr, hf=hf, Uf=Uf: e.matmul(
                        out=o, lhsT=tok[:, pr, 0, :, :].rearrange("p a e -> p (a e)"), rhs=Uf[:, h, :],
                        start=True, stop=False), reads=[r_tok[pr], rUf], writes=[rb[6]])
                    sch.op("pe", lambda e, o=o, pr=pr, hf=hf: e.matmul(
                        out=o, lhsT=tok[:, pr, 1, :, :].rearrange("p a e -> p (a e)"), rhs=tok[:, pr, 2, hf, :],
                        start=False, stop=True), reads=[r_tok[pr]], writes=[rb[6]])
                sch.op("act", lambda e: e.copy(out=yv[:, :, :].rearrange("p h e -> p (h e)"), in_=b6[0:64, 0:256]),
                       reads=[rb[6]], writes=[r_yv])
                for h in range(4):
                    pr, pb = h // 2, (h % 2) * 64
                    sch.op("dve", lambda e, h=h, pr=pr, pb=pb, cc=cc: e.scalar_tensor_tensor(
                        out=Hs[pb:pb + 64, pr, :], in0=Hs[pb:pb + 64, pr, :],
                        scalar=E1[pr][pb:pb + 64, cc * L + L - 1:cc * L + L],
                        in1=b6[pb:pb + 64, 256 + h * 64:256 + (h + 1) * 64], op0=ALU.mult, op1=ALU.add),
                        reads=[r_Hs, r_E1[pr], rb[6]], wacc=[r_Hs])
                sch.op("act", lambda e: e.copy(out=Hb[:, :, :], in_=Hs[:, :, :]), reads=[r_Hs], writes=[r_Hb])
                b4 = banks[4]
                sch.op("pe", lambda e, cs=cs: e.matmul(out=b4[0:64, 256:512], lhsT=lin_g[:, cs], rhs=lup[0:64, 2, :],
                                                       start=True, stop=True), reads=[r_ling, r_par, r_nns], writes=[rb[4]])
                sch.op("dve", lambda e: e.tensor_reduce(out=st[:, 0, :, :], in_=yv[:, :, :], axis=AX.X, op=ALU.add),
                       reads=[r_yv], writes=[r_st])
                sch.op("pool", lambda e: e.tensor_tensor(out=ysq[:, :, :], in0=yv[:, :, :], in1=yv[:, :, :], op=ALU.mult),
                       reads=[r_yv], writes=[r_ysq])
                sch.op("dve", lambda e: e.tensor_reduce(out=st[:, 1, :, :], in_=ysq[:, :, :], axis=AX.X, op=ALU.add),
                       reads=[r_ysq], writes=[r_st])
                sch.op("dve", lambda e: e.tensor_scalar(out=st[:, 2, :, :], in0=st[:, 0, :, :], scalar1=1.0 / 64,
                                                        scalar2=None, op0=ALU.mult), reads=[r_st], writes=[r_st])
                sch.op("dve", lambda e: e.tensor_tensor(out=st[:, 3, :, :], in0=st[:, 2, :, :], in1=st[:, 2, :, :],
                                                        op=ALU.mult), reads=[r_st], writes=[r_st])
                sch.op("dve", lambda e: e.scalar_tensor_tensor(out=st[:, 4, :, :], in0=st[:, 1, :, :], scalar=1.0 / 64,
                                                               in1=st[:, 3, :, :], op0=ALU.mult, op1=ALU.subtract),
                       reads=[r_st], writes=[r_st])
                sch.op("dve", lambda e: e.tensor_scalar(out=st[:, 4, :, :], in0=st[:, 4, :, :], scalar1=64e-5,
                                                        scalar2=None, op0=ALU.add), reads=[r_st], writes=[r_st])
                sch.op("act", lambda e: e.activation(out=st[:, 5, :, :], in_=st[:, 4, :, :], func=AF.Sqrt),
                       reads=[r_st], writes=[r_st])
                sch.op("dve", lambda e: e.reciprocal(out=st[:, 6, :, :], in_=st[:, 5, :, :]), reads=[r_st], writes=[r_st])
                sch.op("dve", lambda e: e.tensor_tensor(out=ysq[:, :, :], in0=yv[:, :, :],
                                                        in1=st[:, 2, :, :].broadcast_to([64, 4, 64]), op=ALU.subtract),
                       reads=[r_yv, r_st], writes=[r_ysq])
                sch.op("dve", lambda e: e.tensor_tensor(out=ysq[:, :, :], in0=ysq[:, :, :],
                                                        in1=st[:, 6, :, :].broadcast_to([64, 4, 64]), op=ALU.mult),
                       reads=[r_ysq, r_st], writes=[r_ysq])
                y2 = ysq[:, :, :].rearrange("p h e -> p (h e)")
                sch.op("pool", lambda e: e.tensor_tensor(out=y2, in0=y2, in1=lng[:, :], op=ALU.mult),
                       reads=[r_ysq, r_par], writes=[r_ysq])
                sch.op("pool", lambda e: e.tensor_tensor(out=y2, in0=y2, in1=lnb[:, :], op=ALU.add),
                       reads=[r_ysq, r_par], writes=[r_ysq])
                for pr in range(2):
                    sch.op("pe", lambda e, pr=pr, cs=cs: e.matmul(out=b4[0:64, 2 * pr:2 * pr + 2], lhsT=prodb[:, pr, cs],
                                                                  rhs=C.sel2b[:, :], start=True, stop=True),
                           reads=[r_prodb[pr], r_nns], writes=[rb[4]])
                sch.op("act", lambda e: e.copy(out=st[:, 7, :, 0], in_=b4[0:64, 0:4]), reads=[rb[4]], writes=[r_st])
                for pr in range(2):
                    sch.op("dve", lambda e, pr=pr: e.tensor_tensor(
                        out=yv[:, 2 * pr:2 * pr + 2, :], in0=tok[:, pr, 2, :, :],
                        in1=st[:, 7, 2 * pr:2 * pr + 2, :].broadcast_to([64, 2, 64]), op=ALU.mult),
                        reads=[r_tok[pr], r_st], wacc=[r_yv])
                sch.op("pool", lambda e: e.tensor_tensor(out=ysq[:, :, :], in0=ysq[:, :, :], in1=yv[:, :, :], op=ALU.add),
                       reads=[r_ysq, r_yv], writes=[r_ysq])
                sch.op("dve", lambda e: e.tensor_tensor(out=yo[:, :], in0=y2, in1=b4[0:64, 256:512], op=ALU.mult),
                       reads=[r_ysq, rb[4]], writes=[r_yo])
                for j in range(2):
                    sch.op("pe", lambda e, j=j: e.transpose(out=banks[0][:, 384 + j * 64:384 + (j + 1) * 64],
                                                            in_=yo[:, j * 128:(j + 1) * 128],
                                                            identity=C.identf[0:64, 0:64]),
                           reads=[r_yo], writes=[rb[0]])
                sch.op("act", lambda e, gcs=gcs: e.copy(out=yrT[:, :, gcs],
                                                        in_=banks[0][:, 384:512].rearrange("p (j t) -> p j t", j=2)),
                       reads=[rb[0]], wacc=[r_yrT])
        for j in range(2):
            sch.dma("sp", d["mixT"][256 + j * 128:256 + (j + 1) * 128, :], yrT[:, j, :], reads=[r_yrT],
                    wacc=[C.r_mixT])


DOFF_ = (2, 4, 6, 9, 12, 15)


def phase_C(C, l, bg=None, bg_every=5):
    nc, sch, d = C.nc, C.sch, C.d
    QB = 512
    lam_init = 0.8 - 0.6 * math.exp(-0.3 * l)
    with ExitStack() as es:
        def sb(name, shape, dt):
            return _sb(es, nc, "C_" + name, shape, dt)
        lq = sb("lq", [128, 4, 64], F32)
        lt = sb("lt", [128, 2, 64], F32)
        ls = sb("ls", [128, 4], F32)
        nlam = sb("nlam", [128, 1], F32)
        ag = sb("ag", [128, 4], F32)
        r_par = Res()
        NHB = 1 if bg is not None else 2
        qT = [[sb("qT%d_%d" % (i, c), [128, S], BF16) for c in range(2)] for i in range(NHB)]
        kT = [sb("kT%d" % i, [128, S], BF16) for i in range(NHB)]
        vt = [sb("vt%d" % i, [128, NT, 128], BF16) for i in range(NHB)]
        r_q, r_k, r_v = [Res(), Res()], [Res(), Res()], [Res(), Res()]
        NB = 3
        pt = [sb("pt%d" % i, [128, QB], BF16) for i in range(NB)]
        r_pt = [Res() for _ in range(NB)]
        rl = sb("rl", [128, QB], F32)
        oc = [sb("oc%d" % i, [128, QB], F32) for i in range(2)]
        od = sb("od", [128, QB], F32)
        sq = sb("sq", [128, QB], F32)
        rs = sb("rs", [128, QB], F32)
        ya = sb("ya", [128, QB], BF16)
        r_rl, r_od, r_sq, r_rs, r_ya = Res(), Res(), Res(), Res(), Res()
        r_oc = [Res(), Res()]
        p_st = [_ps(es, nc, "C_pst%d" % i, [128, 512], F32) for i in range(3)]
        rp_st = [PRes() for _ in range(3)]
        p_os = [_ps(es, nc, "C_po%d" % i, [128, 512], F32) for i in range(2)]
        p_ls = [_ps(es, nc, "C_pl%d" % i, [128, 512], F32) for i in range(2)]
        p_ss = _ps(es, nc, "C_pss", [128, 512], F32)
        rp_os, rp_ls, rp_ss = [PRes(), PRes()], [PRes(), PRes()], PRes()
        deferred = []

        for i, nm in enumerate(("a_lq1", "a_lk1", "a_lq2", "a_lk2")):
            sch.dma("sp", lq[:, i, :], d[nm][l], wacc=[r_par])
        sch.dma("sp", ag[:, :], d["a_norm_g"][l], wacc=[r_par])
        for i in range(2):
            sch.op("dve", lambda e, i=i: e.tensor_tensor(out=lt[:, i, :], in0=lq[:, 2 * i, :], in1=lq[:, 2 * i + 1, :],
                                                         op=ALU.mult), reads=[r_par], writes=[r_par])
        sch.op("dve", lambda e: e.tensor_reduce(out=ls[:, 0:2], in_=lt[:, :, :], axis=AX.X, op=ALU.add),
               reads=[r_par], writes=[r_par])
        sch.op("act", lambda e: e.activation(out=ls[:, 2:4], in_=ls[:, 0:2], func=AF.Exp), reads=[r_par], writes=[r_par])
        sch.op("dve", lambda e: e.tensor_tensor(out=nlam[:, :], in0=ls[:, 3:4], in1=ls[:, 2:3], op=ALU.subtract),
               reads=[r_par], writes=[r_par])
        sch.op("dve", lambda e: e.tensor_scalar(out=nlam[:, :], in0=nlam[:, :], scalar1=-lam_init, scalar2=None,
                                                op0=ALU.add), reads=[r_par], writes=[r_par])
        sch.op("dve", lambda e: e.tensor_scalar(out=ag[:, :], in0=ag[:, :], scalar1=1.0 - lam_init, scalar2=None,
                                                op0=ALU.mult), reads=[r_par], writes=[r_par])
        ist = 0
        ipt = 0
        for h in range(4):
            hb = h % NHB
            for c in range(2):
                if h < NHB:
                    sch.op("pool", lambda e, hb=hb, c=c: e.memset(qT[hb][c][(1 - c) * 64:(1 - c) * 64 + 64, :], 0.0),
                           wacc=[r_q[hb]])
                sch.dma("sp", qT[hb][c][c * 64:c * 64 + 64, :], d["aqkT"][h * 128 + c * 64:h * 128 + c * 64 + 64, :],
                        reads=[C.r_aqkT], wacc=[r_q[hb]])
            sch.dma("sp", kT[hb][:, :], d["aqkT"][512 + h * 128:512 + (h + 1) * 128, :], reads=[C.r_aqkT],
                    writes=[r_k[hb]])
            sch.dma("sp", vt[hb][:, :, :], d["av"][:, h * 128:(h + 1) * 128].rearrange("(t p) e -> p t e", p=128),
                    reads=[C.r_av], writes=[r_v[hb]])
            items = [(j, c, kt) for j in range(S // QB) for c in range(2) for kt in range(4 * (j + 1))]

            def front(it, idx):
                j, c, kt = it
                i0 = max(0, kt - 4 * j)
                n0 = i0 * 128
                ps, rps = p_st[idx % NB], rp_st[idx % NB]
                p_, rp_ = pt[idx % NB], r_pt[idx % NB]
                sch.op("pe", lambda e: e.matmul(
                    out=ps[:, n0:QB], lhsT=kT[hb][:, kt * 128:(kt + 1) * 128],
                    rhs=qT[hb][c][:, j * QB + n0:(j + 1) * QB], start=True, stop=True),
                    reads=[r_k[hb], r_q[hb]], writes=[rps])
                sch.op("act", lambda e: e.activation(out=p_[:, n0:QB], in_=ps[:, n0:QB], func=AF.Exp, scale=0.125),
                       reads=[rps], writes=[rp_])
                if kt >= 4 * j:
                    eng = "dve"
                    sch.op(eng, lambda e: e.tensor_tensor(out=p_[:, n0:n0 + 128], in0=p_[:, n0:n0 + 128],
                                                          in1=C.trib[:, :], op=ALU.mult), reads=[rp_], writes=[rp_])

            def back(it, idx):
                j, c, kt = it
                nk = 4 * (j + 1)
                gpar = (2 * j + c) % 2
                p_o, rp_o = p_os[gpar], rp_os[gpar]
                p_l, rp_l = p_ls[gpar], rp_ls[gpar]
                i0 = max(0, kt - 4 * j)
                n0 = i0 * 128
                p_, rp_ = pt[idx % NB], r_pt[idx % NB]
                sch.op("pe", lambda e: e.matmul(out=p_o[:, n0:QB], lhsT=vt[hb][:, kt, :], rhs=p_[:, n0:QB],
                                                start=(kt == 0), stop=(kt == nk - 1)),
                       reads=[r_v[hb], rp_], writes=[rp_o])
                sch.op("pe", lambda e: e.matmul(out=p_l[:, n0:QB], lhsT=C.onesb[:, :], rhs=p_[:, n0:QB],
                                                start=(kt == 0), stop=(kt == nk - 1)), reads=[rp_], writes=[rp_l])
                if kt != nk - 1:
                    return
                def d0():
                    sch.op("act", lambda e: e.activation(out=rl[:, :], in_=p_l[:, :], func=AF.Ln),
                           reads=[rp_l], writes=[r_rl])
                    sch.op("act", lambda e: e.activation(out=rl[:, :], in_=rl[:, :], func=AF.Exp, scale=-1.0),
                           reads=[r_rl], writes=[r_rl])

                def d0b():
                    sch.op("dve", lambda e: e.tensor_tensor(out=oc[c][:, :], in0=p_o[:, :], in1=rl[:, :],
                                                            op=ALU.mult), reads=[rp_o, r_rl], writes=[r_oc[c]])
                deferred.extend([[DOFF_[0], d0], [DOFF_[1], d0b]])
                if c == 0:
                    return
                qs = slice(j * QB, (j + 1) * QB)

                def d1():
                    sch.op("dve", lambda e: e.scalar_tensor_tensor(out=od[:, :], in0=oc[1][:, :], scalar=nlam[:, 0:1],
                                                                   in1=oc[0][:, :], op0=ALU.mult, op1=ALU.add),
                           reads=[r_oc[0], r_oc[1], r_par], writes=[r_od])
                    sch.op("pool", lambda e: e.tensor_tensor(out=sq[:, :], in0=od[:, :], in1=od[:, :], op=ALU.mult),
                           reads=[r_od], writes=[r_sq])

                def d2():
                    sch.op("pe", lambda e: e.matmul(out=p_ss[:, :], lhsT=C.onesf[:, :], rhs=sq[:, :], start=True,
                                                    stop=True), reads=[r_sq], writes=[rp_ss])
                    sch.op("dve", lambda e: e.tensor_scalar(out=rs[:, :], in0=p_ss[:, :], scalar1=1.0 / 128,
                                                            scalar2=1e-5, op0=ALU.mult, op1=ALU.add),
                           reads=[rp_ss], writes=[r_rs])

                def d3():
                    sch.op("act", lambda e: e.activation(out=rs[:, :], in_=rs[:, :], func=AF.Ln),
                           reads=[r_rs], writes=[r_rs])
                    sch.op("act", lambda e: e.activation(out=rs[:, :], in_=rs[:, :], func=AF.Exp, scale=-0.5),
                           reads=[r_rs], writes=[r_rs])

                def d4():
                    sch.op("dve", lambda e: e.scalar_tensor_tensor(out=ya[:, :], in0=od[:, :], scalar=ag[:, h:h + 1],
                                                                   in1=rs[:, :], op0=ALU.mult, op1=ALU.mult),
                           reads=[r_od, r_rs, r_par], writes=[r_ya])
                    sch.dma("sp", d["mixT"][512 + h * 128:512 + (h + 1) * 128, qs], ya[:, :], reads=[r_ya],
                            wacc=[C.r_mixT])
                deferred.extend([[DOFF_[2], d1], [DOFF_[3], d2], [DOFF_[4], d3], [DOFF_[5], d4]])

            def run_deferred(force=False):
                for e_ in list(deferred):
                    e_[0] -= 1
                    if force or e_[0] <= 0:
                        deferred.remove(e_)
                        e_[1]()

            LOOK = 2
            for i in range(min(LOOK, len(items))):
                front(items[i], ist + i)
            for i in range(len(items)):
                if i + LOOK < len(items):
                    front(items[i + LOOK], ist + i + LOOK)
                back(items[i], ist + i)
                run_deferred()
                if bg is not None and i % bg_every == bg_every // 2:
                    for _ in range(globals().get("BGN_", 1)):
                        next(bg, None)
            ist += len(items)
            while deferred:
                run_deferred(force=True)
        if bg is not None:
            for _ in bg:
                pass


PARAM_SHAPES = {
    "norm1_g": [DEPTH, 128, 8], "norm2_g": [DEPTH, 128, 8], "final_gb": [128, D],
    "w_in": [DEPTH, D, IN_PROJ], "w_out": [DEPTH, D, D], "w_ff_up": [DEPTH, D, DFF], "w_ff_down": [DEPTH, DFF, D],
    "m_conv_w": [DEPTH, 128, 4, 4], "m_conv_b": [DEPTH, 128, 4], "m_b_i": [DEPTH, 4, 1], "m_b_f": [DEPTH, 4, 1],
    "m_norm_g": [DEPTH, 128, 256],
    "r_mu": [DEPTH, 128, 7], "r_w0": [DEPTH, 128, 2], "r_a0": [DEPTH, 128, 2], "r_k_k": [DEPTH, 128, 2],
    "r_k_a": [DEPTH, 128, 2], "r_r_k": [DEPTH, 128, 2], "r_w_up": [DEPTH, 32, 256], "r_a_up": [DEPTH, 32, 256],
    "r_g_up": [DEPTH, 64, 256], "r_ln_g": [DEPTH, 64, 256], "r_ln_b": [DEPTH, 64, 256],
    "a_lq1": [DEPTH, 128, 64], "a_lk1": [DEPTH, 128, 64], "a_lq2": [DEPTH, 128, 64], "a_lk2": [DEPTH, 128, 64],
    "a_norm_g": [DEPTH, 128, 4],
    "c_ident": [128, 128], "c_tri": [128, 128], "c_rwmask": [64, 512], "c_nmask": [64, 256],
    "c_sel4": [4, 2, 128], "c_blk": [128, 128], "c_sel2": [128, 2],
}
SCRATCH = {
    "mqkT": ([512, S], F32), "mgT": ([8, S], F32), "mvo": ([S, 512], F32), "rT": ([896, S], F32),
    "aqkT": ([1024, S], BF16), "av": ([S, 512], BF16), "mixT": ([1024, S], BF16), "xres": ([S, D], F32),
    "rpb": ([1024, 15 * 512], BF16), "rpf": ([1024, 8 * 512], F32),
}


BIGW = ("w_in", "w_out", "w_ff_up", "w_ff_down")


def build(phases=None, debug=(), ext_in=(), limit=None):
    nc = bass.Bass("TRN2", target_bir_lowering=False)
    C = Ctx()
    C.nc = nc
    d = {}
    d["x"] = nc.dram_tensor("x", [S, D], F32, kind="ExternalInput").ap()
    need_big = phases is None or any(ph in ("A", "O") for ph, _ in phases)
    for k, shp in PARAM_SHAPES.items():
        if k in BIGW and not need_big:
            continue
        d[k] = nc.dram_tensor(k, shp, F32, kind="ExternalInput").ap()
    for k, (shp, dt) in SCRATCH.items():
        kind = "ExternalOutput" if k in debug else "Internal"
        if k in ext_in:
            kind = "ExternalInput"
        d[k] = nc.dram_tensor(k, shp, dt, kind=kind).ap()
    d["y"] = nc.dram_tensor("y", [S, D], F32, kind="ExternalOutput").ap()
    C.d = d
    for k in ("mqkT", "mgT", "mvo", "rT", "aqkT", "av", "mixT", "xres", "x", "y", "rpb", "rpf"):
        setattr(C, "r_" + k, Res(k))
    with ExitStack() as es:
        sch = Sched(nc, es)
        sch.limit = limit
        C.sch = sch
        C.identf = _sb(es, nc, "identf", [128, 128], F32)
        C.identb = _sb(es, nc, "identb", [128, 128], BF16)
        C.trif = _sb(es, nc, "trif", [128, 128], F32)
        C.trib = _sb(es, nc, "trib", [128, 128], BF16)
        C.rwmask = _sb(es, nc, "rwmask", [64, 512], F32)
        C.nmask = _sb(es, nc, "nmask", [64, 256], F32)
        C.sel4 = _sb(es, nc, "sel4", [4, 2, 128], F32)
        C.blkf = _sb(es, nc, "blkf", [128, 128], F32)
        C.sel2f = _sb(es, nc, "sel2f", [128, 2], F32)
        C.sel2b = _sb(es, nc, "sel2b", [128, 2], BF16)
        C.onesf = _sb(es, nc, "onesf", [128, 128], F32)
        C.onesb = _sb(es, nc, "onesb", [128, 128], BF16)
        rc = Res()
        for t, k in ((C.identf, "c_ident"), (C.trif, "c_tri"), (C.rwmask, "c_rwmask"), (C.nmask, "c_nmask"),
                     (C.blkf, "c_blk"), (C.sel2f, "c_sel2")):
            sch.dma("sp", t[:, :], d[k], wacc=[rc])
        sch.dma("sp", C.sel4[:, :, :], d["c_sel4"], wacc=[rc])
        sch.op("dve", lambda e: e.tensor_copy(out=C.identb[:, :], in_=C.identf[:, :]), reads=[rc], wacc=[rc])
        sch.op("dve", lambda e: e.tensor_copy(out=C.trib[:, :], in_=C.trif[:, :]), reads=[rc], wacc=[rc])
        sch.op("dve", lambda e: e.tensor_copy(out=C.sel2b[:, :], in_=C.sel2f[:, :]), reads=[rc], wacc=[rc])
        sch.op("pool", lambda e: e.memset(C.onesf[:, :], 1.0), wacc=[rc])
        sch.op("pool", lambda e: e.memset(C.onesb[:, :], 1.0), wacc=[rc])
        for eng in ("pe", "dve", "act", "pool"):
            sch._deps(eng, [rc], [])
        if phases is None:
            phases = []
            for l in range(DEPTH):
                phases += [("A", l), ("M", l), ("R", l), ("C", l), ("O", l)]
        phases_all = list(phases)
        for (ph, l) in phases:
            sch.barrier()
            X, rX = (d["x"], C.r_x) if l == 0 else (d["xres"], C.r_xres)
            if ph == "A":
                phase_A(C, l, X, rX)
            elif ph == "M":
                phase_M(C, l)
            elif ph == "R":
                phase_R2(C, l)
            elif ph == "C":
                if getattr(build, "preload", True) and (phases_all.count(("O", l)) > 0):
                    wes = ExitStack()
                    C.ow = OWeights(C, wes, l)
                    C.ow_es = wes
                    phase_C(C, l, bg=C.ow.gen())
                else:
                    C.ow = None
                    phase_C(C, l)
            elif ph == "RC":
                phase_RC(C, l)
            elif ph == "CP":
                wes = ExitStack()
                gen = phase_R2(C, l, mode="prep", es_ext=wes)
                phase_C(C, l, bg=gen, bg_every=1)
                wes.close()
            elif ph == "R3":
                phase_R2(C, l, mode="chunks")
            elif ph == "RP":
                wes = ExitStack()
                for _ in phase_R2(C, l, mode="prep", es_ext=wes):
                    pass
                wes.close()
            elif ph == "O":
                last = (l == DEPTH - 1)
                XO, rXO = (d["y"], C.r_y) if last else (d["xres"], C.r_xres)
                ow = getattr(C, "ow", None)
                phase_O(C, l, X, rX, XO, rXO, last, W=ow)
                if ow is not None:
                    C.ow_es.close()
                    C.ow = None
        sch.limit = None
        sch.finish()
        C.nops = sch.nops
    build.last_nops = sch.nops
    return nc


def host_params(inp):
    f = lambda a: np.ascontiguousarray(np.asarray(a, dtype=np.float32))
    L = DEPTH
    p = {}
    p["norm1_g"] = f(np.asarray(inp["norm1_g"]).reshape(L, 8, 128).transpose(0, 2, 1))
    p["norm2_g"] = f(np.asarray(inp["norm2_g"]).reshape(L, 8, 128).transpose(0, 2, 1))
    p["final_gb"] = f(np.broadcast_to(np.asarray(inp["final_g"]).reshape(1, D), (128, D)))
    for k in ("w_in", "w_out", "w_ff_up", "w_ff_down", "r_w_up", "r_a_up", "r_g_up"):
        p[k] = f(inp[k])
    p["m_conv_w"] = f(np.asarray(inp["m_conv_w"]).reshape(L, 4, 4, 128).transpose(0, 3, 2, 1))
    p["m_conv_b"] = f(np.asarray(inp["m_conv_b"]).reshape(L, 4, 128).transpose(0, 2, 1))
    p["m_b_i"] = f(np.asarray(inp["m_b_i"]).reshape(L, 4, 1))
    p["m_b_f"] = f(np.asarray(inp["m_b_f"]).reshape(L, 4, 1))
    p["m_norm_g"] = f(np.broadcast_to(np.asarray(inp["m_norm_g"]).reshape(L, 1, 256), (L, 128, 256)))
    p["r_mu"] = f(np.asarray(inp["r_mu"]).reshape(L, 7, 128).transpose(0, 2, 1))
    for k in ("r_w0", "r_a0", "r_k_k", "r_k_a", "r_r_k"):
        p[k] = f(np.asarray(inp[k]).reshape(L, 2, 128).transpose(0, 2, 1))
    p["r_ln_g"] = f(np.broadcast_to(np.asarray(inp["r_ln_g"]).reshape(L, 1, 256), (L, 64, 256)))
    p["r_ln_b"] = f(np.broadcast_to(np.asarray(inp["r_ln_b"]).reshape(L, 1, 256), (L, 64, 256)))
    for k in ("a_lq1", "a_lk1", "a_lq2", "a_lk2"):
        p[k] = f(np.broadcast_to(np.asarray(inp[k]).reshape(L, 1, 64), (L, 128, 64)))
    p["a_norm_g"] = f(np.asarray(inp["a_norm_g"]).reshape(L, 4, 128).transpose(0, 2, 1))
    pp = np.arange(128)[:, None]
    nn = np.arange(128)[None, :]
    p["c_ident"] = f(pp == nn)
    p["c_tri"] = f(pp <= nn)
    p64, n64 = np.arange(64)[:, None], np.arange(64)[None, :]
    strict, incl = f(p64 < n64), f(p64 <= n64)
    one = np.concatenate([strict, incl, strict, incl], axis=1)
    p["c_rwmask"] = f(np.concatenate([one, one], axis=1))
    p["c_nmask"] = f(np.tile(f(n64 < p64), (1, 4)))
    sel4 = np.zeros((4, 2, 128), np.float32)
    for h in range(4):
        sel4[h, h // 2, (h % 2) * 64:(h % 2) * 64 + 64] = 1.0
    p["c_sel4"] = sel4
    p["c_blk"] = f((pp // 64) == (nn // 64))
    p["c_sel2"] = f((np.arange(128)[:, None] // 64) == np.arange(2)[None, :])
    return p


_NC_CACHE = {}


def kernel(**inputs):
    x = np.asarray(inputs["x"], dtype=np.float32)
    B = x.shape[0]
    p = host_params(inputs)
    if "full" not in _NC_CACHE:
        _NC_CACHE["full"] = build()
    nc = _NC_CACHE["full"]
    in_maps = []
    for b in range(B):
        m = dict(p)
        m["x"] = np.ascontiguousarray(x[b])
        in_maps.append(m)
    res = run_bass_kernel_spmd(nc, in_maps, core_ids=list(range(B)))
    return np.stack([np.asarray(r["y"]) for r in res.results], axis=0).astype(np.float32)


class _NullCtx:
    def __init__(self, v):
        self.v = v

    def __enter__(self):
        return self.v

    def __exit__(self, *a):
        return False


def phase_R2(C, l, mode="all", es_ext=None):
    nc, sch, d = C.nc, C.sch, C.d
    DO_PREP = mode in ("all", "prep")
    SLACK = globals().get("SLACK_", 2) if mode == "prep" else 0
    DO_CHUNKS = mode in ("all", "chunks")
    T = 512
    L = 64
    NCH = T // L
    NBLK = S // T
    NEG = -0.6065306597126334
    with (ExitStack() if es_ext is None else _NullCtx(es_ext)) as es:
        def sb(name, shape, dt):
            return _sb(es, nc, "R_" + name, shape, dt)
        mu = sb("mu", [128, 7], F32)
        omu = sb("omu", [128, 7], F32)
        pv = sb("pv", [128, 6, 2], F32)
        lup_f = sb("lupf", [64, 3, 256], F32)
        lup = sb("lup", [64, 3, 256], BF16)
        lng = sb("lng", [64, 256], F32)
        lnb = sb("lnb", [64, 256], F32)
        r_par, r_lup = Res(), Res()
        r_praw = Res()
        r_lin, r_lina = Res(), Res()
        r_tmp = Res()
        PT = []
        r_Gp = [Res(), Res()]
        if DO_PREP:
            praw = sb("praw", [128, 1 + T], F32)
            pm_r = sb("pm_r", [128, 2, T], F32)
            pm_k = sb("pm_k", [128, 2, T], F32)
            pm_l = sb("pm_l", [128, T], F32)
            lin = sb("lin", [128, T], BF16)
            lin_a = sb("lin_a", [32, T], BF16)
            tmp = sb("tmp", [128, T], F32)
            for i in range(2):
                t_ = Ctx()
                for nm in ("tmp", "lw", "aa", "kx", "sq", "kp", "bb", "gi", "ge", "E2"):
                    setattr(t_, nm, sb("%s_p%d" % (nm, i), [128, T], F32))
                    setattr(t_, "r_" + nm, Res())
                PT.append(t_)
            Gp = [sb("Gp%d" % i, [128, 1 + T], F32) for i in range(2)]
        B = []
        for i in range(2):
            o = Ctx()
            o.bbf = sb("Bbf%d" % i, [128, 15, T], BF16)
            o.bf32 = sb("Bf32_%d" % i, [128, 8, T], F32)
            o.ARz = [o.bbf[:, 2 * h:2 * h + 2, :] for h in range(4)]
            o.Bt = o.bbf[:, 8:10, :]
            o.Kt = o.bbf[:, 10:12, :]
            o.prodb = o.bbf[:, 12:14, :]
            o.lin_g = o.bbf[0:64, 14, :]
            o.Bh = o.bf32[:, 0:2, :]
            o.Kh = o.bf32[:, 2:4, :]
            o.pm_v = o.bf32[:, 4:6, :]
            o.E1 = [o.bf32[:, 6 + p, :] for p in range(2)]
            o.r_AR, o.r_Bt, o.r_Kt, o.r_Bh, o.r_Kh, o.r_v, o.r_prodb, o.r_E1 = ([Res(), Res()] for _ in range(8))
            o.r_ling = Res()
            o.all_bf = o.r_AR + o.r_Bt + o.r_Kt + o.r_prodb + [o.r_ling]
            o.all_f32 = o.r_Bh + o.r_Kh + o.r_v + o.r_E1
            B.append(o)
        NK = 3
        K_ = []
        for i in range(NK if DO_CHUNKS else 0):
            o = Ctx()
            o.tok = sb("tok%d" % i, [64, 2, 4, 2, 64], BF16)
            o.scs = sb("scs%d" % i, [64, 4, 4, 64], BF16)
            o.nns = sb("nns%d" % i, [64, 4, 64], BF16)
            o.pws = [sb("pws%d_%d" % (i, j), [64, 4, 2, 64], BF16) for j in range(2)]
            o.Q = [sb("Q%d_%d" % (i, j), [64, 4, 64], BF16) for j in range(2)]
            o.r_tok = [Res(), Res()]
            o.r_scs = [Res(), Res()]
            o.r_nns = Res()
            o.r_pws = [Res() for _ in range(2)]
            o.r_Q = [Res(), Res()]
            K_.append(o)
        r_Hs, r_Hb = Res(), Res()
        r_yrT = Res()
        r_Wt, r_Ut = Res(), Res()
        r_yv = [Res(), Res()]
        r_bvv, r_st7, r_gsb = [Res(), Res()], [Res(), Res()], [Res(), Res()]
        r_yo2 = [Res(), Res()]
        r_ysq, r_bv, r_yo, r_st = Res(), Res(), Res(), Res()
        if DO_CHUNKS:
            Hs = sb("Hs", [128, 2, 64], F32)
            Hb = sb("Hb", [128, 2, 64], BF16)
            yrT = sb("yrT", [128, 2, S], BF16)
            Wt = sb("Wt", [64, 4, 64], BF16)
            Ut = sb("Ut", [64, 4, 64], BF16)
            yv = [sb("yv%d" % i, [64, 4, 64], F32) for i in range(2)]
            ysq = sb("ysq", [64, 4, 64], F32)
            bv = [sb("bv%d" % i, [64, 4, 64], F32) for i in range(2)]
            st7 = [sb("st7_%d" % i, [64, 4, 1], F32) for i in range(2)]
            gsb = [sb("gsb%d" % i, [64, 256], F32) for i in range(2)]
            yo = [sb("yo%d" % i, [64, 256], F32) for i in range(2)]
            st = sb("st", [64, 8, 4, 1], F32)
            banks = [_ps(es, nc, "R_b%d" % i, [128, 512], F32) for i in range(8)]
            rb = [PRes() for _ in range(8)]
        else:
            pb_only = _ps(es, nc, "R_pb", [128, 512], F32)
            banks = [None] * 7 + [pb_only]
            rb = [None] * 7 + [PRes()]

        sch.dma("sp", mu[:, :], d["r_mu"][l], wacc=[r_par])
        for i, nm in enumerate(("r_w0", "r_a0", "r_k_k", "r_k_a", "r_r_k")):
            sch.dma("sp", pv[:, i, :], d[nm][l], wacc=[r_par])
        sch.op("pool", lambda e: e.memset(lup_f[:, :, :], 0.0), writes=[r_lup])
        sch.dma("sp", lup_f[0:32, 0, :], d["r_w_up"][l], writes=[r_lup])
        sch.dma("sp", lup_f[0:32, 1, :], d["r_a_up"][l], writes=[r_lup])
        sch.dma("sp", lup_f[0:64, 2, :], d["r_g_up"][l], writes=[r_lup])
        sch.dma("sp", lng[:, :], d["r_ln_g"][l], wacc=[r_par])
        sch.dma("sp", lnb[:, :], d["r_ln_b"][l], wacc=[r_par])
        sch.op("dve", lambda e: e.tensor_scalar(out=omu[:, :], in0=mu[:, :], scalar1=-1.0, scalar2=1.0,
                                                op0=ALU.mult, op1=ALU.add), reads=[r_par], writes=[r_par])
        sch.op("dve", lambda e: e.tensor_scalar(out=pv[:, 5, :], in0=pv[:, 3, :], scalar1=-1.0, scalar2=1.0,
                                                op0=ALU.mult, op1=ALU.add), reads=[r_par], writes=[r_par])
        sch.op("dve", lambda e: e.tensor_copy(out=lup[:, :, :], in_=lup_f[:, :, :]), reads=[r_par, r_lup],
               writes=[r_par])
        if DO_CHUNKS:
            sch.op("pool", lambda e: e.memset(Hs[:, :, :], 0.0), writes=[r_Hs])
            sch.op("pool", lambda e: e.memset(Hb[:, :, :], 0.0), writes=[r_Hb])
        if DO_PREP:
            for pr in range(2):
                sch.op("pool", lambda e, pr=pr: e.memset(Gp[pr][:, :], 0.0), writes=[r_Gp[pr]])
            for i in range(2):
                for h in range(4):
                    zs_ = slice((1 - h % 2) * 64, (1 - h % 2) * 64 + 64)
                    sch.op("pool", lambda e, i=i, h=h, zs_=zs_: e.memset(B[i].ARz[h][zs_, :, :], 0.0),
                           wacc=[B[i].r_AR[h // 2]])
            sch.op("pool", lambda e: e.memset(praw[:, 0:1], 0.0), writes=[r_praw])

        def v3(ap2):
            return ap2.rearrange("p (c t) -> p c t", t=L)

        def prep_gen(blk):
            o = B[blk % 2]
            t0 = blk * T
            dests = [(pm_r, 0), (pm_r, 1), (pm_k, 0), (pm_k, 1), (o.pm_v, 0), (o.pm_v, 1), (None, 0)]
            r_pmr, r_pmk = [Res(), Res()], [Res(), Res()]
            r_pml = Res()
            rr_list = [r_pmr[0], r_pmr[1], r_pmk[0], r_pmk[1], o.r_v[0], o.r_v[1], r_pml]
            for rc in range(7):
                if blk == 0:
                    sch.dma("sp", praw[:, 1:1 + T], d["rT"][rc * 128:(rc + 1) * 128, 0:T],
                            reads=[C.r_rT], wacc=[r_praw])
                else:
                    sch.dma("sp", praw[:, :], d["rT"][rc * 128:(rc + 1) * 128, t0 - 1:t0 + T],
                            reads=[C.r_rT], writes=[r_praw])
                dt_, pi = dests[rc]
                ot = pm_l[:, :] if dt_ is None else dt_[:, pi, :]
                sch.op("pool", lambda e, rc=rc: e.tensor_scalar(out=tmp[:, :], in0=praw[:, 1:1 + T],
                                                                scalar1=omu[:, rc:rc + 1], scalar2=0.0, op0=ALU.mult, op1=ALU.add),
                       reads=[r_praw, r_par], writes=[r_tmp])
                yield
                sch.op("dve", lambda e, rc=rc, ot=ot: e.scalar_tensor_tensor(
                    out=ot, in0=praw[:, 0:T], scalar=mu[:, rc:rc + 1], in1=tmp[:, :], op0=ALU.mult, op1=ALU.add),
                    reads=[r_praw, r_tmp, r_par], writes=[rr_list[rc]])
                yield
            for _ in range(SLACK):
                yield
            sch.op("act", lambda e: e.activation(out=lin[0:32, :], in_=pm_l[0:32, :], func=AF.Tanh),
                   reads=[r_pml], wacc=[r_lin])
            for _ in range(SLACK):
                yield
            sch.op("act", lambda e: e.copy(out=lin[32:64, :], in_=pm_l[32:64, :]), reads=[r_pml], wacc=[r_lin])
            for _ in range(SLACK):
                yield
            sch.op("act", lambda e: e.activation(out=lin[64:128, :], in_=pm_l[64:128, :], func=AF.Sigmoid),
                   reads=[r_pml], wacc=[r_lin])
            sch.dma("sp", lin_a[:, :], lin[32:64, :], reads=[r_lin], writes=[r_lina])
            sch.dma("sp", o.lin_g[:, :], lin[64:128, :], reads=[r_lin], writes=[o.r_ling])
            yield
            pb7 = banks[7]

            def pair_gen(pr):
                P_ = PT[pr]
                tmp, lw, aa, kx, sq, kp, bb, gi, ge, E2 = (P_.tmp, P_.lw, P_.aa, P_.kx, P_.sq, P_.kp, P_.bb, P_.gi,
                                                          P_.ge, P_.E2)
                r_tmp, r_lw, r_aa, r_kx, r_sq, r_kp, r_bb, r_gi, r_ge, r_E2 = (
                    P_.r_tmp, P_.r_lw, P_.r_aa, P_.r_kx, P_.r_sq, P_.r_kp, P_.r_bb, P_.r_gi, P_.r_ge, P_.r_E2)
                rk, rr_ = r_pmk[pr], r_pmr[pr]
                pcs = slice(pr * 128, (pr + 1) * 128)
                sch.op("pe", lambda e, pcs=pcs: e.matmul(out=pb7[:, 0:T], lhsT=lup[0:32, 0, pcs], rhs=lin[0:32, :],
                                                         start=True, stop=True), reads=[r_lin, r_par], writes=[rb[7]])
                for _ in range(SLACK):
                    yield
                sch.op("act", lambda e, pr=pr: e.activation(out=lw[:, :], in_=pb7[:, 0:T], func=AF.Sigmoid,
                                                            bias=pv[:, 0, pr:pr + 1], scale=1.0),
                       reads=[rb[7], r_par], writes=[r_lw])
                yield
                sch.op("pe", lambda e, pcs=pcs: e.matmul(out=pb7[:, 0:T], lhsT=lup[0:32, 1, pcs], rhs=lin_a[:, :],
                                                         start=True, stop=True), reads=[r_lina, r_par], writes=[rb[7]])
                for _ in range(SLACK):
                    yield
                sch.op("act", lambda e, pr=pr: e.activation(out=aa[:, :], in_=pb7[:, 0:T], func=AF.Sigmoid,
                                                            bias=pv[:, 1, pr:pr + 1], scale=1.0),
                       reads=[rb[7], r_par], writes=[r_aa])
                yield
                sch.op("pool", lambda e: e.tensor_scalar(out=lw[:, :], in0=lw[:, :], scalar1=NEG, scalar2=0.0,
                                                         op0=ALU.mult, op1=ALU.add), reads=[r_lw], writes=[r_lw])
                sch.op("dve", lambda e, pr=pr: e.tensor_scalar(out=kx[:, :], in0=pm_k[:, pr, :],
                                                               scalar1=pv[:, 2, pr:pr + 1], scalar2=None,
                                                               op0=ALU.mult), reads=[rk, r_par], writes=[r_kx])
                yield
                sch.op("pool", lambda e: e.tensor_tensor(out=sq[:, :], in0=kx[:, :], in1=kx[:, :], op=ALU.mult),
                       reads=[r_kx], writes=[r_sq])
                sch.op("pe", lambda e: e.matmul(out=pb7[:, 0:T], lhsT=C.blkf[:, :], rhs=sq[:, :],
                                                start=True, stop=True), reads=[r_sq], writes=[rb[7]])
                for _ in range(SLACK):
                    yield
                sch.op("act", lambda e: e.activation(out=tmp[:, :], in_=pb7[:, 0:T], func=AF.Sqrt),
                       reads=[rb[7]], writes=[r_tmp])
                yield
                sch.op("dve", lambda e: e.tensor_scalar(out=tmp[:, :], in0=tmp[:, :], scalar1=1e-12, scalar2=None,
                                                        op0=ALU.max), reads=[r_tmp], writes=[r_tmp])
                yield
                sch.op("dve", lambda e: e.reciprocal(out=tmp[:, :], in_=tmp[:, :]), reads=[r_tmp], writes=[r_tmp])
                yield
                sch.op("dve", lambda e: e.tensor_tensor(out=kx[:, :], in0=kx[:, :], in1=tmp[:, :], op=ALU.mult),
                       reads=[r_kx, r_tmp], writes=[r_kx])
                yield
                sch.op("pool", lambda e, pr=pr: e.tensor_scalar(out=kp[:, :], in0=aa[:, :], scalar1=pv[:, 3, pr:pr + 1],
                                                                scalar2=pv[:, 5, pr:pr + 1], op0=ALU.mult, op1=ALU.add),
                       reads=[r_aa, r_par], writes=[r_kp])
                yield
                sch.op("pool", lambda e, pr=pr: e.tensor_tensor(out=kp[:, :], in0=kp[:, :], in1=pm_k[:, pr, :],
                                                                op=ALU.mult), reads=[r_kp, rk], writes=[r_kp])
                yield
                sch.op("pool", lambda e: e.tensor_tensor(out=bb[:, :], in0=kx[:, :], in1=aa[:, :], op=ALU.mult),
                       reads=[r_kx, r_aa], writes=[r_bb])
                yield
                sch.op("dve", lambda e, pr=pr: e.scalar_tensor_tensor(
                    out=o.prodb[:, pr, :], in0=pm_r[:, pr, :], scalar=pv[:, 4, pr:pr + 1], in1=kp[:, :],
                    op0=ALU.mult, op1=ALU.mult), reads=[rr_, r_kp, r_par], writes=[o.r_prodb[pr]])
                yield
                G = Gp[pr]
                sch.op("dve", lambda e, G=G: e.tensor_copy(out=G[:, 0:1], in_=G[:, T:T + 1]),
                       reads=[r_Gp[pr]], writes=[r_Gp[pr]])
                sch.op("dve", lambda e, G=G: e.tensor_tensor_scan(out=G[:, 1:1 + T], data0=lw[:, :], data1=lw[:, :],
                                                                  initial=G[:, 0:1], op0=ALU.add, op1=ALU.min),
                       reads=[r_lw, r_Gp[pr]], writes=[r_Gp[pr]])
                yield
                base = v3(G[:, 0:T])[:, :, 0:1].broadcast_to([128, NCH, L])
                sch.op("dve", lambda e, G=G, base=base: e.tensor_tensor(out=v3(gi[:, :]), in0=v3(G[:, 1:1 + T]),
                                                                        in1=base, op=ALU.subtract),
                       reads=[r_Gp[pr]], writes=[r_gi])
                yield
                sch.op("pool", lambda e: e.tensor_tensor(out=ge[:, :], in0=gi[:, :], in1=lw[:, :], op=ALU.subtract),
                       reads=[r_gi, r_lw], writes=[r_ge])
                e1 = o.E1[pr]
                for _ in range(SLACK):
                    yield
                sch.op("act", lambda e, e1=e1: e.activation(out=e1[:, :], in_=gi[:, :], func=AF.Exp),
                       reads=[r_gi], writes=[o.r_E1[pr]])
                yield
                for _ in range(SLACK):
                    yield
                sch.op("act", lambda e: e.activation(out=E2[:, :], in_=ge[:, :], func=AF.Exp),
                       reads=[r_ge], writes=[r_E2])
                yield
                for hf in range(2):
                    ps_ = slice(hf * 64, hf * 64 + 64)
                    hq = 2 * pr + hf
                    sch.op("dve", lambda e, hq=hq, ps_=ps_: e.scalar_tensor_tensor(
                        out=o.ARz[hq][ps_, 0, :], in0=kx[ps_, :], scalar=-1.0, in1=E2[ps_, :],
                        op0=ALU.mult, op1=ALU.mult), reads=[r_kx, r_E2], wacc=[o.r_AR[pr]])
                    sch.op("pool", lambda e, hq=hq, ps_=ps_, pr=pr, e1=e1: e.tensor_tensor(
                        out=o.ARz[hq][ps_, 1, :], in0=pm_r[ps_, pr, :], in1=e1[ps_, :], op=ALU.mult),
                        reads=[rr_, o.r_E1[pr]], wacc=[o.r_AR[pr]])
                    yield
                for _ in range(SLACK):
                    yield
                sch.op("act", lambda e: e.activation(out=E2[:, :], in_=gi[:, :], func=AF.Exp, scale=-1.0),
                       reads=[r_gi], writes=[r_E2])
                yield
                sch.op("dve", lambda e, pr=pr: e.tensor_tensor(out=o.Bt[:, pr, :], in0=bb[:, :], in1=E2[:, :],
                                                               op=ALU.mult), reads=[r_bb, r_E2], writes=[o.r_Bt[pr]])
                sch.op("pool", lambda e, pr=pr: e.tensor_tensor(out=o.Kt[:, pr, :], in0=kp[:, :], in1=E2[:, :],
                                                                op=ALU.mult), reads=[r_kp, r_E2], writes=[o.r_Kt[pr]])
                yield
                gend = v3(gi[:, :])[:, :, L - 1:L].broadcast_to([128, NCH, L])
                sch.op("dve", lambda e, gend=gend: e.tensor_tensor(out=v3(ge[:, :]), in0=gend, in1=v3(gi[:, :]),
                                                                   op=ALU.subtract), reads=[r_gi], writes=[r_ge])
                yield
                for _ in range(SLACK):
                    yield
                sch.op("act", lambda e: e.activation(out=ge[:, :], in_=ge[:, :], func=AF.Exp),
                       reads=[r_ge], writes=[r_ge])
                yield
                sch.op("dve", lambda e, pr=pr: e.tensor_tensor(out=o.Bh[:, pr, :], in0=bb[:, :], in1=ge[:, :],
                                                               op=ALU.mult), reads=[r_bb, r_ge], writes=[o.r_Bh[pr]])
                sch.op("pool", lambda e, pr=pr: e.tensor_tensor(out=o.Kh[:, pr, :], in0=kp[:, :], in1=ge[:, :],
                                                                op=ALU.mult), reads=[r_kp, r_ge], writes=[o.r_Kh[pr]])
                yield

            gens = [pair_gen(0), pair_gen(1)]
            while gens:
                for g in list(gens):
                    try:
                        next(g)
                        yield
                    except StopIteration:
                        gens.remove(g)

        def flat(t3):
            return t3[:, :, :].rearrange("p a t -> p (a t)")

        def store_block(blk):
            o = B[blk % 2]
            rows = slice(blk * 128, (blk + 1) * 128)
            sch.dma("sp", d["rpb"][rows, :], flat(o.bbf), reads=o.all_bf, wacc=[C.r_rpb])
            sch.dma("sp", d["rpf"][rows, :], flat(o.bf32), reads=o.all_f32, wacc=[C.r_rpf])

        def load_block(blk):
            o = B[blk % 2]
            rows = slice(blk * 128, (blk + 1) * 128)
            sch.dma("sp", flat(o.bbf), d["rpb"][rows, :], reads=[C.r_rpb], writes=o.all_bf)
            sch.dma("sp", flat(o.bf32), d["rpf"][rows, :], reads=[C.r_rpf], writes=o.all_f32)

        if mode == "prep":
            def prep_all():
                for blk in range(NBLK):
                    yield from prep_gen(blk)
                    store_block(blk)
                    yield
            return prep_all()

        def pre_gen(gc):
            blk, cc = divmod(gc, NCH)
            o = B[blk % 2]
            k = K_[gc % NK]
            cs = slice(cc * L, (cc + 1) * L)
            for pr in range(2):
                for j, (src, rs) in enumerate(((o.Bh, o.r_Bh[pr]), (o.Kh, o.r_Kh[pr]), (o.pm_v, o.r_v[pr]))):
                    sch.op("pe", lambda e, j=j, src=src, pr=pr: e.transpose(
                        out=banks[0][0:64, j * 128:(j + 1) * 128], in_=src[:, pr, cs], identity=C.identf[:, :]),
                        reads=[rs], writes=[rb[0]])
                sch.op("act", lambda e, pr=pr: e.copy(
                    out=k.tok[:, pr, 0:3, :, :],
                    in_=banks[0][0:64, 0:384].rearrange("p (j h e) -> p j h e", j=3, h=2)),
                    reads=[rb[0]], writes=[k.r_tok[pr]])
                yield
            for h in range(4):
                pr = h // 2
                bk = banks[1 + pr]
                oo = (h % 2) * 256
                rhs_ar = o.ARz[h][:, :, cs]
                sch.op("pe", lambda e, bk=bk, oo=oo, pr=pr, rhs_ar=rhs_ar: e.matmul(
                    out=bk[0:64, oo:oo + 128], lhsT=o.Bt[:, pr, cs], rhs=rhs_ar, start=True, stop=True),
                    reads=[o.r_Bt[pr], o.r_AR[pr]], writes=[rb[1 + pr]])
                sch.op("pe", lambda e, bk=bk, oo=oo, pr=pr, rhs_ar=rhs_ar: e.matmul(
                    out=bk[0:64, oo + 128:oo + 256], lhsT=o.Kt[:, pr, cs], rhs=rhs_ar, start=True, stop=True),
                    reads=[o.r_Kt[pr], o.r_AR[pr]], writes=[rb[1 + pr]])
                sch.op("pe", lambda e, h=h, pr=pr: e.matmul(
                    out=banks[4][0:64, h * 64:(h + 1) * 64], lhsT=o.ARz[h][:, 0, cs],
                    rhs=o.Bt[:, pr, cs], start=True, stop=True),
                    reads=[o.r_Bt[pr], o.r_AR[pr]], writes=[rb[4]])
                if h % 2 == 1:
                    sch.op("dve", lambda e, pr=pr: e.tensor_tensor(
                        out=k.scs[:, 2 * pr:2 * pr + 2, :, :].rearrange("p h b t -> p (h b t)"),
                        in0=banks[1 + pr][0:64, :], in1=C.rwmask[:, :], op=ALU.mult),
                        reads=[rb[1 + pr]], writes=[k.r_scs[pr]])
            sch.op("dve", lambda e: e.tensor_tensor(out=k.nns[:, :, :].rearrange("p h t -> p (h t)"),
                                                    in0=banks[4][0:64, 0:256], in1=C.nmask[:, :], op=ALU.mult),
                   reads=[rb[4]], writes=[k.r_nns])
            yield

            def Mlev(lev, h):
                return k.scs[:, h, 0, :] if lev == 0 else k.pws[(lev - 1) % 2][:, h, 1, :]

            def Nlev(lev, h):
                return k.nns[:, h, :] if lev == 0 else k.pws[(lev - 1) % 2][:, h, 0, :]

            def Rlev(lev, h):
                return [k.r_scs[h // 2], k.r_nns] if lev == 0 else [k.r_pws[(lev - 1) % 2]]
            sch.op("pool", lambda e: e.tensor_tensor(
                out=k.Q[0][:, :, :], in0=k.scs[:, :, 0, :],
                in1=C.identf[0:64, 0:64].unsqueeze(1).broadcast_to([64, 4, 64]), op=ALU.add),
                reads=[k.r_scs[0], k.r_scs[1]], writes=[k.r_Q[0]])
            yield
            qi = 0
            for lev in range(1, 6):
                for h in range(4):
                    sch.op("pe", lambda e, lev=lev, h=h: e.matmul(
                        out=banks[3][0:64, h * 128:h * 128 + 64], lhsT=Mlev(lev - 1, h), rhs=Nlev(lev - 1, h),
                        start=True, stop=True), reads=Rlev(lev - 1, h), writes=[rb[3]])
                    if lev < 5:
                        sch.op("pe", lambda e, lev=lev, h=h: e.matmul(
                            out=banks[3][0:64, h * 128 + 64:h * 128 + 128], lhsT=Nlev(lev - 1, h),
                            rhs=Mlev(lev - 1, h), start=True, stop=True), reads=Rlev(lev - 1, h), writes=[rb[3]])
                if lev < 5:
                    evac(C, 0, k.pws[(lev - 1) % 2][:, :, :, :].rearrange("p h b t -> p (h b t)"),
                         banks[3][0:64, :], [rb[3]], writes=[k.r_pws[(lev - 1) % 2]])
                    nsrc = lambda h, lev=lev: k.pws[(lev - 1) % 2][:, h, 0, :]
                    rn = [k.r_pws[(lev - 1) % 2]]
                else:
                    sch.op("act", lambda e: e.copy(
                        out=k.nns[:, :, :], in_=banks[3][0:64, :].rearrange("p (h b t) -> p h b t", h=4, b=2)[:, :, 0, :]),
                        reads=[rb[3]], writes=[k.r_nns])
                    nsrc = lambda h: k.nns[:, h, :]
                    rn = [k.r_nns]
                yield
                for h in range(4):
                    sch.op("pe", lambda e, h=h, qi=qi: e.matmul(
                        out=banks[5][0:64, 256 + h * 64:256 + (h + 1) * 64], lhsT=C.identb[0:64, 0:64],
                        rhs=k.Q[qi][:, h, :], start=True, stop=False), reads=[k.r_Q[qi]], writes=[rb[5]])
                    sch.op("pe", lambda e, h=h, qi=qi, nsrc=nsrc: e.matmul(
                        out=banks[5][0:64, 256 + h * 64:256 + (h + 1) * 64], lhsT=nsrc(h), rhs=k.Q[qi][:, h, :],
                        start=False, stop=True), reads=rn + [k.r_Q[qi]], writes=[rb[5]])
                sch.op("act", lambda e, qi=qi: e.copy(
                    out=k.Q[1 - qi][:, :, :].rearrange("p h e -> p (h e)"), in_=banks[5][0:64, 256:512]),
                    reads=[rb[5]], writes=[k.r_Q[1 - qi]])
                qi = 1 - qi
                yield
            k.qfin = qi

        def epi_early(gc):
            blk, cc = divmod(gc, NCH)
            o = B[blk % 2]
            k = K_[gc % NK]
            cs = slice(cc * L, (cc + 1) * L)
            b7 = banks[7]
            i2 = gc % 2
            sch.op("pe", lambda e: e.matmul(out=b7[0:64, 0:256], lhsT=o.lin_g[:, cs], rhs=lup[0:64, 2, :],
                                            start=True, stop=True), reads=[o.r_ling, r_par], writes=[rb[7]])
            for pr in range(2):
                sch.op("pe", lambda e, pr=pr: e.matmul(out=b7[0:64, 256 + 2 * pr:256 + 2 * pr + 2],
                                                       lhsT=o.prodb[:, pr, cs], rhs=C.sel2b[:, :],
                                                       start=True, stop=True), reads=[o.r_prodb[pr]], writes=[rb[7]])
            sch.op("act", lambda e: e.copy(out=gsb[i2][:, :], in_=b7[0:64, 0:256]), reads=[rb[7]], writes=[r_gsb[i2]])
            sch.op("act", lambda e: e.copy(out=st7[i2][:, :, 0], in_=b7[0:64, 256:260]), reads=[rb[7]],
                   writes=[r_st7[i2]])
            for pr in range(2):
                sch.op("pool", lambda e, pr=pr: e.tensor_tensor(
                    out=bv[i2][:, 2 * pr:2 * pr + 2, :], in0=k.tok[:, pr, 2, :, :],
                    in1=st7[i2][:, 2 * pr:2 * pr + 2, :].broadcast_to([64, 2, 64]), op=ALU.mult),
                    reads=[k.r_tok[pr], r_st7[i2]], wacc=[r_bvv[i2]])

        def epi_gen(gc):
            gcs = slice(gc * L, (gc + 1) * L)
            i2 = gc % 2
            y_, ry_ = yv[i2], r_yv[i2]
            b7 = banks[7]
            sch.op("dve", lambda e: e.tensor_reduce(out=st[:, 0, :, :], in_=y_[:, :, :], axis=AX.X, op=ALU.add),
                   reads=[ry_], writes=[r_st])
            sch.op("pool", lambda e: e.tensor_tensor(out=ysq[:, :, :], in0=y_[:, :, :], in1=y_[:, :, :], op=ALU.mult),
                   reads=[ry_], writes=[r_ysq])
            yield
            sch.op("dve", lambda e: e.tensor_reduce(out=st[:, 1, :, :], in_=ysq[:, :, :], axis=AX.X, op=ALU.add),
                   reads=[r_ysq], writes=[r_st])
            yield
            sch.op("pool", lambda e: e.tensor_scalar(out=st[:, 2, :, :], in0=st[:, 0, :, :], scalar1=1.0 / 64,
                                                     scalar2=0.0, op0=ALU.mult, op1=ALU.add), reads=[r_st], writes=[r_st])
            sch.op("pool", lambda e: e.tensor_tensor(out=st[:, 3, :, :], in0=st[:, 2, :, :], in1=st[:, 2, :, :],
                                                     op=ALU.mult), reads=[r_st], writes=[r_st])
            yield
            sch.op("pool", lambda e: e.tensor_scalar(out=st[:, 4, :, :], in0=st[:, 1, :, :], scalar1=1.0 / 64,
                                                     scalar2=64e-5, op0=ALU.mult, op1=ALU.add), reads=[r_st],
                   writes=[r_st])
            sch.op("pool", lambda e: e.tensor_tensor(out=st[:, 4, :, :], in0=st[:, 4, :, :], in1=st[:, 3, :, :],
                                                     op=ALU.subtract), reads=[r_st], writes=[r_st])
            yield
            sch.op("act", lambda e: e.activation(out=st[:, 5, :, :], in_=st[:, 4, :, :], func=AF.Ln),
                   reads=[r_st], writes=[r_st])
            sch.op("act", lambda e: e.activation(out=st[:, 6, :, :], in_=st[:, 5, :, :], func=AF.Exp, scale=-0.5),
                   reads=[r_st], writes=[r_st])
            yield
            sch.op("dve", lambda e: e.tensor_tensor(out=ysq[:, :, :], in0=y_[:, :, :],
                                                    in1=st[:, 2, :, :].broadcast_to([64, 4, 64]), op=ALU.subtract),
                   reads=[ry_, r_st], writes=[r_ysq])
            yield
            sch.op("dve", lambda e: e.tensor_tensor(out=ysq[:, :, :], in0=ysq[:, :, :],
                                                    in1=st[:, 6, :, :].broadcast_to([64, 4, 64]), op=ALU.mult),
                   reads=[r_ysq, r_st], writes=[r_ysq])
            yield
            y2 = ysq[:, :, :].rearrange("p h e -> p (h e)")
            sch.op("pool", lambda e: e.tensor_tensor(out=y2, in0=y2, in1=lng[:, :], op=ALU.mult),
                   reads=[r_ysq, r_par], writes=[r_ysq])
            yield
            sch.op("pool", lambda e: e.tensor_tensor(out=y2, in0=y2, in1=lnb[:, :], op=ALU.add),
                   reads=[r_ysq, r_par], writes=[r_ysq])
            yield
            sch.op("pool", lambda e: e.tensor_tensor(out=ysq[:, :, :], in0=ysq[:, :, :], in1=bv[i2][:, :, :],
                                                     op=ALU.add), reads=[r_ysq, r_bvv[i2]], writes=[r_ysq])
            yield
            sch.op("pool", lambda e: e.tensor_tensor(out=yo[i2][:, :], in0=y2, in1=gsb[i2][:, :], op=ALU.mult),
                   reads=[r_ysq, r_gsb[i2]], writes=[r_yo2[i2]])
            yield

        def epi_tail(gc):
            gcs = slice(gc * L, (gc + 1) * L)
            i2 = gc % 2
            b7 = banks[7]
            for j in range(2):
                sch.op("pe", lambda e, j=j: e.transpose(out=b7[:, 384 + j * 64:384 + (j + 1) * 64],
                                                        in_=yo[i2][:, j * 128:(j + 1) * 128],
                                                        identity=C.identf[0:64, 0:64]), reads=[r_yo2[i2]],
                       writes=[rb[7]])
            sch.op("act", lambda e: e.copy(out=yrT[:, :, gcs], in_=b7[:, 384:512].rearrange("p (j t) -> p j t", j=2)),
                   reads=[rb[7]], wacc=[r_yrT])
            yield

        hi = []
        lo = []
        pq = []
        rr = [0]
        pqc = [0]
        PQX = globals().get("PQX_", 0)

        def pull(q, idx):
            try:
                next(q[idx][1])
                return True
            except StopIteration:
                q.pop(idx)
                return False

        def pump(n):
            for _ in range(n):
                if hi:
                    hm = globals().get("HIMODE_", 2)
                    if hm == 0:
                        rr[0] = (rr[0] + 1) % len(hi)
                        pull(hi, rr[0])
                    elif hm == 1:
                        pull(hi, 0)
                    else:
                        rr[0] = (rr[0] + 1) % 3
                        pull(hi, 0 if (rr[0] < 2 or len(hi) < 2) else len(hi) - 1)
                if lo:
                    pull(lo, 0)
                pqc[0] += 1
                npq = 1 + (1 if (PQX and pqc[0] % PQX == 0) else 0)
                PQS = globals().get("PQS_", 5)
                if PQS and pqc[0] % PQS == 0:
                    npq = 0
                for _q in range(npq):
                    if pq:
                        pull(pq, 0)

        def drain_tag(q, pred):
            i = 0
            while i < len(q):
                if pred(q[i][0]):
                    while pull(q, i):
                        pass
                else:
                    i += 1

        def chain(gc):
            blk, cc = divmod(gc, NCH)
            o = B[blk % 2]
            k = K_[gc % NK]
            cs = slice(cc * L, (cc + 1) * L)
            Q = k.Q[k.qfin]
            rQ = k.r_Q[k.qfin]
            wq = banks[5]
            for h in range(4):
                pr, hf = h // 2, h % 2
                sch.op("pe", lambda e, h=h, pr=pr: e.matmul(
                    out=wq[0:64, h * 64:(h + 1) * 64], lhsT=o.ARz[h][:, 0, cs], rhs=Hb[:, pr, :],
                    start=True, stop=False), reads=[o.r_AR[pr], r_Hb], writes=[rb[5]])
                sch.op("pe", lambda e, h=h, pr=pr, hf=hf: e.matmul(
                    out=wq[0:64, h * 64:(h + 1) * 64], lhsT=k.scs[:, h, 2, :], rhs=k.tok[:, pr, 2, hf, :],
                    start=False, stop=True), reads=[k.r_scs[pr], k.r_tok[pr]], writes=[rb[5]])
            sch.op("act", lambda e: e.copy(out=Wt[:, :, :].rearrange("p h e -> p (h e)"), in_=wq[0:64, 0:256]),
                   reads=[rb[5]], writes=[r_Wt])
            pump(globals().get('PUMPS_', (3, 4, 5, 0))[0])
            for h in range(4):
                sch.op("pe", lambda e, h=h: e.matmul(out=wq[0:64, h * 64:(h + 1) * 64], lhsT=Q[:, h, :],
                                                     rhs=Wt[:, h, :], start=True, stop=True),
                       reads=[rQ, r_Wt], writes=[rb[5]])
            sch.op("act", lambda e: e.copy(out=Ut[:, :, :].rearrange("p h e -> p (h e)"), in_=wq[0:64, 0:256]),
                   reads=[rb[5]], writes=[r_Ut])
            pump(globals().get('PUMPS_', (3, 4, 5, 0))[1])
            b6 = banks[6]
            for h in range(4):
                pr, hf = h // 2, h % 2
                oo = b6[:, 256 + h * 64:256 + (h + 1) * 64]
                sch.op("pe", lambda e, oo=oo, h=h, pr=pr: e.matmul(
                    out=oo, lhsT=k.tok[:, pr, 0, :, :].rearrange("p a e -> p (a e)"), rhs=Ut[:, h, :],
                    start=True, stop=False), reads=[k.r_tok[pr], r_Ut], writes=[rb[6]])
                sch.op("pe", lambda e, oo=oo, pr=pr, hf=hf: e.matmul(
                    out=oo, lhsT=k.tok[:, pr, 1, :, :].rearrange("p a e -> p (a e)"), rhs=k.tok[:, pr, 2, hf, :],
                    start=False, stop=True), reads=[k.r_tok[pr]], writes=[rb[6]])
            for h in range(4):
                pr, hf = h // 2, h % 2
                oo = b6[0:64, h * 64:(h + 1) * 64]
                sch.op("pe", lambda e, oo=oo, h=h, pr=pr: e.matmul(
                    out=oo, lhsT=o.ARz[h][:, 1, cs], rhs=Hb[:, pr, :], start=True, stop=False),
                    reads=[o.r_AR[pr], r_Hb], writes=[rb[6]])
                sch.op("pe", lambda e, oo=oo, h=h, pr=pr: e.matmul(
                    out=oo, lhsT=k.scs[:, h, 1, :], rhs=Ut[:, h, :], start=False, stop=False),
                    reads=[k.r_scs[pr], r_Ut], writes=[rb[6]])
                sch.op("pe", lambda e, oo=oo, h=h, pr=pr, hf=hf: e.matmul(
                    out=oo, lhsT=k.scs[:, h, 3, :], rhs=k.tok[:, pr, 2, hf, :], start=False, stop=True),
                    reads=[k.r_scs[pr], k.r_tok[pr]], writes=[rb[6]])
            for h in range(4):
                pr, pb = h // 2, (h % 2) * 64
                sch.op("dve", lambda e, h=h, pr=pr, pb=pb: e.scalar_tensor_tensor(
                    out=Hs[pb:pb + 64, pr, :], in0=Hs[pb:pb + 64, pr, :],
                    scalar=o.E1[pr][pb:pb + 64, cc * L + L - 1:cc * L + L],
                    in1=b6[pb:pb + 64, 256 + h * 64:256 + (h + 1) * 64], op0=ALU.mult, op1=ALU.add),
                    reads=[r_Hs, o.r_E1[pr], rb[6]], wacc=[r_Hs])
            sch.op("act", lambda e: e.copy(out=Hb[:, :, :], in_=Hs[:, :, :]), reads=[r_Hs], writes=[r_Hb])
            y_, ry_ = yv[gc % 2], r_yv[gc % 2]
            sch.op("act", lambda e: e.copy(out=y_[:, :, :].rearrange("p h e -> p (h e)"), in_=b6[0:64, 0:256]),
                   reads=[rb[6]], writes=[ry_])
            pump(globals().get('PUMPS_', (3, 4, 5, 0))[2])

        NG = NBLK * NCH
        if mode == "chunks":
            load_block(0)
        else:
            for _ in prep_gen(0):
                pass
        for _ in pre_gen(0):
            pass
        hi.append((1, pre_gen(1)))
        for gc in range(NG):
            blk, cc = divmod(gc, NCH)
            drain_tag(lo, lambda t: t[0] == "epi" and t[1] <= gc - 2)
            if gc + 2 < NG:
                if (gc + 2) // NCH != (gc + 1) // NCH or (gc + 2) % NCH == 0:
                    drain_tag(pq, lambda t: True)
                hi.append((gc + 2, pre_gen(gc + 2)))
            chain(gc)
            epi_early(gc)
            lo.append((("epi", gc), epi_gen(gc)))
            if cc == 0 and blk + 1 < NBLK:
                if mode == "chunks":
                    load_block(blk + 1)
                else:
                    pq.append((("prep", blk + 1), prep_gen(blk + 1)))
            pump(globals().get('PUMPS_', (3, 4, 5, 0))[3])
            drain_tag(hi, lambda t: t == gc + 1)
            if gc >= 1:
                drain_tag(lo, lambda t: t[0] == "epi" and t[1] <= gc - 1)
                for _ in epi_tail(gc - 1):
                    pass
        drain_tag(lo, lambda t: True)
        for _ in epi_tail(NG - 1):
            pass
        for j in range(2):
            sch.dma("sp", d["mixT"][256 + j * 128:256 + (j + 1) * 128, :], yrT[:, j, :], reads=[r_yrT],
                    wacc=[C.r_mixT])


def phase_RC(C, l):
    nc, sch, d = C.nc, C.sch, C.d
    T = 256
    L = 64
    NCH = T // L
    NBLK = S // T
    NEG = -0.6065306597126334
    with ExitStack() as es:
        def sb(name, shape, dt):
            return _sb(es, nc, "R_" + name, shape, dt)
        mu = sb("mu", [128, 7], F32)
        omu = sb("omu", [128, 7], F32)
        pv = sb("pv", [128, 6, 2], F32)
        lup_f = sb("lupf", [64, 3, 256], F32)
        lup = sb("lup", [64, 3, 256], BF16)
        lng = sb("lng", [64, 256], F32)
        lnb = sb("lnb", [64, 256], F32)
        r_par, r_lup = Res(), Res()
        praw = sb("praw", [128, 1 + T], F32)
        r_praw = Res()
        pm_r = sb("pm_r", [128, 2, T], F32)
        pm_k = sb("pm_k", [128, 2, T], F32)
        pm_l = sb("pm_l", [128, T], F32)
        lin = sb("lin", [128, T], BF16)
        lin_a = sb("lin_a", [32, T], BF16)
        r_lin, r_lina = Res(), Res()
        tmp = sb("tmp", [128, T], F32)
        lw = sb("lw", [128, T], F32)
        aa = sb("aa", [128, T], F32)
        kx = sb("kx", [128, T], F32)
        sq = sb("sq", [128, T], F32)
        kp = sb("kp", [128, T], F32)
        bb = sb("bb", [128, T], F32)
        gi = sb("gi", [128, T], F32)
        ge = sb("ge", [128, T], F32)
        E2 = sb("E2", [128, T], F32)
        r_tmp, r_lw, r_aa, r_kx, r_sq, r_kp, r_bb, r_gi, r_ge, r_E2 = (Res() for _ in range(10))
        Gp = [sb("Gp%d" % i, [128, 1 + T], F32) for i in range(2)]
        r_Gp = [Res(), Res()]
        B = []
        for i in range(2):
            o = Ctx()
            o.ARz = [sb("ARz%d_%d" % (i, h), [128, 2, T], BF16) for h in range(4)]
            o.Bt = sb("Bt%d" % i, [128, 2, T], BF16)
            o.Kt = sb("Kt%d" % i, [128, 2, T], BF16)
            o.Bh = sb("Bh%d" % i, [128, 2, T], F32)
            o.Kh = sb("Kh%d" % i, [128, 2, T], F32)
            o.pm_v = sb("pmv%d" % i, [128, 2, T], F32)
            o.prodb = sb("prodb%d" % i, [128, 2, T], BF16)
            o.lin_g = sb("ling%d" % i, [64, T], BF16)
            o.E1 = [sb("E1_%d_%d" % (i, p), [128, T], F32) for p in range(2)]
            o.r_AR, o.r_Bt, o.r_Kt, o.r_Bh, o.r_Kh, o.r_v, o.r_prodb, o.r_E1 = ([Res(), Res()] for _ in range(8))
            o.r_ling = Res()
            B.append(o)
        NK = 3
        K_ = []
        for i in range(NK):
            o = Ctx()
            o.tok = sb("tok%d" % i, [64, 2, 4, 2, 64], BF16)
            o.scs = sb("scs%d" % i, [64, 4, 4, 64], BF16)
            o.nns = sb("nns%d" % i, [64, 4, 64], BF16)
            o.pws = [sb("pws%d_%d" % (i, j), [64, 4, 2, 64], BF16) for j in range(2)]
            o.Q = [sb("Q%d_%d" % (i, j), [64, 4, 64], BF16) for j in range(2)]
            o.r_tok = [Res(), Res()]
            o.r_scs = [Res(), Res()]
            o.r_nns = Res()
            o.r_pws = [Res() for _ in range(2)]
            o.r_Q = [Res(), Res()]
            K_.append(o)
        Hs = sb("Hs", [128, 2, 64], F32)
        Hb = sb("Hb", [128, 2, 64], BF16)
        r_Hs, r_Hb = Res(), Res()
        yrT = sb("yrT", [128, 2, S], BF16)
        r_yrT = Res()
        Wt = sb("Wt", [64, 4, 64], BF16)
        Ut = sb("Ut", [64, 4, 64], BF16)
        r_Wt, r_Ut = Res(), Res()
        yv = [sb("yv%d" % i, [64, 4, 64], F32) for i in range(2)]
        r_yv = [Res(), Res()]
        ysq = sb("ysq", [64, 4, 64], F32)
        bv = [sb("bv%d" % i, [64, 4, 64], F32) for i in range(2)]
        st7 = [sb("st7_%d" % i, [64, 4, 1], F32) for i in range(2)]
        gsb = [sb("gsb%d" % i, [64, 256], F32) for i in range(2)]
        r_bvv, r_st7, r_gsb = [Res(), Res()], [Res(), Res()], [Res(), Res()]
        yo = [sb("yo%d" % i, [64, 256], F32) for i in range(2)]
        r_yo2 = [Res(), Res()]
        st = sb("st", [64, 8, 4, 1], F32)
        r_ysq, r_bv, r_yo, r_st = Res(), Res(), Res(), Res()
        phys = [_ps(es, nc, "RC_b%d" % i, [128, 512], F32) for i in range(8)]
        prs = [PRes() for _ in range(8)]
        amap = [0, 1, 1, 2, 3, 4, 4, 0]
        banks = [phys[i] for i in amap]
        rb = [prs[i] for i in amap]
        bkq, rbq = phys[3], prs[3]

        sch.dma("sp", mu[:, :], d["r_mu"][l], wacc=[r_par])
        for i, nm in enumerate(("r_w0", "r_a0", "r_k_k", "r_k_a", "r_r_k")):
            sch.dma("sp", pv[:, i, :], d[nm][l], wacc=[r_par])
        sch.op("pool", lambda e: e.memset(lup_f[:, :, :], 0.0), writes=[r_lup])
        sch.dma("sp", lup_f[0:32, 0, :], d["r_w_up"][l], writes=[r_lup])
        sch.dma("sp", lup_f[0:32, 1, :], d["r_a_up"][l], writes=[r_lup])
        sch.dma("sp", lup_f[0:64, 2, :], d["r_g_up"][l], writes=[r_lup])
        sch.dma("sp", lng[:, :], d["r_ln_g"][l], wacc=[r_par])
        sch.dma("sp", lnb[:, :], d["r_ln_b"][l], wacc=[r_par])
        sch.op("dve", lambda e: e.tensor_scalar(out=omu[:, :], in0=mu[:, :], scalar1=-1.0, scalar2=1.0,
                                                op0=ALU.mult, op1=ALU.add), reads=[r_par], writes=[r_par])
        sch.op("dve", lambda e: e.tensor_scalar(out=pv[:, 5, :], in0=pv[:, 3, :], scalar1=-1.0, scalar2=1.0,
                                                op0=ALU.mult, op1=ALU.add), reads=[r_par], writes=[r_par])
        sch.op("dve", lambda e: e.tensor_copy(out=lup[:, :, :], in_=lup_f[:, :, :]), reads=[r_par, r_lup],
               writes=[r_par])
        sch.op("pool", lambda e: e.memset(Hs[:, :, :], 0.0), writes=[r_Hs])
        sch.op("pool", lambda e: e.memset(Hb[:, :, :], 0.0), writes=[r_Hb])
        for pr in range(2):
            sch.op("pool", lambda e, pr=pr: e.memset(Gp[pr][:, :], 0.0), writes=[r_Gp[pr]])
        for i in range(2):
            for h in range(4):
                zs_ = slice((1 - h % 2) * 64, (1 - h % 2) * 64 + 64)
                sch.op("pool", lambda e, i=i, h=h, zs_=zs_: e.memset(B[i].ARz[h][zs_, :, :], 0.0),
                       wacc=[B[i].r_AR[h // 2]])
        sch.op("pool", lambda e: e.memset(praw[:, 0:1], 0.0), writes=[r_praw])

        def v3(ap2):
            return ap2.rearrange("p (c t) -> p c t", t=L)

        def prep_gen(blk):
            o = B[blk % 2]
            t0 = blk * T
            dests = [(pm_r, 0), (pm_r, 1), (pm_k, 0), (pm_k, 1), (o.pm_v, 0), (o.pm_v, 1), (None, 0)]
            r_pmr, r_pmk = [Res(), Res()], [Res(), Res()]
            r_pml = Res()
            rr_list = [r_pmr[0], r_pmr[1], r_pmk[0], r_pmk[1], o.r_v[0], o.r_v[1], r_pml]
            for rc in range(7):
                if blk == 0:
                    sch.dma("sp", praw[:, 1:1 + T], d["rT"][rc * 128:(rc + 1) * 128, 0:T],
                            reads=[C.r_rT], wacc=[r_praw])
                else:
                    sch.dma("sp", praw[:, :], d["rT"][rc * 128:(rc + 1) * 128, t0 - 1:t0 + T],
                            reads=[C.r_rT], writes=[r_praw])
                dt_, pi = dests[rc]
                ot = pm_l[:, :] if dt_ is None else dt_[:, pi, :]
                sch.op("pool", lambda e, rc=rc: e.tensor_scalar(out=tmp[:, :], in0=praw[:, 1:1 + T],
                                                                scalar1=omu[:, rc:rc + 1], scalar2=0.0, op0=ALU.mult, op1=ALU.add),
                       reads=[r_praw, r_par], writes=[r_tmp])
                yield
                sch.op("dve", lambda e, rc=rc, ot=ot: e.scalar_tensor_tensor(
                    out=ot, in0=praw[:, 0:T], scalar=mu[:, rc:rc + 1], in1=tmp[:, :], op0=ALU.mult, op1=ALU.add),
                    reads=[r_praw, r_tmp, r_par], writes=[rr_list[rc]])
                yield
            sch.op("act", lambda e: e.activation(out=lin[0:32, :], in_=pm_l[0:32, :], func=AF.Tanh),
                   reads=[r_pml], wacc=[r_lin])
            sch.op("act", lambda e: e.copy(out=lin[32:64, :], in_=pm_l[32:64, :]), reads=[r_pml], wacc=[r_lin])
            sch.op("act", lambda e: e.activation(out=lin[64:128, :], in_=pm_l[64:128, :], func=AF.Sigmoid),
                   reads=[r_pml], wacc=[r_lin])
            sch.dma("sp", lin_a[:, :], lin[32:64, :], reads=[r_lin], writes=[r_lina])
            sch.dma("sp", o.lin_g[:, :], lin[64:128, :], reads=[r_lin], writes=[o.r_ling])
            yield
            pb7 = banks[7]
            for pr in range(2):
                rk, rr_ = r_pmk[pr], r_pmr[pr]
                pcs = slice(pr * 128, (pr + 1) * 128)
                sch.op("pe", lambda e, pcs=pcs: e.matmul(out=pb7[:, 0:T], lhsT=lup[0:32, 0, pcs], rhs=lin[0:32, :],
                                                         start=True, stop=True), reads=[r_lin, r_par], writes=[rb[7]])
                sch.op("act", lambda e, pr=pr: e.activation(out=lw[:, :], in_=pb7[:, 0:T], func=AF.Sigmoid,
                                                            bias=pv[:, 0, pr:pr + 1], scale=1.0),
                       reads=[rb[7], r_par], writes=[r_lw])
                yield
                sch.op("pe", lambda e, pcs=pcs: e.matmul(out=pb7[:, 0:T], lhsT=lup[0:32, 1, pcs], rhs=lin_a[:, :],
                                                         start=True, stop=True), reads=[r_lina, r_par], writes=[rb[7]])
                sch.op("act", lambda e, pr=pr: e.activation(out=aa[:, :], in_=pb7[:, 0:T], func=AF.Sigmoid,
                                                            bias=pv[:, 1, pr:pr + 1], scale=1.0),
                       reads=[rb[7], r_par], writes=[r_aa])
                yield
                sch.op("pool", lambda e: e.tensor_scalar(out=lw[:, :], in0=lw[:, :], scalar1=NEG, scalar2=0.0,
                                                         op0=ALU.mult, op1=ALU.add), reads=[r_lw], writes=[r_lw])
                sch.op("dve", lambda e, pr=pr: e.tensor_scalar(out=kx[:, :], in0=pm_k[:, pr, :],
                                                               scalar1=pv[:, 2, pr:pr + 1], scalar2=None,
                                                               op0=ALU.mult), reads=[rk, r_par], writes=[r_kx])
                yield
                sch.op("pool", lambda e: e.tensor_tensor(out=sq[:, :], in0=kx[:, :], in1=kx[:, :], op=ALU.mult),
                       reads=[r_kx], writes=[r_sq])
                sch.op("pe", lambda e: e.matmul(out=pb7[:, 0:T], lhsT=C.blkf[:, :], rhs=sq[:, :],
                                                start=True, stop=True), reads=[r_sq], writes=[rb[7]])
                sch.op("act", lambda e: e.activation(out=tmp[:, :], in_=pb7[:, 0:T], func=AF.Sqrt),
                       reads=[rb[7]], writes=[r_tmp])
                yield
                sch.op("dve", lambda e: e.tensor_scalar(out=tmp[:, :], in0=tmp[:, :], scalar1=1e-12, scalar2=None,
                                                        op0=ALU.max), reads=[r_tmp], writes=[r_tmp])
                yield
                sch.op("dve", lambda e: e.reciprocal(out=tmp[:, :], in_=tmp[:, :]), reads=[r_tmp], writes=[r_tmp])
                yield
                sch.op("dve", lambda e: e.tensor_tensor(out=kx[:, :], in0=kx[:, :], in1=tmp[:, :], op=ALU.mult),
                       reads=[r_kx, r_tmp], writes=[r_kx])
                yield
                sch.op("pool", lambda e, pr=pr: e.tensor_scalar(out=kp[:, :], in0=aa[:, :], scalar1=pv[:, 3, pr:pr + 1],
                                                                scalar2=pv[:, 5, pr:pr + 1], op0=ALU.mult, op1=ALU.add),
                       reads=[r_aa, r_par], writes=[r_kp])
                yield
                sch.op("pool", lambda e, pr=pr: e.tensor_tensor(out=kp[:, :], in0=kp[:, :], in1=pm_k[:, pr, :],
                                                                op=ALU.mult), reads=[r_kp, rk], writes=[r_kp])
                yield
                sch.op("pool", lambda e: e.tensor_tensor(out=bb[:, :], in0=kx[:, :], in1=aa[:, :], op=ALU.mult),
                       reads=[r_kx, r_aa], writes=[r_bb])
                yield
                sch.op("dve", lambda e, pr=pr: e.scalar_tensor_tensor(
                    out=o.prodb[:, pr, :], in0=pm_r[:, pr, :], scalar=pv[:, 4, pr:pr + 1], in1=kp[:, :],
                    op0=ALU.mult, op1=ALU.mult), reads=[rr_, r_kp, r_par], writes=[o.r_prodb[pr]])
                yield
                G = Gp[pr]
                sch.op("dve", lambda e, G=G: e.tensor_copy(out=G[:, 0:1], in_=G[:, T:T + 1]),
                       reads=[r_Gp[pr]], writes=[r_Gp[pr]])
                sch.op("dve", lambda e, G=G: e.tensor_tensor_scan(out=G[:, 1:1 + T], data0=lw[:, :], data1=lw[:, :],
                                                                  initial=G[:, 0:1], op0=ALU.add, op1=ALU.min),
                       reads=[r_lw, r_Gp[pr]], writes=[r_Gp[pr]])
                yield
                base = v3(G[:, 0:T])[:, :, 0:1].broadcast_to([128, NCH, L])
                sch.op("dve", lambda e, G=G, base=base: e.tensor_tensor(out=v3(gi[:, :]), in0=v3(G[:, 1:1 + T]),
                                                                        in1=base, op=ALU.subtract),
                       reads=[r_Gp[pr]], writes=[r_gi])
                yield
                sch.op("pool", lambda e: e.tensor_tensor(out=ge[:, :], in0=gi[:, :], in1=lw[:, :], op=ALU.subtract),
                       reads=[r_gi, r_lw], writes=[r_ge])
                e1 = o.E1[pr]
                sch.op("act", lambda e, e1=e1: e.activation(out=e1[:, :], in_=gi[:, :], func=AF.Exp),
                       reads=[r_gi], writes=[o.r_E1[pr]])
                yield
                sch.op("act", lambda e: e.activation(out=E2[:, :], in_=ge[:, :], func=AF.Exp),
                       reads=[r_ge], writes=[r_E2])
                yield
                for hf in range(2):
                    ps_ = slice(hf * 64, hf * 64 + 64)
                    hq = 2 * pr + hf
                    sch.op("dve", lambda e, hq=hq, ps_=ps_: e.scalar_tensor_tensor(
                        out=o.ARz[hq][ps_, 0, :], in0=kx[ps_, :], scalar=-1.0, in1=E2[ps_, :],
                        op0=ALU.mult, op1=ALU.mult), reads=[r_kx, r_E2], wacc=[o.r_AR[pr]])
                    sch.op("pool", lambda e, hq=hq, ps_=ps_, pr=pr, e1=e1: e.tensor_tensor(
                        out=o.ARz[hq][ps_, 1, :], in0=pm_r[ps_, pr, :], in1=e1[ps_, :], op=ALU.mult),
                        reads=[rr_, o.r_E1[pr]], wacc=[o.r_AR[pr]])
                    yield
                sch.op("act", lambda e: e.activation(out=E2[:, :], in_=gi[:, :], func=AF.Exp, scale=-1.0),
                       reads=[r_gi], writes=[r_E2])
                yield
                sch.op("dve", lambda e, pr=pr: e.tensor_tensor(out=o.Bt[:, pr, :], in0=bb[:, :], in1=E2[:, :],
                                                               op=ALU.mult), reads=[r_bb, r_E2], writes=[o.r_Bt[pr]])
                sch.op("pool", lambda e, pr=pr: e.tensor_tensor(out=o.Kt[:, pr, :], in0=kp[:, :], in1=E2[:, :],
                                                                op=ALU.mult), reads=[r_kp, r_E2], writes=[o.r_Kt[pr]])
                yield
                gend = v3(gi[:, :])[:, :, L - 1:L].broadcast_to([128, NCH, L])
                sch.op("dve", lambda e, gend=gend: e.tensor_tensor(out=v3(ge[:, :]), in0=gend, in1=v3(gi[:, :]),
                                                                   op=ALU.subtract), reads=[r_gi], writes=[r_ge])
                yield
                sch.op("act", lambda e: e.activation(out=ge[:, :], in_=ge[:, :], func=AF.Exp),
                       reads=[r_ge], writes=[r_ge])
                yield
                sch.op("dve", lambda e, pr=pr: e.tensor_tensor(out=o.Bh[:, pr, :], in0=bb[:, :], in1=ge[:, :],
                                                               op=ALU.mult), reads=[r_bb, r_ge], writes=[o.r_Bh[pr]])
                sch.op("pool", lambda e, pr=pr: e.tensor_tensor(out=o.Kh[:, pr, :], in0=kp[:, :], in1=ge[:, :],
                                                                op=ALU.mult), reads=[r_kp, r_ge], writes=[o.r_Kh[pr]])
                yield

        def pre_gen(gc):
            blk, cc = divmod(gc, NCH)
            o = B[blk % 2]
            k = K_[gc % NK]
            cs = slice(cc * L, (cc + 1) * L)
            for pr in range(2):
                for j, (src, rs) in enumerate(((o.Bh, o.r_Bh[pr]), (o.Kh, o.r_Kh[pr]), (o.pm_v, o.r_v[pr]))):
                    sch.op("pe", lambda e, j=j, src=src, pr=pr: e.transpose(
                        out=banks[0][0:64, j * 128:(j + 1) * 128], in_=src[:, pr, cs], identity=C.identf[:, :]),
                        reads=[rs], writes=[rb[0]])
                sch.op("act", lambda e, pr=pr: e.copy(
                    out=k.tok[:, pr, 0:3, :, :],
                    in_=banks[0][0:64, 0:384].rearrange("p (j h e) -> p j h e", j=3, h=2)),
                    reads=[rb[0]], writes=[k.r_tok[pr]])
                yield
            for h in range(4):
                pr = h // 2
                bk = banks[1 + pr]
                oo = (h % 2) * 256
                rhs_ar = o.ARz[h][:, :, cs]
                sch.op("pe", lambda e, bk=bk, oo=oo, pr=pr, rhs_ar=rhs_ar: e.matmul(
                    out=bk[0:64, oo:oo + 128], lhsT=o.Bt[:, pr, cs], rhs=rhs_ar, start=True, stop=True),
                    reads=[o.r_Bt[pr], o.r_AR[pr]], writes=[rb[1 + pr]])
                sch.op("pe", lambda e, bk=bk, oo=oo, pr=pr, rhs_ar=rhs_ar: e.matmul(
                    out=bk[0:64, oo + 128:oo + 256], lhsT=o.Kt[:, pr, cs], rhs=rhs_ar, start=True, stop=True),
                    reads=[o.r_Kt[pr], o.r_AR[pr]], writes=[rb[1 + pr]])
                sch.op("pe", lambda e, h=h, pr=pr: e.matmul(
                    out=banks[4][0:64, h * 64:(h + 1) * 64], lhsT=o.ARz[h][:, 0, cs],
                    rhs=o.Bt[:, pr, cs], start=True, stop=True),
                    reads=[o.r_Bt[pr], o.r_AR[pr]], writes=[rb[4]])
                if h % 2 == 1:
                    sch.op("dve", lambda e, pr=pr: e.tensor_tensor(
                        out=k.scs[:, 2 * pr:2 * pr + 2, :, :].rearrange("p h b t -> p (h b t)"),
                        in0=banks[1 + pr][0:64, :], in1=C.rwmask[:, :], op=ALU.mult),
                        reads=[rb[1 + pr]], writes=[k.r_scs[pr]])
            sch.op("dve", lambda e: e.tensor_tensor(out=k.nns[:, :, :].rearrange("p h t -> p (h t)"),
                                                    in0=banks[4][0:64, 0:256], in1=C.nmask[:, :], op=ALU.mult),
                   reads=[rb[4]], writes=[k.r_nns])
            yield

            def Mlev(lev, h):
                return k.scs[:, h, 0, :] if lev == 0 else k.pws[(lev - 1) % 2][:, h, 1, :]

            def Nlev(lev, h):
                return k.nns[:, h, :] if lev == 0 else k.pws[(lev - 1) % 2][:, h, 0, :]

            def Rlev(lev, h):
                return [k.r_scs[h // 2], k.r_nns] if lev == 0 else [k.r_pws[(lev - 1) % 2]]
            sch.op("pool", lambda e: e.tensor_tensor(
                out=k.Q[0][:, :, :], in0=k.scs[:, :, 0, :],
                in1=C.identf[0:64, 0:64].unsqueeze(1).broadcast_to([64, 4, 64]), op=ALU.add),
                reads=[k.r_scs[0], k.r_scs[1]], writes=[k.r_Q[0]])
            yield
            qi = 0
            for lev in range(1, 6):
                for h in range(4):
                    sch.op("pe", lambda e, lev=lev, h=h: e.matmul(
                        out=banks[3][0:64, h * 128:h * 128 + 64], lhsT=Mlev(lev - 1, h), rhs=Nlev(lev - 1, h),
                        start=True, stop=True), reads=Rlev(lev - 1, h), writes=[rb[3]])
                    if lev < 5:
                        sch.op("pe", lambda e, lev=lev, h=h: e.matmul(
                            out=banks[3][0:64, h * 128 + 64:h * 128 + 128], lhsT=Nlev(lev - 1, h),
                            rhs=Mlev(lev - 1, h), start=True, stop=True), reads=Rlev(lev - 1, h), writes=[rb[3]])
                if lev < 5:
                    evac(C, lev, k.pws[(lev - 1) % 2][:, :, :, :].rearrange("p h b t -> p (h b t)"),
                         banks[3][0:64, :], [rb[3]], writes=[k.r_pws[(lev - 1) % 2]])
                    nsrc = lambda h, lev=lev: k.pws[(lev - 1) % 2][:, h, 0, :]
                    rn = [k.r_pws[(lev - 1) % 2]]
                else:
                    sch.op("act", lambda e: e.copy(
                        out=k.nns[:, :, :], in_=banks[3][0:64, :].rearrange("p (h b t) -> p h b t", h=4, b=2)[:, :, 0, :]),
                        reads=[rb[3]], writes=[k.r_nns])
                    nsrc = lambda h: k.nns[:, h, :]
                    rn = [k.r_nns]
                yield
                for h in range(4):
                    sch.op("pe", lambda e, h=h, qi=qi, nsrc=nsrc: e.matmul(
                        out=bkq[0:64, 256 + h * 64:256 + (h + 1) * 64], lhsT=nsrc(h), rhs=k.Q[qi][:, h, :],
                        start=True, stop=True), reads=rn + [k.r_Q[qi]], writes=[rbq])
                sch.op("dve", lambda e, qi=qi: e.tensor_tensor(
                    out=k.Q[1 - qi][:, :, :].rearrange("p h e -> p (h e)"), in0=bkq[0:64, 256:512],
                    in1=k.Q[qi][:, :, :].rearrange("p h e -> p (h e)"), op=ALU.add),
                    reads=[rbq, k.r_Q[qi]], writes=[k.r_Q[1 - qi]])
                qi = 1 - qi
                yield
            k.qfin = qi

        def epi_early(gc):
            blk, cc = divmod(gc, NCH)
            o = B[blk % 2]
            k = K_[gc % NK]
            cs = slice(cc * L, (cc + 1) * L)
            b7 = banks[7]
            i2 = gc % 2
            sch.op("pe", lambda e: e.matmul(out=b7[0:64, 0:256], lhsT=o.lin_g[:, cs], rhs=lup[0:64, 2, :],
                                            start=True, stop=True), reads=[o.r_ling, r_par], writes=[rb[7]])
            for pr in range(2):
                sch.op("pe", lambda e, pr=pr: e.matmul(out=b7[0:64, 256 + 2 * pr:256 + 2 * pr + 2],
                                                       lhsT=o.prodb[:, pr, cs], rhs=C.sel2b[:, :],
                                                       start=True, stop=True), reads=[o.r_prodb[pr]], writes=[rb[7]])
            sch.op("act", lambda e: e.copy(out=gsb[i2][:, :], in_=b7[0:64, 0:256]), reads=[rb[7]], writes=[r_gsb[i2]])
            sch.op("act", lambda e: e.copy(out=st7[i2][:, :, 0], in_=b7[0:64, 256:260]), reads=[rb[7]],
                   writes=[r_st7[i2]])
            for pr in range(2):
                sch.op("pool", lambda e, pr=pr: e.tensor_tensor(
                    out=bv[i2][:, 2 * pr:2 * pr + 2, :], in0=k.tok[:, pr, 2, :, :],
                    in1=st7[i2][:, 2 * pr:2 * pr + 2, :].broadcast_to([64, 2, 64]), op=ALU.mult),
                    reads=[k.r_tok[pr], r_st7[i2]], wacc=[r_bvv[i2]])

        def epi_gen(gc):
            gcs = slice(gc * L, (gc + 1) * L)
            i2 = gc % 2
            y_, ry_ = yv[i2], r_yv[i2]
            b7 = banks[7]
            sch.op("dve", lambda e: e.tensor_reduce(out=st[:, 0, :, :], in_=y_[:, :, :], axis=AX.X, op=ALU.add),
                   reads=[ry_], writes=[r_st])
            sch.op("pool", lambda e: e.tensor_tensor(out=ysq[:, :, :], in0=y_[:, :, :], in1=y_[:, :, :], op=ALU.mult),
                   reads=[ry_], writes=[r_ysq])
            yield
            sch.op("dve", lambda e: e.tensor_reduce(out=st[:, 1, :, :], in_=ysq[:, :, :], axis=AX.X, op=ALU.add),
                   reads=[r_ysq], writes=[r_st])
            yield
            sch.op("dve", lambda e: e.tensor_scalar(out=st[:, 2, :, :], in0=st[:, 0, :, :], scalar1=1.0 / 64,
                                                    scalar2=None, op0=ALU.mult), reads=[r_st], writes=[r_st])
            sch.op("dve", lambda e: e.tensor_tensor(out=st[:, 3, :, :], in0=st[:, 2, :, :], in1=st[:, 2, :, :],
                                                    op=ALU.mult), reads=[r_st], writes=[r_st])
            yield
            sch.op("dve", lambda e: e.scalar_tensor_tensor(out=st[:, 4, :, :], in0=st[:, 1, :, :], scalar=1.0 / 64,
                                                           in1=st[:, 3, :, :], op0=ALU.mult, op1=ALU.subtract),
                   reads=[r_st], writes=[r_st])
            sch.op("dve", lambda e: e.tensor_scalar(out=st[:, 4, :, :], in0=st[:, 4, :, :], scalar1=64e-5,
                                                    scalar2=None, op0=ALU.add), reads=[r_st], writes=[r_st])
            yield
            sch.op("act", lambda e: e.activation(out=st[:, 5, :, :], in_=st[:, 4, :, :], func=AF.Sqrt),
                   reads=[r_st], writes=[r_st])
            sch.op("dve", lambda e: e.reciprocal(out=st[:, 6, :, :], in_=st[:, 5, :, :]), reads=[r_st], writes=[r_st])
            yield
            sch.op("dve", lambda e: e.tensor_tensor(out=ysq[:, :, :], in0=y_[:, :, :],
                                                    in1=st[:, 2, :, :].broadcast_to([64, 4, 64]), op=ALU.subtract),
                   reads=[ry_, r_st], writes=[r_ysq])
            yield
            sch.op("dve", lambda e: e.tensor_tensor(out=ysq[:, :, :], in0=ysq[:, :, :],
                                                    in1=st[:, 6, :, :].broadcast_to([64, 4, 64]), op=ALU.mult),
                   reads=[r_ysq, r_st], writes=[r_ysq])
            yield
            y2 = ysq[:, :, :].rearrange("p h e -> p (h e)")
            sch.op("pool", lambda e: e.tensor_tensor(out=y2, in0=y2, in1=lng[:, :], op=ALU.mult),
                   reads=[r_ysq, r_par], writes=[r_ysq])
            yield
            sch.op("pool", lambda e: e.tensor_tensor(out=y2, in0=y2, in1=lnb[:, :], op=ALU.add),
                   reads=[r_ysq, r_par], writes=[r_ysq])
            yield
            sch.op("pool", lambda e: e.tensor_tensor(out=ysq[:, :, :], in0=ysq[:, :, :], in1=bv[i2][:, :, :],
                                                     op=ALU.add), reads=[r_ysq, r_bvv[i2]], writes=[r_ysq])
            yield
            sch.op("dve", lambda e: e.tensor_tensor(out=yo[i2][:, :], in0=y2, in1=gsb[i2][:, :], op=ALU.mult),
                   reads=[r_ysq, r_gsb[i2]], writes=[r_yo2[i2]])
            yield

        def epi_tail(gc):
            gcs = slice(gc * L, (gc + 1) * L)
            i2 = gc % 2
            b7 = banks[7]
            for j in range(2):
                sch.op("pe", lambda e, j=j: e.transpose(out=b7[:, 384 + j * 64:384 + (j + 1) * 64],
                                                        in_=yo[i2][:, j * 128:(j + 1) * 128],
                                                        identity=C.identf[0:64, 0:64]), reads=[r_yo2[i2]],
                       writes=[rb[7]])
            sch.op("act", lambda e: e.copy(out=yrT[:, :, gcs], in_=b7[:, 384:512].rearrange("p (j t) -> p j t", j=2)),
                   reads=[rb[7]], wacc=[r_yrT])
            yield

        QB = 256
        lam_init = 0.8 - 0.6 * math.exp(-0.3 * l)

        def csb(name, shape, dt):
            return _sb(es, nc, "C_" + name, shape, dt)
        c_lq = csb("lq", [128, 4, 64], F32)
        c_lt = csb("lt", [128, 2, 64], F32)
        c_ls = csb("ls", [128, 4], F32)
        nlam = csb("nlam", [128, 1], F32)
        ag = csb("ag", [128, 4], F32)
        rc_par = Res()
        qTc = [csb("qT%d" % c, [128, S], BF16) for c in range(2)]
        kTc = csb("kT", [128, S], BF16)
        vtc = csb("vt", [128, NT, 128], BF16)
        rc_q, rc_k, rc_v = Res(), Res(), Res()
        ptc = [csb("pt%d" % i, [128, QB], BF16) for i in range(2)]
        rc_pt = [Res(), Res()]
        c_rl = csb("rl", [128, QB], F32)
        c_oc = [csb("oc%d" % i, [128, QB], F32) for i in range(2)]
        c_od = csb("od", [128, QB], F32)
        c_sq = csb("sq", [128, QB], F32)
        c_rs = csb("rs", [128, QB], F32)
        c_ya = csb("ya", [128, QB], BF16)
        rc_rl, rc_od, rc_sq, rc_rs, rc_ya = Res(), Res(), Res(), Res(), Res()
        rc_oc = [Res(), Res()]
        p_st = [phys[5], phys[6]]
        rp_st = [prs[5], prs[6]]
        p_o = phys[7][:, 0:QB]
        p_l = phys[7][:, 256:256 + QB]
        rp_ol = prs[7]
        p_ss = phys[5][:, 256:256 + QB]
        rp_ss = prs[5]

        def c_gen():
            for i, nm in enumerate(("a_lq1", "a_lk1", "a_lq2", "a_lk2")):
                sch.dma("sp", c_lq[:, i, :], d[nm][l], wacc=[rc_par])
            sch.dma("sp", ag[:, :], d["a_norm_g"][l], wacc=[rc_par])
            for i in range(2):
                sch.op("dve", lambda e, i=i: e.tensor_tensor(out=c_lt[:, i, :], in0=c_lq[:, 2 * i, :],
                                                             in1=c_lq[:, 2 * i + 1, :], op=ALU.mult),
                       reads=[rc_par], writes=[rc_par])
            sch.op("dve", lambda e: e.tensor_reduce(out=c_ls[:, 0:2], in_=c_lt[:, :, :], axis=AX.X, op=ALU.add),
                   reads=[rc_par], writes=[rc_par])
            sch.op("act", lambda e: e.activation(out=c_ls[:, 2:4], in_=c_ls[:, 0:2], func=AF.Exp),
                   reads=[rc_par], writes=[rc_par])
            sch.op("dve", lambda e: e.tensor_tensor(out=nlam[:, :], in0=c_ls[:, 3:4], in1=c_ls[:, 2:3], op=ALU.subtract),
                   reads=[rc_par], writes=[rc_par])
            sch.op("dve", lambda e: e.tensor_scalar(out=nlam[:, :], in0=nlam[:, :], scalar1=-lam_init, scalar2=None,
                                                    op0=ALU.add), reads=[rc_par], writes=[rc_par])
            sch.op("dve", lambda e: e.tensor_scalar(out=ag[:, :], in0=ag[:, :], scalar1=1.0 - lam_init, scalar2=None,
                                                    op0=ALU.mult), reads=[rc_par], writes=[rc_par])
            yield
            idx0 = 0
            for h in range(4):
                for c in range(2):
                    if h == 0:
                        sch.op("pool", lambda e, c=c: e.memset(qTc[c][(1 - c) * 64:(1 - c) * 64 + 64, :], 0.0),
                               wacc=[rc_q])
                    sch.dma("sp", qTc[c][c * 64:c * 64 + 64, :], d["aqkT"][h * 128 + c * 64:h * 128 + c * 64 + 64, :],
                            reads=[C.r_aqkT], wacc=[rc_q])
                sch.dma("sp", kTc[:, :], d["aqkT"][512 + h * 128:512 + (h + 1) * 128, :], reads=[C.r_aqkT],
                        writes=[rc_k])
                sch.dma("sp", vtc[:, :, :], d["av"][:, h * 128:(h + 1) * 128].rearrange("(t p) e -> p t e", p=128),
                        reads=[C.r_av], writes=[rc_v])
                yield
                NQ = QB // 128
                items = [(j, c, kt) for j in range(S // QB) for c in range(2) for kt in range(NQ * (j + 1))]

                def front(it, idx):
                    j, c, kt = it
                    n0 = max(0, kt - NQ * j) * 128
                    ps, rps = p_st[idx % 2], rp_st[idx % 2]
                    p_, rp_ = ptc[idx % 2], rc_pt[idx % 2]
                    sch.op("pe", lambda e: e.matmul(
                        out=ps[:, n0:QB], lhsT=kTc[:, kt * 128:(kt + 1) * 128],
                        rhs=qTc[c][:, j * QB + n0:(j + 1) * QB], start=True, stop=True),
                        reads=[rc_k, rc_q], writes=[rps])
                    sch.op("act", lambda e: e.activation(out=p_[:, n0:QB], in_=ps[:, n0:QB], func=AF.Exp, scale=0.125),
                           reads=[rps], writes=[rp_])
                    if kt >= NQ * j:
                        sch.op("pool", lambda e: e.tensor_tensor(out=p_[:, n0:n0 + 128], in0=p_[:, n0:n0 + 128],
                                                                 in1=C.trib[:, :], op=ALU.mult), reads=[rp_], writes=[rp_])

                def back(it, idx):
                    j, c, kt = it
                    nk = NQ * (j + 1)
                    n0 = max(0, kt - NQ * j) * 128
                    p_, rp_ = ptc[idx % 2], rc_pt[idx % 2]
                    sch.op("pe", lambda e: e.matmul(out=p_o[:, n0:QB], lhsT=vtc[:, kt, :], rhs=p_[:, n0:QB],
                                                    start=(kt == 0), stop=(kt == nk - 1)),
                           reads=[rc_v, rp_], writes=[rp_ol])
                    sch.op("pe", lambda e: e.matmul(out=p_l[:, n0:QB], lhsT=C.onesb[:, :], rhs=p_[:, n0:QB],
                                                    start=False, stop=(kt == nk - 1), skip_group_check=True),
                           reads=[rp_], writes=[rp_ol])
                    if kt != nk - 1:
                        return
                    sch.op("act", lambda e: e.activation(out=c_rl[:, :], in_=p_l, func=AF.Ln),
                           reads=[rp_ol], writes=[rc_rl])
                    sch.op("act", lambda e: e.activation(out=c_rl[:, :], in_=c_rl[:, :], func=AF.Exp, scale=-1.0),
                           reads=[rc_rl], writes=[rc_rl])
                    sch.op("dve", lambda e: e.tensor_tensor(out=c_oc[c][:, :], in0=p_o, in1=c_rl[:, :], op=ALU.mult),
                           reads=[rp_ol, rc_rl], writes=[rc_oc[c]])
                    if c == 0:
                        return
                    qs = slice(j * QB, (j + 1) * QB)
                    sch.op("dve", lambda e: e.scalar_tensor_tensor(out=c_od[:, :], in0=c_oc[1][:, :], scalar=nlam[:, 0:1],
                                                                   in1=c_oc[0][:, :], op0=ALU.mult, op1=ALU.add),
                           reads=[rc_oc[0], rc_oc[1], rc_par], writes=[rc_od])
                    sch.op("pool", lambda e: e.tensor_tensor(out=c_sq[:, :], in0=c_od[:, :], in1=c_od[:, :], op=ALU.mult),
                           reads=[rc_od], writes=[rc_sq])
                    sch.op("pe", lambda e: e.matmul(out=p_ss, lhsT=C.onesf[:, :], rhs=c_sq[:, :], start=True, stop=True),
                           reads=[rc_sq], writes=[rp_ss])
                    sch.op("dve", lambda e: e.tensor_scalar(out=c_rs[:, :], in0=p_ss, scalar1=1.0 / 128, scalar2=1e-5,
                                                            op0=ALU.mult, op1=ALU.add), reads=[rp_ss], writes=[rc_rs])
                    sch.op("act", lambda e: e.activation(out=c_rs[:, :], in_=c_rs[:, :], func=AF.Ln),
                           reads=[rc_rs], writes=[rc_rs])
                    sch.op("act", lambda e: e.activation(out=c_rs[:, :], in_=c_rs[:, :], func=AF.Exp, scale=-0.5),
                           reads=[rc_rs], writes=[rc_rs])
                    sch.op("dve", lambda e: e.scalar_tensor_tensor(out=c_ya[:, :], in0=c_od[:, :], scalar=ag[:, h:h + 1],
                                                                   in1=c_rs[:, :], op0=ALU.mult, op1=ALU.mult),
                           reads=[rc_od, rc_rs, rc_par], writes=[rc_ya])
                    sch.dma("sp", d["mixT"][512 + h * 128:512 + (h + 1) * 128, qs], c_ya[:, :], reads=[rc_ya],
                            wacc=[C.r_mixT])

                front(items[0], idx0)
                for i in range(len(items)):
                    if i + 1 < len(items):
                        front(items[i + 1], idx0 + i + 1)
                    back(items[i], idx0 + i)
                    yield
                idx0 += len(items)

        hi = []
        lo = []
        pq = []
        cq = [("attn", c_gen())] if not globals().get("NO_ATTN", False) else []
        CRATE = globals().get("CRATE_", 2)
        rr = [0]

        def pull(q, idx):
            try:
                next(q[idx][1])
                return True
            except StopIteration:
                q.pop(idx)
                return False

        def pump(n):
            for _ in range(n):
                if hi:
                    rr[0] = (rr[0] + 1) % len(hi)
                    pull(hi, rr[0])
                if lo:
                    pull(lo, 0)
                if pq:
                    pull(pq, 0)
                for _ in range(CRATE):
                    if cq:
                        pull(cq, 0)

        def drain_tag(q, pred):
            i = 0
            while i < len(q):
                if pred(q[i][0]):
                    while pull(q, i):
                        pass
                else:
                    i += 1

        def chain(gc):
            blk, cc = divmod(gc, NCH)
            o = B[blk % 2]
            k = K_[gc % NK]
            cs = slice(cc * L, (cc + 1) * L)
            Q = k.Q[k.qfin]
            rQ = k.r_Q[k.qfin]
            wq = banks[5]
            for h in range(4):
                pr, hf = h // 2, h % 2
                sch.op("pe", lambda e, h=h, pr=pr: e.matmul(
                    out=wq[0:64, h * 64:(h + 1) * 64], lhsT=o.ARz[h][:, 0, cs], rhs=Hb[:, pr, :],
                    start=True, stop=False), reads=[o.r_AR[pr], r_Hb], writes=[rb[5]])
                sch.op("pe", lambda e, h=h, pr=pr, hf=hf: e.matmul(
                    out=wq[0:64, h * 64:(h + 1) * 64], lhsT=k.scs[:, h, 2, :], rhs=k.tok[:, pr, 2, hf, :],
                    start=False, stop=True), reads=[k.r_scs[pr], k.r_tok[pr]], writes=[rb[5]])
            sch.op("dve", lambda e: e.tensor_copy(out=Wt[:, :, :].rearrange("p h e -> p (h e)"), in_=wq[0:64, 0:256]),
                   reads=[rb[5]], writes=[r_Wt])
            pump(6)
            for h in range(4):
                sch.op("pe", lambda e, h=h: e.matmul(out=wq[0:64, h * 64:(h + 1) * 64], lhsT=Q[:, h, :],
                                                     rhs=Wt[:, h, :], start=True, stop=True),
                       reads=[rQ, r_Wt], writes=[rb[5]])
            sch.op("dve", lambda e: e.tensor_copy(out=Ut[:, :, :].rearrange("p h e -> p (h e)"), in_=wq[0:64, 0:256]),
                   reads=[rb[5]], writes=[r_Ut])
            pump(6)
            b6 = banks[6]
            for h in range(4):
                pr, hf = h // 2, h % 2
                oo = b6[:, 256 + h * 64:256 + (h + 1) * 64]
                sch.op("pe", lambda e, oo=oo, h=h, pr=pr: e.matmul(
                    out=oo, lhsT=k.tok[:, pr, 0, :, :].rearrange("p a e -> p (a e)"), rhs=Ut[:, h, :],
                    start=True, stop=False), reads=[k.r_tok[pr], r_Ut], writes=[rb[6]])
                sch.op("pe", lambda e, oo=oo, pr=pr, hf=hf: e.matmul(
                    out=oo, lhsT=k.tok[:, pr, 1, :, :].rearrange("p a e -> p (a e)"), rhs=k.tok[:, pr, 2, hf, :],
                    start=False, stop=True), reads=[k.r_tok[pr]], writes=[rb[6]])
            for h in range(4):
                pr, hf = h // 2, h % 2
                oo = b6[0:64, h * 64:(h + 1) * 64]
                sch.op("pe", lambda e, oo=oo, h=h, pr=pr: e.matmul(
                    out=oo, lhsT=o.ARz[h][:, 1, cs], rhs=Hb[:, pr, :], start=True, stop=False),
                    reads=[o.r_AR[pr], r_Hb], writes=[rb[6]])
                sch.op("pe", lambda e, oo=oo, h=h, pr=pr: e.matmul(
                    out=oo, lhsT=k.scs[:, h, 1, :], rhs=Ut[:, h, :], start=False, stop=False),
                    reads=[k.r_scs[pr], r_Ut], writes=[rb[6]])
                sch.op("pe", lambda e, oo=oo, h=h, pr=pr, hf=hf: e.matmul(
                    out=oo, lhsT=k.scs[:, h, 3, :], rhs=k.tok[:, pr, 2, hf, :], start=False, stop=True),
                    reads=[k.r_scs[pr], k.r_tok[pr]], writes=[rb[6]])
            for h in range(4):
                pr, pb = h // 2, (h % 2) * 64
                sch.op("dve", lambda e, h=h, pr=pr, pb=pb: e.scalar_tensor_tensor(
                    out=Hs[pb:pb + 64, pr, :], in0=Hs[pb:pb + 64, pr, :],
                    scalar=o.E1[pr][pb:pb + 64, cc * L + L - 1:cc * L + L],
                    in1=b6[pb:pb + 64, 256 + h * 64:256 + (h + 1) * 64], op0=ALU.mult, op1=ALU.add),
                    reads=[r_Hs, o.r_E1[pr], rb[6]], wacc=[r_Hs])
            sch.op("act", lambda e: e.copy(out=Hb[:, :, :], in_=Hs[:, :, :]), reads=[r_Hs], writes=[r_Hb])
            y_, ry_ = yv[gc % 2], r_yv[gc % 2]
            sch.op("act", lambda e: e.copy(out=y_[:, :, :].rearrange("p h e -> p (h e)"), in_=b6[0:64, 0:256]),
                   reads=[rb[6]], writes=[ry_])
            pump(6)

        NG = NBLK * NCH
        for _ in prep_gen(0):
            pass
        for _ in pre_gen(0):
            pass
        hi.append((1, pre_gen(1)))
        for gc in range(NG):
            blk, cc = divmod(gc, NCH)
            drain_tag(lo, lambda t: t[0] == "epi" and t[1] <= gc - 2)
            if gc + 2 < NG:
                if (gc + 2) // NCH != (gc + 1) // NCH or (gc + 2) % NCH == 0:
                    drain_tag(pq, lambda t: True)
                hi.append((gc + 2, pre_gen(gc + 2)))
            chain(gc)
            epi_early(gc)
            lo.append((("epi", gc), epi_gen(gc)))
            if cc == 0 and blk + 1 < NBLK:
                pq.append((("prep", blk + 1), prep_gen(blk + 1)))
            pump(6)
            drain_tag(hi, lambda t: t == gc + 1)
            if gc >= 1:
                drain_tag(lo, lambda t: t[0] == "epi" and t[1] <= gc - 1)
                for _ in epi_tail(gc - 1):
                    pass
        drain_tag(lo, lambda t: True)
        for _ in epi_tail(NG - 1):
            pass
        drain_tag(cq, lambda t: True)
        for j in range(2):
            sch.dma("sp", d["mixT"][256 + j * 128:256 + (j + 1) * 128, :], yrT[:, j, :], reads=[r_yrT],
                    wacc=[C.r_mixT])
```
